# Optimizing a Trainium2 kernel written in Bass

```python
import math
import jax
import jax.numpy as jnp
from jax import lax
import numpy as np

D_MODEL = 1024
BATCH = 8
SEQ = 4096
DEPTH = 2

D_PLE = 256
CONV_WIDTH = 4
NORM_EPS = 1e-6
RG_WIDTH = D_MODEL // 2
RG_BLOCKS = 8
RG_BLOCK = RG_WIDTH // RG_BLOCKS
RG_C = 8.0
GDN_HEADS = 8
GDN_DK = 128
GDN_DV = 128
GDN_KEY = GDN_HEADS * GDN_DK
GDN_VAL = GDN_HEADS * GDN_DV
GDN_CHUNK = 64
S5_WIDTH = D_MODEL // 2
S5_GROUP = 16
S5_GROUPS = S5_WIDTH // S5_GROUP
S5_STATE = 64
D_MIX = RG_WIDTH + GDN_VAL + S5_WIDTH
N_IN = 2 * RG_WIDTH + 2 * GDN_KEY + 2 * GDN_VAL + 2 * GDN_HEADS + 2 * S5_WIDTH

kernel_name = 'hymba_style_rglru_gdn_s5_hybrid'


def rms_norm(x, g):
    xf = x.astype(jnp.float32)
    y = xf * lax.rsqrt(jnp.mean(xf * xf, axis=-1, keepdims=True) + NORM_EPS)
    return (y * g.astype(jnp.float32)).astype(x.dtype)


def l2_normalize(x):
    return x * lax.rsqrt(jnp.sum(x * x, axis=-1, keepdims=True) + NORM_EPS)


def causal_dwconv(x, w, b=None):
    k_w, ch = w.shape
    y = lax.conv_general_dilated(x, w[:, None, :].astype(x.dtype), window_strides=(1,),
                                 padding=((k_w - 1, 0),), dimension_numbers=('NWC', 'WIO', 'NWC'),
                                 feature_group_count=ch)
    if b is not None:
        y = y + b.astype(y.dtype)
    return y


def linear_combine(left, right):
    a1, b1 = left
    a2, b2 = right
    return a1 * a2, a2 * b1 + b2


def split_columns(proj):
    sizes = [RG_WIDTH, RG_WIDTH, GDN_KEY, GDN_KEY, GDN_VAL, GDN_VAL, GDN_HEADS, GDN_HEADS, S5_WIDTH, S5_WIDTH]
    offsets = [int(o) for o in np.cumsum(sizes)[:-1]]
    return jnp.split(proj, offsets, axis=-1)


def rg_lru(x, w_a, b_a, w_x, b_x, lam):
    f32 = jnp.float32
    xf = x.astype(f32)
    b_, s_, _ = x.shape
    xb = xf.reshape(b_, s_, RG_BLOCKS, RG_BLOCK)
    r = jax.nn.sigmoid(jnp.einsum('bshi,hij->bshj', xb, w_a.astype(f32)).reshape(b_, s_, RG_WIDTH) + b_a.astype(f32))
    gi = jax.nn.sigmoid(jnp.einsum('bshi,hij->bshj', xb, w_x.astype(f32)).reshape(b_, s_, RG_WIDTH) + b_x.astype(f32))
    log_a = -RG_C * r * jax.nn.softplus(-lam.astype(f32))
    a = jnp.exp(log_a)
    inp = jnp.sqrt(-jnp.expm1(2.0 * log_a)) * (gi * xf)
    _, h = lax.associative_scan(linear_combine, (a, inp), axis=1)
    return h


def gated_delta_rule(q, k, v, g, beta):
    b_, s_, nh, dk = q.shape
    dv = v.shape[-1]
    c = GDN_CHUNK
    n = s_ // c

    def chunk(t):
        t = t.reshape((b_, n, c, nh) + t.shape[3:])
        return jnp.moveaxis(t, (1, 3), (0, 2))

    qc, kc, vc, gc, bc = chunk(q), chunk(k), chunk(v), chunk(g), chunk(beta)
    gcum = jnp.cumsum(gc, axis=-1)
    causal = jnp.tril(jnp.ones((c, c), dtype=bool))
    strict = jnp.tril(jnp.ones((c, c), dtype=bool), -1)
    diff = gcum[..., :, None] - gcum[..., None, :]
    decay = jnp.where(causal, jnp.exp(jnp.where(causal, diff, 0.0)), 0.0)
    kb = kc * bc[..., None]
    a_kk = jnp.where(strict, jnp.einsum('nbhid,nbhjd->nbhij', kb, kc) * decay, 0.0)
    lhs = a_kk + jnp.eye(c, dtype=jnp.float32)
    rhs = jnp.concatenate([vc * bc[..., None], kb * jnp.exp(gcum)[..., None]], axis=-1)
    sol = lax.linalg.triangular_solve(lhs, rhs, left_side=True, lower=True, unit_diagonal=True)
    u_c, w_c = sol[..., :dv], sol[..., dv:]
    a_qk = jnp.einsum('nbhid,nbhjd->nbhij', qc, kc) * decay

    def step(state, inp):
        q_i, k_i, u_i, w_i, aqk_i, gc_i = inp
        v_new = u_i - jnp.einsum('bhcd,bhde->bhce', w_i, state)
        o_i = jnp.einsum('bhcd,bhde->bhce', q_i * jnp.exp(gc_i)[..., None], state) + jnp.einsum('bhij,bhje->bhie', aqk_i, v_new)
        g_last = gc_i[..., -1]
        k_dec = k_i * jnp.exp(g_last[..., None] - gc_i)[..., None]
        state = state * jnp.exp(g_last)[..., None, None] + jnp.einsum('bhcd,bhce->bhde', k_dec, v_new)
        return state, o_i

    state0 = jnp.zeros((b_, nh, dk, dv), jnp.float32)
    _, o = lax.scan(step, state0, (qc, kc, u_c, w_c, a_qk, gcum))
    return jnp.moveaxis(o, (0, 2), (1, 3)).reshape(b_, s_, nh, dv)


def gdn_branch(q, k, v, z, b, a, conv_w, a_log, dt_bias, norm_g):
    f32 = jnp.float32
    b_, s_, _ = q.shape
    qkv = jax.nn.silu(causal_dwconv(jnp.concatenate([q, k, v], axis=-1), conv_w)).astype(f32)
    q, k, v = jnp.split(qkv, [GDN_KEY, 2 * GDN_KEY], axis=-1)
    q = l2_normalize(q.reshape(b_, s_, GDN_HEADS, GDN_DK)) * (GDN_DK ** -0.5)
    k = l2_normalize(k.reshape(b_, s_, GDN_HEADS, GDN_DK))
    v = v.reshape(b_, s_, GDN_HEADS, GDN_DV)
    beta = jax.nn.sigmoid(b.astype(f32))
    g = -jnp.exp(a_log.astype(f32)) * jax.nn.softplus(a.astype(f32) + dt_bias.astype(f32))
    o = gated_delta_rule(q, k, v, g, beta)
    o = rms_norm(o, norm_g) * jax.nn.silu(z.astype(f32).reshape(b_, s_, GDN_HEADS, GDN_DV))
    return o.reshape(b_, s_, GDN_VAL)


def s5_ssm(u, a_re, a_im, b_re, b_im, c_re, c_im, d, log_dt):
    f32 = jnp.float32
    lam = lax.complex(a_re.astype(f32), a_im.astype(f32))
    dt = jnp.exp(log_dt.astype(f32))[:, None]
    lam_bar = jnp.exp(lam * dt)
    b_bar = ((lam_bar - 1.0) / lam)[..., None] * lax.complex(b_re.astype(f32), b_im.astype(f32))
    c_mat = lax.complex(c_re.astype(f32), c_im.astype(f32))
    d_g = d.astype(f32).reshape(S5_GROUPS, S5_GROUP)

    def one(u_b):
        ub = u_b.reshape(u_b.shape[0], S5_GROUPS, S5_GROUP)
        bu = jnp.einsum('gnc,sgc->sgn', b_bar, ub.astype(jnp.complex64))
        a = jnp.broadcast_to(lam_bar, bu.shape)
        _, xs = lax.associative_scan(linear_combine, (a, bu), axis=0)
        y = jnp.real(jnp.einsum('gcn,sgn->sgc', c_mat, xs)) + d_g * ub
        return y.reshape(u_b.shape)

    return lax.map(one, u.astype(f32))


def setup_inputs(seed: int = 0) -> dict:
    key = jax.random.key(seed)
    ks = jax.random.split(key, 32)
    f32 = jnp.float32
    nrm = lambda k, shape, s: jax.random.normal(k, shape, f32) * s
    L = DEPTH
    u_rg = jax.random.uniform(ks[10], (L, RG_WIDTH), f32, 0.9, 0.999)
    a_rg = u_rg ** (1.0 / RG_C)
    dt_gdn = jnp.exp(jax.random.uniform(ks[13], (L, GDN_HEADS), f32, math.log(1e-3), math.log(1e-1)))
    n_idx = jnp.arange(S5_STATE, dtype=f32)
    return {
        'x': nrm(ks[0], (BATCH, SEQ, D_MODEL), 1.0),
        'p': nrm(ks[1], (DEPTH, BATCH, SEQ, D_PLE), 1.0),
        'norm_g': 1.0 + nrm(ks[2], (L, D_MODEL), 0.02),
        'w_in': nrm(ks[3], (L, D_MODEL, N_IN), D_MODEL ** -0.5),
        'rg_conv_w': nrm(ks[4], (L, CONV_WIDTH, RG_WIDTH), CONV_WIDTH ** -0.5),
        'rg_conv_b': nrm(ks[5], (L, RG_WIDTH), 0.01),
        'rg_w_a': nrm(ks[6], (L, RG_BLOCKS, RG_BLOCK, RG_BLOCK), RG_BLOCK ** -0.5),
        'rg_b_a': nrm(ks[7], (L, RG_WIDTH), 0.01),
        'rg_w_x': nrm(ks[8], (L, RG_BLOCKS, RG_BLOCK, RG_BLOCK), RG_BLOCK ** -0.5),
        'rg_b_x': nrm(ks[9], (L, RG_WIDTH), 0.01),
        'rg_lambda': jnp.log(a_rg) - jnp.log1p(-a_rg),
        'gdn_conv_w': nrm(ks[11], (L, CONV_WIDTH, 2 * GDN_KEY + GDN_VAL), CONV_WIDTH ** -0.5),
        'gdn_a_log': jnp.log(jax.random.uniform(ks[12], (L, GDN_HEADS), f32, 1.0, 16.0)),
        'gdn_dt_bias': dt_gdn + jnp.log(-jnp.expm1(-dt_gdn)),
        'gdn_norm_g': 1.0 + nrm(ks[14], (L, GDN_DV), 0.02),
        's5_a_re': -0.5 + nrm(ks[15], (L, S5_GROUPS, S5_STATE), 0.01),
        's5_a_im': jnp.pi * n_idx + nrm(ks[16], (L, S5_GROUPS, S5_STATE), 0.01),
        's5_b_re': nrm(ks[17], (L, S5_GROUPS, S5_STATE, S5_GROUP), (2.0 * S5_GROUP) ** -0.5),
        's5_b_im': nrm(ks[18], (L, S5_GROUPS, S5_STATE, S5_GROUP), (2.0 * S5_GROUP) ** -0.5),
        's5_c_re': nrm(ks[19], (L, S5_GROUPS, S5_GROUP, S5_STATE), (2.0 * S5_STATE) ** -0.5),
        's5_c_im': nrm(ks[20], (L, S5_GROUPS, S5_GROUP, S5_STATE), (2.0 * S5_STATE) ** -0.5),
        's5_d': nrm(ks[21], (L, S5_WIDTH), 1.0),
        's5_log_dt': jax.random.uniform(ks[22], (L, S5_GROUPS), f32, math.log(1e-3), math.log(1e-1)),
        's5_w_glu': nrm(ks[23], (L, S5_WIDTH, S5_WIDTH), S5_WIDTH ** -0.5),
        's5_b_glu': nrm(ks[24], (L, S5_WIDTH), 0.01),
        'w_out': nrm(ks[25], (L, D_MIX, D_MODEL), D_MIX ** -0.5),
        'ple_norm_g': 1.0 + nrm(ks[26], (L, D_MODEL), 0.02),
        'ple_w_gate': nrm(ks[27], (L, D_MODEL, D_MODEL), D_MODEL ** -0.5),
        'ple_w_proj': nrm(ks[28], (L, D_PLE, D_MODEL), D_PLE ** -0.5),
        'final_norm_g': 1.0 + nrm(ks[29], (D_MODEL,), 0.02),
    }


def reference(x, p, norm_g, w_in, rg_conv_w, rg_conv_b, rg_w_a, rg_b_a, rg_w_x, rg_b_x, rg_lambda,
              gdn_conv_w, gdn_a_log, gdn_dt_bias, gdn_norm_g, s5_a_re, s5_a_im, s5_b_re, s5_b_im,
              s5_c_re, s5_c_im, s5_d, s5_log_dt, s5_w_glu, s5_b_glu, w_out, ple_norm_g, ple_w_gate,
              ple_w_proj, final_norm_g):
    dt = x.dtype
    f32 = jnp.float32
    h = x
    for i in range(DEPTH):
        hn = rms_norm(h, norm_g[i])
        proj = hn @ w_in[i]
        rg_x, rg_gate, gq, gk, gv, gz, gb, ga, s5_u, s5_gate = split_columns(proj)
        xr = causal_dwconv(rg_x, rg_conv_w[i], rg_conv_b[i])
        y_rg = rg_lru(xr, rg_w_a[i], rg_b_a[i], rg_w_x[i], rg_b_x[i], rg_lambda[i]) * jax.nn.silu(rg_gate.astype(f32))
        y_gdn = gdn_branch(gq, gk, gv, gz, gb, ga, gdn_conv_w[i], gdn_a_log[i], gdn_dt_bias[i], gdn_norm_g[i])
        y_s5 = s5_ssm(s5_u, s5_a_re[i], s5_a_im[i], s5_b_re[i], s5_b_im[i], s5_c_re[i], s5_c_im[i], s5_d[i], s5_log_dt[i])
        z5 = jax.nn.gelu(y_s5)
        y_s5 = z5 * jax.nn.sigmoid(z5 @ s5_w_glu[i].astype(f32) + s5_b_glu[i].astype(f32))
        y_s5 = y_s5 * jax.nn.silu(s5_gate.astype(f32))
        mix = jnp.concatenate([y_rg.astype(dt), y_gdn.astype(dt), y_s5.astype(dt)], axis=-1)
        h = h + mix @ w_out[i]
        gate = jax.nn.sigmoid(rms_norm(h, ple_norm_g[i]) @ ple_w_gate[i])
        h = h + gate * (p[i] @ ple_w_proj[i])
    return rms_norm(h, final_norm_g)
```

```python
import math
import numpy as np
import concourse.bass as bass
import concourse.mybir as mybir
from concourse.bass_utils import run_bass_kernel_spmd
from contextlib import ExitStack

F32 = mybir.dt.float32
BF16 = mybir.dt.bfloat16
I32 = mybir.dt.int32
AF = mybir.ActivationFunctionType
ALU = mybir.AluOpType
AX = mybir.AxisListType

D = 1024
NIN = 6160
DMIX = 2048
DPLE = 256
T = 256
GC = 64
L5 = 8
CB = T // L5
EPS = 1e-6
TWO_PI = 2.0 * math.pi

ENG = ['pe', 'dve', 'act', 'pool', 'sp']

PARAM_SHAPES = lambda L: [
    ('norm_g', [L, 1024]), ('w_in', [L, 1024, 6160]), ('rg_conv_w', [L, 4, 512]), ('rg_conv_b', [L, 512]),
    ('rg_w_a', [L, 8, 64, 64]), ('rg_b_a', [L, 512]), ('rg_w_x', [L, 8, 64, 64]), ('rg_b_x', [L, 512]),
    ('rg_lambda', [L, 512]), ('gdn_conv_w', [L, 4, 3072]), ('gdn_a_log', [L, 8]), ('gdn_dt_bias', [L, 8]),
    ('gdn_norm_g', [L, 128]), ('s5_a_re', [L, 32, 64]), ('s5_a_im', [L, 32, 64]), ('s5_b_re', [L, 32, 64, 16]),
    ('s5_b_im', [L, 32, 64, 16]), ('s5_c_re', [L, 32, 16, 64]), ('s5_c_im', [L, 32, 16, 64]), ('s5_d', [L, 512]),
    ('s5_log_dt', [L, 32]), ('s5_w_glu', [L, 512, 512]), ('s5_b_glu', [L, 512]), ('w_out', [L, 2048, 1024]),
    ('ple_norm_g', [L, 1024]), ('ple_w_gate', [L, 1024, 1024]), ('ple_w_proj', [L, 256, 1024]),
    ('final_norm_g', [1024]),
]


class Buf:
    def __init__(self, t, k):
        self.t = t
        self.k = k


def _keys(lst):
    out = []
    for r in lst:
        if isinstance(r, Buf):
            if isinstance(r.k, (list, tuple)):
                out.extend(r.k)
            else:
                out.append(r.k)
        elif isinstance(r, (list, tuple, set)):
            out.extend(_keys(r))
        else:
            out.append(r)
    return out


class Prog:
    def __init__(self, nc, es, same_engine_sync=True):
        self.nc = nc
        self.es = es
        self.ops = {e: [] for e in ENG}
        self.esem = {e: es.enter_context(nc.semaphore('s_' + e)) for e in ENG}
        self.ecnt = {e: 0 for e in ENG}
        self.waited = {e: {} for e in ENG}
        self.writers = {}
        self.readers = {}
        self.dsems = {}
        self.dcnt = {}
        self.dsem_name = {}
        self.same_engine_sync = same_engine_sync

    def sbuf(self, name, shape, dt):
        return Buf(self.es.enter_context(self.nc.sbuf_tensor(name, list(shape), dt)), name)

    def _deps(self, eng, reads, writes):
        need = {}

        def add(ev):
            s, v = ev
            k = id(s)
            if k not in need or need[k][1] < v:
                need[k] = (s, v)

        for r in reads:
            for ev in self.writers.get(r, {}).values():
                add(ev)
        for w in writes:
            for ev in self.writers.get(w, {}).values():
                add(ev)
            for ev in self.readers.get(w, {}).values():
                add(ev)
        waits = []
        for k, (s, v) in need.items():
            if s is self.esem[eng] and (eng == 'pe' or not self.same_engine_sync):
                continue
            nm = self.dsem_name.get(k)
            if nm is not None and self.dsems[nm] is s:
                v = max(v, self.dcnt[nm])
            if self.waited[eng].get(k, 0) >= v:
                continue
            self.waited[eng][k] = v
            waits.append((s, v))
        return waits

    def _commit(self, ev, reads, writes):
        k = id(ev[0])
        for r in reads:
            d = self.readers.setdefault(r, {})
            if k not in d or d[k][1] < ev[1]:
                d[k] = ev
        for w in writes:
            d = self.writers.setdefault(w, {})
            if k not in d or d[k][1] < ev[1]:
                d[k] = ev
            self.readers[w] = {}

    def op(self, eng, fn, reads=(), writes=()):
        reads = _keys(reads)
        writes = _keys(writes)
        waits = self._deps(eng, reads, writes)
        if self.ecnt[eng] >= 16000:
            self.esem[eng] = self.es.enter_context(self.nc.semaphore('s_%s_%d' % (eng, len(self.ops[eng]))))
            self.ecnt[eng] = 0
        self.ecnt[eng] += 1
        ev = (self.esem[eng], self.ecnt[eng])
        self.ops[eng].append((waits, fn, ev, 1))
        self._commit(ev, reads, writes)

    def dma(self, q, fn, reads, writes, sem_name, dram_w=()):
        reads = _keys(reads)
        writes = _keys(writes)
        waits = self._deps(q, reads, writes)
        writes = writes + _keys(dram_w)
        if sem_name not in self.dsems:
            self.dsems[sem_name] = self.es.enter_context(self.nc.semaphore('d_' + sem_name))
            self.dcnt[sem_name] = 0
            self.dsem_name[id(self.dsems[sem_name])] = sem_name
        if self.dcnt[sem_name] >= 16000:
            self.dsems[sem_name] = self.es.enter_context(
                self.nc.semaphore('d_%s_%d' % (sem_name, len(self.ops[q]))))
            self.dcnt[sem_name] = 0
            self.dsem_name[id(self.dsems[sem_name])] = sem_name
        self.dcnt[sem_name] += 16
        ev = (self.dsems[sem_name], self.dcnt[sem_name])
        self.ops[q].append((waits, fn, ev, 16))
        self._commit(ev, reads, writes)

    def final_wait(self, eng, resources):
        resources = _keys(resources)
        waits = self._deps(eng, resources, resources)
        self.ops[eng].append((waits, None, None, 0))

    def emit(self):
        nc = self.nc
        with nc.Block() as block:
            for e, deco in [('pe', block.tensor), ('dve', block.vector), ('act', block.scalar),
                            ('pool', block.gpsimd), ('sp', block.sync)]:
                ops = self.ops[e]

                @deco
                def _(engine, ops=ops):
                    for waits, fn, ev, inc in ops:
                        for (s, v) in waits:
                            engine.wait_ge(s, v)
                        if fn is None:
                            continue
                        ins = fn(engine)
                        ins.then_inc(ev[0], inc)

    def mm(self, out, lhsT, rhs, start, stop, R, W, **kw):
        self.op('pe', lambda e: e.matmul(out, lhsT=lhsT, rhs=rhs, start=start, stop=stop, **kw), R, W)

    def tr(self, out, in_, ident, R, W):
        self.op('pe', lambda e: e.transpose(out, in_, ident), R, W)

    def act(self, out, in_, func, R, W, scale=None, bias=None):
        kw = {}
        if scale is not None:
            kw['scale'] = scale
        if bias is not None:
            kw['bias'] = bias
        self.op('act', lambda e: e.activation(out=out, in_=in_, func=func, **kw), R, W)

    def tt(self, eng, out, in0, in1, op, R, W):
        self.op(eng, lambda e: e.tensor_tensor(out=out, in0=in0, in1=in1, op=op), R, W)

    def ts(self, eng, out, in0, s1, op0, R, W, s2=None, op1=None):
        if op1 is None:
            if eng == 'pool':
                s2, op1 = (0.0, ALU.add) if op0 == ALU.mult else (1.0, ALU.mult)
                self.op(eng, lambda e: e.tensor_scalar(out=out, in0=in0, scalar1=s1, scalar2=s2, op0=op0, op1=op1), R, W)
            else:
                self.op(eng, lambda e: e.tensor_scalar(out=out, in0=in0, scalar1=s1, scalar2=None, op0=op0), R, W)
        else:
            self.op(eng, lambda e: e.tensor_scalar(out=out, in0=in0, scalar1=s1, scalar2=s2, op0=op0, op1=op1), R, W)

    def stt(self, out, in0, scalar, in1, op0, op1, R, W):
        self.op('dve', lambda e: e.scalar_tensor_tensor(out=out, in0=in0, scalar=scalar, in1=in1, op0=op0, op1=op1), R, W)

    def cp(self, eng, out, in_, R, W):
        if eng == 'act':
            self.op('act', lambda e: e.activation(out=out, in_=in_, func=AF.Copy), R, W)
        else:
            self.op(eng, lambda e: e.tensor_copy(out=out, in_=in_), R, W)

    def memset(self, eng, ap, val, W):
        self.op(eng, lambda e: e.memset(ap, val), [], W)

    def load(self, out, in_, W, sem=None, R=(), q='sp', slow=False):
        sem = 'ld_' + _keys(W)[0]
        if slow:
            self.dma(q, lambda e: e.dma_start(out=out, in_=in_, allow_slow_non_contiguous=True), R, W, sem)
        else:
            self.dma(q, lambda e: e.dma_start(out=out, in_=in_), R, W, sem)

    def store(self, out, in_, src, dram_key, q='act'):
        sem = 'st_' + _keys([src])[0]
        self.dma(q, lambda e: e.dma_start(out=out, in_=in_), [src], [], sem, dram_w=[dram_key])


def bc(ap, shape):
    return ap.broadcast_to(list(shape))


def build_program(S, depth, use_rg=True, use_gdn=True, use_s5=True, gdn_stop=99, same_engine_sync=True):
    nc = bass.Bass("TRN2", target_bir_lowering=False)
    NCH = S // T
    assert S % T == 0

    def din(name, shape):
        return nc.dram_tensor(name, list(shape), F32, kind="ExternalInput").ap()

    x_d = din("x", [S, D])
    p_d = din("p", [depth, S, DPLE])
    prm = {name: din(name, shape) for name, shape in PARAM_SHAPES(depth)}
    out_d = nc.dram_tensor("out", [S, D], F32, kind="ExternalOutput").ap()
    win_s = [nc.dram_tensor("win_s%d" % l, [24, 128, 8, 256], BF16, kind="Internal").ap() for l in range(depth)]
    wba_s = [nc.dram_tensor("wba_s%d" % l, [128, 8, 16], BF16, kind="Internal").ap() for l in range(depth)]
    wout_s = [nc.dram_tensor("wout_s%d" % l, [8, 128, 16, 128], BF16, kind="Internal").ap() for l in range(depth)]
    wgate_s = [nc.dram_tensor("wgate_s%d" % l, [8, 128, 8, 128], BF16, kind="Internal").ap() for l in range(depth)]
    hscr = nc.dram_tensor("hscr", [8, 128, S], F32, kind="Internal").ap()

    with ExitStack() as es:
        P = Prog(nc, es, same_engine_sync=same_engine_sync)
        sb = P.sbuf

        psd = [es.enter_context(nc.psum_tensor("psd%d" % i, [128, 1024], F32)) for i in range(4)]

        def ps(bank, c0=0, w=512, p0=0, p1=128):
            base = (bank % 2) * 512 + c0
            ap = psd[bank // 2][p0:p1, base:base + w]
            keys = ['psb%d' % b for b in range(bank + c0 // 512, bank + (c0 + w - 1) // 512 + 1)]
            return ap, keys

        def psbf(bank, c0, w, p0=0, p1=128):
            t = psd[bank // 2][p0:p1, (bank % 2) * 512:(bank % 2) * 512 + 512].bitcast(BF16)
            return t[:, c0:c0 + w], ['psb%d' % bank]

        ident = sb('ident', [128, 128], F32)
        identb = sb('identb', [128, 128], BF16)
        onesb = sb('onesb', [128, 128], BF16)
        U64 = sb('U64', [64, 64], F32)
        Mst = sb('Mst', [64, 64], F32)
        ones64 = sb('ones64', [64, 64], F32)
        sel63 = sb('sel63', [64, 128], F32)
        P.memset('pool', ident.t[:], 1.0, [ident])
        P.op('pool', lambda e: e.affine_select(out=ident.t[:], in_=ident.t[:], pattern=[[-1, 128]], compare_op=ALU.is_equal,
                                               fill=0.0, base=0, channel_multiplier=1), [ident], [ident])
        P.cp('pool', identb.t[:], ident.t[:], [ident], [identb])
        P.memset('pool', onesb.t[:], 1.0, [onesb])
        P.memset('pool', U64.t[:], 1.0, [U64])
        P.op('pool', lambda e: e.affine_select(out=U64.t[:], in_=U64.t[:], pattern=[[1, 64]], compare_op=ALU.is_ge,
                                               fill=0.0, base=0, channel_multiplier=-1), [U64], [U64])
        P.memset('pool', Mst.t[:], 1.0, [Mst])
        P.op('pool', lambda e: e.affine_select(out=Mst.t[:], in_=Mst.t[:], pattern=[[-1, 64]], compare_op=ALU.is_gt,
                                               fill=0.0, base=0, channel_multiplier=1), [Mst], [Mst])
        Ust = sb('Ust', [64, 64], F32)
        P.memset('pool', Ust.t[:], 1.0, [Ust])
        P.op('pool', lambda e: e.affine_select(out=Ust.t[:], in_=Ust.t[:], pattern=[[1, 64]], compare_op=ALU.is_gt,
                                               fill=0.0, base=0, channel_multiplier=-1), [Ust], [Ust])
        P.memset('pool', ones64.t[:], 1.0, [ones64])
        P.memset('pool', sel63.t[:], 1.0, [sel63])
        P.op('pool', lambda e: e.affine_select(out=sel63.t[:], in_=sel63.t[:], pattern=[[0, 128]], compare_op=ALU.is_equal,
                                               fill=0.0, base=-63, channel_multiplier=1), [sel63], [sel63])

        stg = [sb('stg%d' % i, [128, 1536], F32) for i in range(2)]
        G = [sb('G%d' % i, [128, 1024], F32) for i in range(4)]
        stgb = [Buf(G[i].t[:].bitcast(BF16), G[i].k) for i in range(2)]

        colsB = sb('colsB', [128, 72], F32)
        gcw = sb('gcw', [128, 96], F32)
        rgc = sb('rgc', [128, 16], F32)
        rgWa = sb('rgWa', [128, 4, 128], BF16)
        rgWx = sb('rgWx', [128, 4, 128], BF16)
        wglu = sb('wglu', [128, 4, 512], BF16)
        wproj = sb('wproj', [128, 2, 1024], BF16)
        negA = sb('negA', [64, 8], F32)
        dtb = sb('dtb', [64, 8], F32)
        RB = dict(norm_g=0, ple_norm_g=8, rg_conv_b=16, rg_b_a=20, rg_b_x=24, rg_lambda=28, s5_d=32, s5_b_glu=36,
                  gdn_norm_g=40, rg_conv_w=41, final=57)

        KT = sb('KT', [128, 4, 8, 128], BF16)
        EB = sb('EB', [128, 4, 8, 2, 128], BF16)
        Ctab = sb('Ctab', [128, 8, 2, 16, 32], BF16)
        Dcos = sb('Dcos', [128, 16, CB], F32)
        Dsin = sb('Dsin', [128, 16, CB], F32)
        Rho = sb('Rho', [128, 16, CB], F32)
        rhoc = sb('rhoc', [128, 16], F32)

        hT = sb('hT', [128, 8, T], F32)
        hnT = sb('hnT', [128, 8, T], BF16)
        sqb = Buf(G[3].t[:].bitcast(BF16).rearrange("p (k t) -> p k t", k=8), G[3].k)
        rstd = sb('rstd', [128, T], F32)
        mixT = sb('mixT', [128, 16, T], BF16)
        raw = sb('raw', [128, 24, 3 + T], BF16)
        histrg = sb('histrg', [128, 4, 3], BF16)
        histg = sb('histg', [128, 24, 3], BF16)
        gate = sb('gate', [128, 8, T], BF16)
        u5 = sb('u5', [128, 4, T], BF16)
        NRING = 5
        wring = [sb('wring%d' % i, [128, 2048], BF16) for i in range(NRING)]
        wba = sb('wba', [128, 8, 16], BF16)
        dg = [sb('dg%d' % i, [128, 128], BF16) for i in range(4)]
        tf = [sb('tf%d' % i, [128, T], F32) for i in range(8)]
        tb = [sb('tb%d' % i, [128, T], BF16) for i in range(2)]
        rgcar = sb('rgcar', [128, 4], F32)
        pT = sb('pT', [128, 2, T], BF16)
        XPre = sb('XPre', [128, 16, CB + 1], F32)
        XPim = sb('XPim', [128, 16, CB + 1], F32)
        XPb = sb('XPb', [128, 2, 16, CB], BF16)
        z5b = sb('z5b', [128, 4, T], BF16)
        s5big = [Buf(G[i // 2].t[:, (i % 2) * 512:(i % 2) * 512 + 512], G[i // 2].k) for i in range(6)]
        qn = sb('qn', [128, 8, T], BF16)
        kn = sb('kn', [128, 8, T], BF16)
        vs = sb('vs', [128, 8, T], BF16)
        Sst = sb('Sst', [128, 8, 128], F32)
        Sb = sb('Sb', [128, 8, 128], BF16)
        NI = T // GC
        gsm = {n: sb('g_' + n, [64, NI * 8], F32) for n in ['bt', 'nbt', 'xa', 'ex', 'sp', 'gp', 'g', 'gcum', 'egc', 'dgl', 'kds', 'bge']}
        gsm.update({n: sb('g_' + n, [64, 8], F32) for n in ['ss', 'ln', 'rs']})
        egl = sb('egl', [128, NI * 8], F32)
        gw = [sb('gw%d' % i, [64, 8, 64], F32) for i in range(7)]
        aqkT = sb('aqkT', [64, 8, 64], BF16)
        gx = [Buf(G[i].t[0:64, :].rearrange("p (h e) -> p h e", h=8), G[i].k) for i in range(4)]
        kdec = sb('kdec', [64, 8, 128], BF16)
        vnew = sb('vnew', [64, 8, 128], BF16)
        wTb = sb('wTb', [128, 8, 64], BF16)

        s5t = {n: sb('s5_' + n, [128, 16], F32) for n in
               ['are', 'aim', 'ldt', 'dt', 'ard', 'th', 'cr', 'den', 'inv', 'cfr', 'cfi', 't0', 't1', 'tqb']}
        hTf = hT.t[:].rearrange("p k t -> p (k t)")
        Lre = Buf(hTf[:, 0:144].rearrange("p (j i) -> p j i", j=9), hT.k)
        Lim = Buf(hTf[:, 256:400].rearrange("p (j i) -> p j i", j=9), hT.k)
        jv = sb('jv', [128, 8, 16], F32)
        jvi = Buf(G[2].t[:, 0:128].bitcast(I32).rearrange("p (j i) -> p j i", j=8), G[2].k)
        mask2 = sb('mask2', [128, 2, 16], F32)
        s5int = Buf(G[3].t[:, 0:512].bitcast(I32), G[3].k)
        cvi = Buf(G[3].t[:, 512:1024].bitcast(I32).rearrange("p (i c) -> p i c", i=16), G[3].k)
        knf = kn.t[:].rearrange("p k t -> p (k t)").bitcast(F32)
        qnf = qn.t[:].rearrange("p k t -> p (k t)").bitcast(F32)
        vsf = vs.t[:].rearrange("p k t -> p (k t)").bitcast(F32)
        hnf = hnT.t[:].rearrange("p k t -> p (k t)").bitcast(F32)
        v3_ = lambda ap, a: ap.rearrange("p (a b) -> p a b", a=a)
        bre = Buf(v3_(vsf[:, 0:256], 16), vs.k)
        bim = Buf(v3_(vsf[:, 256:512], 16), vs.k)
        cre = Buf(v3_(vsf[:, 512:768], 16), vs.k)
        cim = Buf(v3_(vsf[:, 768:1024], 16), vs.k)
        Bre = Buf(v3_(qnf[:, 0:256], 16), qn.k)
        Bim = Buf(v3_(qnf[:, 256:512], 16), qn.k)
        Cbr = Buf(v3_(qnf[:, 512:1024], 16), qn.k)
        Cbi = Buf(v3_(knf[:, 0:512], 16), kn.k)
        cv = Buf(v3_(knf[:, 512:1024], 16), kn.k)
        maskW = Buf(hnf[:, 0:512].rearrange("p (a g c) -> p a g c", a=4, g=8), hnT.k)
        P.memset('pool', mask2.t[:], 0.0, [mask2])
        P.memset('pool', mask2.t[0:64, 0, :], 1.0, [mask2])
        P.memset('pool', mask2.t[64:128, 1, :], 1.0, [mask2])
        P.op('pool', lambda e: e.iota(jvi.t[:], pattern=[[1, 8], [0, 16]], base=1, channel_multiplier=0), [], [jvi])
        P.cp('pool', jv.t[:], jvi.t[:], [jvi], [jv])

        def load_layer_consts(l):
            sA = stg[0]
            P.load(sA.t[0:96, 0:128], prm['gdn_conv_w'][l].rearrange("k (j p) -> (k j) p", p=128), [sA], 'stg0')
            o, kk = ps(7, 0, 96)
            P.tr(o, sA.t[0:96, 0:128], ident.t[0:96, 0:96], [sA, ident], kk)
            P.cp('dve', gcw.t[:], o, kk, [gcw])
            sBt = stg[1]
            rows = [('norm_g', prm['norm_g'][l], 8), ('ple_norm_g', prm['ple_norm_g'][l], 8), ('rg_conv_b', prm['rg_conv_b'][l], 4),
                    ('rg_b_a', prm['rg_b_a'][l], 4), ('rg_b_x', prm['rg_b_x'][l], 4), ('rg_lambda', prm['rg_lambda'][l], 4),
                    ('s5_d', prm['s5_d'][l], 4), ('s5_b_glu', prm['s5_b_glu'][l], 4), ('gdn_norm_g', prm['gdn_norm_g'][l], 1)]
            for name, ap, n in rows:
                r0 = RB[name]
                P.load(sBt.t[r0:r0 + n, 0:128], ap.rearrange("(k p) -> k p", p=128), [sBt], 'stg1')
            P.load(sBt.t[41:57, 0:128], prm['rg_conv_w'][l].rearrange("k (j p) -> (k j) p", p=128), [sBt], 'stg1')
            P.load(sBt.t[57:65, 0:128], prm['final_norm_g'].rearrange("(k p) -> k p", p=128), [sBt], 'stg1')
            o, kk = ps(7, 128, 65)
            P.tr(o, sBt.t[0:65, 0:128], ident.t[0:65, 0:65], [sBt, ident], kk)
            P.cp('dve', colsB.t[:, 0:65], o, kk, [colsB])
            z = s5t['t0'].t[:, 0:4]
            acc = s5t['t1'].t[:, 0:4]
            P.act(z, colsB.t[:, 28:32], AF.Exp, [colsB], [s5t['t0']], scale=-1.0)
            P.ts('dve', acc, z, -1.0 / 9.0, ALU.mult, [s5t['t0']], [s5t['t1']], s2=1.0 / 8.0, op1=ALU.add)
            for k in range(7, 0, -1):
                P.tt('dve', acc, acc, z, ALU.mult, [s5t['t0'], s5t['t1']], [s5t['t1']])
                P.ts('dve', acc, acc, -1.0, ALU.mult, [s5t['t1']], [s5t['t1']], s2=1.0 / k, op1=ALU.add)
            P.tt('dve', acc, acc, z, ALU.mult, [s5t['t0'], s5t['t1']], [s5t['t1']])
            P.ts('dve', rgc.t[:, 0:4], acc, -8.0, ALU.mult, [s5t['t1']], [rgc])
            P.ts('dve', rgc.t[:, 4:8], acc, -16.0, ALU.mult, [s5t['t1']], [rgc])
            P.ts('dve', rgc.t[:, 8:16], colsB.t[:, 20:28], -1.0, ALU.mult, [colsB], [rgc])
            for (src, dst) in [(prm['rg_w_a'][l], rgWa), (prm['rg_w_x'][l], rgWx)]:
                s = stg[0]
                P.memset('pool', s.t[:, 0:512], 0.0, [s])
                sv = s.t[:, 0:512].rearrange("p (t j) -> p t j", t=4)
                for h2 in range(2):
                    P.load(sv[h2 * 64:(h2 + 1) * 64, :, h2 * 64:(h2 + 1) * 64],
                           src.rearrange("(t h2) i j -> h2 i t j", h2=2)[h2], [s], 'stg0')
                P.cp('pool', dst.t[:], sv, [s], [dst])
            for hh in range(2):
                s = stg[hh]
                P.load(s.t[:, 0:1024].rearrange("p (k n) -> p k n", k=2),
                       prm['s5_w_glu'][l].rearrange("(k p) n -> p k n", p=128)[:, 2 * hh:2 * hh + 2, :], [s], 'stg%d' % hh)
                P.cp('dve', wglu.t[:, 2 * hh:2 * hh + 2, :], s.t[:, 0:1024].rearrange("p (k n) -> p k n", k=2), [s], [wglu])
            for hh in range(2):
                s = stg[hh]
                P.load(s.t[:, 0:1024], prm['ple_w_proj'][l][hh * 128:(hh + 1) * 128, :], [s], 'stg%d' % hh)
                P.cp('pool', wproj.t[:, hh, :], s.t[:, 0:1024], [s], [wproj])
            P.load(gsm['xa'].t[:, 0:8], prm['gdn_a_log'][l].partition_broadcast(64), [gsm['xa']], 'gsm')
            P.act(negA.t[:], gsm['xa'].t[:, 0:8], AF.Exp, [gsm['xa']], [negA])
            P.load(dtb.t[:], prm['gdn_dt_bias'][l].partition_broadcast(64), [dtb], 'gsm')
            if use_s5:
                s5_setup(l)

        def sincos(tq, n, out_sin, out_cos, Rk, Wk_sin, Wk_cos):
            ti = s5int.t[:, 0:n]
            tfl = s5big[4].t[:, 0:n]
            fr = s5big[5].t[:, 0:n]
            for (shift, out, Wk) in [(0.0, out_sin, Wk_sin), (0.25, out_cos, Wk_cos)]:
                src = tq
                if shift != 0.0:
                    P.ts('dve', fr, tq, shift, ALU.add, Rk, [s5big[5]])
                    src = fr
                    R2 = [s5big[5]]
                else:
                    R2 = Rk
                P.cp('dve', ti, src, R2, [s5int])
                P.cp('dve', tfl, ti, [s5int], [s5big[4]])
                P.tt('dve', fr, src, tfl, ALU.subtract, R2 + [s5big[4]], [s5big[5]])
                P.act(out, fr, AF.Sin, [s5big[5]], Wk, scale=TWO_PI)

        def s5_setup(l):
            t = s5t
            P.op('pool', lambda e: e.iota(cvi.t[:], pattern=[[0, 16], [1, CB]], base=1, channel_multiplier=0), [], [cvi])
            P.cp('pool', cv.t[:], cvi.t[:], [cvi], [cv])
            P.memset('pool', maskW.t[:], 0.0, [maskW])
            for jj in range(4):
                P.memset('pool', maskW.t[0:64, jj, 2 * jj, :], 1.0, [maskW])
                P.memset('pool', maskW.t[64:128, jj, 2 * jj + 1, :], 1.0, [maskW])
            for name, src in [('are', prm['s5_a_re'][l]), ('aim', prm['s5_a_im'][l])]:
                for g2 in range(2):
                    P.load(t[name].t[g2 * 64:(g2 + 1) * 64, :], src.rearrange("(i g2) n -> g2 n i", g2=2)[g2], [t[name]], 's5ld', slow=True)
            for g2 in range(2):
                P.load(t['ldt'].t[g2 * 64:(g2 + 1) * 64, :], prm['s5_log_dt'][l].rearrange("(i g2) -> g2 i", g2=2)[g2].partition_broadcast(64),
                       [t['ldt']], 's5ld', slow=True)
            for (dst, src) in [(bre, prm['s5_b_re'][l]), (bim, prm['s5_b_im'][l])]:
                for g2 in range(2):
                    P.load(dst.t[g2 * 64:(g2 + 1) * 64, :, :], src.rearrange("(i g2) n c -> g2 n i c", g2=2)[g2], [dst], 's5ld')
            for (dst, src) in [(cre, prm['s5_c_re'][l]), (cim, prm['s5_c_im'][l])]:
                for g2 in range(2):
                    for i_ in range(16):
                        P.load(dst.t[g2 * 64:(g2 + 1) * 64, i_, :],
                               src.rearrange("(i g2) c n -> g2 i n c", g2=2)[g2, i_], [dst], 's5ld', slow=True)
            P.act(t['dt'].t[:], t['ldt'].t[:], AF.Exp, [t['ldt']], [t['dt']])
            P.tt('dve', t['ard'].t[:], t['are'].t[:], t['dt'].t[:], ALU.mult, [t['are'], t['dt']], [t['ard']])
            P.tt('dve', t['th'].t[:], t['aim'].t[:], t['dt'].t[:], ALU.mult, [t['aim'], t['dt']], [t['th']])
            A0 = s5big[0].t[:, 0:128].rearrange("p (j i) -> p j i", j=8)
            A1 = s5big[1].t[:, 0:128].rearrange("p (j i) -> p j i", j=8)
            A2 = s5big[2].t[:, 0:128].rearrange("p (j i) -> p j i", j=8)
            A3 = s5big[3].t[:, 0:128].rearrange("p (j i) -> p j i", j=8)
            P.tt('dve', A0, jv.t[:], bc(t['ard'].t[:].unsqueeze(1), [128, 8, 16]), ALU.mult, [jv, t['ard']], [s5big[0]])
            P.act(A0, A0, AF.Exp, [s5big[0]], [s5big[0]])
            P.tt('dve', A1, jv.t[:], bc(t['th'].t[:].unsqueeze(1), [128, 8, 16]), ALU.mult, [jv, t['th']], [s5big[1]])
            P.ts('dve', A1, A1, 1.0 / TWO_PI, ALU.mult, [s5big[1]], [s5big[1]])
            sincos(s5big[1].t[:, 0:128], 128, s5big[2].t[:, 0:128], s5big[3].t[:, 0:128], [s5big[1]], [s5big[2]], [s5big[3]])
            P.memset('dve', Lre.t[:, 0, :], 1.0, [Lre])
            P.memset('dve', Lim.t[:, 0, :], 0.0, [Lim])
            P.tt('dve', Lre.t[:, 1:9, :], A0, A3, ALU.mult, [s5big[0], s5big[3]], [Lre])
            P.tt('dve', Lim.t[:, 1:9, :], A0, A2, ALU.mult, [s5big[0], s5big[2]], [Lim])
            P.ts('dve', t['tqb'].t[:], t['th'].t[:], float(L5) / TWO_PI, ALU.mult, [t['th']], [t['tqb']])
            B0 = s5big[0].t[:, 0:16 * CB].rearrange("p (i c) -> p i c", i=16)
            P.tt('dve', B0, cv.t[:], bc(t['tqb'].t[:].unsqueeze(2), [128, 16, CB]), ALU.mult, [cv, t['tqb']], [s5big[0]])
            sincos(s5big[0].t[:, 0:16 * CB], 16 * CB, Dsin.t[:].rearrange("p i c -> p (i c)"), Dcos.t[:].rearrange("p i c -> p (i c)"),
                   [s5big[0]], [Dsin], [Dcos])
            P.act(rhoc.t[:], t['ard'].t[:], AF.Exp, [t['ard']], [rhoc], scale=float(L5))
            P.cp('dve', Rho.t[:], bc(rhoc.t[:].unsqueeze(2), [128, 16, CB]), [rhoc], [Rho])
            P.memset('dve', Rho.t[:, :, 0:1], 0.0, [Rho])
            P.ts('dve', t['cr'].t[:], Lre.t[:, 1, :], -1.0, ALU.add, [Lre], [t['cr']])
            P.tt('dve', t['den'].t[:], t['are'].t[:], t['are'].t[:], ALU.mult, [t['are']], [t['den']])
            P.tt('dve', t['t0'].t[:], t['aim'].t[:], t['aim'].t[:], ALU.mult, [t['aim']], [t['t0']])
            P.tt('dve', t['den'].t[:], t['den'].t[:], t['t0'].t[:], ALU.add, [t['den'], t['t0']], [t['den']])
            P.op('dve', lambda e: e.reciprocal(out=t['inv'].t[:], in_=t['den'].t[:]), [t['den']], [t['inv']])
            P.tt('dve', t['t0'].t[:], t['cr'].t[:], t['are'].t[:], ALU.mult, [t['cr'], t['are']], [t['t0']])
            P.tt('dve', t['t1'].t[:], Lim.t[:, 1, :], t['aim'].t[:], ALU.mult, [Lim, t['aim']], [t['t1']])
            P.tt('dve', t['t0'].t[:], t['t0'].t[:], t['t1'].t[:], ALU.add, [t['t0'], t['t1']], [t['t0']])
            P.tt('dve', t['cfr'].t[:], t['t0'].t[:], t['inv'].t[:], ALU.mult, [t['t0'], t['inv']], [t['cfr']])
            P.tt('dve', t['t0'].t[:], Lim.t[:, 1, :], t['are'].t[:], ALU.mult, [Lim, t['are']], [t['t0']])
            P.tt('dve', t['t1'].t[:], t['cr'].t[:], t['aim'].t[:], ALU.mult, [t['cr'], t['aim']], [t['t1']])
            P.tt('dve', t['t0'].t[:], t['t0'].t[:], t['t1'].t[:], ALU.subtract, [t['t0'], t['t1']], [t['t0']])
            P.tt('dve', t['cfi'].t[:], t['t0'].t[:], t['inv'].t[:], ALU.mult, [t['t0'], t['inv']], [t['cfi']])

            def cmul(out_re, out_im, a_re, a_im, b_re, b_im, shape, Ra, Rb, Wre, Wim, neg_im=False):
                n = 1
                for d_ in shape[1:]:
                    n *= d_
                v0 = s5big[4].t[:, 0:n]
                v1 = s5big[5].t[:, 0:n]
                if len(shape) == 3:
                    v0 = v0.rearrange("p (a b) -> p a b", a=shape[1])
                    v1 = v1.rearrange("p (a b) -> p a b", a=shape[1])
                P.tt('dve', v0, a_re, b_re, ALU.mult, Ra + Rb, [s5big[4]])
                P.tt('dve', v1, a_im, b_im, ALU.mult, Ra + Rb, [s5big[5]])
                P.tt('dve', out_re, v0, v1, ALU.subtract, [s5big[4], s5big[5]], Wre)
                P.tt('dve', v0, a_re, b_im, ALU.mult, Ra + Rb, [s5big[4]])
                P.tt('dve', v1, a_im, b_re, ALU.mult, Ra + Rb, [s5big[5]])
                P.tt('dve', out_im, v0, v1, ALU.add, [s5big[4], s5big[5]], Wim)
                if neg_im:
                    P.ts('dve', out_im, out_im, -1.0, ALU.mult, Wim, Wim)

            sh3 = [128, 16, 16]
            cmul(Bre.t[:], Bim.t[:], bc(t['cfr'].t[:].unsqueeze(2), sh3), bc(t['cfi'].t[:].unsqueeze(2), sh3), bre.t[:], bim.t[:],
                 sh3, [t['cfr'], t['cfi']], [bre, bim], [Bre], [Bim])
            for (dst, src) in [(Cbr, cre), (Cbi, cim)]:
                for i4 in range(4):
                    P.tt('dve', dst.t[:, 4 * i4:4 * i4 + 4, :].rearrange("p i (g c) -> p i g c", g=2),
                         bc(src.t[:, 4 * i4:4 * i4 + 4, :].unsqueeze(2), [128, 4, 2, 16]),
                         bc(mask2.t[:].unsqueeze(1), [128, 4, 2, 16]), ALU.mult, [src, mask2], [dst])
            shb = [128, 16, 32]
            for s in range(8):
                lr = bc(Lre.t[:, s + 1, :].unsqueeze(2), shb)
                li = bc(Lim.t[:, s + 1, :].unsqueeze(2), shb)
                cr_o = s5big[0].t[:, 0:512].rearrange("p (a b) -> p a b", a=16)
                ci_o = s5big[1].t[:, 0:512].rearrange("p (a b) -> p a b", a=16)
                cmul(cr_o, ci_o, lr, li, Cbr.t[:], Cbi.t[:], shb, [Lre, Lim], [Cbr, Cbi], [s5big[0]], [s5big[1]], neg_im=True)
                P.cp('pool', Ctab.t[:, s, 0, :, :], cr_o, [s5big[0]], [Ctab])
                P.cp('pool', Ctab.t[:, s, 1, :, :], ci_o, [s5big[1]], [Ctab])
            Pre = s5big[0].t[:, 0:256].rearrange("p (a b) -> p a b", a=16)
            Pim = s5big[1].t[:, 0:256].rearrange("p (a b) -> p a b", a=16)
            Pbr = s5big[2].t[:, 0:512].rearrange("p (a b) -> p a b", a=16)
            Pbi = s5big[3].t[:, 0:512].rearrange("p (a b) -> p a b", a=16)
            Pwr = stg[0].t[:, 0:512]
            Pwi = stg[0].t[:, 512:1024]
            for j in range(8):
                lr = bc(Lre.t[:, j, :].unsqueeze(2), sh3)
                li = bc(Lim.t[:, j, :].unsqueeze(2), sh3)
                cmul(Pre, Pim, lr, li, Bre.t[:], Bim.t[:], sh3, [Lre, Lim], [Bre, Bim], [s5big[0]], [s5big[1]])
                for (dst, src, kd, ks) in [(Pbr, Pre, s5big[2], s5big[0]), (Pbi, Pim, s5big[3], s5big[1])]:
                    for i4 in range(4):
                        P.tt('dve', dst[:, 4 * i4:4 * i4 + 4, :].rearrange("p i (g c) -> p i g c", g=2),
                             bc(src[:, 4 * i4:4 * i4 + 4, :].unsqueeze(2), [128, 4, 2, 16]),
                             bc(mask2.t[:].unsqueeze(1), [128, 4, 2, 16]), ALU.mult, [ks, mask2], [kd])
                sp_ = 7 - j
                for ct in range(4):
                    for part, (src, kd) in enumerate([(Pbr, s5big[2]), (Pbi, s5big[3])]):
                        o, kk = ps(6, part * 128, 128)
                        P.tr(o, src[:, 4 * ct:4 * ct + 4, :].rearrange("p a b -> p (a b)"), ident.t[:], [kd, ident], kk)
                        P.cp('act', EB.t[:, ct, sp_, part, :], o, kk, [EB])
                    for (dstw, src, ks, sgn) in [(Pwr, Pre, s5big[0], 1.0), (Pwi, Pim, s5big[1], -1.0)]:
                        P.tt('dve', dstw.rearrange("p (a g c) -> p a g c", a=4, g=8),
                             bc(src[:, 4 * ct:4 * ct + 4, :].unsqueeze(2), [128, 4, 8, 16]), maskW.t[:], ALU.mult, [ks, maskW], [stg[0]])
                    P.ts('dve', Pwi, Pwi, -1.0, ALU.mult, [stg[0]], [stg[0]])
                    o, kk = ps(7, 256, 128)
                    for jj in range(4):
                        i = 4 * ct + jj
                        P.mm(o[:, 32 * jj:32 * jj + 32], Pwr[:, 128 * jj:128 * jj + 128], Cbr.t[:, i, :], True, False, [stg[0], Cbr], kk)
                        P.mm(o[:, 32 * jj:32 * jj + 32], Pwi[:, 128 * jj:128 * jj + 128], Cbi.t[:, i, :], False, True, [stg[0], Cbi], kk)
                    P.cp('act', KT.t[:, ct, j, :], o, kk, [KT])

        def rmsnorm(gcol0, out_bf=None, out_f32=None):
            P.act(sqb.t[:], hT.t[:], AF.Square, [hT], [sqb])
            o, kk = ps(7, 0, T)
            for k in range(8):
                P.mm(o, onesb.t[:], sqb.t[:, k, :], k == 0, k == 7, [onesb, sqb], kk)
            P.act(rstd.t[:], o, AF.Ln, kk, [rstd], scale=1.0 / D, bias=EPS)
            P.act(rstd.t[:], rstd.t[:], AF.Exp, [rstd], [rstd], scale=-0.5)
            dst = out_bf if out_bf is not None else out_f32
            for k in range(8):
                P.stt(dst.t[:, k, :], hT.t[:, k, :], colsB.t[:, gcol0 + k:gcol0 + k + 1], rstd.t[:], ALU.mult, ALU.mult,
                      [hT, colsB, rstd], [dst])

        wcount = [0]

        def ring_next():
            b = wring[wcount[0] % NRING]
            wcount[0] += 1
            return b

        def stream_w(l, g):
            b = ring_next()
            v = Buf(b.t[:].rearrange("p (k n) -> p k n", k=8), b.k)
            P.load(v.t, win_s[l][g], [b], R=['win_s%d' % l])
            return v

        pcount = [0]

        def inproj_tile(wb, col0):
            slot = pcount[0] % 4
            pcount[0] += 1
            o, kk = ps(slot, 0, T)
            for k in range(8):
                P.mm(o, wb.t[:, k, col0:col0 + 128], hnT.t[:, k, :], k == 0, k == 7, [wb, hnT], kk)
            return o, kk

        ecount = [0]

        def evac_engine():
            ecount[0] += 1
            return 'act' if ecount[0] % 2 else 'dve'

        dgc = [0]

        def conv_tile(src_buf, jt, wcols, col_of_tap, R):
            slot = pcount[0] % 4
            pcount[0] += 1
            o, kk = ps(slot, 0, T)
            for k in range(4):
                d = dg[dgc[0] % 4]
                dgc[0] += 1
                c = col_of_tap(k)
                P.ts('pool', d.t[:], identb.t[:], wcols.t[:, c:c + 1], ALU.mult, [identb, wcols], [d])
                P.mm(o, d.t[:], src_buf.t[:, jt, k:k + T], k == 0, k == 3, [d] + R, kk)
            return o, kk

        def rg_chunk(l, c):
            if c == 0:
                P.memset('pool', raw.t[:, 0:4, 0:3], 0.0, ['raw_rg'])
            else:
                P.cp('pool', raw.t[:, 0:4, 0:3], histrg.t[:], [histrg], ['raw_rg'])
            for g in range(4):
                wb = stream_w(l, g)
                for n in range(2):
                    o, kk = inproj_tile(wb, n * 128)
                    j = (g % 2) * 2 + n
                    if g < 2:
                        P.cp(evac_engine(), raw.t[:, j, 3:3 + T], o, kk, ['raw_rg'])
                    else:
                        P.act(gate.t[:, j, :], o, AF.Silu, kk, ['gate_rg'])
            P.cp('pool', histrg.t[:], raw.t[:, 0:4, T:T + 3], ['raw_rg'], [histrg])
            yield 'inproj'
            def rg_tile(j):
                p_ = j % 2
                r_, gi_, xj, m_ = tf[4 * p_], tf[4 * p_ + 1], tf[4 * p_ + 2], tf[4 * p_ + 3]
                xb_ = tb[p_]
                o, kk = conv_tile(raw, j, colsB, lambda k: RB['rg_conv_w'] + k * 4 + j, ['raw_rg'])
                yield
                P.act(xj.t[:], o, AF.Identity, kk, [xj], bias=colsB.t[:, 16 + j:17 + j])
                yield
                P.cp('dve', xb_.t[:], xj.t[:], [xj], [xb_])
                yield
                oa, ka = ps(4 + 2 * p_, 0, T)
                ox, kx = ps(5 + 2 * p_, 0, T)
                P.mm(oa, rgWa.t[:, j, :], xb_.t[:], True, True, [rgWa, xb_], ka)
                P.mm(ox, rgWx.t[:, j, :], xb_.t[:], True, True, [rgWx, xb_], kx)
                yield
                P.act(r_.t[:], oa, AF.Exp, ka + [rgc], [r_], scale=-1.0, bias=rgc.t[:, 8 + j:9 + j])
                P.act(gi_.t[:], ox, AF.Exp, kx + [rgc], [gi_], scale=-1.0, bias=rgc.t[:, 12 + j:13 + j])
                yield
                P.act(r_.t[:], r_.t[:], AF.Ln, [r_], [r_], bias=1.0)
                P.act(gi_.t[:], gi_.t[:], AF.Ln, [gi_], [gi_], bias=1.0)
                yield
                P.act(r_.t[:], r_.t[:], AF.Exp, [r_], [r_], scale=-1.0)
                P.act(gi_.t[:], gi_.t[:], AF.Exp, [gi_], [gi_], scale=-1.0)
                yield
                P.tt('pool', gi_.t[:], gi_.t[:], xj.t[:], ALU.mult, [gi_, xj], [gi_])
                a_ = xj
                P.act(m_.t[:], r_.t[:], AF.Exp, [r_, rgc], [m_], scale=rgc.t[:, 4 + j:5 + j])
                yield
                P.act(a_.t[:], r_.t[:], AF.Exp, [r_, rgc, gi_], [a_], scale=rgc.t[:, j:j + 1])
                yield
                P.act(m_.t[:], m_.t[:], AF.Sqrt, [m_], [m_], scale=-1.0, bias=1.0)
                yield
                P.tt('dve', m_.t[:], m_.t[:], gi_.t[:], ALU.mult, [m_, gi_], [m_])
                yield
                hr = r_
                if c == 0:
                    P.op('dve', lambda e: e.tensor_tensor_scan(out=hr.t[:], data0=a_.t[:], data1=m_.t[:], initial=0.0,
                                                               op0=ALU.mult, op1=ALU.add), [a_, m_], [hr])
                else:
                    P.op('dve', lambda e: e.tensor_tensor_scan(out=hr.t[:], data0=a_.t[:], data1=m_.t[:],
                                                               initial=rgcar.t[:, j:j + 1], op0=ALU.mult, op1=ALU.add),
                         [a_, m_, rgcar], [hr])
                yield
                P.cp('dve', rgcar.t[:, j:j + 1], hr.t[:, T - 1:T], [hr], [rgcar])
                P.tt('pool', mixT.t[:, j, :], hr.t[:], gate.t[:, j, :], ALU.mult, [hr, 'gate_rg'], ['mix_rg'])

            for j0 in (0, 2):
                pair = [rg_tile(j0), rg_tile(j0 + 1)]
                live = [True, True]
                next(pair[0], None)
                next(pair[0], None)
                while any(live):
                    for q_ in (1, 0):
                        if live[q_]:
                            try:
                                next(pair[q_])
                            except StopIteration:
                                live[q_] = False
                yield 'pair'

        def s5_chunk(l, c):
            for g in range(4):
                wb = stream_w(l, 20 + g)
                for n in range(2):
                    o, kk = inproj_tile(wb, n * 128)
                    j = (g % 2) * 2 + n
                    if g < 2:
                        P.cp(evac_engine(), u5.t[:, j, :], o, kk, [u5])
                    else:
                        P.act(gate.t[:, 4 + j, :], o, AF.Silu, kk, ['gate_s5'])
            if c == 0:
                P.memset('pool', XPre.t[:, :, 0:1], 0.0, [XPre])
                P.memset('pool', XPim.t[:, :, 0:1], 0.0, [XPim])
            pe_ = [ps(4, 0, 512), ps(5, 0, 512)]
            for i in range(16):
                ct, jj = i // 4, i % 4
                uv = u5.t[32 * jj:32 * jj + 32, ct, :].rearrange("p (c s) -> p s c", s=L5)
                for part in range(2):
                    o, kk = pe_[part]
                    for s_ in range(L5):
                        P.mm(o[:, i * CB:(i + 1) * CB], EB.t[32 * jj:32 * jj + 32, ct, s_, part, :], uv[:, s_, :],
                             s_ == 0, s_ == L5 - 1, [EB, u5], kk, tile_position=(32 * jj, 0))
            ere, kre = pe_[0]
            eim, kim = pe_[1]
            t1, t2, mre, mim, qre, qim = s5big[0], s5big[1], s5big[2], s5big[3], s5big[4], s5big[5]
            dcs = Dcos.t[:].rearrange("p i c -> p (i c)")
            dsn = Dsin.t[:].rearrange("p i c -> p (i c)")
            P.tt('dve', t1.t[:], ere, dcs, ALU.mult, kre + [Dcos], [t1])
            P.tt('dve', t2.t[:], eim, dsn, ALU.mult, kim + [Dsin], [t2])
            P.tt('pool', mre.t[:], t1.t[:], t2.t[:], ALU.add, [t1, t2], [mre])
            P.tt('dve', t1.t[:], eim, dcs, ALU.mult, kim + [Dcos], [t1])
            P.tt('dve', t2.t[:], ere, dsn, ALU.mult, kre + [Dsin], [t2])
            P.tt('pool', mim.t[:], t1.t[:], t2.t[:], ALU.subtract, [t1, t2], [mim])
            for (m_, XP) in [(mre, XPre), (mim, XPim)]:
                mv = m_.t[:].rearrange("p (i c) -> p i c", i=16)
                P.tt('pool', s5t['t0'].t[:].unsqueeze(2), rhoc.t[:].unsqueeze(2), XP.t[:, :, 0:1], ALU.mult, [rhoc, XP], [s5t['t0']])
                P.tt('pool', mv[:, :, 0:1], mv[:, :, 0:1], s5t['t0'].t[:].unsqueeze(2), ALU.add, [m_, s5t['t0']], [m_])
            rhf = Rho.t[:].rearrange("p i c -> p (i c)")
            for (m_, q_) in [(mre, qre), (mim, qim)]:
                P.op('dve', lambda e, m_=m_, q_=q_: e.tensor_tensor_scan(out=q_.t[:], data0=rhf, data1=m_.t[:], initial=0.0,
                                                                        op0=ALU.mult, op1=ALU.add), [Rho, m_], [q_])
            qrv = qre.t[:].rearrange("p (i c) -> p i c", i=16)
            qiv = qim.t[:].rearrange("p (i c) -> p i c", i=16)
            t1v = t1.t[:].rearrange("p (i c) -> p i c", i=16)
            t2v = t2.t[:].rearrange("p (i c) -> p i c", i=16)
            P.tt('dve', t1v, qrv, Dcos.t[:], ALU.mult, [qre, Dcos], [t1])
            P.tt('pool', t2v, qiv, Dsin.t[:], ALU.mult, [qim, Dsin], [t2])
            P.tt('dve', XPre.t[:, :, 1:CB + 1], t1v, t2v, ALU.subtract, [t1, t2], [XPre])
            P.tt('dve', t1v, qrv, Dsin.t[:], ALU.mult, [qre, Dsin], [t1])
            P.tt('pool', t2v, qiv, Dcos.t[:], ALU.mult, [qim, Dcos], [t2])
            P.tt('dve', XPim.t[:, :, 1:CB + 1], t1v, t2v, ALU.add, [t1, t2], [XPim])
            P.cp('pool', XPb.t[:, 0, :, :], XPre.t[:, :, 0:CB], [XPre], [XPb])
            P.cp('pool', XPb.t[:, 1, :, :], XPim.t[:, :, 0:CB], [XPim], [XPb])
            P.cp('pool', XPre.t[:, :, 0:1], XPre.t[:, :, CB:CB + 1], [XPre, XPb], [XPre])
            P.cp('pool', XPim.t[:, :, 0:1], XPim.t[:, :, CB:CB + 1], [XPim, XPb], [XPim])
            y5t = [Buf(G[3].t[:, ct_ * T:(ct_ + 1) * T], G[3].k) for ct_ in range(4)]
            yield 'part1'
            for ct in range(4):
                if ct == 2:
                    yield 'y01'
                o, kk = ps(6 + (ct % 2), 0, T)
                uv = u5.t[:, ct, :].rearrange("p (c s) -> p s c", s=L5)
                for s_ in range(L5):
                    oc = o[:, s_ * CB:(s_ + 1) * CB]
                    for sp_ in range(s_ + 1):
                        P.mm(oc, KT.t[:, ct, s_ - sp_, :], uv[:, sp_, :], sp_ == 0, False, [KT, u5], kk)
                    for jj in range(4):
                        i = 4 * ct + jj
                        for part in range(2):
                            P.mm(o[32 * jj:32 * jj + 32, s_ * CB:(s_ + 1) * CB], Ctab.t[:, s_, part, i, :], XPb.t[:, part, i, :],
                                 False, (part == 1), [Ctab, XPb], kk, tile_position=(0, 32 * jj))
                P.stt(y5t[ct].t.rearrange("p (c s) -> p s c", s=L5), uv, colsB.t[:, 32 + ct:33 + ct],
                      o.rearrange("p (s c) -> p s c", s=L5), ALU.mult, ALU.add, [u5, colsB] + kk, [y5t[ct]])
            yield 'y23'
            def gelu_tile(ct):
                y_, z_ = y5t[ct], tf[4 + ct]
                P.tt('pool', z_.t[:], y_.t, y_.t, ALU.mult, [y_], [z_])
                yield
                P.ts('pool', z_.t[:], z_.t[:], 0.044715, ALU.mult, [z_], [z_], s2=1.0, op1=ALU.add)
                yield
                P.tt('pool', z_.t[:], z_.t[:], y_.t, ALU.mult, [z_, y_], [z_])
                yield
                P.act(z_.t[:], z_.t[:], AF.Sigmoid, [z_], [z_], scale=1.5957691216057308)
                yield
                P.tt('dve', z_.t[:], z_.t[:], y_.t, ALU.mult, [z_, y_], [z_])
                yield
                P.cp('pool', z5b.t[:, ct, :], z_.t[:], [z_], [z5b])

            def rr(gens):
                live = [True] * len(gens)
                while any(live):
                    for q_ in range(len(gens)):
                        if live[q_]:
                            try:
                                next(gens[q_])
                            except StopIteration:
                                live[q_] = False

            rr([gelu_tile(ct) for ct in range(4)])

            def glu_tile(m):
                slot = pcount[0] % 4
                pcount[0] += 1
                o, kk = ps(slot, 0, T)
                for k in range(4):
                    P.mm(o, wglu.t[:, k, m * 128:(m + 1) * 128], z5b.t[:, k, :], k == 0, k == 3, [wglu, z5b], kk)
                yield
                gl = tf[m]
                P.act(gl.t[:], o, AF.Sigmoid, kk, [gl], bias=colsB.t[:, 36 + m:37 + m])
                yield
                P.tt('pool', gl.t[:], gl.t[:], tf[4 + m].t[:], ALU.mult, [gl, tf[4 + m]], [gl])
                yield
                P.tt('dve', mixT.t[:, 12 + m, :], gl.t[:], gate.t[:, 4 + m, :], ALU.mult, [gl, 'gate_s5'], ['mix_s5'])

            rr([glu_tile(m) for m in range(4)])

        def gdn_chunk(l, c):
            if c == 0:
                P.memset('pool', raw.t[:, :, 0:3], 0.0, ['raw_rg', 'raw_g'])
                P.memset('pool', Sst.t[:], 0.0, [Sst])
                P.memset('pool', Sb.t[:], 0.0, [Sb])
            else:
                P.cp('pool', raw.t[:, :, 0:3], histg.t[:], [histg], ['raw_rg', 'raw_g'])
            P.load(wba.t[:], wba_s[l], [wba], R=['wba_s%d' % l])
            for qtr in range(4):
                for g in range(4 * qtr, 4 * qtr + 4):
                    wb = stream_w(l, 4 + g)
                    for n in range(2):
                        o, kk = inproj_tile(wb, n * 128)
                        j = g * 2 + n
                        if j < 24:
                            P.cp(evac_engine(), raw.t[:, j, 3:3 + T], o, kk, ['raw_rg', 'raw_g'])
                        else:
                            P.act(gate.t[:, j - 24, :], o, AF.Silu, kk, ['gate_rg', 'gate_s5', 'gate_g'])
                if qtr == 3:
                    break
                P.cp('pool', histg.t[:, 8 * qtr:8 * qtr + 8, :], raw.t[:, 8 * qtr:8 * qtr + 8, T:T + 3], ['raw_g'], [histg])
                def conv_norm_tile(j):
                    o, kk = conv_tile(raw, j, gcw, lambda k: k * 24 + j, ['raw_g'])
                    yield
                    sl, lv, rs_ = tf[(j % 2) * 3], tf[(j % 2) * 3 + 1], tf[(j % 2) * 3 + 2]
                    P.act(lv.t[:], o, AF.Exp, kk, [lv], scale=-1.0)
                    yield
                    P.act(lv.t[:], lv.t[:], AF.Ln, [lv], [lv], bias=1.0)
                    yield
                    P.act(lv.t[:], lv.t[:], AF.Exp, [lv], [lv], scale=-1.0)
                    yield
                    if j >= 16:
                        P.tt('dve', vs.t[:, j - 16, :], o, lv.t[:], ALU.mult, kk + [lv], [vs])
                        return
                    sq_ = tb[j % 2]
                    P.tt('dve', sl.t[:], o, lv.t[:], ALU.mult, kk + [lv], [sl])
                    yield
                    P.tt('pool', sq_.t[:], sl.t[:], sl.t[:], ALU.mult, [sl], [sq_])
                    yield
                    o2, k2 = ps(4 + (j % 2), 0, T)
                    P.mm(o2, onesb.t[:], sq_.t[:], True, True, [onesb, sq_], k2)
                    yield
                    P.act(lv.t[:], o2, AF.Ln, k2, [lv], bias=EPS)
                    yield
                    if j < 8:
                        P.act(rs_.t[:], lv.t[:], AF.Exp, [lv], [rs_], scale=-0.5, bias=-0.5 * math.log(128.0))
                        yield
                        P.tt('dve', qn.t[:, j, :], sl.t[:], rs_.t[:], ALU.mult, [sl, rs_], [qn])
                    else:
                        P.act(rs_.t[:], lv.t[:], AF.Exp, [lv], [rs_], scale=-0.5)
                        yield
                        P.tt('dve', kn.t[:, j - 8, :], sl.t[:], rs_.t[:], ALU.mult, [sl, rs_], [kn])

                for j0 in range(8 * qtr, 8 * qtr + 8, 2):
                    pair = [conv_norm_tile(j0), conv_norm_tile(j0 + 1)]
                    live = [True, True]
                    next(pair[0], None)
                    next(pair[0], None)
                    while any(live):
                        for q_ in (1, 0):
                            if live[q_]:
                                try:
                                    next(pair[q_])
                                except StopIteration:
                                    live[q_] = False
            if gdn_stop < 7 and c == 0 and l == 0:
                P.memset('pool', mixT.t[:, 4:12, :], 0.0, ['mix_g'])
            if gdn_stop >= 1:
                gdn_scalars(l, c)
            gens = [gdn_inner(l, c, gci) for gci in range(T // GC)]
            next(gens[0], None)
            for gci in range(T // GC):
                next(gens[gci], None)
                if gci + 1 < T // GC:
                    next(gens[gci + 1], None)
                next(gens[gci], None)

        def gdn_scalars(l, c):
            g = gsm
            v3 = lambda ap: ap.rearrange("p (i h) -> p i h", i=NI)
            o, kk = ps(0, 0, NI * 16, 0, 64)
            for gci in range(NI):
                for k in range(8):
                    P.mm(o[:, gci * 16:(gci + 1) * 16], hnT.t[:, k, gci * GC:(gci + 1) * GC], wba.t[:, k, :], k == 0, k == 7, [hnT, wba], kk)
            ov = o.rearrange("p (i c) -> p i c", i=NI)
            P.act(v3(g['bt'].t[:]), ov[:, :, 0:8], AF.Sigmoid, kk, [g['bt']])
            P.tt('dve', v3(g['xa'].t[:]), ov[:, :, 8:16], bc(dtb.t[:].unsqueeze(1), [64, NI, 8]), ALU.add, kk + [dtb], [g['xa']])
            P.act(g['ex'].t[:], g['xa'].t[:], AF.Exp, [g['xa']], [g['ex']])
            P.act(g['sp'].t[:], g['ex'].t[:], AF.Ln, [g['ex']], [g['sp']], bias=1.0)
            P.tt('dve', v3(g['gp'].t[:]), v3(g['sp'].t[:]), bc(negA.t[:].unsqueeze(1), [64, NI, 8]), ALU.mult, [g['sp'], negA], [g['gp']])
            P.ts('dve', g['g'].t[:], g['gp'].t[:], -1.0, ALU.mult, [g['gp']], [g['g']])
            P.ts('pool', g['nbt'].t[:], g['bt'].t[:], -1.0, ALU.mult, [g['bt']], [g['nbt']])
            oc, kc = ps(0, 64, NI * 8, 0, 64)
            P.mm(oc, U64.t[:], g['g'].t[:], True, True, [U64, g['g']], kc)
            P.cp('dve', g['gcum'].t[:], oc, kc, [g['gcum']])
            ol, kl = ps(0, 128, NI * 8)
            P.mm(ol, sel63.t[:], g['gcum'].t[:], True, True, [sel63, g['gcum']], kl)
            P.act(egl.t[:], ol, AF.Exp, kl, [egl])
            P.act(g['egc'].t[:], g['gcum'].t[:], AF.Exp, [g['gcum']], [g['egc']])
            P.tt('dve', g['dgl'].t[:], ol[0:64, :], g['gcum'].t[:], ALU.subtract, kl + [g['gcum']], [g['dgl']])
            P.act(g['kds'].t[:], g['dgl'].t[:], AF.Exp, [g['dgl']], [g['kds']])
            P.tt('dve', g['bge'].t[:], g['bt'].t[:], g['egc'].t[:], ALU.mult, [g['bt'], g['egc']], [g['bge']])

        def hb(ap, n):
            return bc(ap.unsqueeze(2), [64, 8, n])

        def m8(ap64):
            return bc(ap64.unsqueeze(1), [64, 8, 64])

        def gdn_inner(l, c, gci):
            t0 = gci * GC
            cs = slice(t0, t0 + GC)
            fl = lambda b_: b_.t[:].rearrange("p h j -> p (h j)")
            if gdn_stop < 1:
                return
            g = {n: (Buf(gsm[n].t[:, gci * 8:(gci + 1) * 8], gsm[n].k) if n not in ('ss', 'ln', 'rs') else Buf(gsm[n].t[:], gsm[n].k)) for n in gsm}
            eglv = egl.t[:, gci * 8:(gci + 1) * 8]
            if gdn_stop < 2:
                return
            NGU, GBC, E1, E2, DK = gw[0], gw[1], gw[2], gw[3], gw[4]
            P.tt('dve', NGU.t[:], m8(U64.t[:]), hb(g['gp'].t, 64), ALU.mult, [U64, g['gp']], [NGU])
            P.cp('pool', GBC.t[:], hb(g['g'].t, 64), [g['g']], [GBC])
            oD, kD = ps(1, 0, 512, 0, 64)
            P.mm(oD, U64.t[:], fl(GBC), True, False, [U64, GBC], kD)
            P.mm(oD, ones64.t[:], fl(NGU), False, True, [ones64, NGU], kD)
            P.ts('dve', fl(E1), oD, 0.0, ALU.min, kD, [E1])
            P.ts('dve', fl(E2), oD, -1.0, ALU.mult, kD, [E2], s2=0.0, op1=ALU.min)
            P.act(fl(E1), fl(E1), AF.Exp, [E1], [E1])
            P.act(fl(E2), fl(E2), AF.Exp, [E2], [E2])
            P.tt('pool', DK.t[:], m8(Mst.t[:]), hb(g['nbt'].t, 64), ALU.mult, [Mst, g['nbt']], [DK])
            P.tt('pool', DK.t[:], DK.t[:], E1.t[:], ALU.mult, [DK, E1], [DK])
            E2u = E2
            DQ = gw[6]
            P.tt('pool', DQ.t[:], E2u.t[:], m8(U64.t[:]), ALU.mult, [E2u, U64], [DQ])
            if gdn_stop < 3:
                return
            Bd = gw[1]
            P.tt('dve', Bd.t[:], m8(ident.t[0:64, 0:64]), hb(g['nbt'].t, 64), ALU.mult, [ident, g['nbt']], [Bd])
            oB, kB = ps(1, 0, 512, 0, 64)
            P.mm(oB, ones64.t[:], fl(Bd), True, True, [ones64, Bd], kB)
            MK = gw[1]
            P.tt('dve', MK.t[:], E2u.t[:], m8(Ust.t[:]), ALU.mult, [E2u, Ust], [MK])
            P.tt('dve', fl(MK), fl(MK), oB, ALU.mult, [MK] + kB, [MK])
            okk, kkk = ps(2, 0, 512, 0, 64)
            oqk, kqk = ps(3, 0, 512, 0, 64)
            for h in range(8):
                P.mm(okk[:, h * 64:(h + 1) * 64], kn.t[:, h, cs], kn.t[:, h, cs], True, True, [kn], kkk)
            for h in range(8):
                P.mm(oqk[:, h * 64:(h + 1) * 64], kn.t[:, h, cs], qn.t[:, h, cs], True, True, [kn, qn], kqk)
            def bfv(b_):
                return Buf(b_.t[:].rearrange("p h j -> p (h j)").bitcast(BF16)[:, 0:512].rearrange("p (h j) -> p h j", h=8), b_.k)
            CH_BF16 = True
            if CH_BF16:
                Nb = [bfv(gw[5]), bfv(gw[6])]
                Mb = [bfv(gw[0]), bfv(gw[1])]
                PTb = [bfv(gw[2]), bfv(gw[3])]
            else:
                Nb = [gw[5], gw[6]]
                Mb = [gw[0], gw[1]]
                PTb = [gw[2], gw[4]]
            P.tt('dve', fl(Nb[0]), okk, fl(DK), ALU.mult, kkk + [DK], [Nb[0]])
            P.tt('dve', fl(aqkT), oqk, fl(DQ), ALU.mult, kqk + [DQ], [aqkT])
            if gdn_stop < 4:
                return
            P.tt('dve', fl(Mb[0]), okk, fl(MK), ALU.mult, kkk + [MK], [Mb[0]])
            P.tt('dve', PTb[0].t[:], Mb[0].t[:], m8(ident.t[0:64, 0:64]), ALU.add, [Mb[0], ident], [PTb[0]])
            cur = 0
            for lev in range(1, 6):
                nxt = 1 - cur
                oN, kN = ps(1, 0, 512, 0, 64)
                for h in range(8):
                    P.mm(oN[:, h * 64:(h + 1) * 64], Mb[cur].t[:, h, :], Nb[cur].t[:, h, :], True, True, [Mb[cur], Nb[cur]], kN)
                if lev < 5:
                    oM, kM = ps(2, 0, 512, 0, 64)
                    for h in range(8):
                        P.mm(oM[:, h * 64:(h + 1) * 64], Nb[cur].t[:, h, :], Mb[cur].t[:, h, :], True, True, [Mb[cur], Nb[cur]], kM)
                P.cp('act', fl(Nb[nxt]), oN, kN, [Nb[nxt]])
                if lev < 5:
                    P.cp('dve', fl(Mb[nxt]), oM, kM, [Mb[nxt]])
                oP, kP = ps(3, 0, 512, 0, 64)
                for h in range(8):
                    P.mm(oP[:, h * 64:(h + 1) * 64], Nb[nxt].t[:, h, :], PTb[cur].t[:, h, :], True, True, [Nb[nxt], PTb[cur]], kP)
                P.tt('dve', fl(PTb[nxt]), oP, fl(PTb[cur]), ALU.add, kP + [PTb[cur]], [PTb[nxt]])
                cur = nxt
            PT = PTb[cur]
            if gdn_stop < 5:
                return
            okt, kkt = psbf(4, 0, 1024, 0, 64)
            ovt, kvt = psbf(5, 0, 1024, 0, 64)
            for h in range(8):
                P.tr(okt[:, h * 128:(h + 1) * 128], kn.t[:, h, cs], identb.t[:], [kn, identb], kkt)
            for h in range(8):
                P.tr(ovt[:, h * 128:(h + 1) * 128], vs.t[:, h, cs], identb.t[:], [vs, identb], kvt)
            kbg, vb, usb, ob = gx[0], gx[1], gx[2], gx[3]
            if CH_BF16:
                bx = lambda b_: Buf(b_.t[:].rearrange("p h e -> p (h e)").bitcast(BF16)[:, 0:1024].rearrange("p (h e) -> p h e", h=8), b_.k)
                kbg, vb = bx(gx[0]), bx(gx[1])
            fx = lambda b_: b_.t[:].rearrange("p h e -> p (h e)")
            v3 = lambda ap: ap.rearrange("p (h e) -> p h e", h=8)
            P.tt('dve', kbg.t[:], v3(okt), hb(g['bge'].t, 128), ALU.mult, kkt + [g['bge']], [kbg])
            P.tt('dve', kdec.t[:], v3(okt), hb(g['kds'].t, 128), ALU.mult, kkt + [g['kds']], [kdec])
            P.tt('dve', vb.t[:], v3(ovt), hb(g['bt'].t, 128), ALU.mult, kvt + [g['bt']], [vb])
            ou, ku = ps(6, 0, 1024, 0, 64)
            for h in range(8):
                P.mm(ou[:, h * 128:(h + 1) * 128], PT.t[:, h, :], vb.t[:, h, :], True, True, [PT, vb], ku)
            ow, kw = ps(0, 0, 512)
            for h in range(8):
                P.mm(ow[:, h * 64:(h + 1) * 64], kbg.t[:, h, :], PT.t[:, h, :], True, True, [PT, kbg], kw)
            P.cp('act', fx(usb), ou, ku, [usb])
            P.cp('act', wTb.t[:].rearrange("p h c -> p (h c)"), ow, kw, [wTb])
            if gdn_stop < 6:
                return
            yield 'AB'
            o1, k1 = ps(4, 0, 1024, 0, 64)
            for h in range(8):
                P.mm(o1[:, h * 128:(h + 1) * 128], wTb.t[:, h, :], Sb.t[:, h, :], True, True, [wTb, Sb], k1)
            o2, k2 = ps(2, 0, 1024, 0, 64)
            for h in range(8):
                P.mm(o2[:, h * 128:(h + 1) * 128], qn.t[:, h, cs], Sb.t[:, h, :], True, True, [qn, Sb], k2)
            P.tt('dve', fx(vnew), fx(usb), o1, ALU.subtract, [usb] + k1, [vnew])
            o3, k3 = ps(6, 0, 1024, 0, 64)
            for h in range(8):
                P.mm(o3[:, h * 128:(h + 1) * 128], aqkT.t[:, h, :], vnew.t[:, h, :], True, True, [aqkT, vnew], k3)
            o4, k4 = ps(0, 0, 1024)
            for h in range(8):
                P.mm(o4[:, h * 128:(h + 1) * 128], kdec.t[:, h, :], vnew.t[:, h, :], True, True, [kdec, vnew], k4)
            P.tt('dve', ob.t[:], v3(o2), hb(g['egc'].t, 128), ALU.mult, k2 + [g['egc']], [ob])
            P.tt('dve', fx(ob), fx(ob), o3, ALU.add, [ob] + k3, [ob])
            for h in range(8):
                P.stt(Sst.t[:, h, :], Sst.t[:, h, :], eglv[:, h:h + 1], o4[:, h * 128:(h + 1) * 128], ALU.mult, ALU.add,
                      [Sst, egl] + k4, [Sst])
            P.cp('pool', Sb.t[:], Sst.t[:], [Sst], [Sb])
            if gdn_stop < 7:
                return
            yield 'C'
            osq = gx[0]
            P.tt('pool', osq.t[:], ob.t[:], ob.t[:], ALU.mult, [ob], [osq])
            P.op('dve', lambda e: e.tensor_reduce(out=g['ss'].t, in_=osq.t[:], axis=AX.X, op=ALU.add), [osq], [g['ss']])
            P.act(g['ln'].t, g['ss'].t, AF.Ln, [g['ss']], [g['ln']], scale=1.0 / 128.0, bias=EPS)
            P.act(g['rs'].t, g['ln'].t, AF.Exp, [g['ln']], [g['rs']], scale=-0.5)
            on = Buf(gx[1].t[:].rearrange("p h e -> p (h e)").bitcast(BF16)[:, 0:1024].rearrange("p (h e) -> p h e", h=8), gx[1].k)
            P.tt('dve', on.t[:], ob.t[:], hb(g['rs'].t, 128), ALU.mult, [ob, g['rs']], [on])
            oo, ko = psbf(5, 0, 512)
            for h in range(8):
                P.tr(oo[:, h * 64:(h + 1) * 64], on.t[:, h, :], identb.t[0:64, 0:64], [on, identb], ko)
            P.stt(mixT.t[:, 4:12, cs], oo.rearrange("p (h c) -> p h c", h=8), colsB.t[:, 40:41], gate.t[:, :, cs],
                  ALU.mult, ALU.mult, ko + [colsB, 'gate_g'], ['mix_g'])

        ocount = [0]

        def chunk(l, c):
            tok0 = c * T
            last = (l == depth - 1)
            if l == 0:
                for a_ in range(2):
                    P.load(stg[a_].t[:, 0:1024], x_d[tok0 + a_ * 128:tok0 + (a_ + 1) * 128, :], [stg[a_]], 'stg%d' % a_)
                for half in range(2):
                    o, kk = ps(6, 0, 1024)
                    for k4 in range(4):
                        for a_ in range(2):
                            kt_ = 4 * half + k4
                            P.tr(o[:, k4 * 256 + a_ * 128:k4 * 256 + a_ * 128 + 128], stg[a_].t[:, kt_ * 128:(kt_ + 1) * 128], ident.t[:],
                                 [stg[a_], ident], kk)
                    P.cp('act' if half else 'dve', hT.t[:, 4 * half:4 * half + 4, :].rearrange("p k t -> p (k t)"), o, kk, [hT])
            else:
                P.load(hT.t[:], hscr.rearrange("k p t -> p k t")[:, :, tok0:tok0 + T], [hT], 'hT', R=['hscr'])
            rmsnorm(RB['norm_g'], out_bf=hnT)
            if not use_rg and c == 0 and l == 0:
                P.memset('pool', mixT.t[:, 0:4, :], 0.0, ['mix_rg'])
            if not use_s5 and c == 0 and l == 0:
                P.memset('pool', mixT.t[:, 12:16, :], 0.0, ['mix_s5'])
            gr = rg_chunk(l, c) if use_rg else iter(())
            g5 = s5_chunk(l, c) if use_s5 else iter(())
            next(gr, None)
            next(g5, None)
            next(gr, None)
            next(g5, None)
            next(gr, None)
            for _ in gr:
                pass
            for _ in g5:
                pass
            if use_gdn:
                gdn_chunk(l, c)
            elif c == 0 and l == 0:
                P.memset('pool', mixT.t[:, 4:12, :], 0.0, ['mix_g'])
            for m in range(8):
                b_ = ring_next()
                wo = Buf(b_.t[:].rearrange("p (k n) -> p k n", k=16), b_.k)
                P.load(wo.t, wout_s[l][m], [b_], R=['wout_s%d' % l])
                slot = pcount[0] % 4
                pcount[0] += 1
                o, kk = ps(slot, 0, T)
                for k in range(16):
                    mk = 'mix_rg' if k < 4 else ('mix_g' if k < 12 else 'mix_s5')
                    P.mm(o, wo.t[:, k, :], mixT.t[:, k, :], k == 0, k == 15, [wo, mk], kk)
                P.tt('dve', hT.t[:, m, :], hT.t[:, m, :], o, ALU.add, [hT] + kk, [hT])
            rmsnorm(RB['ple_norm_g'], out_bf=hnT)
            ptok = stg[1]
            pv = ptok.t[:, 1024:1536].rearrange("p (a d) -> p a d", a=2)
            P.load(pv, p_d[l, tok0:tok0 + T, :].rearrange("(a p) d -> p a d", p=128), [ptok], 'stg1')
            o, kk = ps(6, 0, 512)
            for k in range(2):
                for a in range(2):
                    P.tr(o[:, k * 256 + a * 128:k * 256 + a * 128 + 128], pv[:, a, k * 128:(k + 1) * 128], ident.t[:], [ptok, ident], kk)
            P.cp('act', pT.t[:].rearrange("p k t -> p (k t)"), o, kk, [pT])
            for m in range(8):
                b_ = ring_next()
                wg = Buf(b_.t[:, 0:1024].rearrange("p (k n) -> p k n", k=8), b_.k)
                P.load(wg.t, wgate_s[l][m], [b_], R=['wgate_s%d' % l])
                slot = pcount[0] % 4
                pcount[0] += 1
                o, kk = ps(slot, 0, T)
                for k in range(8):
                    P.mm(o, wg.t[:, k, :], hnT.t[:, k, :], k == 0, k == 7, [wg, hnT], kk)
                gt_ = tf[m % 2]
                P.act(gt_.t[:], o, AF.Sigmoid, kk, [gt_])
                o2, k2 = ps(4 + m % 2, 0, T)
                for k in range(2):
                    P.mm(o2, wproj.t[:, k, m * 128:(m + 1) * 128], pT.t[:, k, :], k == 0, k == 1, [wproj, pT], k2)
                P.tt('dve', gt_.t[:], gt_.t[:], o2, ALU.mult, [gt_] + k2, [gt_])
                P.tt('pool', hT.t[:, m, :], hT.t[:, m, :], gt_.t[:], ALU.add, [hT, gt_], [hT])
            if not last:
                P.store(hscr.rearrange("k p t -> p k t")[:, :, tok0:tok0 + T], hT.t[:], hT, 'hscr')
            else:
                P.act(sqb.t[:], hT.t[:], AF.Square, [hT], [sqb])
                o, kk = ps(7, 0, T)
                for k in range(8):
                    P.mm(o, onesb.t[:], sqb.t[:, k, :], k == 0, k == 7, [onesb, sqb], kk)
                P.act(rstd.t[:], o, AF.Ln, kk, [rstd], scale=1.0 / D, bias=EPS)
                P.act(rstd.t[:], rstd.t[:], AF.Exp, [rstd], [rstd], scale=-0.5)
                hf = stg[0]
                hfv = hf.t[:, 0:1024].rearrange("p (k t) -> p k t", k=4)
                otok = stg[1]
                for half in range(2):
                    for k4 in range(4):
                        kt_ = 4 * half + k4
                        P.stt(hfv[:, k4, :], hT.t[:, kt_, :], colsB.t[:, 57 + kt_:58 + kt_], rstd.t[:], ALU.mult, ALU.mult,
                              [hT, colsB, rstd], [hf])
                    for a_ in range(2):
                        o, kk = ps(6 + a_, 0, 512)
                        for k4 in range(4):
                            P.tr(o[:, k4 * 128:(k4 + 1) * 128], hfv[:, k4, a_ * 128:(a_ + 1) * 128], ident.t[:], [hf, ident], kk)
                        P.cp('act' if a_ else 'dve', otok.t[:, a_ * 512:(a_ + 1) * 512], o, kk, [otok])
                    P.store(out_d[tok0:tok0 + T, half * 512:(half + 1) * 512].rearrange("(a p) d -> p a d", p=128),
                            otok.t[:, 0:1024].rearrange("p (a d) -> p a d", a=2), otok, 'out')

        cnt = [0]
        NPSTG = 2
        pstg = [stg[0], stg[1],
                Buf(hT.t[:].rearrange("p k t -> p (k t)")[:, 0:1536], hT.k),
                Buf(mixT.t[:].rearrange("p a t -> p (a t)").bitcast(F32)[:, 0:1536], ('mix_rg', 'mix_g', 'mix_s5'))]
        pstgb = [stgb[0], stgb[1],
                 Buf(qn.t[:].rearrange("p k t -> p (k t)"), qn.k),
                 Buf(kn.t[:].rearrange("p k t -> p (k t)"), kn.k)]

        def prep_piece(src_ap, ncols, stores):
            i = cnt[0] % NPSTG
            cnt[0] += 1
            sf, sbf = pstg[i], pstgb[i]
            P.load(sf.t[:, 0:ncols], src_ap, [sf])
            P.cp('dve' if cnt[0] % 2 else 'pool', sbf.t[:, 0:ncols], sf.t[:, 0:ncols], [sf], [sbf])
            for (dst, c0, w, key) in stores:
                srcv = sbf.t[:, c0:c0 + w]
                if len(dst.shape) == 3:
                    srcv = srcv.rearrange("p (g j) -> p g j", g=dst.shape[1])
                P.store(dst, srcv, sbf, key)

        for l in range(depth):
            w_in = prm['w_in'][l]
            for r in range(8):
                rows = slice(r * 128, (r + 1) * 128)
                prep_piece(w_in[rows, 0:1024], 1024, [(win_s[l][0:4, :, r, :].rearrange("g p j -> p g j"), 0, 1024, 'win_s%d' % l)])
                prep_piece(w_in[rows, 1024:2560], 1536, [(win_s[l][4:10, :, r, :].rearrange("g p j -> p g j"), 0, 1536, 'win_s%d' % l)])
                prep_piece(w_in[rows, 2560:4096], 1536, [(win_s[l][10:16, :, r, :].rearrange("g p j -> p g j"), 0, 1536, 'win_s%d' % l)])
                prep_piece(w_in[rows, 4096:5120], 1024, [(win_s[l][16:20, :, r, :].rearrange("g p j -> p g j"), 0, 1024, 'win_s%d' % l)])
                prep_piece(w_in[rows, 5120:6160], 1040, [(wba_s[l][:, r, :], 0, 16, 'wba_s%d' % l),
                                                         (win_s[l][20:24, :, r, :].rearrange("g p j -> p g j"), 16, 1024, 'win_s%d' % l)])
            for r in range(16):
                prep_piece(prm['w_out'][l][r * 128:(r + 1) * 128, :], 1024,
                           [(wout_s[l][:, :, r, :].rearrange("m p j -> p m j"), 0, 1024, 'wout_s%d' % l)])
            for r in range(8):
                prep_piece(prm['ple_w_gate'][l][r * 128:(r + 1) * 128, :], 1024,
                           [(wgate_s[l][:, :, r, :].rearrange("m p j -> p m j"), 0, 1024, 'wgate_s%d' % l)])


        for l in range(depth):
            load_layer_consts(l)
            for c in range(NCH):
                chunk(l, c)
        P.final_wait('act', ['out'])
        P.emit()
    return nc


_CACHE = {}


def kernel(**inputs):
    B = inputs['x'].shape[0]
    S = inputs['x'].shape[1]
    depth = inputs['p'].shape[0]
    key = (S, depth)
    if key not in _CACHE:
        _CACHE[key] = build_program(S, depth)
    nc = _CACHE[key]
    shared = {name: np.ascontiguousarray(inputs[name], dtype=np.float32) for name, _ in PARAM_SHAPES(depth)}
    in_maps = []
    for b in range(B):
        m = dict(shared)
        m['x'] = np.ascontiguousarray(inputs['x'][b], dtype=np.float32)
        m['p'] = np.ascontiguousarray(inputs['p'][:, b], dtype=np.float32)
        in_maps.append(m)
    res = run_bass_kernel_spmd(nc, in_maps, core_ids=list(range(B)))
    return np.stack([r['out'] for r in res.results], axis=0).astype(np.float32)
```

```python
import math
import numpy as np
import concourse.bass as bass
import concourse.mybir as mybir
from concourse.bass_utils import run_bass_kernel_spmd
from contextlib import ExitStack

F32 = mybir.dt.float32
BF16 = mybir.dt.bfloat16
I32 = mybir.dt.int32
AF = mybir.ActivationFunctionType
ALU = mybir.AluOpType
AX = mybir.AxisListType

D = 1024
NIN = 6160
DMIX = 2048
DPLE = 256
T = 256
GC = 64
L5 = 8
CB = T // L5
EPS = 1e-6
TWO_PI = 2.0 * math.pi

ENG = ['pe', 'dve', 'act', 'pool', 'sp']

PARAM_SHAPES = lambda L: [
    ('norm_g', [L, 1024]), ('w_in', [L, 1024, 6160]), ('rg_conv_w', [L, 4, 512]), ('rg_conv_b', [L, 512]),
    ('rg_w_a', [L, 8, 64, 64]), ('rg_b_a', [L, 512]), ('rg_w_x', [L, 8, 64, 64]), ('rg_b_x', [L, 512]),
    ('rg_lambda', [L, 512]), ('gdn_conv_w', [L, 4, 3072]), ('gdn_a_log', [L, 8]), ('gdn_dt_bias', [L, 8]),
    ('gdn_norm_g', [L, 128]), ('s5_a_re', [L, 32, 64]), ('s5_a_im', [L, 32, 64]), ('s5_b_re', [L, 32, 64, 16]),
    ('s5_b_im', [L, 32, 64, 16]), ('s5_c_re', [L, 32, 16, 64]), ('s5_c_im', [L, 32, 16, 64]), ('s5_d', [L, 512]),
    ('s5_log_dt', [L, 32]), ('s5_w_glu', [L, 512, 512]), ('s5_b_glu', [L, 512]), ('w_out', [L, 2048, 1024]),
    ('ple_norm_g', [L, 1024]), ('ple_w_gate', [L, 1024, 1024]), ('ple_w_proj', [L, 256, 1024]),
    ('final_norm_g', [1024]),
]


class Buf:
    def __init__(self, t, k):
        self.t = t
        self.k = k


def _keys(lst):
    out = []
    for r in lst:
        if isinstance(r, Buf):
            if isinstance(r.k, (list, tuple)):
                out.extend(r.k)
            else:
                out.append(r.k)
        elif isinstance(r, (list, tuple, set)):
            out.extend(_keys(r))
        else:
            out.append(r)
    return out


class Prog:
    def __init__(self, nc, es, same_engine_sync=True):
        self.nc = nc
        self.es = es
        self.ops = {e: [] for e in ENG}
        self.esem = {e: es.enter_context(nc.semaphore('s_' + e)) for e in ENG}
        self.ecnt = {e: 0 for e in ENG}
        self.waited = {e: {} for e in ENG}
        self.writers = {}
        self.readers = {}
        self.dsems = {}
        self.dcnt = {}
        self.dsem_name = {}
        self.same_engine_sync = same_engine_sync

    def sbuf(self, name, shape, dt):
        return Buf(self.es.enter_context(self.nc.sbuf_tensor(name, list(shape), dt)), name)

    def _deps(self, eng, reads, writes):
        need = {}

        def add(ev):
            s, v = ev
            k = id(s)
            if k not in need or need[k][1] < v:
                need[k] = (s, v)

        for r in reads:
            for ev in self.writers.get(r, {}).values():
                add(ev)
        for w in writes:
            for ev in self.writers.get(w, {}).values():
                add(ev)
            for ev in self.readers.get(w, {}).values():
                add(ev)
        waits = []
        for k, (s, v) in need.items():
            if s is self.esem[eng] and (eng == 'pe' or not self.same_engine_sync):
                continue
            nm = self.dsem_name.get(k)
            if nm is not None and self.dsems[nm] is s:
                v = max(v, self.dcnt[nm])
            if self.waited[eng].get(k, 0) >= v:
                continue
            self.waited[eng][k] = v
            waits.append((s, v))
        return waits

    def _commit(self, ev, reads, writes):
        k = id(ev[0])
        for r in reads:
            d = self.readers.setdefault(r, {})
            if k not in d or d[k][1] < ev[1]:
                d[k] = ev
        for w in writes:
            d = self.writers.setdefault(w, {})
            if k not in d or d[k][1] < ev[1]:
                d[k] = ev
            self.readers[w] = {}

    def op(self, eng, fn, reads=(), writes=()):
        reads = _keys(reads)
        writes = _keys(writes)
        waits = self._deps(eng, reads, writes)
        if self.ecnt[eng] >= 16000:
            self.esem[eng] = self.es.enter_context(self.nc.semaphore('s_%s_%d' % (eng, len(self.ops[eng]))))
            self.ecnt[eng] = 0
        self.ecnt[eng] += 1
        ev = (self.esem[eng], self.ecnt[eng])
        self.ops[eng].append((waits, fn, ev, 1))
        self._commit(ev, reads, writes)

    def dma(self, q, fn, reads, writes, sem_name, dram_w=()):
        reads = _keys(reads)
        writes = _keys(writes)
        waits = self._deps(q, reads, writes)
        writes = writes + _keys(dram_w)
        if sem_name not in self.dsems:
            self.dsems[sem_name] = self.es.enter_context(self.nc.semaphore('d_' + sem_name))
            self.dcnt[sem_name] = 0
            self.dsem_name[id(self.dsems[sem_name])] = sem_name
        if self.dcnt[sem_name] >= 16000:
            self.dsems[sem_name] = self.es.enter_context(
                self.nc.semaphore('d_%s_%d' % (sem_name, len(self.ops[q]))))
            self.dcnt[sem_name] = 0
            self.dsem_name[id(self.dsems[sem_name])] = sem_name
        self.dcnt[sem_name] += 16
        ev = (self.dsems[sem_name], self.dcnt[sem_name])
        self.ops[q].append((waits, fn, ev, 16))
        self._commit(ev, reads, writes)

    def final_wait(self, eng, resources):
        resources = _keys(resources)
        waits = self._deps(eng, resources, resources)
        self.ops[eng].append((waits, None, None, 0))

    def emit(self):
        nc = self.nc
        with nc.Block() as block:
            for e, deco in [('pe', block.tensor), ('dve', block.vector), ('act', block.scalar),
                            ('pool', block.gpsimd), ('sp', block.sync)]:
                ops = self.ops[e]

                @deco
                def _(engine, ops=ops):
                    for waits, fn, ev, inc in ops:
                        for (s, v) in waits:
                            engine.wait_ge(s, v)
                        if fn is None:
                            continue
                        ins = fn(engine)
                        ins.then_inc(ev[0], inc)

    def mm(self, out, lhsT, rhs, start, stop, R, W, **kw):
        self.op('pe', lambda e: e.matmul(out, lhsT=lhsT, rhs=rhs, start=start, stop=stop, **kw), R, W)

    def tr(self, out, in_, ident, R, W):
        self.op('pe', lambda e: e.transpose(out, in_, ident), R, W)

    def act(self, out, in_, func, R, W, scale=None, bias=None):
        kw = {}
        if scale is not None:
            kw['scale'] = scale
        if bias is not None:
            kw['bias'] = bias
        self.op('act', lambda e: e.activation(out=out, in_=in_, func=func, **kw), R, W)

    def tt(self, eng, out, in0, in1, op, R, W):
        self.op(eng, lambda e: e.tensor_tensor(out=out, in0=in0, in1=in1, op=op), R, W)

    def ts(self, eng, out, in0, s1, op0, R, W, s2=None, op1=None):
        if op1 is None:
            if eng == 'pool':
                s2, op1 = (0.0, ALU.add) if op0 == ALU.mult else (1.0, ALU.mult)
                self.op(eng, lambda e: e.tensor_scalar(out=out, in0=in0, scalar1=s1, scalar2=s2, op0=op0, op1=op1), R, W)
            else:
                self.op(eng, lambda e: e.tensor_scalar(out=out, in0=in0, scalar1=s1, scalar2=None, op0=op0), R, W)
        else:
            self.op(eng, lambda e: e.tensor_scalar(out=out, in0=in0, scalar1=s1, scalar2=s2, op0=op0, op1=op1), R, W)

    def stt(self, out, in0, scalar, in1, op0, op1, R, W):
        self.op('dve', lambda e: e.scalar_tensor_tensor(out=out, in0=in0, scalar=scalar, in1=in1, op0=op0, op1=op1), R, W)

    def cp(self, eng, out, in_, R, W):
        if eng == 'act':
            self.op('act', lambda e: e.activation(out=out, in_=in_, func=AF.Copy), R, W)
        else:
            self.op(eng, lambda e: e.tensor_copy(out=out, in_=in_), R, W)

    def memset(self, eng, ap, val, W):
        self.op(eng, lambda e: e.memset(ap, val), [], W)

    def load(self, out, in_, W, sem=None, R=(), q='sp', slow=False):
        sem = 'ld_' + _keys(W)[0]
        if slow:
            self.dma(q, lambda e: e.dma_start(out=out, in_=in_, allow_slow_non_contiguous=True), R, W, sem)
        else:
            self.dma(q, lambda e: e.dma_start(out=out, in_=in_), R, W, sem)

    def store(self, out, in_, src, dram_key, q='act'):
        sem = 'st_' + _keys([src])[0]
        self.dma(q, lambda e: e.dma_start(out=out, in_=in_), [src], [], sem, dram_w=[dram_key])


def bc(ap, shape):
    return ap.broadcast_to(list(shape))


def build_program(S, depth, use_rg=True, use_gdn=True, use_s5=True, gdn_stop=99, same_engine_sync=True):
    nc = bass.Bass("TRN2", target_bir_lowering=False)
    NCH = S // T
    assert S % T == 0

    def din(name, shape):
        return nc.dram_tensor(name, list(shape), F32, kind="ExternalInput").ap()

    x_d = din("x", [S, D])
    p_d = din("p", [depth, S, DPLE])
    prm = {name: din(name, shape) for name, shape in PARAM_SHAPES(depth)}
    out_d = nc.dram_tensor("out", [S, D], F32, kind="ExternalOutput").ap()
    win_s = [nc.dram_tensor("win_s%d" % l, [24, 128, 8, 256], BF16, kind="Internal").ap() for l in range(depth)]
    wba_s = [nc.dram_tensor("wba_s%d" % l, [128, 8, 16], BF16, kind="Internal").ap() for l in range(depth)]
    wout_s = [nc.dram_tensor("wout_s%d" % l, [8, 128, 16, 128], BF16, kind="Internal").ap() for l in range(depth)]
    wgate_s = [nc.dram_tensor("wgate_s%d" % l, [8, 128, 8, 128], BF16, kind="Internal").ap() for l in range(depth)]
    hscr = nc.dram_tensor("hscr", [8, 128, S], F32, kind="Internal").ap()

    with ExitStack() as es:
        P = Prog(nc, es, same_engine_sync=same_engine_sync)
        sb = P.sbuf

        psd = [es.enter_context(nc.psum_tensor("psd%d" % i, [128, 1024], F32)) for i in range(4)]

        def ps(bank, c0=0, w=512, p0=0, p1=128):
            base = (bank % 2) * 512 + c0
            ap = psd[bank // 2][p0:p1, base:base + w]
            keys = ['psb%d' % b for b in range(bank + c0 // 512, bank + (c0 + w - 1) // 512 + 1)]
            return ap, keys

        def psbf(bank, c0, w, p0=0, p1=128):
            t = psd[bank // 2][p0:p1, (bank % 2) * 512:(bank % 2) * 512 + 512].bitcast(BF16)
            return t[:, c0:c0 + w], ['psb%d' % bank]

        ident = sb('ident', [128, 128], F32)
        identb = sb('identb', [128, 128], BF16)
        onesb = sb('onesb', [128, 128], BF16)
        U64 = sb('U64', [64, 64], F32)
        Mst = sb('Mst', [64, 64], F32)
        ones64 = sb('ones64', [64, 64], F32)
        sel63 = sb('sel63', [64, 128], F32)
        P.memset('pool', ident.t[:], 1.0, [ident])
        P.op('pool', lambda e: e.affine_select(out=ident.t[:], in_=ident.t[:], pattern=[[-1, 128]], compare_op=ALU.is_equal,
                                               fill=0.0, base=0, channel_multiplier=1), [ident], [ident])
        P.cp('pool', identb.t[:], ident.t[:], [ident], [identb])
        P.memset('pool', onesb.t[:], 1.0, [onesb])
        P.memset('pool', U64.t[:], 1.0, [U64])
        P.op('pool', lambda e: e.affine_select(out=U64.t[:], in_=U64.t[:], pattern=[[1, 64]], compare_op=ALU.is_ge,
                                               fill=0.0, base=0, channel_multiplier=-1), [U64], [U64])
        P.memset('pool', Mst.t[:], 1.0, [Mst])
        P.op('pool', lambda e: e.affine_select(out=Mst.t[:], in_=Mst.t[:], pattern=[[-1, 64]], compare_op=ALU.is_gt,
                                               fill=0.0, base=0, channel_multiplier=1), [Mst], [Mst])
        Ust = sb('Ust', [64, 64], F32)
        P.memset('pool', Ust.t[:], 1.0, [Ust])
        P.op('pool', lambda e: e.affine_select(out=Ust.t[:], in_=Ust.t[:], pattern=[[1, 64]], compare_op=ALU.is_gt,
                                               fill=0.0, base=0, channel_multiplier=-1), [Ust], [Ust])
        P.memset('pool', ones64.t[:], 1.0, [ones64])
        P.memset('pool', sel63.t[:], 1.0, [sel63])
        P.op('pool', lambda e: e.affine_select(out=sel63.t[:], in_=sel63.t[:], pattern=[[0, 128]], compare_op=ALU.is_equal,
                                               fill=0.0, base=-63, channel_multiplier=1), [sel63], [sel63])

        stg = [sb('stg%d' % i, [128, 1536], F32) for i in range(2)]
        G = [sb('G%d' % i, [128, 1024], F32) for i in range(4)]
        stgb = [Buf(G[i].t[:].bitcast(BF16), G[i].k) for i in range(2)]

        colsB = sb('colsB', [128, 72], F32)
        gcw = sb('gcw', [128, 96], F32)
        rgc = sb('rgc', [128, 16], F32)
        rgWa = sb('rgWa', [128, 4, 128], BF16)
        rgWx = sb('rgWx', [128, 4, 128], BF16)
        wglu = sb('wglu', [128, 4, 512], BF16)
        wproj = sb('wproj', [128, 2, 1024], BF16)
        negA = sb('negA', [64, 8], F32)
        dtb = sb('dtb', [64, 8], F32)
        RB = dict(norm_g=0, ple_norm_g=8, rg_conv_b=16, rg_b_a=20, rg_b_x=24, rg_lambda=28, s5_d=32, s5_b_glu=36,
                  gdn_norm_g=40, rg_conv_w=41, final=57)

        KT = sb('KT', [128, 4, 8, 128], BF16)
        EB = sb('EB', [128, 4, 8, 2, 128], BF16)
        Ctab = sb('Ctab', [128, 8, 2, 16, 32], BF16)
        Dcos = sb('Dcos', [128, 16, CB], F32)
        Dsin = sb('Dsin', [128, 16, CB], F32)
        Rho = sb('Rho', [128, 16, CB], F32)
        rhoc = sb('rhoc', [128, 16], F32)

        hT = sb('hT', [128, 8, T], F32)
        hnT = sb('hnT', [128, 8, T], BF16)
        sqb = Buf(G[3].t[:].bitcast(BF16).rearrange("p (k t) -> p k t", k=8), G[3].k)
        rstd = sb('rstd', [128, T], F32)
        mixT = sb('mixT', [128, 16, T], BF16)
        raw = sb('raw', [128, 24, 3 + T], BF16)
        histrg = sb('histrg', [128, 4, 3], BF16)
        histg = sb('histg', [128, 24, 3], BF16)
        gate = sb('gate', [128, 8, T], BF16)
        u5 = sb('u5', [128, 4, T], BF16)
        NRING = 5
        wring = [sb('wring%d' % i, [128, 2048], BF16) for i in range(NRING)]
        wba = sb('wba', [128, 8, 16], BF16)
        dg = [sb('dg%d' % i, [128, 128], BF16) for i in range(4)]
        tf = [sb('tf%d' % i, [128, T], F32) for i in range(8)]
        tb = [sb('tb%d' % i, [128, T], BF16) for i in range(2)]
        rgcar = sb('rgcar', [128, 4], F32)
        pT = sb('pT', [128, 2, T], BF16)
        XPre = sb('XPre', [128, 16, CB + 1], F32)
        XPim = sb('XPim', [128, 16, CB + 1], F32)
        XPb = sb('XPb', [128, 2, 16, CB], BF16)
        z5b = sb('z5b', [128, 4, T], BF16)
        s5big = [Buf(G[i // 2].t[:, (i % 2) * 512:(i % 2) * 512 + 512], G[i // 2].k) for i in range(6)]
        qn = sb('qn', [128, 8, T], BF16)
        kn = sb('kn', [128, 8, T], BF16)
        vs = sb('vs', [128, 8, T], BF16)
        Sst = sb('Sst', [128, 8, 128], F32)
        Sb = sb('Sb', [128, 8, 128], BF16)
        NI = T // GC
        gsm = {n: sb('g_' + n, [64, NI * 8], F32) for n in ['bt', 'nbt', 'xa', 'ex', 'sp', 'gp', 'g', 'gcum', 'egc', 'dgl', 'kds', 'bge']}
        gsm.update({n: sb('g_' + n, [64, 8], F32) for n in ['ss', 'ln', 'rs']})
        egl = sb('egl', [128, NI * 8], F32)
        gw = [sb('gw%d' % i, [64, 8, 64], F32) for i in range(7)]
        aqkT = sb('aqkT', [64, 8, 64], BF16)
        gx = [Buf(G[i].t[0:64, :].rearrange("p (h e) -> p h e", h=8), G[i].k) for i in range(4)]
        kdec = sb('kdec', [64, 8, 128], BF16)
        vnew = sb('vnew', [64, 8, 128], BF16)
        wTb = sb('wTb', [128, 8, 64], BF16)

        s5t = {n: sb('s5_' + n, [128, 16], F32) for n in
               ['are', 'aim', 'ldt', 'dt', 'ard', 'th', 'cr', 'den', 'inv', 'cfr', 'cfi', 't0', 't1', 'tqb']}
        hTf = hT.t[:].rearrange("p k t -> p (k t)")
        Lre = Buf(hTf[:, 0:144].rearrange("p (j i) -> p j i", j=9), hT.k)
        Lim = Buf(hTf[:, 256:400].rearrange("p (j i) -> p j i", j=9), hT.k)
        jv = sb('jv', [128, 8, 16], F32)
        jvi = Buf(G[2].t[:, 0:128].bitcast(I32).rearrange("p (j i) -> p j i", j=8), G[2].k)
        mask2 = sb('mask2', [128, 2, 16], F32)
        s5int = Buf(G[3].t[:, 0:512].bitcast(I32), G[3].k)
        cvi = Buf(G[3].t[:, 512:1024].bitcast(I32).rearrange("p (i c) -> p i c", i=16), G[3].k)
        knf = kn.t[:].rearrange("p k t -> p (k t)").bitcast(F32)
        qnf = qn.t[:].rearrange("p k t -> p (k t)").bitcast(F32)
        vsf = vs.t[:].rearrange("p k t -> p (k t)").bitcast(F32)
        hnf = hnT.t[:].rearrange("p k t -> p (k t)").bitcast(F32)
        v3_ = lambda ap, a: ap.rearrange("p (a b) -> p a b", a=a)
        bre = Buf(v3_(vsf[:, 0:256], 16), vs.k)
        bim = Buf(v3_(vsf[:, 256:512], 16), vs.k)
        cre = Buf(v3_(vsf[:, 512:768], 16), vs.k)
        cim = Buf(v3_(vsf[:, 768:1024], 16), vs.k)
        Bre = Buf(v3_(qnf[:, 0:256], 16), qn.k)
        Bim = Buf(v3_(qnf[:, 256:512], 16), qn.k)
        Cbr = Buf(v3_(qnf[:, 512:1024], 16), qn.k)
        Cbi = Buf(v3_(knf[:, 0:512], 16), kn.k)
        cv = Buf(v3_(knf[:, 512:1024], 16), kn.k)
        maskW = Buf(hnf[:, 0:512].rearrange("p (a g c) -> p a g c", a=4, g=8), hnT.k)
        P.memset('pool', mask2.t[:], 0.0, [mask2])
        P.memset('pool', mask2.t[0:64, 0, :], 1.0, [mask2])
        P.memset('pool', mask2.t[64:128, 1, :], 1.0, [mask2])
        P.op('pool', lambda e: e.iota(jvi.t[:], pattern=[[1, 8], [0, 16]], base=1, channel_multiplier=0), [], [jvi])
        P.cp('pool', jv.t[:], jvi.t[:], [jvi], [jv])

        def load_layer_consts(l):
            sA = stg[0]
            P.load(sA.t[0:96, 0:128], prm['gdn_conv_w'][l].rearrange("k (j p) -> (k j) p", p=128), [sA], 'stg0')
            o, kk = ps(7, 0, 96)
            P.tr(o, sA.t[0:96, 0:128], ident.t[0:96, 0:96], [sA, ident], kk)
            P.cp('dve', gcw.t[:], o, kk, [gcw])
            sBt = stg[1]
            rows = [('norm_g', prm['norm_g'][l], 8), ('ple_norm_g', prm['ple_norm_g'][l], 8), ('rg_conv_b', prm['rg_conv_b'][l], 4),
                    ('rg_b_a', prm['rg_b_a'][l], 4), ('rg_b_x', prm['rg_b_x'][l], 4), ('rg_lambda', prm['rg_lambda'][l], 4),
                    ('s5_d', prm['s5_d'][l], 4), ('s5_b_glu', prm['s5_b_glu'][l], 4), ('gdn_norm_g', prm['gdn_norm_g'][l], 1)]
            for name, ap, n in rows:
                r0 = RB[name]
                P.load(sBt.t[r0:r0 + n, 0:128], ap.rearrange("(k p) -> k p", p=128), [sBt], 'stg1')
            P.load(sBt.t[41:57, 0:128], prm['rg_conv_w'][l].rearrange("k (j p) -> (k j) p", p=128), [sBt], 'stg1')
            P.load(sBt.t[57:65, 0:128], prm['final_norm_g'].rearrange("(k p) -> k p", p=128), [sBt], 'stg1')
            o, kk = ps(7, 128, 65)
            P.tr(o, sBt.t[0:65, 0:128], ident.t[0:65, 0:65], [sBt, ident], kk)
            P.cp('dve', colsB.t[:, 0:65], o, kk, [colsB])
            z = s5t['t0'].t[:, 0:4]
            acc = s5t['t1'].t[:, 0:4]
            P.act(z, colsB.t[:, 28:32], AF.Exp, [colsB], [s5t['t0']], scale=-1.0)
            P.ts('dve', acc, z, -1.0 / 9.0, ALU.mult, [s5t['t0']], [s5t['t1']], s2=1.0 / 8.0, op1=ALU.add)
            for k in range(7, 0, -1):
                P.tt('dve', acc, acc, z, ALU.mult, [s5t['t0'], s5t['t1']], [s5t['t1']])
                P.ts('dve', acc, acc, -1.0, ALU.mult, [s5t['t1']], [s5t['t1']], s2=1.0 / k, op1=ALU.add)
            P.tt('dve', acc, acc, z, ALU.mult, [s5t['t0'], s5t['t1']], [s5t['t1']])
            P.ts('dve', rgc.t[:, 0:4], acc, -8.0, ALU.mult, [s5t['t1']], [rgc])
            P.ts('dve', rgc.t[:, 4:8], acc, -16.0, ALU.mult, [s5t['t1']], [rgc])
            P.ts('dve', rgc.t[:, 8:16], colsB.t[:, 20:28], -1.0, ALU.mult, [colsB], [rgc])
            for (src, dst) in [(prm['rg_w_a'][l], rgWa), (prm['rg_w_x'][l], rgWx)]:
                s = stg[0]
                P.memset('pool', s.t[:, 0:512], 0.0, [s])
                sv = s.t[:, 0:512].rearrange("p (t j) -> p t j", t=4)
                for h2 in range(2):
                    P.load(sv[h2 * 64:(h2 + 1) * 64, :, h2 * 64:(h2 + 1) * 64],
                           src.rearrange("(t h2) i j -> h2 i t j", h2=2)[h2], [s], 'stg0')
                P.cp('pool', dst.t[:], sv, [s], [dst])
            for hh in range(2):
                s = stg[hh]
                P.load(s.t[:, 0:1024].rearrange("p (k n) -> p k n", k=2),
                       prm['s5_w_glu'][l].rearrange("(k p) n -> p k n", p=128)[:, 2 * hh:2 * hh + 2, :], [s], 'stg%d' % hh)
                P.cp('dve', wglu.t[:, 2 * hh:2 * hh + 2, :], s.t[:, 0:1024].rearrange("p (k n) -> p k n", k=2), [s], [wglu])
            for hh in range(2):
                s = stg[hh]
                P.load(s.t[:, 0:1024], prm['ple_w_proj'][l][hh * 128:(hh + 1) * 128, :], [s], 'stg%d' % hh)
                P.cp('pool', wproj.t[:, hh, :], s.t[:, 0:1024], [s], [wproj])
            P.load(gsm['xa'].t[:, 0:8], prm['gdn_a_log'][l].partition_broadcast(64), [gsm['xa']], 'gsm')
            P.act(negA.t[:], gsm['xa'].t[:, 0:8], AF.Exp, [gsm['xa']], [negA])
            P.load(dtb.t[:], prm['gdn_dt_bias'][l].partition_broadcast(64), [dtb], 'gsm')
            if use_s5:
                s5_setup(l)

        def sincos(tq, n, out_sin, out_cos, Rk, Wk_sin, Wk_cos):
            ti = s5int.t[:, 0:n]
            tfl = s5big[4].t[:, 0:n]
            fr = s5big[5].t[:, 0:n]
            for (shift, out, Wk) in [(0.0, out_sin, Wk_sin), (0.25, out_cos, Wk_cos)]:
                src = tq
                if shift != 0.0:
                    P.ts('dve', fr, tq, shift, ALU.add, Rk, [s5big[5]])
                    src = fr
                    R2 = [s5big[5]]
                else:
                    R2 = Rk
                P.cp('dve', ti, src, R2, [s5int])
                P.cp('dve', tfl, ti, [s5int], [s5big[4]])
                P.tt('dve', fr, src, tfl, ALU.subtract, R2 + [s5big[4]], [s5big[5]])
                P.act(out, fr, AF.Sin, [s5big[5]], Wk, scale=TWO_PI)

        def s5_setup(l):
            t = s5t
            P.op('pool', lambda e: e.iota(cvi.t[:], pattern=[[0, 16], [1, CB]], base=1, channel_multiplier=0), [], [cvi])
            P.cp('pool', cv.t[:], cvi.t[:], [cvi], [cv])
            P.memset('pool', maskW.t[:], 0.0, [maskW])
            for jj in range(4):
                P.memset('pool', maskW.t[0:64, jj, 2 * jj, :], 1.0, [maskW])
                P.memset('pool', maskW.t[64:128, jj, 2 * jj + 1, :], 1.0, [maskW])
            for name, src in [('are', prm['s5_a_re'][l]), ('aim', prm['s5_a_im'][l])]:
                for g2 in range(2):
                    P.load(t[name].t[g2 * 64:(g2 + 1) * 64, :], src.rearrange("(i g2) n -> g2 n i", g2=2)[g2], [t[name]], 's5ld', slow=True)
            for g2 in range(2):
                P.load(t['ldt'].t[g2 * 64:(g2 + 1) * 64, :], prm['s5_log_dt'][l].rearrange("(i g2) -> g2 i", g2=2)[g2].partition_broadcast(64),
                       [t['ldt']], 's5ld', slow=True)
            for (dst, src) in [(bre, prm['s5_b_re'][l]), (bim, prm['s5_b_im'][l])]:
                for g2 in range(2):
                    P.load(dst.t[g2 * 64:(g2 + 1) * 64, :, :], src.rearrange("(i g2) n c -> g2 n i c", g2=2)[g2], [dst], 's5ld')
            for (dst, src) in [(cre, prm['s5_c_re'][l]), (cim, prm['s5_c_im'][l])]:
                for g2 in range(2):
                    for i_ in range(16):
                        P.load(dst.t[g2 * 64:(g2 + 1) * 64, i_, :],
                               src.rearrange("(i g2) c n -> g2 i n c", g2=2)[g2, i_], [dst], 's5ld', slow=True)
            P.act(t['dt'].t[:], t['ldt'].t[:], AF.Exp, [t['ldt']], [t['dt']])
            P.tt('dve', t['ard'].t[:], t['are'].t[:], t['dt'].t[:], ALU.mult, [t['are'], t['dt']], [t['ard']])
            P.tt('dve', t['th'].t[:], t['aim'].t[:], t['dt'].t[:], ALU.mult, [t['aim'], t['dt']], [t['th']])
            A0 = s5big[0].t[:, 0:128].rearrange("p (j i) -> p j i", j=8)
            A1 = s5big[1].t[:, 0:128].rearrange("p (j i) -> p j i", j=8)
            A2 = s5big[2].t[:, 0:128].rearrange("p (j i) -> p j i", j=8)
            A3 = s5big[3].t[:, 0:128].rearrange("p (j i) -> p j i", j=8)
            P.tt('dve', A0, jv.t[:], bc(t['ard'].t[:].unsqueeze(1), [128, 8, 16]), ALU.mult, [jv, t['ard']], [s5big[0]])
            P.act(A0, A0, AF.Exp, [s5big[0]], [s5big[0]])
            P.tt('dve', A1, jv.t[:], bc(t['th'].t[:].unsqueeze(1), [128, 8, 16]), ALU.mult, [jv, t['th']], [s5big[1]])
            P.ts('dve', A1, A1, 1.0 / TWO_PI, ALU.mult, [s5big[1]], [s5big[1]])
            sincos(s5big[1].t[:, 0:128], 128, s5big[2].t[:, 0:128], s5big[3].t[:, 0:128], [s5big[1]], [s5big[2]], [s5big[3]])
            P.memset('dve', Lre.t[:, 0, :], 1.0, [Lre])
            P.memset('dve', Lim.t[:, 0, :], 0.0, [Lim])
            P.tt('dve', Lre.t[:, 1:9, :], A0, A3, ALU.mult, [s5big[0], s5big[3]], [Lre])
            P.tt('dve', Lim.t[:, 1:9, :], A0, A2, ALU.mult, [s5big[0], s5big[2]], [Lim])
            P.ts('dve', t['tqb'].t[:], t['th'].t[:], float(L5) / TWO_PI, ALU.mult, [t['th']], [t['tqb']])
            B0 = s5big[0].t[:, 0:16 * CB].rearrange("p (i c) -> p i c", i=16)
            P.tt('dve', B0, cv.t[:], bc(t['tqb'].t[:].unsqueeze(2), [128, 16, CB]), ALU.mult, [cv, t['tqb']], [s5big[0]])
            sincos(s5big[0].t[:, 0:16 * CB], 16 * CB, Dsin.t[:].rearrange("p i c -> p (i c)"), Dcos.t[:].rearrange("p i c -> p (i c)"),
                   [s5big[0]], [Dsin], [Dcos])
            P.act(rhoc.t[:], t['ard'].t[:], AF.Exp, [t['ard']], [rhoc], scale=float(L5))
            P.cp('dve', Rho.t[:], bc(rhoc.t[:].unsqueeze(2), [128, 16, CB]), [rhoc], [Rho])
            P.memset('dve', Rho.t[:, :, 0:1], 0.0, [Rho])
            P.ts('dve', t['cr'].t[:], Lre.t[:, 1, :], -1.0, ALU.add, [Lre], [t['cr']])
            P.tt('dve', t['den'].t[:], t['are'].t[:], t['are'].t[:], ALU.mult, [t['are']], [t['den']])
            P.tt('dve', t['t0'].t[:], t['aim'].t[:], t['aim'].t[:], ALU.mult, [t['aim']], [t['t0']])
            P.tt('dve', t['den'].t[:], t['den'].t[:], t['t0'].t[:], ALU.add, [t['den'], t['t0']], [t['den']])
            P.op('dve', lambda e: e.reciprocal(out=t['inv'].t[:], in_=t['den'].t[:]), [t['den']], [t['inv']])
            P.tt('dve', t['t0'].t[:], t['cr'].t[:], t['are'].t[:], ALU.mult, [t['cr'], t['are']], [t['t0']])
            P.tt('dve', t['t1'].t[:], Lim.t[:, 1, :], t['aim'].t[:], ALU.mult, [Lim, t['aim']], [t['t1']])
            P.tt('dve', t['t0'].t[:], t['t0'].t[:], t['t1'].t[:], ALU.add, [t['t0'], t['t1']], [t['t0']])
            P.tt('dve', t['cfr'].t[:], t['t0'].t[:], t['inv'].t[:], ALU.mult, [t['t0'], t['inv']], [t['cfr']])
            P.tt('dve', t['t0'].t[:], Lim.t[:, 1, :], t['are'].t[:], ALU.mult, [Lim, t['are']], [t['t0']])
            P.tt('dve', t['t1'].t[:], t['cr'].t[:], t['aim'].t[:], ALU.mult, [t['cr'], t['aim']], [t['t1']])
            P.tt('dve', t['t0'].t[:], t['t0'].t[:], t['t1'].t[:], ALU.subtract, [t['t0'], t['t1']], [t['t0']])
            P.tt('dve', t['cfi'].t[:], t['t0'].t[:], t['inv'].t[:], ALU.mult, [t['t0'], t['inv']], [t['cfi']])

            def cmul(out_re, out_im, a_re, a_im, b_re, b_im, shape, Ra, Rb, Wre, Wim, neg_im=False):
                n = 1
                for d_ in shape[1:]:
                    n *= d_
                v0 = s5big[4].t[:, 0:n]
                v1 = s5big[5].t[:, 0:n]
                if len(shape) == 3:
                    v0 = v0.rearrange("p (a b) -> p a b", a=shape[1])
                    v1 = v1.rearrange("p (a b) -> p a b", a=shape[1])
                P.tt('dve', v0, a_re, b_re, ALU.mult, Ra + Rb, [s5big[4]])
                P.tt('dve', v1, a_im, b_im, ALU.mult, Ra + Rb, [s5big[5]])
                P.tt('dve', out_re, v0, v1, ALU.subtract, [s5big[4], s5big[5]], Wre)
                P.tt('dve', v0, a_re, b_im, ALU.mult, Ra + Rb, [s5big[4]])
                P.tt('dve', v1, a_im, b_re, ALU.mult, Ra + Rb, [s5big[5]])
                P.tt('dve', out_im, v0, v1, ALU.add, [s5big[4], s5big[5]], Wim)
                if neg_im:
                    P.ts('dve', out_im, out_im, -1.0, ALU.mult, Wim, Wim)

            sh3 = [128, 16, 16]
            cmul(Bre.t[:], Bim.t[:], bc(t['cfr'].t[:].unsqueeze(2), sh3), bc(t['cfi'].t[:].unsqueeze(2), sh3), bre.t[:], bim.t[:],
                 sh3, [t['cfr'], t['cfi']], [bre, bim], [Bre], [Bim])
            for (dst, src) in [(Cbr, cre), (Cbi, cim)]:
                for i4 in range(4):
                    P.tt('dve', dst.t[:, 4 * i4:4 * i4 + 4, :].rearrange("p i (g c) -> p i g c", g=2),
                         bc(src.t[:, 4 * i4:4 * i4 + 4, :].unsqueeze(2), [128, 4, 2, 16]),
                         bc(mask2.t[:].unsqueeze(1), [128, 4, 2, 16]), ALU.mult, [src, mask2], [dst])
            shb = [128, 16, 32]
            for s in range(8):
                lr = bc(Lre.t[:, s + 1, :].unsqueeze(2), shb)
                li = bc(Lim.t[:, s + 1, :].unsqueeze(2), shb)
                cr_o = s5big[0].t[:, 0:512].rearrange("p (a b) -> p a b", a=16)
                ci_o = s5big[1].t[:, 0:512].rearrange("p (a b) -> p a b", a=16)
                cmul(cr_o, ci_o, lr, li, Cbr.t[:], Cbi.t[:], shb, [Lre, Lim], [Cbr, Cbi], [s5big[0]], [s5big[1]], neg_im=True)
                P.cp('pool', Ctab.t[:, s, 0, :, :], cr_o, [s5big[0]], [Ctab])
                P.cp('pool', Ctab.t[:, s, 1, :, :], ci_o, [s5big[1]], [Ctab])
            Pre = s5big[0].t[:, 0:256].rearrange("p (a b) -> p a b", a=16)
            Pim = s5big[1].t[:, 0:256].rearrange("p (a b) -> p a b", a=16)
            Pbr = s5big[2].t[:, 0:512].rearrange("p (a b) -> p a b", a=16)
            Pbi = s5big[3].t[:, 0:512].rearrange("p (a b) -> p a b", a=16)
            Pwr = stg[0].t[:, 0:512]
            Pwi = stg[0].t[:, 512:1024]
            for j in range(8):
                lr = bc(Lre.t[:, j, :].unsqueeze(2), sh3)
                li = bc(Lim.t[:, j, :].unsqueeze(2), sh3)
                cmul(Pre, Pim, lr, li, Bre.t[:], Bim.t[:], sh3, [Lre, Lim], [Bre, Bim], [s5big[0]], [s5big[1]])
                for (dst, src, kd, ks) in [(Pbr, Pre, s5big[2], s5big[0]), (Pbi, Pim, s5big[3], s5big[1])]:
                    for i4 in range(4):
                        P.tt('dve', dst[:, 4 * i4:4 * i4 + 4, :].rearrange("p i (g c) -> p i g c", g=2),
                             bc(src[:, 4 * i4:4 * i4 + 4, :].unsqueeze(2), [128, 4, 2, 16]),
                             bc(mask2.t[:].unsqueeze(1), [128, 4, 2, 16]), ALU.mult, [ks, mask2], [kd])
                sp_ = 7 - j
                for ct in range(4):
                    for part, (src, kd) in enumerate([(Pbr, s5big[2]), (Pbi, s5big[3])]):
                        o, kk = ps(6, part * 128, 128)
                        P.tr(o, src[:, 4 * ct:4 * ct + 4, :].rearrange("p a b -> p (a b)"), ident.t[:], [kd, ident], kk)
                        P.cp('act', EB.t[:, ct, sp_, part, :], o, kk, [EB])
                    for (dstw, src, ks, sgn) in [(Pwr, Pre, s5big[0], 1.0), (Pwi, Pim, s5big[1], -1.0)]:
                        P.tt('dve', dstw.rearrange("p (a g c) -> p a g c", a=4, g=8),
                             bc(src[:, 4 * ct:4 * ct + 4, :].unsqueeze(2), [128, 4, 8, 16]), maskW.t[:], ALU.mult, [ks, maskW], [stg[0]])
                    P.ts('dve', Pwi, Pwi, -1.0, ALU.mult, [stg[0]], [stg[0]])
                    o, kk = ps(7, 256, 128)
                    for jj in range(4):
                        i = 4 * ct + jj
                        P.mm(o[:, 32 * jj:32 * jj + 32], Pwr[:, 128 * jj:128 * jj + 128], Cbr.t[:, i, :], True, False, [stg[0], Cbr], kk)
                        P.mm(o[:, 32 * jj:32 * jj + 32], Pwi[:, 128 * jj:128 * jj + 128], Cbi.t[:, i, :], False, True, [stg[0], Cbi], kk)
                    P.cp('act', KT.t[:, ct, j, :], o, kk, [KT])

        def rmsnorm(gcol0, out_bf=None, out_f32=None):
            P.act(sqb.t[:], hT.t[:], AF.Square, [hT], [sqb])
            o, kk = ps(7, 0, T)
            for k in range(8):
                P.mm(o, onesb.t[:], sqb.t[:, k, :], k == 0, k == 7, [onesb, sqb], kk)
            P.act(rstd.t[:], o, AF.Ln, kk, [rstd], scale=1.0 / D, bias=EPS)
            P.act(rstd.t[:], rstd.t[:], AF.Exp, [rstd], [rstd], scale=-0.5)
            dst = out_bf if out_bf is not None else out_f32
            for k in range(8):
                P.stt(dst.t[:, k, :], hT.t[:, k, :], colsB.t[:, gcol0 + k:gcol0 + k + 1], rstd.t[:], ALU.mult, ALU.mult,
                      [hT, colsB, rstd], [dst])

        wcount = [0]

        def ring_next():
            b = wring[wcount[0] % NRING]
            wcount[0] += 1
            return b

        def stream_w(l, g):
            b = ring_next()
            v = Buf(b.t[:].rearrange("p (k n) -> p k n", k=8), b.k)
            P.load(v.t, win_s[l][g], [b], R=['win_s%d' % l])
            return v

        pcount = [0]

        def inproj_tile(wb, col0):
            slot = pcount[0] % 4
            pcount[0] += 1
            o, kk = ps(slot, 0, T)
            for k in range(8):
                P.mm(o, wb.t[:, k, col0:col0 + 128], hnT.t[:, k, :], k == 0, k == 7, [wb, hnT], kk)
            return o, kk

        ecount = [0]

        def evac_engine():
            ecount[0] += 1
            return 'act' if ecount[0] % 2 else 'dve'

        dgc = [0]

        def conv_tile(src_buf, jt, wcols, col_of_tap, R):
            slot = pcount[0] % 4
            pcount[0] += 1
            o, kk = ps(slot, 0, T)
            for k in range(4):
                d = dg[dgc[0] % 4]
                dgc[0] += 1
                c = col_of_tap(k)
                P.ts('pool', d.t[:], identb.t[:], wcols.t[:, c:c + 1], ALU.mult, [identb, wcols], [d])
                P.mm(o, d.t[:], src_buf.t[:, jt, k:k + T], k == 0, k == 3, [d] + R, kk)
            return o, kk

        def rg_chunk(l, c):
            if c == 0:
                P.memset('pool', raw.t[:, 0:4, 0:3], 0.0, ['raw_rg'])
            else:
                P.cp('pool', raw.t[:, 0:4, 0:3], histrg.t[:], [histrg], ['raw_rg'])
            for g in range(4):
                wb = stream_w(l, g)
                for n in range(2):
                    o, kk = inproj_tile(wb, n * 128)
                    j = (g % 2) * 2 + n
                    if g < 2:
                        P.cp(evac_engine(), raw.t[:, j, 3:3 + T], o, kk, ['raw_rg'])
                    else:
                        P.act(gate.t[:, j, :], o, AF.Silu, kk, ['gate_rg'])
            P.cp('pool', histrg.t[:], raw.t[:, 0:4, T:T + 3], ['raw_rg'], [histrg])
            yield 'inproj'
            def rg_tile(j):
                p_ = j % 2
                r_, gi_, xj, m_ = tf[4 * p_], tf[4 * p_ + 1], tf[4 * p_ + 2], tf[4 * p_ + 3]
                xb_ = tb[p_]
                o, kk = conv_tile(raw, j, colsB, lambda k: RB['rg_conv_w'] + k * 4 + j, ['raw_rg'])
                yield
                P.act(xj.t[:], o, AF.Identity, kk, [xj], bias=colsB.t[:, 16 + j:17 + j])
                yield
                P.cp('dve', xb_.t[:], xj.t[:], [xj], [xb_])
                yield
                oa, ka = ps(4 + 2 * p_, 0, T)
                ox, kx = ps(5 + 2 * p_, 0, T)
                P.mm(oa, rgWa.t[:, j, :], xb_.t[:], True, True, [rgWa, xb_], ka)
                P.mm(ox, rgWx.t[:, j, :], xb_.t[:], True, True, [rgWx, xb_], kx)
                yield
                P.act(r_.t[:], oa, AF.Exp, ka + [rgc], [r_], scale=-1.0, bias=rgc.t[:, 8 + j:9 + j])
                P.act(gi_.t[:], ox, AF.Exp, kx + [rgc], [gi_], scale=-1.0, bias=rgc.t[:, 12 + j:13 + j])
                yield
                P.act(r_.t[:], r_.t[:], AF.Ln, [r_], [r_], bias=1.0)
                P.act(gi_.t[:], gi_.t[:], AF.Ln, [gi_], [gi_], bias=1.0)
                yield
                P.act(r_.t[:], r_.t[:], AF.Exp, [r_], [r_], scale=-1.0)
                P.act(gi_.t[:], gi_.t[:], AF.Exp, [gi_], [gi_], scale=-1.0)
                yield
                P.tt('pool', gi_.t[:], gi_.t[:], xj.t[:], ALU.mult, [gi_, xj], [gi_])
                a_ = xj
                P.act(m_.t[:], r_.t[:], AF.Exp, [r_, rgc], [m_], scale=rgc.t[:, 4 + j:5 + j])
                yield
                P.act(a_.t[:], r_.t[:], AF.Exp, [r_, rgc, gi_], [a_], scale=rgc.t[:, j:j + 1])
                yield
                P.act(m_.t[:], m_.t[:], AF.Sqrt, [m_], [m_], scale=-1.0, bias=1.0)
                yield
                P.tt('dve', m_.t[:], m_.t[:], gi_.t[:], ALU.mult, [m_, gi_], [m_])
                yield
                hr = r_
                if c == 0:
                    P.op('dve', lambda e: e.tensor_tensor_scan(out=hr.t[:], data0=a_.t[:], data1=m_.t[:], initial=0.0,
                                                               op0=ALU.mult, op1=ALU.add), [a_, m_], [hr])
                else:
                    P.op('dve', lambda e: e.tensor_tensor_scan(out=hr.t[:], data0=a_.t[:], data1=m_.t[:],
                                                               initial=rgcar.t[:, j:j + 1], op0=ALU.mult, op1=ALU.add),
                         [a_, m_, rgcar], [hr])
                yield
                P.cp('dve', rgcar.t[:, j:j + 1], hr.t[:, T - 1:T], [hr], [rgcar])
                P.tt('pool', mixT.t[:, j, :], hr.t[:], gate.t[:, j, :], ALU.mult, [hr, 'gate_rg'], ['mix_rg'])

            for j0 in (0, 2):
                pair = [rg_tile(j0), rg_tile(j0 + 1)]
                live = [True, True]
                next(pair[0], None)
                next(pair[0], None)
                while any(live):
                    for q_ in (1, 0):
                        if live[q_]:
                            try:
                                next(pair[q_])
                            except StopIteration:
                                live[q_] = False
                yield 'pair'

        def s5_chunk(l, c):
            for g in range(4):
                wb = stream_w(l, 20 + g)
                for n in range(2):
                    o, kk = inproj_tile(wb, n * 128)
                    j = (g % 2) * 2 + n
                    if g < 2:
                        P.cp(evac_engine(), u5.t[:, j, :], o, kk, [u5])
                    else:
                        P.act(gate.t[:, 4 + j, :], o, AF.Silu, kk, ['gate_s5'])
            if c == 0:
                P.memset('pool', XPre.t[:, :, 0:1], 0.0, [XPre])
                P.memset('pool', XPim.t[:, :, 0:1], 0.0, [XPim])
            pe_ = [ps(4, 0, 512), ps(5, 0, 512)]
            for i in range(16):
                ct, jj = i // 4, i % 4
                uv = u5.t[32 * jj:32 * jj + 32, ct, :].rearrange("p (c s) -> p s c", s=L5)
                for part in range(2):
                    o, kk = pe_[part]
                    for s_ in range(L5):
                        P.mm(o[:, i * CB:(i + 1) * CB], EB.t[32 * jj:32 * jj + 32, ct, s_, part, :], uv[:, s_, :],
                             s_ == 0, s_ == L5 - 1, [EB, u5], kk, tile_position=(32 * jj, 0))
            ere, kre = pe_[0]
            eim, kim = pe_[1]
            t1, t2, mre, mim, qre, qim = s5big[0], s5big[1], s5big[2], s5big[3], s5big[4], s5big[5]
            dcs = Dcos.t[:].rearrange("p i c -> p (i c)")
            dsn = Dsin.t[:].rearrange("p i c -> p (i c)")
            P.tt('dve', t1.t[:], ere, dcs, ALU.mult, kre + [Dcos], [t1])
            P.tt('dve', t2.t[:], eim, dsn, ALU.mult, kim + [Dsin], [t2])
            P.tt('pool', mre.t[:], t1.t[:], t2.t[:], ALU.add, [t1, t2], [mre])
            P.tt('dve', t1.t[:], eim, dcs, ALU.mult, kim + [Dcos], [t1])
            P.tt('dve', t2.t[:], ere, dsn, ALU.mult, kre + [Dsin], [t2])
            P.tt('pool', mim.t[:], t1.t[:], t2.t[:], ALU.subtract, [t1, t2], [mim])
            for (m_, XP) in [(mre, XPre), (mim, XPim)]:
                mv = m_.t[:].rearrange("p (i c) -> p i c", i=16)
                P.tt('pool', s5t['t0'].t[:].unsqueeze(2), rhoc.t[:].unsqueeze(2), XP.t[:, :, 0:1], ALU.mult, [rhoc, XP], [s5t['t0']])
                P.tt('pool', mv[:, :, 0:1], mv[:, :, 0:1], s5t['t0'].t[:].unsqueeze(2), ALU.add, [m_, s5t['t0']], [m_])
            rhf = Rho.t[:].rearrange("p i c -> p (i c)")
            for (m_, q_) in [(mre, qre), (mim, qim)]:
                P.op('dve', lambda e, m_=m_, q_=q_: e.tensor_tensor_scan(out=q_.t[:], data0=rhf, data1=m_.t[:], initial=0.0,
                                                                        op0=ALU.mult, op1=ALU.add), [Rho, m_], [q_])
            qrv = qre.t[:].rearrange("p (i c) -> p i c", i=16)
            qiv = qim.t[:].rearrange("p (i c) -> p i c", i=16)
            t1v = t1.t[:].rearrange("p (i c) -> p i c", i=16)
            t2v = t2.t[:].rearrange("p (i c) -> p i c", i=16)
            P.tt('dve', t1v, qrv, Dcos.t[:], ALU.mult, [qre, Dcos], [t1])
            P.tt('pool', t2v, qiv, Dsin.t[:], ALU.mult, [qim, Dsin], [t2])
            P.tt('dve', XPre.t[:, :, 1:CB + 1], t1v, t2v, ALU.subtract, [t1, t2], [XPre])
            P.tt('dve', t1v, qrv, Dsin.t[:], ALU.mult, [qre, Dsin], [t1])
            P.tt('pool', t2v, qiv, Dcos.t[:], ALU.mult, [qim, Dcos], [t2])
            P.tt('dve', XPim.t[:, :, 1:CB + 1], t1v, t2v, ALU.add, [t1, t2], [XPim])
            P.cp('pool', XPb.t[:, 0, :, :], XPre.t[:, :, 0:CB], [XPre], [XPb])
            P.cp('pool', XPb.t[:, 1, :, :], XPim.t[:, :, 0:CB], [XPim], [XPb])
            P.cp('pool', XPre.t[:, :, 0:1], XPre.t[:, :, CB:CB + 1], [XPre, XPb], [XPre])
            P.cp('pool', XPim.t[:, :, 0:1], XPim.t[:, :, CB:CB + 1], [XPim, XPb], [XPim])
            y5t = [Buf(G[3].t[:, ct_ * T:(ct_ + 1) * T], G[3].k) for ct_ in range(4)]
            yield 'part1'
            for ct in range(4):
                if ct == 2:
                    yield 'y01'
                o, kk = ps(6 + (ct % 2), 0, T)
                uv = u5.t[:, ct, :].rearrange("p (c s) -> p s c", s=L5)
                for s_ in range(L5):
                    oc = o[:, s_ * CB:(s_ + 1) * CB]
                    for sp_ in range(s_ + 1):
                        P.mm(oc, KT.t[:, ct, s_ - sp_, :], uv[:, sp_, :], sp_ == 0, False, [KT, u5], kk)
                    for jj in range(4):
                        i = 4 * ct + jj
                        for part in range(2):
                            P.mm(o[32 * jj:32 * jj + 32, s_ * CB:(s_ + 1) * CB], Ctab.t[:, s_, part, i, :], XPb.t[:, part, i, :],
                                 False, (part == 1), [Ctab, XPb], kk, tile_position=(0, 32 * jj))
                P.stt(y5t[ct].t.rearrange("p (c s) -> p s c", s=L5), uv, colsB.t[:, 32 + ct:33 + ct],
                      o.rearrange("p (s c) -> p s c", s=L5), ALU.mult, ALU.add, [u5, colsB] + kk, [y5t[ct]])
            yield 'y23'
            def gelu_tile(ct):
                y_, z_ = y5t[ct], tf[4 + ct]
                P.tt('pool', z_.t[:], y_.t, y_.t, ALU.mult, [y_], [z_])
                yield
                P.ts('pool', z_.t[:], z_.t[:], 0.044715, ALU.mult, [z_], [z_], s2=1.0, op1=ALU.add)
                yield
                P.tt('pool', z_.t[:], z_.t[:], y_.t, ALU.mult, [z_, y_], [z_])
                yield
                P.act(z_.t[:], z_.t[:], AF.Sigmoid, [z_], [z_], scale=1.5957691216057308)
                yield
                P.tt('dve', z_.t[:], z_.t[:], y_.t, ALU.mult, [z_, y_], [z_])
                yield
                P.cp('pool', z5b.t[:, ct, :], z_.t[:], [z_], [z5b])

            def rr(gens):
                live = [True] * len(gens)
                while any(live):
                    for q_ in range(len(gens)):
                        if live[q_]:
                            try:
                                next(gens[q_])
                            except StopIteration:
                                live[q_] = False

            rr([gelu_tile(ct) for ct in range(4)])

            def glu_tile(m):
                slot = pcount[0] % 4
                pcount[0] += 1
                o, kk = ps(slot, 0, T)
                for k in range(4):
                    P.mm(o, wglu.t[:, k, m * 128:(m + 1) * 128], z5b.t[:, k, :], k == 0, k == 3, [wglu, z5b], kk)
                yield
                gl = tf[m]
                P.act(gl.t[:], o, AF.Sigmoid, kk, [gl], bias=colsB.t[:, 36 + m:37 + m])
                yield
                P.tt('pool', gl.t[:], gl.t[:], tf[4 + m].t[:], ALU.mult, [gl, tf[4 + m]], [gl])
                yield
                P.tt('dve', mixT.t[:, 12 + m, :], gl.t[:], gate.t[:, 4 + m, :], ALU.mult, [gl, 'gate_s5'], ['mix_s5'])

            rr([glu_tile(m) for m in range(4)])

        def gdn_chunk(l, c):
            if c == 0:
                P.memset('pool', raw.t[:, :, 0:3], 0.0, ['raw_rg', 'raw_g0', 'raw_g1', 'raw_g2'])
                P.memset('pool', Sst.t[:], 0.0, [Sst])
                P.memset('pool', Sb.t[:], 0.0, [Sb])
            else:
                P.cp('pool', raw.t[:, :, 0:3], histg.t[:], [histg], ['raw_rg', 'raw_g0', 'raw_g1', 'raw_g2'])
            P.load(wba.t[:], wba_s[l], [wba], R=['wba_s%d' % l])
            def rawk(j):
                return (['raw_rg'] if j < 4 else []) + ['raw_g%d' % (j // 8)]

            def inproj_q(qtr):
                for g in range(4 * qtr, 4 * qtr + 4):
                    wb = stream_w(l, 4 + g)
                    for n in range(2):
                        o, kk = inproj_tile(wb, n * 128)
                        j = g * 2 + n
                        if j < 24:
                            P.cp(evac_engine(), raw.t[:, j, 3:3 + T], o, kk, rawk(j))
                        else:
                            P.act(gate.t[:, j - 24, :], o, AF.Silu, kk, ['gate_rg', 'gate_s5', 'gate_g'])

            def conv_q(qtr):
                P.cp('pool', histg.t[:, 8 * qtr:8 * qtr + 8, :], raw.t[:, 8 * qtr:8 * qtr + 8, T:T + 3], ['raw_rg', 'raw_g%d' % qtr], [histg])
                def conv_norm_tile(j):
                    o, kk = conv_tile(raw, j, gcw, lambda k: k * 24 + j, rawk(j))
                    yield
                    sl, lv, rs_ = tf[(j % 2) * 3], tf[(j % 2) * 3 + 1], tf[(j % 2) * 3 + 2]
                    P.act(lv.t[:], o, AF.Exp, kk, [lv], scale=-1.0)
                    yield
                    P.act(lv.t[:], lv.t[:], AF.Ln, [lv], [lv], bias=1.0)
                    yield
                    P.act(lv.t[:], lv.t[:], AF.Exp, [lv], [lv], scale=-1.0)
                    yield
                    if j >= 16:
                        P.tt('dve', vs.t[:, j - 16, :], o, lv.t[:], ALU.mult, kk + [lv], [vs])
                        return
                    sq_ = tb[j % 2]
                    P.tt('dve', sl.t[:], o, lv.t[:], ALU.mult, kk + [lv], [sl])
                    yield
                    P.tt('pool', sq_.t[:], sl.t[:], sl.t[:], ALU.mult, [sl], [sq_])
                    yield
                    o2, k2 = ps(4 + (j % 2), 0, T)
                    P.mm(o2, onesb.t[:], sq_.t[:], True, True, [onesb, sq_], k2)
                    yield
                    P.act(lv.t[:], o2, AF.Ln, k2, [lv], bias=EPS)
                    yield
                    if j < 8:
                        P.act(rs_.t[:], lv.t[:], AF.Exp, [lv], [rs_], scale=-0.5, bias=-0.5 * math.log(128.0))
                        yield
                        P.tt('dve', qn.t[:, j, :], sl.t[:], rs_.t[:], ALU.mult, [sl, rs_], [qn])
                    else:
                        P.act(rs_.t[:], lv.t[:], AF.Exp, [lv], [rs_], scale=-0.5)
                        yield
                        P.tt('dve', kn.t[:, j - 8, :], sl.t[:], rs_.t[:], ALU.mult, [sl, rs_], [kn])

                for j0 in range(8 * qtr, 8 * qtr + 8, 2):
                    pair = [conv_norm_tile(j0), conv_norm_tile(j0 + 1)]
                    live = [True, True]
                    next(pair[0], None)
                    next(pair[0], None)
                    while any(live):
                        for q_ in (1, 0):
                            if live[q_]:
                                try:
                                    next(pair[q_])
                                except StopIteration:
                                    live[q_] = False
            inproj_q(0)
            inproj_q(1)
            conv_q(0)
            inproj_q(2)
            conv_q(1)
            inproj_q(3)
            conv_q(2)
            if gdn_stop < 7 and c == 0 and l == 0:
                P.memset('pool', mixT.t[:, 4:12, :], 0.0, ['mix_g'])
            if gdn_stop >= 1:
                gdn_scalars(l, c)
            gens = [gdn_inner(l, c, gci) for gci in range(T // GC)]
            next(gens[0], None)
            for gci in range(T // GC):
                next(gens[gci], None)
                if gci + 1 < T // GC:
                    next(gens[gci + 1], None)
                next(gens[gci], None)

        def gdn_scalars(l, c):
            g = gsm
            v3 = lambda ap: ap.rearrange("p (i h) -> p i h", i=NI)
            o, kk = ps(0, 0, NI * 16, 0, 64)
            for gci in range(NI):
                for k in range(8):
                    P.mm(o[:, gci * 16:(gci + 1) * 16], hnT.t[:, k, gci * GC:(gci + 1) * GC], wba.t[:, k, :], k == 0, k == 7, [hnT, wba], kk)
            ov = o.rearrange("p (i c) -> p i c", i=NI)
            P.act(v3(g['bt'].t[:]), ov[:, :, 0:8], AF.Sigmoid, kk, [g['bt']])
            P.tt('dve', v3(g['xa'].t[:]), ov[:, :, 8:16], bc(dtb.t[:].unsqueeze(1), [64, NI, 8]), ALU.add, kk + [dtb], [g['xa']])
            P.act(g['ex'].t[:], g['xa'].t[:], AF.Exp, [g['xa']], [g['ex']])
            P.act(g['sp'].t[:], g['ex'].t[:], AF.Ln, [g['ex']], [g['sp']], bias=1.0)
            P.tt('dve', v3(g['gp'].t[:]), v3(g['sp'].t[:]), bc(negA.t[:].unsqueeze(1), [64, NI, 8]), ALU.mult, [g['sp'], negA], [g['gp']])
            P.ts('dve', g['g'].t[:], g['gp'].t[:], -1.0, ALU.mult, [g['gp']], [g['g']])
            P.ts('pool', g['nbt'].t[:], g['bt'].t[:], -1.0, ALU.mult, [g['bt']], [g['nbt']])
            oc, kc = ps(0, 64, NI * 8, 0, 64)
            P.mm(oc, U64.t[:], g['g'].t[:], True, True, [U64, g['g']], kc)
            P.cp('dve', g['gcum'].t[:], oc, kc, [g['gcum']])
            ol, kl = ps(0, 128, NI * 8)
            P.mm(ol, sel63.t[:], g['gcum'].t[:], True, True, [sel63, g['gcum']], kl)
            P.act(egl.t[:], ol, AF.Exp, kl, [egl])
            P.act(g['egc'].t[:], g['gcum'].t[:], AF.Exp, [g['gcum']], [g['egc']])
            P.tt('dve', g['dgl'].t[:], ol[0:64, :], g['gcum'].t[:], ALU.subtract, kl + [g['gcum']], [g['dgl']])
            P.act(g['kds'].t[:], g['dgl'].t[:], AF.Exp, [g['dgl']], [g['kds']])
            P.tt('dve', g['bge'].t[:], g['bt'].t[:], g['egc'].t[:], ALU.mult, [g['bt'], g['egc']], [g['bge']])

        def hb(ap, n):
            return bc(ap.unsqueeze(2), [64, 8, n])

        def m8(ap64):
            return bc(ap64.unsqueeze(1), [64, 8, 64])

        def gdn_inner(l, c, gci):
            t0 = gci * GC
            cs = slice(t0, t0 + GC)
            fl = lambda b_: b_.t[:].rearrange("p h j -> p (h j)")
            if gdn_stop < 1:
                return
            g = {n: (Buf(gsm[n].t[:, gci * 8:(gci + 1) * 8], gsm[n].k) if n not in ('ss', 'ln', 'rs') else Buf(gsm[n].t[:], gsm[n].k)) for n in gsm}
            eglv = egl.t[:, gci * 8:(gci + 1) * 8]
            if gdn_stop < 2:
                return
            NGU, GBC, E1, E2, DK = gw[0], gw[1], gw[2], gw[3], gw[4]
            P.tt('dve', NGU.t[:], m8(U64.t[:]), hb(g['gp'].t, 64), ALU.mult, [U64, g['gp']], [NGU])
            P.cp('pool', GBC.t[:], hb(g['g'].t, 64), [g['g']], [GBC])
            oD, kD = ps(1, 0, 512, 0, 64)
            P.mm(oD, U64.t[:], fl(GBC), True, False, [U64, GBC], kD)
            P.mm(oD, ones64.t[:], fl(NGU), False, True, [ones64, NGU], kD)
            P.ts('dve', fl(E1), oD, 0.0, ALU.min, kD, [E1])
            P.ts('dve', fl(E2), oD, -1.0, ALU.mult, kD, [E2], s2=0.0, op1=ALU.min)
            P.act(fl(E1), fl(E1), AF.Exp, [E1], [E1])
            P.act(fl(E2), fl(E2), AF.Exp, [E2], [E2])
            P.tt('pool', DK.t[:], m8(Mst.t[:]), hb(g['nbt'].t, 64), ALU.mult, [Mst, g['nbt']], [DK])
            P.tt('pool', DK.t[:], DK.t[:], E1.t[:], ALU.mult, [DK, E1], [DK])
            E2u = E2
            DQ = gw[6]
            P.tt('pool', DQ.t[:], E2u.t[:], m8(U64.t[:]), ALU.mult, [E2u, U64], [DQ])
            if gdn_stop < 3:
                return
            Bd = gw[1]
            P.tt('dve', Bd.t[:], m8(ident.t[0:64, 0:64]), hb(g['nbt'].t, 64), ALU.mult, [ident, g['nbt']], [Bd])
            oB, kB = ps(1, 0, 512, 0, 64)
            P.mm(oB, ones64.t[:], fl(Bd), True, True, [ones64, Bd], kB)
            MK = gw[1]
            P.tt('dve', MK.t[:], E2u.t[:], m8(Ust.t[:]), ALU.mult, [E2u, Ust], [MK])
            P.tt('dve', fl(MK), fl(MK), oB, ALU.mult, [MK] + kB, [MK])
            okk, kkk = ps(2, 0, 512, 0, 64)
            oqk, kqk = ps(3, 0, 512, 0, 64)
            for h in range(8):
                P.mm(okk[:, h * 64:(h + 1) * 64], kn.t[:, h, cs], kn.t[:, h, cs], True, True, [kn], kkk)
            for h in range(8):
                P.mm(oqk[:, h * 64:(h + 1) * 64], kn.t[:, h, cs], qn.t[:, h, cs], True, True, [kn, qn], kqk)
            def bfv(b_):
                return Buf(b_.t[:].rearrange("p h j -> p (h j)").bitcast(BF16)[:, 0:512].rearrange("p (h j) -> p h j", h=8), b_.k)
            CH_BF16 = True
            if CH_BF16:
                Nb = [bfv(gw[5]), bfv(gw[6])]
                Mb = [bfv(gw[0]), bfv(gw[1])]
                PTb = [bfv(gw[2]), bfv(gw[3])]
            else:
                Nb = [gw[5], gw[6]]
                Mb = [gw[0], gw[1]]
                PTb = [gw[2], gw[4]]
            P.tt('dve', fl(Nb[0]), okk, fl(DK), ALU.mult, kkk + [DK], [Nb[0]])
            P.tt('dve', fl(aqkT), oqk, fl(DQ), ALU.mult, kqk + [DQ], [aqkT])
            if gdn_stop < 4:
                return
            P.tt('dve', fl(Mb[0]), okk, fl(MK), ALU.mult, kkk + [MK], [Mb[0]])
            P.tt('dve', PTb[0].t[:], Mb[0].t[:], m8(ident.t[0:64, 0:64]), ALU.add, [Mb[0], ident], [PTb[0]])
            cur = 0
            for lev in range(1, 6):
                nxt = 1 - cur
                oN, kN = ps(1, 0, 512, 0, 64)
                for h in range(8):
                    P.mm(oN[:, h * 64:(h + 1) * 64], Mb[cur].t[:, h, :], Nb[cur].t[:, h, :], True, True, [Mb[cur], Nb[cur]], kN)
                if lev < 5:
                    oM, kM = ps(2, 0, 512, 0, 64)
                    for h in range(8):
                        P.mm(oM[:, h * 64:(h + 1) * 64], Nb[cur].t[:, h, :], Mb[cur].t[:, h, :], True, True, [Mb[cur], Nb[cur]], kM)
                P.cp('act', fl(Nb[nxt]), oN, kN, [Nb[nxt]])
                if lev < 5:
                    P.cp('dve', fl(Mb[nxt]), oM, kM, [Mb[nxt]])
                oP, kP = ps(3, 0, 512, 0, 64)
                for h in range(8):
                    P.mm(oP[:, h * 64:(h + 1) * 64], Nb[nxt].t[:, h, :], PTb[cur].t[:, h, :], True, True, [Nb[nxt], PTb[cur]], kP)
                P.tt('dve', fl(PTb[nxt]), oP, fl(PTb[cur]), ALU.add, kP + [PTb[cur]], [PTb[nxt]])
                cur = nxt
            PT = PTb[cur]
            if gdn_stop < 5:
                return
            okt, kkt = psbf(4, 0, 1024, 0, 64)
            ovt, kvt = psbf(5, 0, 1024, 0, 64)
            for h in range(8):
                P.tr(okt[:, h * 128:(h + 1) * 128], kn.t[:, h, cs], identb.t[:], [kn, identb], kkt)
            for h in range(8):
                P.tr(ovt[:, h * 128:(h + 1) * 128], vs.t[:, h, cs], identb.t[:], [vs, identb], kvt)
            kbg, vb, usb, ob = gx[0], gx[1], gx[2], gx[3]
            if CH_BF16:
                bx = lambda b_: Buf(b_.t[:].rearrange("p h e -> p (h e)").bitcast(BF16)[:, 0:1024].rearrange("p (h e) -> p h e", h=8), b_.k)
                kbg, vb = bx(gx[0]), bx(gx[1])
            fx = lambda b_: b_.t[:].rearrange("p h e -> p (h e)")
            v3 = lambda ap: ap.rearrange("p (h e) -> p h e", h=8)
            P.tt('dve', kbg.t[:], v3(okt), hb(g['bge'].t, 128), ALU.mult, kkt + [g['bge']], [kbg])
            P.tt('dve', kdec.t[:], v3(okt), hb(g['kds'].t, 128), ALU.mult, kkt + [g['kds']], [kdec])
            P.tt('dve', vb.t[:], v3(ovt), hb(g['bt'].t, 128), ALU.mult, kvt + [g['bt']], [vb])
            ou, ku = ps(6, 0, 1024, 0, 64)
            for h in range(8):
                P.mm(ou[:, h * 128:(h + 1) * 128], PT.t[:, h, :], vb.t[:, h, :], True, True, [PT, vb], ku)
            ow, kw = ps(0, 0, 512)
            for h in range(8):
                P.mm(ow[:, h * 64:(h + 1) * 64], kbg.t[:, h, :], PT.t[:, h, :], True, True, [PT, kbg], kw)
            P.cp('act', fx(usb), ou, ku, [usb])
            P.cp('act', wTb.t[:].rearrange("p h c -> p (h c)"), ow, kw, [wTb])
            if gdn_stop < 6:
                return
            yield 'AB'
            o1, k1 = ps(4, 0, 1024, 0, 64)
            for h in range(8):
                P.mm(o1[:, h * 128:(h + 1) * 128], wTb.t[:, h, :], Sb.t[:, h, :], True, True, [wTb, Sb], k1)
            o2, k2 = ps(2, 0, 1024, 0, 64)
            for h in range(8):
                P.mm(o2[:, h * 128:(h + 1) * 128], qn.t[:, h, cs], Sb.t[:, h, :], True, True, [qn, Sb], k2)
            P.tt('dve', fx(vnew), fx(usb), o1, ALU.subtract, [usb] + k1, [vnew])
            o3, k3 = ps(6, 0, 1024, 0, 64)
            for h in range(8):
                P.mm(o3[:, h * 128:(h + 1) * 128], aqkT.t[:, h, :], vnew.t[:, h, :], True, True, [aqkT, vnew], k3)
            o4, k4 = ps(0, 0, 1024)
            for h in range(8):
                P.mm(o4[:, h * 128:(h + 1) * 128], kdec.t[:, h, :], vnew.t[:, h, :], True, True, [kdec, vnew], k4)
            P.tt('dve', ob.t[:], v3(o2), hb(g['egc'].t, 128), ALU.mult, k2 + [g['egc']], [ob])
            P.tt('dve', fx(ob), fx(ob), o3, ALU.add, [ob] + k3, [ob])
            for h in range(8):
                P.stt(Sst.t[:, h, :], Sst.t[:, h, :], eglv[:, h:h + 1], o4[:, h * 128:(h + 1) * 128], ALU.mult, ALU.add,
                      [Sst, egl] + k4, [Sst])
            P.cp('pool', Sb.t[:], Sst.t[:], [Sst], [Sb])
            if gdn_stop < 7:
                return
            yield 'C'
            osq = gx[0]
            P.tt('pool', osq.t[:], ob.t[:], ob.t[:], ALU.mult, [ob], [osq])
            P.op('dve', lambda e: e.tensor_reduce(out=g['ss'].t, in_=osq.t[:], axis=AX.X, op=ALU.add), [osq], [g['ss']])
            P.act(g['ln'].t, g['ss'].t, AF.Ln, [g['ss']], [g['ln']], scale=1.0 / 128.0, bias=EPS)
            P.act(g['rs'].t, g['ln'].t, AF.Exp, [g['ln']], [g['rs']], scale=-0.5)
            on = Buf(gx[1].t[:].rearrange("p h e -> p (h e)").bitcast(BF16)[:, 0:1024].rearrange("p (h e) -> p h e", h=8), gx[1].k)
            P.tt('dve', on.t[:], ob.t[:], hb(g['rs'].t, 128), ALU.mult, [ob, g['rs']], [on])
            oo, ko = psbf(5, 0, 512)
            for h in range(8):
                P.tr(oo[:, h * 64:(h + 1) * 64], on.t[:, h, :], identb.t[0:64, 0:64], [on, identb], ko)
            P.stt(mixT.t[:, 4:12, cs], oo.rearrange("p (h c) -> p h c", h=8), colsB.t[:, 40:41], gate.t[:, :, cs],
                  ALU.mult, ALU.mult, ko + [colsB, 'gate_g'], ['mix_g'])

        ocount = [0]

        def chunk(l, c):
            tok0 = c * T
            last = (l == depth - 1)
            if l == 0:
                for a_ in range(2):
                    P.load(stg[a_].t[:, 0:1024], x_d[tok0 + a_ * 128:tok0 + (a_ + 1) * 128, :], [stg[a_]], 'stg%d' % a_)
                for half in range(2):
                    o, kk = ps(6, 0, 1024)
                    for k4 in range(4):
                        for a_ in range(2):
                            kt_ = 4 * half + k4
                            P.tr(o[:, k4 * 256 + a_ * 128:k4 * 256 + a_ * 128 + 128], stg[a_].t[:, kt_ * 128:(kt_ + 1) * 128], ident.t[:],
                                 [stg[a_], ident], kk)
                    P.cp('act' if half else 'dve', hT.t[:, 4 * half:4 * half + 4, :].rearrange("p k t -> p (k t)"), o, kk, [hT])
            else:
                P.load(hT.t[:], hscr.rearrange("k p t -> p k t")[:, :, tok0:tok0 + T], [hT], 'hT', R=['hscr'])
            rmsnorm(RB['norm_g'], out_bf=hnT)
            if not use_rg and c == 0 and l == 0:
                P.memset('pool', mixT.t[:, 0:4, :], 0.0, ['mix_rg'])
            if not use_s5 and c == 0 and l == 0:
                P.memset('pool', mixT.t[:, 12:16, :], 0.0, ['mix_s5'])
            gr = rg_chunk(l, c) if use_rg else iter(())
            g5 = s5_chunk(l, c) if use_s5 else iter(())
            next(gr, None)
            next(g5, None)
            next(gr, None)
            next(g5, None)
            next(gr, None)
            for _ in gr:
                pass
            for _ in g5:
                pass
            if use_gdn:
                gdn_chunk(l, c)
            elif c == 0 and l == 0:
                P.memset('pool', mixT.t[:, 4:12, :], 0.0, ['mix_g'])
            for m in range(8):
                b_ = ring_next()
                wo = Buf(b_.t[:].rearrange("p (k n) -> p k n", k=16), b_.k)
                P.load(wo.t, wout_s[l][m], [b_], R=['wout_s%d' % l])
                slot = pcount[0] % 4
                pcount[0] += 1
                o, kk = ps(slot, 0, T)
                for k in range(16):
                    mk = 'mix_rg' if k < 4 else ('mix_g' if k < 12 else 'mix_s5')
                    P.mm(o, wo.t[:, k, :], mixT.t[:, k, :], k == 0, k == 15, [wo, mk], kk)
                P.tt('dve', hT.t[:, m, :], hT.t[:, m, :], o, ALU.add, [hT] + kk, [hT])
            rmsnorm(RB['ple_norm_g'], out_bf=hnT)
            ptok = stg[1]
            pv = ptok.t[:, 1024:1536].rearrange("p (a d) -> p a d", a=2)
            P.load(pv, p_d[l, tok0:tok0 + T, :].rearrange("(a p) d -> p a d", p=128), [ptok], 'stg1')
            o, kk = ps(6, 0, 512)
            for k in range(2):
                for a in range(2):
                    P.tr(o[:, k * 256 + a * 128:k * 256 + a * 128 + 128], pv[:, a, k * 128:(k + 1) * 128], ident.t[:], [ptok, ident], kk)
            P.cp('act', pT.t[:].rearrange("p k t -> p (k t)"), o, kk, [pT])
            for m in range(8):
                b_ = ring_next()
                wg = Buf(b_.t[:, 0:1024].rearrange("p (k n) -> p k n", k=8), b_.k)
                P.load(wg.t, wgate_s[l][m], [b_], R=['wgate_s%d' % l])
                slot = pcount[0] % 4
                pcount[0] += 1
                o, kk = ps(slot, 0, T)
                for k in range(8):
                    P.mm(o, wg.t[:, k, :], hnT.t[:, k, :], k == 0, k == 7, [wg, hnT], kk)
                gt_ = tf[m % 2]
                P.act(gt_.t[:], o, AF.Sigmoid, kk, [gt_])
                o2, k2 = ps(4 + m % 2, 0, T)
                for k in range(2):
                    P.mm(o2, wproj.t[:, k, m * 128:(m + 1) * 128], pT.t[:, k, :], k == 0, k == 1, [wproj, pT], k2)
                P.tt('dve', gt_.t[:], gt_.t[:], o2, ALU.mult, [gt_] + k2, [gt_])
                P.tt('pool', hT.t[:, m, :], hT.t[:, m, :], gt_.t[:], ALU.add, [hT, gt_], [hT])
            if not last:
                P.store(hscr.rearrange("k p t -> p k t")[:, :, tok0:tok0 + T], hT.t[:], hT, 'hscr')
            else:
                P.act(sqb.t[:], hT.t[:], AF.Square, [hT], [sqb])
                o, kk = ps(7, 0, T)
                for k in range(8):
                    P.mm(o, onesb.t[:], sqb.t[:, k, :], k == 0, k == 7, [onesb, sqb], kk)
                P.act(rstd.t[:], o, AF.Ln, kk, [rstd], scale=1.0 / D, bias=EPS)
                P.act(rstd.t[:], rstd.t[:], AF.Exp, [rstd], [rstd], scale=-0.5)
                hf = stg[0]
                hfv = hf.t[:, 0:1024].rearrange("p (k t) -> p k t", k=4)
                otok = stg[1]
                for half in range(2):
                    for k4 in range(4):
                        kt_ = 4 * half + k4
                        P.stt(hfv[:, k4, :], hT.t[:, kt_, :], colsB.t[:, 57 + kt_:58 + kt_], rstd.t[:], ALU.mult, ALU.mult,
                              [hT, colsB, rstd], [hf])
                    for a_ in range(2):
                        o, kk = ps(6 + a_, 0, 512)
                        for k4 in range(4):
                            P.tr(o[:, k4 * 128:(k4 + 1) * 128], hfv[:, k4, a_ * 128:(a_ + 1) * 128], ident.t[:], [hf, ident], kk)
                        P.cp('act' if a_ else 'dve', otok.t[:, a_ * 512:(a_ + 1) * 512], o, kk, [otok])
                    P.store(out_d[tok0:tok0 + T, half * 512:(half + 1) * 512].rearrange("(a p) d -> p a d", p=128),
                            otok.t[:, 0:1024].rearrange("p (a d) -> p a d", a=2), otok, 'out')

        cnt = [0]
        NPSTG = 2
        pstg = [stg[0], stg[1],
                Buf(hT.t[:].rearrange("p k t -> p (k t)")[:, 0:1536], hT.k),
                Buf(mixT.t[:].rearrange("p a t -> p (a t)").bitcast(F32)[:, 0:1536], ('mix_rg', 'mix_g', 'mix_s5'))]
        pstgb = [stgb[0], stgb[1],
                 Buf(qn.t[:].rearrange("p k t -> p (k t)"), qn.k),
                 Buf(kn.t[:].rearrange("p k t -> p (k t)"), kn.k)]

        def prep_piece(src_ap, ncols, stores):
            i = cnt[0] % NPSTG
            cnt[0] += 1
            sf, sbf = pstg[i], pstgb[i]
            P.load(sf.t[:, 0:ncols], src_ap, [sf])
            P.cp('dve' if cnt[0] % 2 else 'pool', sbf.t[:, 0:ncols], sf.t[:, 0:ncols], [sf], [sbf])
            for (dst, c0, w, key) in stores:
                srcv = sbf.t[:, c0:c0 + w]
                if len(dst.shape) == 3:
                    srcv = srcv.rearrange("p (g j) -> p g j", g=dst.shape[1])
                P.store(dst, srcv, sbf, key)

        for l in range(depth):
            w_in = prm['w_in'][l]
            for r in range(8):
                rows = slice(r * 128, (r + 1) * 128)
                prep_piece(w_in[rows, 0:1024], 1024, [(win_s[l][0:4, :, r, :].rearrange("g p j -> p g j"), 0, 1024, 'win_s%d' % l)])
                prep_piece(w_in[rows, 1024:2560], 1536, [(win_s[l][4:10, :, r, :].rearrange("g p j -> p g j"), 0, 1536, 'win_s%d' % l)])
                prep_piece(w_in[rows, 2560:4096], 1536, [(win_s[l][10:16, :, r, :].rearrange("g p j -> p g j"), 0, 1536, 'win_s%d' % l)])
                prep_piece(w_in[rows, 4096:5120], 1024, [(win_s[l][16:20, :, r, :].rearrange("g p j -> p g j"), 0, 1024, 'win_s%d' % l)])
                prep_piece(w_in[rows, 5120:6160], 1040, [(wba_s[l][:, r, :], 0, 16, 'wba_s%d' % l),
                                                         (win_s[l][20:24, :, r, :].rearrange("g p j -> p g j"), 16, 1024, 'win_s%d' % l)])
            for r in range(16):
                prep_piece(prm['w_out'][l][r * 128:(r + 1) * 128, :], 1024,
                           [(wout_s[l][:, :, r, :].rearrange("m p j -> p m j"), 0, 1024, 'wout_s%d' % l)])
            for r in range(8):
                prep_piece(prm['ple_w_gate'][l][r * 128:(r + 1) * 128, :], 1024,
                           [(wgate_s[l][:, :, r, :].rearrange("m p j -> p m j"), 0, 1024, 'wgate_s%d' % l)])


        for l in range(depth):
            load_layer_consts(l)
            for c in range(NCH):
                chunk(l, c)
        P.final_wait('act', ['out'])
        P.emit()
    return nc


_CACHE = {}


def kernel(**inputs):
    B = inputs['x'].shape[0]
    S = inputs['x'].shape[1]
    depth = inputs['p'].shape[0]
    key = (S, depth)
    if key not in _CACHE:
        _CACHE[key] = build_program(S, depth)
    nc = _CACHE[key]
    shared = {name: np.ascontiguousarray(inputs[name], dtype=np.float32) for name, _ in PARAM_SHAPES(depth)}
    in_maps = []
    for b in range(B):
        m = dict(shared)
        m['x'] = np.ascontiguousarray(inputs['x'][b], dtype=np.float32)
        m['p'] = np.ascontiguousarray(inputs['p'][:, b], dtype=np.float32)
        in_maps.append(m)
    res = run_bass_kernel_spmd(nc, in_maps, core_ids=list(range(B)))
    return np.stack([r['out'] for r in res.results], axis=0).astype(np.float32)
```

```python
import math
import numpy as np
import concourse.bass as bass
import concourse.mybir as mybir
from concourse.bass_utils import run_bass_kernel_spmd
from contextlib import ExitStack

F32 = mybir.dt.float32
BF16 = mybir.dt.bfloat16
I32 = mybir.dt.int32
AF = mybir.ActivationFunctionType
ALU = mybir.AluOpType
AX = mybir.AxisListType

D = 1024
NIN = 6160
DMIX = 2048
DPLE = 256
T = 256
GC = 64
L5 = 8
CB = T // L5
EPS = 1e-6
TWO_PI = 2.0 * math.pi

ENG = ['pe', 'dve', 'act', 'pool', 'sp']

PARAM_SHAPES = lambda L: [
    ('norm_g', [L, 1024]), ('w_in', [L, 1024, 6160]), ('rg_conv_w', [L, 4, 512]), ('rg_conv_b', [L, 512]),
    ('rg_w_a', [L, 8, 64, 64]), ('rg_b_a', [L, 512]), ('rg_w_x', [L, 8, 64, 64]), ('rg_b_x', [L, 512]),
    ('rg_lambda', [L, 512]), ('gdn_conv_w', [L, 4, 3072]), ('gdn_a_log', [L, 8]), ('gdn_dt_bias', [L, 8]),
    ('gdn_norm_g', [L, 128]), ('s5_a_re', [L, 32, 64]), ('s5_a_im', [L, 32, 64]), ('s5_b_re', [L, 32, 64, 16]),
    ('s5_b_im', [L, 32, 64, 16]), ('s5_c_re', [L, 32, 16, 64]), ('s5_c_im', [L, 32, 16, 64]), ('s5_d', [L, 512]),
    ('s5_log_dt', [L, 32]), ('s5_w_glu', [L, 512, 512]), ('s5_b_glu', [L, 512]), ('w_out', [L, 2048, 1024]),
    ('ple_norm_g', [L, 1024]), ('ple_w_gate', [L, 1024, 1024]), ('ple_w_proj', [L, 256, 1024]),
    ('final_norm_g', [1024]),
]


class Buf:
    def __init__(self, t, k):
        self.t = t
        self.k = k


def _keys(lst):
    out = []
    for r in lst:
        if isinstance(r, Buf):
            if isinstance(r.k, (list, tuple)):
                out.extend(r.k)
            else:
                out.append(r.k)
        elif isinstance(r, (list, tuple, set)):
            out.extend(_keys(r))
        else:
            out.append(r)
    return out


class Prog:
    def __init__(self, nc, es, same_engine_sync=True):
        self.nc = nc
        self.es = es
        self.ops = {e: [] for e in ENG}
        self.esem = {e: es.enter_context(nc.semaphore('s_' + e)) for e in ENG}
        self.ecnt = {e: 0 for e in ENG}
        self.waited = {e: {} for e in ENG}
        self.writers = {}
        self.readers = {}
        self.dsems = {}
        self.dcnt = {}
        self.dsem_name = {}
        self.same_engine_sync = same_engine_sync

    def sbuf(self, name, shape, dt):
        return Buf(self.es.enter_context(self.nc.sbuf_tensor(name, list(shape), dt)), name)

    def _deps(self, eng, reads, writes):
        need = {}

        def add(ev):
            s, v = ev
            k = id(s)
            if k not in need or need[k][1] < v:
                need[k] = (s, v)

        for r in reads:
            for ev in self.writers.get(r, {}).values():
                add(ev)
        for w in writes:
            for ev in self.writers.get(w, {}).values():
                add(ev)
            for ev in self.readers.get(w, {}).values():
                add(ev)
        waits = []
        for k, (s, v) in need.items():
            if s is self.esem[eng] and (eng == 'pe' or not self.same_engine_sync):
                continue
            nm = self.dsem_name.get(k)
            if nm is not None and self.dsems[nm] is s:
                v = max(v, self.dcnt[nm])
            if self.waited[eng].get(k, 0) >= v:
                continue
            self.waited[eng][k] = v
            waits.append((s, v))
        return waits

    def _commit(self, ev, reads, writes):
        k = id(ev[0])
        for r in reads:
            d = self.readers.setdefault(r, {})
            if k not in d or d[k][1] < ev[1]:
                d[k] = ev
        for w in writes:
            d = self.writers.setdefault(w, {})
            if k not in d or d[k][1] < ev[1]:
                d[k] = ev
            self.readers[w] = {}

    def op(self, eng, fn, reads=(), writes=()):
        reads = _keys(reads)
        writes = _keys(writes)
        waits = self._deps(eng, reads, writes)
        if self.ecnt[eng] >= 16000:
            self.esem[eng] = self.es.enter_context(self.nc.semaphore('s_%s_%d' % (eng, len(self.ops[eng]))))
            self.ecnt[eng] = 0
        self.ecnt[eng] += 1
        ev = (self.esem[eng], self.ecnt[eng])
        self.ops[eng].append((waits, fn, ev, 1))
        self._commit(ev, reads, writes)

    def dma(self, q, fn, reads, writes, sem_name, dram_w=()):
        reads = _keys(reads)
        writes = _keys(writes)
        waits = self._deps(q, reads, writes)
        writes = writes + _keys(dram_w)
        if sem_name not in self.dsems:
            self.dsems[sem_name] = self.es.enter_context(self.nc.semaphore('d_' + sem_name))
            self.dcnt[sem_name] = 0
            self.dsem_name[id(self.dsems[sem_name])] = sem_name
        if self.dcnt[sem_name] >= 16000:
            self.dsems[sem_name] = self.es.enter_context(
                self.nc.semaphore('d_%s_%d' % (sem_name, len(self.ops[q]))))
            self.dcnt[sem_name] = 0
            self.dsem_name[id(self.dsems[sem_name])] = sem_name
        self.dcnt[sem_name] += 16
        ev = (self.dsems[sem_name], self.dcnt[sem_name])
        self.ops[q].append((waits, fn, ev, 16))
        self._commit(ev, reads, writes)

    def final_wait(self, eng, resources):
        resources = _keys(resources)
        waits = self._deps(eng, resources, resources)
        self.ops[eng].append((waits, None, None, 0))

    def emit(self):
        nc = self.nc
        with nc.Block() as block:
            for e, deco in [('pe', block.tensor), ('dve', block.vector), ('act', block.scalar),
                            ('pool', block.gpsimd), ('sp', block.sync)]:
                ops = self.ops[e]

                @deco
                def _(engine, ops=ops):
                    for waits, fn, ev, inc in ops:
                        for (s, v) in waits:
                            engine.wait_ge(s, v)
                        if fn is None:
                            continue
                        ins = fn(engine)
                        ins.then_inc(ev[0], inc)

    def mm(self, out, lhsT, rhs, start, stop, R, W, **kw):
        self.op('pe', lambda e: e.matmul(out, lhsT=lhsT, rhs=rhs, start=start, stop=stop, **kw), R, W)

    def tr(self, out, in_, ident, R, W):
        self.op('pe', lambda e: e.transpose(out, in_, ident), R, W)

    def act(self, out, in_, func, R, W, scale=None, bias=None):
        kw = {}
        if scale is not None:
            kw['scale'] = scale
        if bias is not None:
            kw['bias'] = bias
        self.op('act', lambda e: e.activation(out=out, in_=in_, func=func, **kw), R, W)

    def tt(self, eng, out, in0, in1, op, R, W):
        self.op(eng, lambda e: e.tensor_tensor(out=out, in0=in0, in1=in1, op=op), R, W)

    def ts(self, eng, out, in0, s1, op0, R, W, s2=None, op1=None):
        if op1 is None:
            if eng == 'pool':
                s2, op1 = (0.0, ALU.add) if op0 == ALU.mult else (1.0, ALU.mult)
                self.op(eng, lambda e: e.tensor_scalar(out=out, in0=in0, scalar1=s1, scalar2=s2, op0=op0, op1=op1), R, W)
            else:
                self.op(eng, lambda e: e.tensor_scalar(out=out, in0=in0, scalar1=s1, scalar2=None, op0=op0), R, W)
        else:
            self.op(eng, lambda e: e.tensor_scalar(out=out, in0=in0, scalar1=s1, scalar2=s2, op0=op0, op1=op1), R, W)

    def stt(self, out, in0, scalar, in1, op0, op1, R, W):
        self.op('dve', lambda e: e.scalar_tensor_tensor(out=out, in0=in0, scalar=scalar, in1=in1, op0=op0, op1=op1), R, W)

    def cp(self, eng, out, in_, R, W):
        if eng == 'act':
            self.op('act', lambda e: e.activation(out=out, in_=in_, func=AF.Copy), R, W)
        else:
            self.op(eng, lambda e: e.tensor_copy(out=out, in_=in_), R, W)

    def memset(self, eng, ap, val, W):
        self.op(eng, lambda e: e.memset(ap, val), [], W)

    def load(self, out, in_, W, sem=None, R=(), q='sp', slow=False):
        sem = 'ld_' + _keys(W)[0]
        if slow:
            self.dma(q, lambda e: e.dma_start(out=out, in_=in_, allow_slow_non_contiguous=True), R, W, sem)
        else:
            self.dma(q, lambda e: e.dma_start(out=out, in_=in_), R, W, sem)

    def store(self, out, in_, src, dram_key, q='act'):
        sem = 'st_' + _keys([src])[0]
        self.dma(q, lambda e: e.dma_start(out=out, in_=in_), [src], [], sem, dram_w=[dram_key])


def bc(ap, shape):
    return ap.broadcast_to(list(shape))


def build_program(S, depth, use_rg=True, use_gdn=True, use_s5=True, gdn_stop=99, same_engine_sync=True):
    nc = bass.Bass("TRN2", target_bir_lowering=False)
    NCH = S // T
    assert S % T == 0

    def din(name, shape):
        return nc.dram_tensor(name, list(shape), F32, kind="ExternalInput").ap()

    x_d = din("x", [S, D])
    p_d = din("p", [depth, S, DPLE])
    prm = {name: din(name, shape) for name, shape in PARAM_SHAPES(depth)}
    out_d = nc.dram_tensor("out", [S, D], F32, kind="ExternalOutput").ap()
    win_s = [nc.dram_tensor("win_s%d" % l, [24, 128, 8, 256], BF16, kind="Internal").ap() for l in range(depth)]
    wba_s = [nc.dram_tensor("wba_s%d" % l, [128, 8, 16], BF16, kind="Internal").ap() for l in range(depth)]
    wout_s = [nc.dram_tensor("wout_s%d" % l, [8, 128, 16, 128], BF16, kind="Internal").ap() for l in range(depth)]
    wgate_s = [nc.dram_tensor("wgate_s%d" % l, [8, 128, 8, 128], BF16, kind="Internal").ap() for l in range(depth)]
    hscr = nc.dram_tensor("hscr", [8, 128, S], F32, kind="Internal").ap()

    with ExitStack() as es:
        P = Prog(nc, es, same_engine_sync=same_engine_sync)
        sb = P.sbuf

        psd = [es.enter_context(nc.psum_tensor("psd%d" % i, [128, 1024], F32)) for i in range(4)]

        def ps(bank, c0=0, w=512, p0=0, p1=128):
            base = (bank % 2) * 512 + c0
            ap = psd[bank // 2][p0:p1, base:base + w]
            keys = ['psb%d' % b for b in range(bank + c0 // 512, bank + (c0 + w - 1) // 512 + 1)]
            return ap, keys

        def psbf(bank, c0, w, p0=0, p1=128):
            t = psd[bank // 2][p0:p1, (bank % 2) * 512:(bank % 2) * 512 + 512].bitcast(BF16)
            return t[:, c0:c0 + w], ['psb%d' % bank]

        ident = sb('ident', [128, 128], F32)
        identb = sb('identb', [128, 128], BF16)
        onesb = sb('onesb', [128, 128], BF16)
        U64 = sb('U64', [64, 64], F32)
        Mst = sb('Mst', [64, 64], F32)
        ones64 = sb('ones64', [64, 64], F32)
        sel63 = sb('sel63', [64, 128], F32)
        P.memset('pool', ident.t[:], 1.0, [ident])
        P.op('pool', lambda e: e.affine_select(out=ident.t[:], in_=ident.t[:], pattern=[[-1, 128]], compare_op=ALU.is_equal,
                                               fill=0.0, base=0, channel_multiplier=1), [ident], [ident])
        P.cp('pool', identb.t[:], ident.t[:], [ident], [identb])
        P.memset('pool', onesb.t[:], 1.0, [onesb])
        P.memset('pool', U64.t[:], 1.0, [U64])
        P.op('pool', lambda e: e.affine_select(out=U64.t[:], in_=U64.t[:], pattern=[[1, 64]], compare_op=ALU.is_ge,
                                               fill=0.0, base=0, channel_multiplier=-1), [U64], [U64])
        P.memset('pool', Mst.t[:], 1.0, [Mst])
        P.op('pool', lambda e: e.affine_select(out=Mst.t[:], in_=Mst.t[:], pattern=[[-1, 64]], compare_op=ALU.is_gt,
                                               fill=0.0, base=0, channel_multiplier=1), [Mst], [Mst])
        Ust = sb('Ust', [64, 64], F32)
        P.memset('pool', Ust.t[:], 1.0, [Ust])
        P.op('pool', lambda e: e.affine_select(out=Ust.t[:], in_=Ust.t[:], pattern=[[1, 64]], compare_op=ALU.is_gt,
                                               fill=0.0, base=0, channel_multiplier=-1), [Ust], [Ust])
        P.memset('pool', ones64.t[:], 1.0, [ones64])
        P.memset('pool', sel63.t[:], 1.0, [sel63])
        P.op('pool', lambda e: e.affine_select(out=sel63.t[:], in_=sel63.t[:], pattern=[[0, 128]], compare_op=ALU.is_equal,
                                               fill=0.0, base=-63, channel_multiplier=1), [sel63], [sel63])

        stg = [sb('stg%d' % i, [128, 1536], F32) for i in range(2)]
        G = [sb('G%d' % i, [128, 1024], F32) for i in range(4)]
        stgb = [Buf(G[i].t[:].bitcast(BF16), G[i].k) for i in range(2)]

        colsB = sb('colsB', [128, 72], F32)
        gcw = sb('gcw', [128, 96], F32)
        rgc = sb('rgc', [128, 16], F32)
        rgWa = sb('rgWa', [128, 4, 128], BF16)
        rgWx = sb('rgWx', [128, 4, 128], BF16)
        wglu = sb('wglu', [128, 4, 512], BF16)
        wproj = sb('wproj', [128, 2, 1024], BF16)
        negA = sb('negA', [64, 8], F32)
        dtb = sb('dtb', [64, 8], F32)
        RB = dict(norm_g=0, ple_norm_g=8, rg_conv_b=16, rg_b_a=20, rg_b_x=24, rg_lambda=28, s5_d=32, s5_b_glu=36,
                  gdn_norm_g=40, rg_conv_w=41, final=57)

        KT = sb('KT', [128, 4, 8, 128], BF16)
        EB = sb('EB', [128, 4, 8, 2, 128], BF16)
        Ctab = sb('Ctab', [128, 8, 2, 16, 32], BF16)
        Dcos = sb('Dcos', [128, 16, CB], F32)
        Dsin = sb('Dsin', [128, 16, CB], F32)
        Rho = sb('Rho', [128, 16, CB], F32)
        rhoc = sb('rhoc', [128, 16], F32)

        hT = sb('hT', [128, 8, T], F32)
        hnT = sb('hnT', [128, 8, T], BF16)
        sqb = Buf(G[3].t[:].bitcast(BF16).rearrange("p (k t) -> p k t", k=8), G[3].k)
        rstd = sb('rstd', [128, T], F32)
        mixT = sb('mixT', [128, 16, T], BF16)
        raw = sb('raw', [128, 24, 3 + T], BF16)
        histrg = sb('histrg', [128, 4, 3], BF16)
        histg = sb('histg', [128, 24, 3], BF16)
        gate = sb('gate', [128, 8, T], BF16)
        u5 = sb('u5', [128, 4, T], BF16)
        NRING = 5
        wring = [sb('wring%d' % i, [128, 2048], BF16) for i in range(NRING)]
        wba = sb('wba', [128, 8, 16], BF16)
        dg = [sb('dg%d' % i, [128, 128], BF16) for i in range(4)]
        tf = [sb('tf%d' % i, [128, T], F32) for i in range(8)]
        tb = [sb('tb%d' % i, [128, T], BF16) for i in range(2)]
        rgcar = sb('rgcar', [128, 4], F32)
        pT = sb('pT', [128, 2, T], BF16)
        XPre = sb('XPre', [128, 16, CB + 1], F32)
        XPim = sb('XPim', [128, 16, CB + 1], F32)
        XPb = sb('XPb', [128, 2, 16, CB], BF16)
        z5b = sb('z5b', [128, 4, T], BF16)
        s5big = [Buf(G[i // 2].t[:, (i % 2) * 512:(i % 2) * 512 + 512], G[i // 2].k) for i in range(6)]
        qn = sb('qn', [128, 8, T], BF16)
        kn = sb('kn', [128, 8, T], BF16)
        vs = sb('vs', [128, 8, T], BF16)
        Sst = sb('Sst', [128, 8, 128], F32)
        Sb = sb('Sb', [128, 8, 128], BF16)
        NI = T // GC
        gsm = {n: sb('g_' + n, [64, NI * 8], F32) for n in ['bt', 'nbt', 'xa', 'ex', 'sp', 'gp', 'g', 'gcum', 'egc', 'dgl', 'kds', 'bge']}
        gsm.update({n: sb('g_' + n, [64, 8], F32) for n in ['ss', 'ln', 'rs']})
        egl = sb('egl', [128, NI * 8], F32)
        gw = [sb('gw%d' % i, [64, 8, 64], F32) for i in range(7)]
        aqkTs = [sb('aqkT%d' % i, [64, 8, 64], BF16) for i in range(2)]
        gx = [Buf(G[i].t[0:64, :].rearrange("p (h e) -> p h e", h=8), G[i].k) for i in range(4)]
        kdec = sb('kdec', [64, 8, 128], BF16)
        vnew = sb('vnew', [64, 8, 128], BF16)
        wTb = sb('wTb', [128, 8, 64], BF16)

        s5t = {n: sb('s5_' + n, [128, 16], F32) for n in
               ['are', 'aim', 'ldt', 'dt', 'ard', 'th', 'cr', 'den', 'inv', 'cfr', 'cfi', 't0', 't1', 'tqb']}
        hTf = hT.t[:].rearrange("p k t -> p (k t)")
        Lre = Buf(hTf[:, 0:144].rearrange("p (j i) -> p j i", j=9), hT.k)
        Lim = Buf(hTf[:, 256:400].rearrange("p (j i) -> p j i", j=9), hT.k)
        jv = Buf(hTf[:, 512:640].rearrange("p (j i) -> p j i", j=8), hT.k)
        jvi = Buf(G[2].t[:, 0:128].bitcast(I32).rearrange("p (j i) -> p j i", j=8), G[2].k)
        mask2 = sb('mask2', [128, 2, 16], F32)
        s5int = Buf(G[3].t[:, 0:512].bitcast(I32), G[3].k)
        cvi = Buf(G[3].t[:, 512:1024].bitcast(I32).rearrange("p (i c) -> p i c", i=16), G[3].k)
        knf = kn.t[:].rearrange("p k t -> p (k t)").bitcast(F32)
        qnf = qn.t[:].rearrange("p k t -> p (k t)").bitcast(F32)
        vsf = vs.t[:].rearrange("p k t -> p (k t)").bitcast(F32)
        hnf = hnT.t[:].rearrange("p k t -> p (k t)").bitcast(F32)
        v3_ = lambda ap, a: ap.rearrange("p (a b) -> p a b", a=a)
        bre = Buf(v3_(vsf[:, 0:256], 16), vs.k)
        bim = Buf(v3_(vsf[:, 256:512], 16), vs.k)
        cre = Buf(v3_(vsf[:, 512:768], 16), vs.k)
        cim = Buf(v3_(vsf[:, 768:1024], 16), vs.k)
        Bre = Buf(v3_(qnf[:, 0:256], 16), qn.k)
        Bim = Buf(v3_(qnf[:, 256:512], 16), qn.k)
        Cbr = Buf(v3_(qnf[:, 512:1024], 16), qn.k)
        Cbi = Buf(v3_(knf[:, 0:512], 16), kn.k)
        cv = Buf(v3_(knf[:, 512:1024], 16), kn.k)
        maskW = Buf(hnf[:, 0:512].rearrange("p (a g c) -> p a g c", a=4, g=8), hnT.k)
        P.memset('pool', mask2.t[:], 0.0, [mask2])
        P.memset('pool', mask2.t[0:64, 0, :], 1.0, [mask2])
        P.memset('pool', mask2.t[64:128, 1, :], 1.0, [mask2])

        def load_layer_consts(l):
            sA = stg[0]
            P.load(sA.t[0:96, 0:128], prm['gdn_conv_w'][l].rearrange("k (j p) -> (k j) p", p=128), [sA], 'stg0')
            o, kk = ps(7, 0, 96)
            P.tr(o, sA.t[0:96, 0:128], ident.t[0:96, 0:96], [sA, ident], kk)
            P.cp('dve', gcw.t[:], o, kk, [gcw])
            sBt = stg[1]
            rows = [('norm_g', prm['norm_g'][l], 8), ('ple_norm_g', prm['ple_norm_g'][l], 8), ('rg_conv_b', prm['rg_conv_b'][l], 4),
                    ('rg_b_a', prm['rg_b_a'][l], 4), ('rg_b_x', prm['rg_b_x'][l], 4), ('rg_lambda', prm['rg_lambda'][l], 4),
                    ('s5_d', prm['s5_d'][l], 4), ('s5_b_glu', prm['s5_b_glu'][l], 4), ('gdn_norm_g', prm['gdn_norm_g'][l], 1)]
            for name, ap, n in rows:
                r0 = RB[name]
                P.load(sBt.t[r0:r0 + n, 0:128], ap.rearrange("(k p) -> k p", p=128), [sBt], 'stg1')
            P.load(sBt.t[41:57, 0:128], prm['rg_conv_w'][l].rearrange("k (j p) -> (k j) p", p=128), [sBt], 'stg1')
            P.load(sBt.t[57:65, 0:128], prm['final_norm_g'].rearrange("(k p) -> k p", p=128), [sBt], 'stg1')
            o, kk = ps(7, 128, 65)
            P.tr(o, sBt.t[0:65, 0:128], ident.t[0:65, 0:65], [sBt, ident], kk)
            P.cp('dve', colsB.t[:, 0:65], o, kk, [colsB])
            z = s5t['t0'].t[:, 0:4]
            acc = s5t['t1'].t[:, 0:4]
            P.act(z, colsB.t[:, 28:32], AF.Exp, [colsB], [s5t['t0']], scale=-1.0)
            P.ts('dve', acc, z, -1.0 / 9.0, ALU.mult, [s5t['t0']], [s5t['t1']], s2=1.0 / 8.0, op1=ALU.add)
            for k in range(7, 0, -1):
                P.tt('dve', acc, acc, z, ALU.mult, [s5t['t0'], s5t['t1']], [s5t['t1']])
                P.ts('dve', acc, acc, -1.0, ALU.mult, [s5t['t1']], [s5t['t1']], s2=1.0 / k, op1=ALU.add)
            P.tt('dve', acc, acc, z, ALU.mult, [s5t['t0'], s5t['t1']], [s5t['t1']])
            P.ts('dve', rgc.t[:, 0:4], acc, -8.0, ALU.mult, [s5t['t1']], [rgc])
            P.ts('dve', rgc.t[:, 4:8], acc, -16.0, ALU.mult, [s5t['t1']], [rgc])
            P.ts('dve', rgc.t[:, 8:16], colsB.t[:, 20:28], -1.0, ALU.mult, [colsB], [rgc])
            for (src, dst) in [(prm['rg_w_a'][l], rgWa), (prm['rg_w_x'][l], rgWx)]:
                s = stg[0]
                P.memset('pool', s.t[:, 0:512], 0.0, [s])
                sv = s.t[:, 0:512].rearrange("p (t j) -> p t j", t=4)
                for h2 in range(2):
                    P.load(sv[h2 * 64:(h2 + 1) * 64, :, h2 * 64:(h2 + 1) * 64],
                           src.rearrange("(t h2) i j -> h2 i t j", h2=2)[h2], [s], 'stg0')
                P.cp('pool', dst.t[:], sv, [s], [dst])
            for hh in range(2):
                s = stg[hh]
                P.load(s.t[:, 0:1024].rearrange("p (k n) -> p k n", k=2),
                       prm['s5_w_glu'][l].rearrange("(k p) n -> p k n", p=128)[:, 2 * hh:2 * hh + 2, :], [s], 'stg%d' % hh)
                P.cp('dve', wglu.t[:, 2 * hh:2 * hh + 2, :], s.t[:, 0:1024].rearrange("p (k n) -> p k n", k=2), [s], [wglu])
            for hh in range(2):
                s = stg[hh]
                P.load(s.t[:, 0:1024], prm['ple_w_proj'][l][hh * 128:(hh + 1) * 128, :], [s], 'stg%d' % hh)
                P.cp('pool', wproj.t[:, hh, :], s.t[:, 0:1024], [s], [wproj])
            P.load(gsm['xa'].t[:, 0:8], prm['gdn_a_log'][l].partition_broadcast(64), [gsm['xa']], 'gsm')
            P.act(negA.t[:], gsm['xa'].t[:, 0:8], AF.Exp, [gsm['xa']], [negA])
            P.load(dtb.t[:], prm['gdn_dt_bias'][l].partition_broadcast(64), [dtb], 'gsm')
            if use_s5:
                s5_setup(l)

        def sincos(tq, n, out_sin, out_cos, Rk, Wk_sin, Wk_cos):
            ti = s5int.t[:, 0:n]
            tfl = s5big[4].t[:, 0:n]
            fr = s5big[5].t[:, 0:n]
            for (shift, out, Wk) in [(0.0, out_sin, Wk_sin), (0.25, out_cos, Wk_cos)]:
                src = tq
                if shift != 0.0:
                    P.ts('dve', fr, tq, shift, ALU.add, Rk, [s5big[5]])
                    src = fr
                    R2 = [s5big[5]]
                else:
                    R2 = Rk
                P.cp('dve', ti, src, R2, [s5int])
                P.cp('dve', tfl, ti, [s5int], [s5big[4]])
                P.tt('dve', fr, src, tfl, ALU.subtract, R2 + [s5big[4]], [s5big[5]])
                P.act(out, fr, AF.Sin, [s5big[5]], Wk, scale=TWO_PI)

        def s5_setup(l):
            t = s5t
            P.op('pool', lambda e: e.iota(cvi.t[:], pattern=[[0, 16], [1, CB]], base=1, channel_multiplier=0), [], [cvi])
            P.cp('pool', cv.t[:], cvi.t[:], [cvi], [cv])
            P.op('pool', lambda e: e.iota(jvi.t, pattern=[[1, 8], [0, 16]], base=1, channel_multiplier=0), [], [jvi])
            P.cp('pool', jv.t, jvi.t, [jvi], [jv])
            P.memset('pool', maskW.t[:], 0.0, [maskW])
            for jj in range(4):
                P.memset('pool', maskW.t[0:64, jj, 2 * jj, :], 1.0, [maskW])
                P.memset('pool', maskW.t[64:128, jj, 2 * jj + 1, :], 1.0, [maskW])
            for name, src in [('are', prm['s5_a_re'][l]), ('aim', prm['s5_a_im'][l])]:
                for g2 in range(2):
                    P.load(t[name].t[g2 * 64:(g2 + 1) * 64, :], src.rearrange("(i g2) n -> g2 n i", g2=2)[g2], [t[name]], 's5ld', slow=True)
            for g2 in range(2):
                P.load(t['ldt'].t[g2 * 64:(g2 + 1) * 64, :], prm['s5_log_dt'][l].rearrange("(i g2) -> g2 i", g2=2)[g2].partition_broadcast(64),
                       [t['ldt']], 's5ld', slow=True)
            for (dst, src) in [(bre, prm['s5_b_re'][l]), (bim, prm['s5_b_im'][l])]:
                for g2 in range(2):
                    P.load(dst.t[g2 * 64:(g2 + 1) * 64, :, :], src.rearrange("(i g2) n c -> g2 n i c", g2=2)[g2], [dst], 's5ld')
            for (dst, src) in [(cre, prm['s5_c_re'][l]), (cim, prm['s5_c_im'][l])]:
                for g2 in range(2):
                    for i_ in range(16):
                        P.load(dst.t[g2 * 64:(g2 + 1) * 64, i_, :],
                               src.rearrange("(i g2) c n -> g2 i n c", g2=2)[g2, i_], [dst], 's5ld', slow=True)
            P.act(t['dt'].t[:], t['ldt'].t[:], AF.Exp, [t['ldt']], [t['dt']])
            P.tt('dve', t['ard'].t[:], t['are'].t[:], t['dt'].t[:], ALU.mult, [t['are'], t['dt']], [t['ard']])
            P.tt('dve', t['th'].t[:], t['aim'].t[:], t['dt'].t[:], ALU.mult, [t['aim'], t['dt']], [t['th']])
            A0 = s5big[0].t[:, 0:128].rearrange("p (j i) -> p j i", j=8)
            A1 = s5big[1].t[:, 0:128].rearrange("p (j i) -> p j i", j=8)
            A2 = s5big[2].t[:, 0:128].rearrange("p (j i) -> p j i", j=8)
            A3 = s5big[3].t[:, 0:128].rearrange("p (j i) -> p j i", j=8)
            P.tt('dve', A0, jv.t, bc(t['ard'].t[:].unsqueeze(1), [128, 8, 16]), ALU.mult, [jv, t['ard']], [s5big[0]])
            P.act(A0, A0, AF.Exp, [s5big[0]], [s5big[0]])
            P.tt('dve', A1, jv.t, bc(t['th'].t[:].unsqueeze(1), [128, 8, 16]), ALU.mult, [jv, t['th']], [s5big[1]])
            P.ts('dve', A1, A1, 1.0 / TWO_PI, ALU.mult, [s5big[1]], [s5big[1]])
            sincos(s5big[1].t[:, 0:128], 128, s5big[2].t[:, 0:128], s5big[3].t[:, 0:128], [s5big[1]], [s5big[2]], [s5big[3]])
            P.memset('dve', Lre.t[:, 0, :], 1.0, [Lre])
            P.memset('dve', Lim.t[:, 0, :], 0.0, [Lim])
            P.tt('dve', Lre.t[:, 1:9, :], A0, A3, ALU.mult, [s5big[0], s5big[3]], [Lre])
            P.tt('dve', Lim.t[:, 1:9, :], A0, A2, ALU.mult, [s5big[0], s5big[2]], [Lim])
            P.ts('dve', t['tqb'].t[:], t['th'].t[:], float(L5) / TWO_PI, ALU.mult, [t['th']], [t['tqb']])
            B0 = s5big[0].t[:, 0:16 * CB].rearrange("p (i c) -> p i c", i=16)
            P.tt('dve', B0, cv.t[:], bc(t['tqb'].t[:].unsqueeze(2), [128, 16, CB]), ALU.mult, [cv, t['tqb']], [s5big[0]])
            sincos(s5big[0].t[:, 0:16 * CB], 16 * CB, Dsin.t[:].rearrange("p i c -> p (i c)"), Dcos.t[:].rearrange("p i c -> p (i c)"),
                   [s5big[0]], [Dsin], [Dcos])
            P.act(rhoc.t[:], t['ard'].t[:], AF.Exp, [t['ard']], [rhoc], scale=float(L5))
            P.cp('dve', Rho.t[:], bc(rhoc.t[:].unsqueeze(2), [128, 16, CB]), [rhoc], [Rho])
            P.memset('dve', Rho.t[:, :, 0:1], 0.0, [Rho])
            P.ts('dve', t['cr'].t[:], Lre.t[:, 1, :], -1.0, ALU.add, [Lre], [t['cr']])
            P.tt('dve', t['den'].t[:], t['are'].t[:], t['are'].t[:], ALU.mult, [t['are']], [t['den']])
            P.tt('dve', t['t0'].t[:], t['aim'].t[:], t['aim'].t[:], ALU.mult, [t['aim']], [t['t0']])
            P.tt('dve', t['den'].t[:], t['den'].t[:], t['t0'].t[:], ALU.add, [t['den'], t['t0']], [t['den']])
            P.op('dve', lambda e: e.reciprocal(out=t['inv'].t[:], in_=t['den'].t[:]), [t['den']], [t['inv']])
            P.tt('dve', t['t0'].t[:], t['cr'].t[:], t['are'].t[:], ALU.mult, [t['cr'], t['are']], [t['t0']])
            P.tt('dve', t['t1'].t[:], Lim.t[:, 1, :], t['aim'].t[:], ALU.mult, [Lim, t['aim']], [t['t1']])
            P.tt('dve', t['t0'].t[:], t['t0'].t[:], t['t1'].t[:], ALU.add, [t['t0'], t['t1']], [t['t0']])
            P.tt('dve', t['cfr'].t[:], t['t0'].t[:], t['inv'].t[:], ALU.mult, [t['t0'], t['inv']], [t['cfr']])
            P.tt('dve', t['t0'].t[:], Lim.t[:, 1, :], t['are'].t[:], ALU.mult, [Lim, t['are']], [t['t0']])
            P.tt('dve', t['t1'].t[:], t['cr'].t[:], t['aim'].t[:], ALU.mult, [t['cr'], t['aim']], [t['t1']])
            P.tt('dve', t['t0'].t[:], t['t0'].t[:], t['t1'].t[:], ALU.subtract, [t['t0'], t['t1']], [t['t0']])
            P.tt('dve', t['cfi'].t[:], t['t0'].t[:], t['inv'].t[:], ALU.mult, [t['t0'], t['inv']], [t['cfi']])

            def cmul(out_re, out_im, a_re, a_im, b_re, b_im, shape, Ra, Rb, Wre, Wim, neg_im=False):
                n = 1
                for d_ in shape[1:]:
                    n *= d_
                v0 = s5big[4].t[:, 0:n]
                v1 = s5big[5].t[:, 0:n]
                if len(shape) == 3:
                    v0 = v0.rearrange("p (a b) -> p a b", a=shape[1])
                    v1 = v1.rearrange("p (a b) -> p a b", a=shape[1])
                P.tt('dve', v0, a_re, b_re, ALU.mult, Ra + Rb, [s5big[4]])
                P.tt('dve', v1, a_im, b_im, ALU.mult, Ra + Rb, [s5big[5]])
                P.tt('dve', out_re, v0, v1, ALU.subtract, [s5big[4], s5big[5]], Wre)
                P.tt('dve', v0, a_re, b_im, ALU.mult, Ra + Rb, [s5big[4]])
                P.tt('dve', v1, a_im, b_re, ALU.mult, Ra + Rb, [s5big[5]])
                P.tt('dve', out_im, v0, v1, ALU.add, [s5big[4], s5big[5]], Wim)
                if neg_im:
                    P.ts('dve', out_im, out_im, -1.0, ALU.mult, Wim, Wim)

            sh3 = [128, 16, 16]
            cmul(Bre.t[:], Bim.t[:], bc(t['cfr'].t[:].unsqueeze(2), sh3), bc(t['cfi'].t[:].unsqueeze(2), sh3), bre.t[:], bim.t[:],
                 sh3, [t['cfr'], t['cfi']], [bre, bim], [Bre], [Bim])
            for (dst, src) in [(Cbr, cre), (Cbi, cim)]:
                for i4 in range(4):
                    P.tt('dve', dst.t[:, 4 * i4:4 * i4 + 4, :].rearrange("p i (g c) -> p i g c", g=2),
                         bc(src.t[:, 4 * i4:4 * i4 + 4, :].unsqueeze(2), [128, 4, 2, 16]),
                         bc(mask2.t[:].unsqueeze(1), [128, 4, 2, 16]), ALU.mult, [src, mask2], [dst])
            shb = [128, 16, 32]
            for s in range(8):
                lr = bc(Lre.t[:, s + 1, :].unsqueeze(2), shb)
                li = bc(Lim.t[:, s + 1, :].unsqueeze(2), shb)
                cr_o = s5big[0].t[:, 0:512].rearrange("p (a b) -> p a b", a=16)
                ci_o = s5big[1].t[:, 0:512].rearrange("p (a b) -> p a b", a=16)
                cmul(cr_o, ci_o, lr, li, Cbr.t[:], Cbi.t[:], shb, [Lre, Lim], [Cbr, Cbi], [s5big[0]], [s5big[1]], neg_im=True)
                P.cp('pool', Ctab.t[:, s, 0, :, :], cr_o, [s5big[0]], [Ctab])
                P.cp('pool', Ctab.t[:, s, 1, :, :], ci_o, [s5big[1]], [Ctab])
            Pre = s5big[0].t[:, 0:256].rearrange("p (a b) -> p a b", a=16)
            Pim = s5big[1].t[:, 0:256].rearrange("p (a b) -> p a b", a=16)
            Pbr = s5big[2].t[:, 0:512].rearrange("p (a b) -> p a b", a=16)
            Pbi = s5big[3].t[:, 0:512].rearrange("p (a b) -> p a b", a=16)
            Pwr = stg[0].t[:, 0:512]
            Pwi = stg[0].t[:, 512:1024]
            for j in range(8):
                lr = bc(Lre.t[:, j, :].unsqueeze(2), sh3)
                li = bc(Lim.t[:, j, :].unsqueeze(2), sh3)
                cmul(Pre, Pim, lr, li, Bre.t[:], Bim.t[:], sh3, [Lre, Lim], [Bre, Bim], [s5big[0]], [s5big[1]])
                for (dst, src, kd, ks) in [(Pbr, Pre, s5big[2], s5big[0]), (Pbi, Pim, s5big[3], s5big[1])]:
                    for i4 in range(4):
                        P.tt('dve', dst[:, 4 * i4:4 * i4 + 4, :].rearrange("p i (g c) -> p i g c", g=2),
                             bc(src[:, 4 * i4:4 * i4 + 4, :].unsqueeze(2), [128, 4, 2, 16]),
                             bc(mask2.t[:].unsqueeze(1), [128, 4, 2, 16]), ALU.mult, [ks, mask2], [kd])
                sp_ = 7 - j
                for ct in range(4):
                    for part, (src, kd) in enumerate([(Pbr, s5big[2]), (Pbi, s5big[3])]):
                        o, kk = ps(6, part * 128, 128)
                        P.tr(o, src[:, 4 * ct:4 * ct + 4, :].rearrange("p a b -> p (a b)"), ident.t[:], [kd, ident], kk)
                        P.cp('act', EB.t[:, ct, sp_, part, :], o, kk, [EB])
                    for (dstw, src, ks, sgn) in [(Pwr, Pre, s5big[0], 1.0), (Pwi, Pim, s5big[1], -1.0)]:
                        P.tt('dve', dstw.rearrange("p (a g c) -> p a g c", a=4, g=8),
                             bc(src[:, 4 * ct:4 * ct + 4, :].unsqueeze(2), [128, 4, 8, 16]), maskW.t[:], ALU.mult, [ks, maskW], [stg[0]])
                    P.ts('dve', Pwi, Pwi, -1.0, ALU.mult, [stg[0]], [stg[0]])
                    o, kk = ps(7, 256, 128)
                    for jj in range(4):
                        i = 4 * ct + jj
                        P.mm(o[:, 32 * jj:32 * jj + 32], Pwr[:, 128 * jj:128 * jj + 128], Cbr.t[:, i, :], True, False, [stg[0], Cbr], kk)
                        P.mm(o[:, 32 * jj:32 * jj + 32], Pwi[:, 128 * jj:128 * jj + 128], Cbi.t[:, i, :], False, True, [stg[0], Cbi], kk)
                    P.cp('act', KT.t[:, ct, j, :], o, kk, [KT])

        def rmsnorm(gcol0, out_bf=None, out_f32=None):
            P.act(sqb.t[:], hT.t[:], AF.Square, [hT], [sqb])
            o, kk = ps(7, 0, T)
            for k in range(8):
                P.mm(o, onesb.t[:], sqb.t[:, k, :], k == 0, k == 7, [onesb, sqb], kk)
            P.act(rstd.t[:], o, AF.Ln, kk, [rstd], scale=1.0 / D, bias=EPS)
            P.act(rstd.t[:], rstd.t[:], AF.Exp, [rstd], [rstd], scale=-0.5)
            dst = out_bf if out_bf is not None else out_f32
            for k in range(8):
                P.stt(dst.t[:, k, :], hT.t[:, k, :], colsB.t[:, gcol0 + k:gcol0 + k + 1], rstd.t[:], ALU.mult, ALU.mult,
                      [hT, colsB, rstd], [dst])

        wcount = [0]

        def ring_next():
            b = wring[wcount[0] % NRING]
            wcount[0] += 1
            return b

        def stream_w(l, g):
            b = ring_next()
            v = Buf(b.t[:].rearrange("p (k n) -> p k n", k=8), b.k)
            P.load(v.t, win_s[l][g], [b], R=['win_s%d' % l])
            return v

        pcount = [0]

        def inproj_tile(wb, col0):
            slot = pcount[0] % 4
            pcount[0] += 1
            o, kk = ps(slot, 0, T)
            for k in range(8):
                P.mm(o, wb.t[:, k, col0:col0 + 128], hnT.t[:, k, :], k == 0, k == 7, [wb, hnT], kk)
            return o, kk

        ecount = [0]

        def evac_engine():
            ecount[0] += 1
            return 'act' if ecount[0] % 2 else 'dve'

        dgc = [0]

        def conv_tile(src_buf, jt, wcols, col_of_tap, R):
            slot = pcount[0] % 4
            pcount[0] += 1
            o, kk = ps(slot, 0, T)
            for k in range(4):
                d = dg[dgc[0] % 4]
                dgc[0] += 1
                c = col_of_tap(k)
                P.ts('pool', d.t[:], identb.t[:], wcols.t[:, c:c + 1], ALU.mult, [identb, wcols], [d])
                P.mm(o, d.t[:], src_buf.t[:, jt, k:k + T], k == 0, k == 3, [d] + R, kk)
            return o, kk

        def rg_chunk(l, c):
            if c == 0:
                P.memset('pool', raw.t[:, 0:4, 0:3], 0.0, ['raw_rg'])
            else:
                P.cp('pool', raw.t[:, 0:4, 0:3], histrg.t[:], [histrg], ['raw_rg'])
            for g in range(4):
                wb = stream_w(l, g)
                for n in range(2):
                    o, kk = inproj_tile(wb, n * 128)
                    j = (g % 2) * 2 + n
                    if g < 2:
                        P.cp(evac_engine(), raw.t[:, j, 3:3 + T], o, kk, ['raw_rg'])
                    else:
                        P.act(gate.t[:, j, :], o, AF.Silu, kk, ['gate_rg'])
            P.cp('pool', histrg.t[:], raw.t[:, 0:4, T:T + 3], ['raw_rg'], [histrg])
            yield 'inproj'
            def rg_tile(j):
                p_ = j % 2
                r_, gi_, xj, m_ = tf[4 * p_], tf[4 * p_ + 1], tf[4 * p_ + 2], tf[4 * p_ + 3]
                xb_ = tb[p_]
                o, kk = conv_tile(raw, j, colsB, lambda k: RB['rg_conv_w'] + k * 4 + j, ['raw_rg'])
                yield
                P.act(xj.t[:], o, AF.Identity, kk, [xj], bias=colsB.t[:, 16 + j:17 + j])
                yield
                P.cp('dve', xb_.t[:], xj.t[:], [xj], [xb_])
                yield
                oa, ka = ps(4 + 2 * p_, 0, T)
                ox, kx = ps(5 + 2 * p_, 0, T)
                P.mm(oa, rgWa.t[:, j, :], xb_.t[:], True, True, [rgWa, xb_], ka)
                P.mm(ox, rgWx.t[:, j, :], xb_.t[:], True, True, [rgWx, xb_], kx)
                yield
                P.act(r_.t[:], oa, AF.Exp, ka + [rgc], [r_], scale=-1.0, bias=rgc.t[:, 8 + j:9 + j])
                P.act(gi_.t[:], ox, AF.Exp, kx + [rgc], [gi_], scale=-1.0, bias=rgc.t[:, 12 + j:13 + j])
                yield
                P.act(r_.t[:], r_.t[:], AF.Ln, [r_], [r_], bias=1.0)
                P.act(gi_.t[:], gi_.t[:], AF.Ln, [gi_], [gi_], bias=1.0)
                yield
                P.act(r_.t[:], r_.t[:], AF.Exp, [r_], [r_], scale=-1.0)
                P.act(gi_.t[:], gi_.t[:], AF.Exp, [gi_], [gi_], scale=-1.0)
                yield
                P.tt('pool', gi_.t[:], gi_.t[:], xj.t[:], ALU.mult, [gi_, xj], [gi_])
                a_ = xj
                P.act(m_.t[:], r_.t[:], AF.Exp, [r_, rgc], [m_], scale=rgc.t[:, 4 + j:5 + j])
                yield
                P.act(a_.t[:], r_.t[:], AF.Exp, [r_, rgc, gi_], [a_], scale=rgc.t[:, j:j + 1])
                yield
                P.act(m_.t[:], m_.t[:], AF.Sqrt, [m_], [m_], scale=-1.0, bias=1.0)
                yield
                P.tt('dve', m_.t[:], m_.t[:], gi_.t[:], ALU.mult, [m_, gi_], [m_])
                yield
                hr = r_
                if c == 0:
                    P.op('dve', lambda e: e.tensor_tensor_scan(out=hr.t[:], data0=a_.t[:], data1=m_.t[:], initial=0.0,
                                                               op0=ALU.mult, op1=ALU.add), [a_, m_], [hr])
                else:
                    P.op('dve', lambda e: e.tensor_tensor_scan(out=hr.t[:], data0=a_.t[:], data1=m_.t[:],
                                                               initial=rgcar.t[:, j:j + 1], op0=ALU.mult, op1=ALU.add),
                         [a_, m_, rgcar], [hr])
                yield
                P.cp('dve', rgcar.t[:, j:j + 1], hr.t[:, T - 1:T], [hr], [rgcar])
                P.tt('pool', mixT.t[:, j, :], hr.t[:], gate.t[:, j, :], ALU.mult, [hr, 'gate_rg'], ['mix_rg'])

            for j0 in (0, 2):
                pair = [rg_tile(j0), rg_tile(j0 + 1)]
                live = [True, True]
                next(pair[0], None)
                next(pair[0], None)
                while any(live):
                    for q_ in (1, 0):
                        if live[q_]:
                            try:
                                next(pair[q_])
                            except StopIteration:
                                live[q_] = False
                yield 'pair'

        def s5_chunk(l, c):
            for g in range(4):
                wb = stream_w(l, 20 + g)
                for n in range(2):
                    o, kk = inproj_tile(wb, n * 128)
                    j = (g % 2) * 2 + n
                    if g < 2:
                        P.cp(evac_engine(), u5.t[:, j, :], o, kk, [u5])
                    else:
                        P.act(gate.t[:, 4 + j, :], o, AF.Silu, kk, ['gate_s5'])
            if c == 0:
                P.memset('pool', XPre.t[:, :, 0:1], 0.0, [XPre])
                P.memset('pool', XPim.t[:, :, 0:1], 0.0, [XPim])
            pe_ = [ps(4, 0, 512), ps(5, 0, 512)]
            for i in range(16):
                ct, jj = i // 4, i % 4
                uv = u5.t[32 * jj:32 * jj + 32, ct, :].rearrange("p (c s) -> p s c", s=L5)
                for part in range(2):
                    o, kk = pe_[part]
                    for s_ in range(L5):
                        P.mm(o[:, i * CB:(i + 1) * CB], EB.t[32 * jj:32 * jj + 32, ct, s_, part, :], uv[:, s_, :],
                             s_ == 0, s_ == L5 - 1, [EB, u5], kk, tile_position=(32 * jj, 0))
            ere, kre = pe_[0]
            eim, kim = pe_[1]
            t1, t2, mre, mim, qre, qim = s5big[0], s5big[1], s5big[2], s5big[3], s5big[4], s5big[5]
            dcs = Dcos.t[:].rearrange("p i c -> p (i c)")
            dsn = Dsin.t[:].rearrange("p i c -> p (i c)")
            P.tt('dve', t1.t[:], ere, dcs, ALU.mult, kre + [Dcos], [t1])
            P.tt('dve', t2.t[:], eim, dsn, ALU.mult, kim + [Dsin], [t2])
            P.tt('pool', mre.t[:], t1.t[:], t2.t[:], ALU.add, [t1, t2], [mre])
            P.tt('dve', t1.t[:], eim, dcs, ALU.mult, kim + [Dcos], [t1])
            P.tt('dve', t2.t[:], ere, dsn, ALU.mult, kre + [Dsin], [t2])
            P.tt('pool', mim.t[:], t1.t[:], t2.t[:], ALU.subtract, [t1, t2], [mim])
            for (m_, XP) in [(mre, XPre), (mim, XPim)]:
                mv = m_.t[:].rearrange("p (i c) -> p i c", i=16)
                P.tt('pool', s5t['t0'].t[:].unsqueeze(2), rhoc.t[:].unsqueeze(2), XP.t[:, :, 0:1], ALU.mult, [rhoc, XP], [s5t['t0']])
                P.tt('pool', mv[:, :, 0:1], mv[:, :, 0:1], s5t['t0'].t[:].unsqueeze(2), ALU.add, [m_, s5t['t0']], [m_])
            rhf = Rho.t[:].rearrange("p i c -> p (i c)")
            for (m_, q_) in [(mre, qre), (mim, qim)]:
                P.op('dve', lambda e, m_=m_, q_=q_: e.tensor_tensor_scan(out=q_.t[:], data0=rhf, data1=m_.t[:], initial=0.0,
                                                                        op0=ALU.mult, op1=ALU.add), [Rho, m_], [q_])
            qrv = qre.t[:].rearrange("p (i c) -> p i c", i=16)
            qiv = qim.t[:].rearrange("p (i c) -> p i c", i=16)
            t1v = t1.t[:].rearrange("p (i c) -> p i c", i=16)
            t2v = t2.t[:].rearrange("p (i c) -> p i c", i=16)
            P.tt('dve', t1v, qrv, Dcos.t[:], ALU.mult, [qre, Dcos], [t1])
            P.tt('pool', t2v, qiv, Dsin.t[:], ALU.mult, [qim, Dsin], [t2])
            P.tt('dve', XPre.t[:, :, 1:CB + 1], t1v, t2v, ALU.subtract, [t1, t2], [XPre])
            P.tt('dve', t1v, qrv, Dsin.t[:], ALU.mult, [qre, Dsin], [t1])
            P.tt('pool', t2v, qiv, Dcos.t[:], ALU.mult, [qim, Dcos], [t2])
            P.tt('dve', XPim.t[:, :, 1:CB + 1], t1v, t2v, ALU.add, [t1, t2], [XPim])
            P.cp('pool', XPb.t[:, 0, :, :], XPre.t[:, :, 0:CB], [XPre], [XPb])
            P.cp('pool', XPb.t[:, 1, :, :], XPim.t[:, :, 0:CB], [XPim], [XPb])
            P.cp('pool', XPre.t[:, :, 0:1], XPre.t[:, :, CB:CB + 1], [XPre, XPb], [XPre])
            P.cp('pool', XPim.t[:, :, 0:1], XPim.t[:, :, CB:CB + 1], [XPim, XPb], [XPim])
            y5t = [Buf(G[3].t[:, ct_ * T:(ct_ + 1) * T], G[3].k) for ct_ in range(4)]
            yield 'part1'
            for ct in range(4):
                if ct == 2:
                    yield 'y01'
                o, kk = ps(6 + (ct % 2), 0, T)
                uv = u5.t[:, ct, :].rearrange("p (c s) -> p s c", s=L5)
                for s_ in range(L5):
                    oc = o[:, s_ * CB:(s_ + 1) * CB]
                    for sp_ in range(s_ + 1):
                        P.mm(oc, KT.t[:, ct, s_ - sp_, :], uv[:, sp_, :], sp_ == 0, False, [KT, u5], kk)
                    for jj in range(4):
                        i = 4 * ct + jj
                        for part in range(2):
                            P.mm(o[32 * jj:32 * jj + 32, s_ * CB:(s_ + 1) * CB], Ctab.t[:, s_, part, i, :], XPb.t[:, part, i, :],
                                 False, (part == 1), [Ctab, XPb], kk, tile_position=(0, 32 * jj))
                P.stt(y5t[ct].t.rearrange("p (c s) -> p s c", s=L5), uv, colsB.t[:, 32 + ct:33 + ct],
                      o.rearrange("p (s c) -> p s c", s=L5), ALU.mult, ALU.add, [u5, colsB] + kk, [y5t[ct]])
            yield 'y23'
            def gelu_tile(ct):
                y_, z_ = y5t[ct], tf[4 + ct]
                P.tt('pool', z_.t[:], y_.t, y_.t, ALU.mult, [y_], [z_])
                yield
                P.ts('pool', z_.t[:], z_.t[:], 0.044715, ALU.mult, [z_], [z_], s2=1.0, op1=ALU.add)
                yield
                P.tt('pool', z_.t[:], z_.t[:], y_.t, ALU.mult, [z_, y_], [z_])
                yield
                P.act(z_.t[:], z_.t[:], AF.Sigmoid, [z_], [z_], scale=1.5957691216057308)
                yield
                P.tt('dve', z_.t[:], z_.t[:], y_.t, ALU.mult, [z_, y_], [z_])
                yield
                P.cp('pool', z5b.t[:, ct, :], z_.t[:], [z_], [z5b])

            def rr(gens):
                live = [True] * len(gens)
                while any(live):
                    for q_ in range(len(gens)):
                        if live[q_]:
                            try:
                                next(gens[q_])
                            except StopIteration:
                                live[q_] = False

            rr([gelu_tile(ct) for ct in range(4)])

            def glu_tile(m):
                slot = pcount[0] % 4
                pcount[0] += 1
                o, kk = ps(slot, 0, T)
                for k in range(4):
                    P.mm(o, wglu.t[:, k, m * 128:(m + 1) * 128], z5b.t[:, k, :], k == 0, k == 3, [wglu, z5b], kk)
                yield
                gl = tf[m]
                P.act(gl.t[:], o, AF.Sigmoid, kk, [gl], bias=colsB.t[:, 36 + m:37 + m])
                yield
                P.tt('pool', gl.t[:], gl.t[:], tf[4 + m].t[:], ALU.mult, [gl, tf[4 + m]], [gl])
                yield
                P.tt('dve', mixT.t[:, 12 + m, :], gl.t[:], gate.t[:, 4 + m, :], ALU.mult, [gl, 'gate_s5'], ['mix_s5'])

            rr([glu_tile(m) for m in range(4)])

        def gdn_chunk(l, c):
            if c == 0:
                P.memset('pool', raw.t[:, :, 0:3], 0.0, ['raw_rg', 'raw_g0', 'raw_g1', 'raw_g2'])
                P.memset('pool', Sst.t[:], 0.0, [Sst])
                P.memset('pool', Sb.t[:], 0.0, [Sb])
            else:
                P.cp('pool', raw.t[:, :, 0:3], histg.t[:], [histg], ['raw_rg', 'raw_g0', 'raw_g1', 'raw_g2'])
            P.load(wba.t[:], wba_s[l], [wba], R=['wba_s%d' % l])
            def rawk(j):
                return (['raw_rg'] if j < 4 else []) + ['raw_g%d' % (j // 8)]

            def inproj_q(qtr):
                for g in range(4 * qtr, 4 * qtr + 4):
                    wb = stream_w(l, 4 + g)
                    for n in range(2):
                        o, kk = inproj_tile(wb, n * 128)
                        j = g * 2 + n
                        if j < 24:
                            P.cp(evac_engine(), raw.t[:, j, 3:3 + T], o, kk, rawk(j))
                        else:
                            P.act(gate.t[:, j - 24, :], o, AF.Silu, kk, ['gate_rg', 'gate_s5', 'gate_g'])

            def conv_q(qtr):
                P.cp('pool', histg.t[:, 8 * qtr:8 * qtr + 8, :], raw.t[:, 8 * qtr:8 * qtr + 8, T:T + 3], ['raw_rg', 'raw_g%d' % qtr], [histg])
                def conv_norm_tile(j):
                    o, kk = conv_tile(raw, j, gcw, lambda k: k * 24 + j, rawk(j))
                    yield
                    sl, lv, rs_ = tf[(j % 2) * 3], tf[(j % 2) * 3 + 1], tf[(j % 2) * 3 + 2]
                    P.act(lv.t[:], o, AF.Exp, kk, [lv], scale=-1.0)
                    yield
                    P.act(lv.t[:], lv.t[:], AF.Ln, [lv], [lv], bias=1.0)
                    yield
                    P.act(lv.t[:], lv.t[:], AF.Exp, [lv], [lv], scale=-1.0)
                    yield
                    if j >= 16:
                        P.tt('dve', vs.t[:, j - 16, :], o, lv.t[:], ALU.mult, kk + [lv], [vs])
                        return
                    sq_ = tb[j % 2]
                    P.tt('dve', sl.t[:], o, lv.t[:], ALU.mult, kk + [lv], [sl])
                    yield
                    P.tt('pool', sq_.t[:], sl.t[:], sl.t[:], ALU.mult, [sl], [sq_])
                    yield
                    o2, k2 = ps(4 + (j % 2), 0, T)
                    P.mm(o2, onesb.t[:], sq_.t[:], True, True, [onesb, sq_], k2)
                    yield
                    P.act(lv.t[:], o2, AF.Ln, k2, [lv], bias=EPS)
                    yield
                    if j < 8:
                        P.act(rs_.t[:], lv.t[:], AF.Exp, [lv], [rs_], scale=-0.5, bias=-0.5 * math.log(128.0))
                        yield
                        P.tt('dve', qn.t[:, j, :], sl.t[:], rs_.t[:], ALU.mult, [sl, rs_], [qn])
                    else:
                        P.act(rs_.t[:], lv.t[:], AF.Exp, [lv], [rs_], scale=-0.5)
                        yield
                        P.tt('dve', kn.t[:, j - 8, :], sl.t[:], rs_.t[:], ALU.mult, [sl, rs_], [kn])

                for j0 in range(8 * qtr, 8 * qtr + 8, 2):
                    pair = [conv_norm_tile(j0), conv_norm_tile(j0 + 1)]
                    live = [True, True]
                    next(pair[0], None)
                    next(pair[0], None)
                    while any(live):
                        for q_ in (1, 0):
                            if live[q_]:
                                try:
                                    next(pair[q_])
                                except StopIteration:
                                    live[q_] = False
            inproj_q(0)
            inproj_q(1)
            conv_q(0)
            inproj_q(2)
            conv_q(1)
            inproj_q(3)
            conv_q(2)
            if gdn_stop < 7 and c == 0 and l == 0:
                P.memset('pool', mixT.t[:, 4:12, :], 0.0, ['mix_g'])
            if gdn_stop >= 1:
                gdn_scalars(l, c)
            gens = [gdn_inner(l, c, gci) for gci in range(T // GC)]

            def run_until(gen, tags):
                while True:
                    try:
                        t_ = next(gen)
                    except StopIteration:
                        return
                    if t_ in tags:
                        return

            def rr_until(g1, tags1, g2, tags2):
                d1 = d2 = False
                while not (d1 and d2):
                    if not d2:
                        try:
                            d2 = next(g2) in tags2
                        except StopIteration:
                            d2 = True
                    if not d1:
                        try:
                            d1 = next(g1) in tags1
                        except StopIteration:
                            d1 = True

            n_i = T // GC
            run_until(gens[0], ('AB',))
            for gci in range(n_i):
                if gci + 1 < n_i:
                    rr_until(gens[gci], ('C',), gens[gci + 1], ('A_done',))
                    run_until(gens[gci + 1], ('AB',))
                else:
                    run_until(gens[gci], ('C',))
                run_until(gens[gci], ())

        def gdn_scalars(l, c):
            g = gsm
            v3 = lambda ap: ap.rearrange("p (i h) -> p i h", i=NI)
            o, kk = ps(0, 0, NI * 16, 0, 64)
            for gci in range(NI):
                for k in range(8):
                    P.mm(o[:, gci * 16:(gci + 1) * 16], hnT.t[:, k, gci * GC:(gci + 1) * GC], wba.t[:, k, :], k == 0, k == 7, [hnT, wba], kk)
            ov = o.rearrange("p (i c) -> p i c", i=NI)
            P.act(v3(g['bt'].t[:]), ov[:, :, 0:8], AF.Sigmoid, kk, [g['bt']])
            P.tt('dve', v3(g['xa'].t[:]), ov[:, :, 8:16], bc(dtb.t[:].unsqueeze(1), [64, NI, 8]), ALU.add, kk + [dtb], [g['xa']])
            P.act(g['ex'].t[:], g['xa'].t[:], AF.Exp, [g['xa']], [g['ex']])
            P.act(g['sp'].t[:], g['ex'].t[:], AF.Ln, [g['ex']], [g['sp']], bias=1.0)
            P.tt('dve', v3(g['gp'].t[:]), v3(g['sp'].t[:]), bc(negA.t[:].unsqueeze(1), [64, NI, 8]), ALU.mult, [g['sp'], negA], [g['gp']])
            P.ts('dve', g['g'].t[:], g['gp'].t[:], -1.0, ALU.mult, [g['gp']], [g['g']])
            P.ts('pool', g['nbt'].t[:], g['bt'].t[:], -1.0, ALU.mult, [g['bt']], [g['nbt']])
            oc, kc = ps(0, 64, NI * 8, 0, 64)
            P.mm(oc, U64.t[:], g['g'].t[:], True, True, [U64, g['g']], kc)
            P.cp('dve', g['gcum'].t[:], oc, kc, [g['gcum']])
            ol, kl = ps(0, 128, NI * 8)
            P.mm(ol, sel63.t[:], g['gcum'].t[:], True, True, [sel63, g['gcum']], kl)
            P.act(egl.t[:], ol, AF.Exp, kl, [egl])
            P.act(g['egc'].t[:], g['gcum'].t[:], AF.Exp, [g['gcum']], [g['egc']])
            P.tt('dve', g['dgl'].t[:], ol[0:64, :], g['gcum'].t[:], ALU.subtract, kl + [g['gcum']], [g['dgl']])
            P.act(g['kds'].t[:], g['dgl'].t[:], AF.Exp, [g['dgl']], [g['kds']])
            P.tt('dve', g['bge'].t[:], g['bt'].t[:], g['egc'].t[:], ALU.mult, [g['bt'], g['egc']], [g['bge']])

        def hb(ap, n):
            return bc(ap.unsqueeze(2), [64, 8, n])

        def m8(ap64):
            return bc(ap64.unsqueeze(1), [64, 8, 64])

        def gdn_inner(l, c, gci):
            t0 = gci * GC
            cs = slice(t0, t0 + GC)
            fl = lambda b_: b_.t[:].rearrange("p h j -> p (h j)")
            aqkT = aqkTs[gci % 2]
            if gdn_stop < 1:
                return
            g = {n: (Buf(gsm[n].t[:, gci * 8:(gci + 1) * 8], gsm[n].k) if n not in ('ss', 'ln', 'rs') else Buf(gsm[n].t[:], gsm[n].k)) for n in gsm}
            eglv = egl.t[:, gci * 8:(gci + 1) * 8]
            if gdn_stop < 2:
                return
            NGU, GBC, E1, E2, DK = gw[0], gw[1], gw[2], gw[3], gw[4]
            P.tt('dve', NGU.t[:], m8(U64.t[:]), hb(g['gp'].t, 64), ALU.mult, [U64, g['gp']], [NGU])
            P.cp('pool', GBC.t[:], hb(g['g'].t, 64), [g['g']], [GBC])
            oD, kD = ps(1, 0, 512, 0, 64)
            P.mm(oD, U64.t[:], fl(GBC), True, False, [U64, GBC], kD)
            P.mm(oD, ones64.t[:], fl(NGU), False, True, [ones64, NGU], kD)
            yield 'a'
            P.ts('dve', fl(E1), oD, 0.0, ALU.min, kD, [E1])
            P.ts('dve', fl(E2), oD, -1.0, ALU.mult, kD, [E2], s2=0.0, op1=ALU.min)
            P.act(fl(E1), fl(E1), AF.Exp, [E1], [E1])
            P.act(fl(E2), fl(E2), AF.Exp, [E2], [E2])
            yield 'a'
            P.tt('pool', DK.t[:], m8(Mst.t[:]), hb(g['nbt'].t, 64), ALU.mult, [Mst, g['nbt']], [DK])
            P.tt('pool', DK.t[:], DK.t[:], E1.t[:], ALU.mult, [DK, E1], [DK])
            E2u = E2
            DQ = gw[6]
            P.tt('pool', DQ.t[:], E2u.t[:], m8(U64.t[:]), ALU.mult, [E2u, U64], [DQ])
            yield 'a'
            if gdn_stop < 3:
                return
            Bd = gw[1]
            P.tt('dve', Bd.t[:], m8(ident.t[0:64, 0:64]), hb(g['nbt'].t, 64), ALU.mult, [ident, g['nbt']], [Bd])
            oB, kB = ps(1, 0, 512, 0, 64)
            P.mm(oB, ones64.t[:], fl(Bd), True, True, [ones64, Bd], kB)
            MK = gw[1]
            P.tt('dve', MK.t[:], E2u.t[:], m8(Ust.t[:]), ALU.mult, [E2u, Ust], [MK])
            P.tt('dve', fl(MK), fl(MK), oB, ALU.mult, [MK] + kB, [MK])
            okk, kkk = ps(2, 0, 512, 0, 64)
            oqk, kqk = ps(3, 0, 512, 0, 64)
            for h in range(8):
                P.mm(okk[:, h * 64:(h + 1) * 64], kn.t[:, h, cs], kn.t[:, h, cs], True, True, [kn], kkk)
            for h in range(8):
                P.mm(oqk[:, h * 64:(h + 1) * 64], kn.t[:, h, cs], qn.t[:, h, cs], True, True, [kn, qn], kqk)
            def bfv(b_):
                return Buf(b_.t[:].rearrange("p h j -> p (h j)").bitcast(BF16)[:, 0:512].rearrange("p (h j) -> p h j", h=8), b_.k)
            CH_BF16 = True
            if CH_BF16:
                Nb = [bfv(gw[5]), bfv(gw[6])]
                Mb = [bfv(gw[0]), bfv(gw[1])]
                PTb = [bfv(gw[2]), bfv(gw[3])]
            else:
                Nb = [gw[5], gw[6]]
                Mb = [gw[0], gw[1]]
                PTb = [gw[2], gw[4]]
            P.tt('dve', fl(Nb[0]), okk, fl(DK), ALU.mult, kkk + [DK], [Nb[0]])
            P.tt('dve', fl(aqkT), oqk, fl(DQ), ALU.mult, kqk + [DQ], [aqkT])
            yield 'a'
            if gdn_stop < 4:
                return
            P.tt('dve', fl(Mb[0]), okk, fl(MK), ALU.mult, kkk + [MK], [Mb[0]])
            P.tt('dve', PTb[0].t[:], Mb[0].t[:], m8(ident.t[0:64, 0:64]), ALU.add, [Mb[0], ident], [PTb[0]])
            cur = 0
            for lev in range(1, 6):
                nxt = 1 - cur
                oN, kN = ps(1, 0, 512, 0, 64)
                for h in range(8):
                    P.mm(oN[:, h * 64:(h + 1) * 64], Mb[cur].t[:, h, :], Nb[cur].t[:, h, :], True, True, [Mb[cur], Nb[cur]], kN)
                if lev < 5:
                    oM, kM = ps(2, 0, 512, 0, 64)
                    for h in range(8):
                        P.mm(oM[:, h * 64:(h + 1) * 64], Nb[cur].t[:, h, :], Mb[cur].t[:, h, :], True, True, [Mb[cur], Nb[cur]], kM)
                P.cp('act', fl(Nb[nxt]), oN, kN, [Nb[nxt]])
                if lev < 5:
                    P.cp('dve', fl(Mb[nxt]), oM, kM, [Mb[nxt]])
                oP, kP = ps(3, 0, 512, 0, 64)
                for h in range(8):
                    P.mm(oP[:, h * 64:(h + 1) * 64], Nb[nxt].t[:, h, :], PTb[cur].t[:, h, :], True, True, [Nb[nxt], PTb[cur]], kP)
                P.tt('dve', fl(PTb[nxt]), oP, fl(PTb[cur]), ALU.add, kP + [PTb[cur]], [PTb[nxt]])
                cur = nxt
                yield 'a'
            PT = PTb[cur]
            yield 'A_done'
            if gdn_stop < 5:
                return
            okt, kkt = psbf(4, 0, 1024, 0, 64)
            ovt, kvt = psbf(5, 0, 1024, 0, 64)
            for h in range(8):
                P.tr(okt[:, h * 128:(h + 1) * 128], kn.t[:, h, cs], identb.t[:], [kn, identb], kkt)
            for h in range(8):
                P.tr(ovt[:, h * 128:(h + 1) * 128], vs.t[:, h, cs], identb.t[:], [vs, identb], kvt)
            kbg, vb, usb, ob = gx[0], gx[1], gx[2], gx[3]
            if CH_BF16:
                bx = lambda b_: Buf(b_.t[:].rearrange("p h e -> p (h e)").bitcast(BF16)[:, 0:1024].rearrange("p (h e) -> p h e", h=8), b_.k)
                kbg, vb = bx(gx[0]), bx(gx[1])
            fx = lambda b_: b_.t[:].rearrange("p h e -> p (h e)")
            v3 = lambda ap: ap.rearrange("p (h e) -> p h e", h=8)
            P.tt('dve', kbg.t[:], v3(okt), hb(g['bge'].t, 128), ALU.mult, kkt + [g['bge']], [kbg])
            P.tt('dve', kdec.t[:], v3(okt), hb(g['kds'].t, 128), ALU.mult, kkt + [g['kds']], [kdec])
            P.tt('dve', vb.t[:], v3(ovt), hb(g['bt'].t, 128), ALU.mult, kvt + [g['bt']], [vb])
            ou, ku = ps(6, 0, 1024, 0, 64)
            for h in range(8):
                P.mm(ou[:, h * 128:(h + 1) * 128], PT.t[:, h, :], vb.t[:, h, :], True, True, [PT, vb], ku)
            ow, kw = ps(0, 0, 512)
            for h in range(8):
                P.mm(ow[:, h * 64:(h + 1) * 64], kbg.t[:, h, :], PT.t[:, h, :], True, True, [PT, kbg], kw)
            P.cp('act', fx(usb), ou, ku, [usb])
            P.cp('act', wTb.t[:].rearrange("p h c -> p (h c)"), ow, kw, [wTb])
            if gdn_stop < 6:
                return
            yield 'AB'
            o1, k1 = ps(4, 0, 1024, 0, 64)
            for h in range(8):
                P.mm(o1[:, h * 128:(h + 1) * 128], wTb.t[:, h, :], Sb.t[:, h, :], True, True, [wTb, Sb], k1)
            o2, k2 = ps(6, 0, 1024, 0, 64)
            for h in range(8):
                P.mm(o2[:, h * 128:(h + 1) * 128], qn.t[:, h, cs], Sb.t[:, h, :], True, True, [qn, Sb], k2)
            yield 'c'
            P.tt('dve', fx(vnew), fx(usb), o1, ALU.subtract, [usb] + k1, [vnew])
            P.tt('dve', ob.t[:], v3(o2), hb(g['egc'].t, 128), ALU.mult, k2 + [g['egc']], [ob])
            yield 'c'
            o3, k3 = ps(4, 0, 1024, 0, 64)
            for h in range(8):
                P.mm(o3[:, h * 128:(h + 1) * 128], aqkT.t[:, h, :], vnew.t[:, h, :], True, True, [aqkT, vnew], k3)
            o4, k4 = ps(6, 0, 1024)
            for h in range(8):
                P.mm(o4[:, h * 128:(h + 1) * 128], kdec.t[:, h, :], vnew.t[:, h, :], True, True, [kdec, vnew], k4)
            yield 'c'
            P.tt('dve', fx(ob), fx(ob), o3, ALU.add, [ob] + k3, [ob])
            for h in range(8):
                P.stt(Sst.t[:, h, :], Sst.t[:, h, :], eglv[:, h:h + 1], o4[:, h * 128:(h + 1) * 128], ALU.mult, ALU.add,
                      [Sst, egl] + k4, [Sst])
            P.cp('pool', Sb.t[:], Sst.t[:], [Sst], [Sb])
            if gdn_stop < 7:
                return
            yield 'C'
            osq = gx[0]
            P.tt('pool', osq.t[:], ob.t[:], ob.t[:], ALU.mult, [ob], [osq])
            P.op('dve', lambda e: e.tensor_reduce(out=g['ss'].t, in_=osq.t[:], axis=AX.X, op=ALU.add), [osq], [g['ss']])
            P.act(g['ln'].t, g['ss'].t, AF.Ln, [g['ss']], [g['ln']], scale=1.0 / 128.0, bias=EPS)
            P.act(g['rs'].t, g['ln'].t, AF.Exp, [g['ln']], [g['rs']], scale=-0.5)
            on = Buf(gx[1].t[:].rearrange("p h e -> p (h e)").bitcast(BF16)[:, 0:1024].rearrange("p (h e) -> p h e", h=8), gx[1].k)
            P.tt('dve', on.t[:], ob.t[:], hb(g['rs'].t, 128), ALU.mult, [ob, g['rs']], [on])
            oo, ko = psbf(5, 0, 512)
            for h in range(8):
                P.tr(oo[:, h * 64:(h + 1) * 64], on.t[:, h, :], identb.t[0:64, 0:64], [on, identb], ko)
            P.stt(mixT.t[:, 4:12, cs], oo.rearrange("p (h c) -> p h c", h=8), colsB.t[:, 40:41], gate.t[:, :, cs],
                  ALU.mult, ALU.mult, ko + [colsB, 'gate_g'], ['mix_g'])

        ocount = [0]

        def chunk(l, c):
            tok0 = c * T
            last = (l == depth - 1)
            if l == 0:
                for a_ in range(2):
                    P.load(stg[a_].t[:, 0:1024], x_d[tok0 + a_ * 128:tok0 + (a_ + 1) * 128, :], [stg[a_]], 'stg%d' % a_)
                for half in range(2):
                    o, kk = ps(6, 0, 1024)
                    for k4 in range(4):
                        for a_ in range(2):
                            kt_ = 4 * half + k4
                            P.tr(o[:, k4 * 256 + a_ * 128:k4 * 256 + a_ * 128 + 128], stg[a_].t[:, kt_ * 128:(kt_ + 1) * 128], ident.t[:],
                                 [stg[a_], ident], kk)
                    P.cp('act' if half else 'dve', hT.t[:, 4 * half:4 * half + 4, :].rearrange("p k t -> p (k t)"), o, kk, [hT])
            else:
                P.load(hT.t[:], hscr.rearrange("k p t -> p k t")[:, :, tok0:tok0 + T], [hT], 'hT', R=['hscr'])
            rmsnorm(RB['norm_g'], out_bf=hnT)
            if not use_rg and c == 0 and l == 0:
                P.memset('pool', mixT.t[:, 0:4, :], 0.0, ['mix_rg'])
            if not use_s5 and c == 0 and l == 0:
                P.memset('pool', mixT.t[:, 12:16, :], 0.0, ['mix_s5'])
            gr = rg_chunk(l, c) if use_rg else iter(())
            g5 = s5_chunk(l, c) if use_s5 else iter(())
            next(gr, None)
            next(g5, None)
            next(gr, None)
            next(g5, None)
            next(gr, None)
            for _ in gr:
                pass
            for _ in g5:
                pass
            if use_gdn:
                gdn_chunk(l, c)
            elif c == 0 and l == 0:
                P.memset('pool', mixT.t[:, 4:12, :], 0.0, ['mix_g'])
            for m in range(8):
                b_ = ring_next()
                wo = Buf(b_.t[:].rearrange("p (k n) -> p k n", k=16), b_.k)
                P.load(wo.t, wout_s[l][m], [b_], R=['wout_s%d' % l])
                slot = pcount[0] % 4
                pcount[0] += 1
                o, kk = ps(slot, 0, T)
                for k in range(16):
                    mk = 'mix_rg' if k < 4 else ('mix_g' if k < 12 else 'mix_s5')
                    P.mm(o, wo.t[:, k, :], mixT.t[:, k, :], k == 0, k == 15, [wo, mk], kk)
                P.tt('dve', hT.t[:, m, :], hT.t[:, m, :], o, ALU.add, [hT] + kk, [hT])
            rmsnorm(RB['ple_norm_g'], out_bf=hnT)
            ptok = stg[1]
            pv = ptok.t[:, 1024:1536].rearrange("p (a d) -> p a d", a=2)
            P.load(pv, p_d[l, tok0:tok0 + T, :].rearrange("(a p) d -> p a d", p=128), [ptok], 'stg1')
            o, kk = ps(6, 0, 512)
            for k in range(2):
                for a in range(2):
                    P.tr(o[:, k * 256 + a * 128:k * 256 + a * 128 + 128], pv[:, a, k * 128:(k + 1) * 128], ident.t[:], [ptok, ident], kk)
            P.cp('act', pT.t[:].rearrange("p k t -> p (k t)"), o, kk, [pT])
            for m in range(8):
                b_ = ring_next()
                wg = Buf(b_.t[:, 0:1024].rearrange("p (k n) -> p k n", k=8), b_.k)
                P.load(wg.t, wgate_s[l][m], [b_], R=['wgate_s%d' % l])
                slot = pcount[0] % 4
                pcount[0] += 1
                o, kk = ps(slot, 0, T)
                for k in range(8):
                    P.mm(o, wg.t[:, k, :], hnT.t[:, k, :], k == 0, k == 7, [wg, hnT], kk)
                gt_ = tf[m % 2]
                P.act(gt_.t[:], o, AF.Sigmoid, kk, [gt_])
                o2, k2 = ps(4 + m % 2, 0, T)
                for k in range(2):
                    P.mm(o2, wproj.t[:, k, m * 128:(m + 1) * 128], pT.t[:, k, :], k == 0, k == 1, [wproj, pT], k2)
                P.tt('dve', gt_.t[:], gt_.t[:], o2, ALU.mult, [gt_] + k2, [gt_])
                P.tt('pool', hT.t[:, m, :], hT.t[:, m, :], gt_.t[:], ALU.add, [hT, gt_], [hT])
            if not last:
                P.store(hscr.rearrange("k p t -> p k t")[:, :, tok0:tok0 + T], hT.t[:], hT, 'hscr')
            else:
                P.act(sqb.t[:], hT.t[:], AF.Square, [hT], [sqb])
                o, kk = ps(7, 0, T)
                for k in range(8):
                    P.mm(o, onesb.t[:], sqb.t[:, k, :], k == 0, k == 7, [onesb, sqb], kk)
                P.act(rstd.t[:], o, AF.Ln, kk, [rstd], scale=1.0 / D, bias=EPS)
                P.act(rstd.t[:], rstd.t[:], AF.Exp, [rstd], [rstd], scale=-0.5)
                hf = stg[0]
                hfv = hf.t[:, 0:1024].rearrange("p (k t) -> p k t", k=4)
                otok = stg[1]
                for half in range(2):
                    for k4 in range(4):
                        kt_ = 4 * half + k4
                        P.stt(hfv[:, k4, :], hT.t[:, kt_, :], colsB.t[:, 57 + kt_:58 + kt_], rstd.t[:], ALU.mult, ALU.mult,
                              [hT, colsB, rstd], [hf])
                    for a_ in range(2):
                        o, kk = ps(6 + a_, 0, 512)
                        for k4 in range(4):
                            P.tr(o[:, k4 * 128:(k4 + 1) * 128], hfv[:, k4, a_ * 128:(a_ + 1) * 128], ident.t[:], [hf, ident], kk)
                        P.cp('act' if a_ else 'dve', otok.t[:, a_ * 512:(a_ + 1) * 512], o, kk, [otok])
                    P.store(out_d[tok0:tok0 + T, half * 512:(half + 1) * 512].rearrange("(a p) d -> p a d", p=128),
                            otok.t[:, 0:1024].rearrange("p (a d) -> p a d", a=2), otok, 'out')

        cnt = [0]
        NPSTG = 2
        pstg = [stg[0], stg[1],
                Buf(hT.t[:].rearrange("p k t -> p (k t)")[:, 0:1536], hT.k),
                Buf(mixT.t[:].rearrange("p a t -> p (a t)").bitcast(F32)[:, 0:1536], ('mix_rg', 'mix_g', 'mix_s5'))]
        pstgb = [stgb[0], stgb[1],
                 Buf(qn.t[:].rearrange("p k t -> p (k t)"), qn.k),
                 Buf(kn.t[:].rearrange("p k t -> p (k t)"), kn.k)]

        def prep_piece(src_ap, ncols, stores):
            i = cnt[0] % NPSTG
            cnt[0] += 1
            sf, sbf = pstg[i], pstgb[i]
            P.load(sf.t[:, 0:ncols], src_ap, [sf])
            P.cp('dve' if cnt[0] % 2 else 'pool', sbf.t[:, 0:ncols], sf.t[:, 0:ncols], [sf], [sbf])
            for (dst, c0, w, key) in stores:
                srcv = sbf.t[:, c0:c0 + w]
                if len(dst.shape) == 3:
                    srcv = srcv.rearrange("p (g j) -> p g j", g=dst.shape[1])
                P.store(dst, srcv, sbf, key)

        for l in range(depth):
            w_in = prm['w_in'][l]
            for r in range(8):
                rows = slice(r * 128, (r + 1) * 128)
                prep_piece(w_in[rows, 0:1024], 1024, [(win_s[l][0:4, :, r, :].rearrange("g p j -> p g j"), 0, 1024, 'win_s%d' % l)])
                prep_piece(w_in[rows, 1024:2560], 1536, [(win_s[l][4:10, :, r, :].rearrange("g p j -> p g j"), 0, 1536, 'win_s%d' % l)])
                prep_piece(w_in[rows, 2560:4096], 1536, [(win_s[l][10:16, :, r, :].rearrange("g p j -> p g j"), 0, 1536, 'win_s%d' % l)])
                prep_piece(w_in[rows, 4096:5120], 1024, [(win_s[l][16:20, :, r, :].rearrange("g p j -> p g j"), 0, 1024, 'win_s%d' % l)])
                prep_piece(w_in[rows, 5120:6160], 1040, [(wba_s[l][:, r, :], 0, 16, 'wba_s%d' % l),
                                                         (win_s[l][20:24, :, r, :].rearrange("g p j -> p g j"), 16, 1024, 'win_s%d' % l)])
            for r in range(16):
                prep_piece(prm['w_out'][l][r * 128:(r + 1) * 128, :], 1024,
                           [(wout_s[l][:, :, r, :].rearrange("m p j -> p m j"), 0, 1024, 'wout_s%d' % l)])
            for r in range(8):
                prep_piece(prm['ple_w_gate'][l][r * 128:(r + 1) * 128, :], 1024,
                           [(wgate_s[l][:, :, r, :].rearrange("m p j -> p m j"), 0, 1024, 'wgate_s%d' % l)])


        for l in range(depth):
            load_layer_consts(l)
            for c in range(NCH):
                chunk(l, c)
        P.final_wait('act', ['out'])
        P.emit()
    return nc


_CACHE = {}


def kernel(**inputs):
    B = inputs['x'].shape[0]
    S = inputs['x'].shape[1]
    depth = inputs['p'].shape[0]
    key = (S, depth)
    if key not in _CACHE:
        _CACHE[key] = build_program(S, depth)
    nc = _CACHE[key]
    shared = {name: np.ascontiguousarray(inputs[name], dtype=np.float32) for name, _ in PARAM_SHAPES(depth)}
    in_maps = []
    for b in range(B):
        m = dict(shared)
        m['x'] = np.ascontiguousarray(inputs['x'][b], dtype=np.float32)
        m['p'] = np.ascontiguousarray(inputs['p'][:, b], dtype=np.float32)
        in_maps.append(m)
    res = run_bass_kernel_spmd(nc, in_maps, core_ids=list(range(B)))
    return np.stack([r['out'] for r in res.results], axis=0).astype(np.float32)
```

```python
import math
import numpy as np
import concourse.bass as bass
import concourse.mybir as mybir
from concourse.bass_utils import run_bass_kernel_spmd
from contextlib import ExitStack

F32 = mybir.dt.float32
BF16 = mybir.dt.bfloat16
I32 = mybir.dt.int32
AF = mybir.ActivationFunctionType
ALU = mybir.AluOpType
AX = mybir.AxisListType

D = 1024
NIN = 6160
DMIX = 2048
DPLE = 256
T = 256
GC = 64
L5 = 8
CB = T // L5
EPS = 1e-6
TWO_PI = 2.0 * math.pi

ENG = ['pe', 'dve', 'act', 'pool', 'sp']

PARAM_SHAPES = lambda L: [
    ('norm_g', [L, 1024]), ('w_in', [L, 1024, 6160]), ('rg_conv_w', [L, 4, 512]), ('rg_conv_b', [L, 512]),
    ('rg_w_a', [L, 8, 64, 64]), ('rg_b_a', [L, 512]), ('rg_w_x', [L, 8, 64, 64]), ('rg_b_x', [L, 512]),
    ('rg_lambda', [L, 512]), ('gdn_conv_w', [L, 4, 3072]), ('gdn_a_log', [L, 8]), ('gdn_dt_bias', [L, 8]),
    ('gdn_norm_g', [L, 128]), ('s5_a_re', [L, 32, 64]), ('s5_a_im', [L, 32, 64]), ('s5_b_re', [L, 32, 64, 16]),
    ('s5_b_im', [L, 32, 64, 16]), ('s5_c_re', [L, 32, 16, 64]), ('s5_c_im', [L, 32, 16, 64]), ('s5_d', [L, 512]),
    ('s5_log_dt', [L, 32]), ('s5_w_glu', [L, 512, 512]), ('s5_b_glu', [L, 512]), ('w_out', [L, 2048, 1024]),
    ('ple_norm_g', [L, 1024]), ('ple_w_gate', [L, 1024, 1024]), ('ple_w_proj', [L, 256, 1024]),
    ('final_norm_g', [1024]),
]


class Buf:
    def __init__(self, t, k):
        self.t = t
        self.k = k


def _keys(lst):
    out = []
    for r in lst:
        if isinstance(r, Buf):
            if isinstance(r.k, (list, tuple)):
                out.extend(r.k)
            else:
                out.append(r.k)
        elif isinstance(r, (list, tuple, set)):
            out.extend(_keys(r))
        else:
            out.append(r)
    return out


class Prog:
    def __init__(self, nc, es, same_engine_sync=True):
        self.nc = nc
        self.es = es
        self.ops = {e: [] for e in ENG}
        self.esem = {e: es.enter_context(nc.semaphore('s_' + e)) for e in ENG}
        self.ecnt = {e: 0 for e in ENG}
        self.waited = {e: {} for e in ENG}
        self.writers = {}
        self.readers = {}
        self.dsems = {}
        self.dcnt = {}
        self.dsem_name = {}
        self.same_engine_sync = same_engine_sync

    def sbuf(self, name, shape, dt):
        return Buf(self.es.enter_context(self.nc.sbuf_tensor(name, list(shape), dt)), name)

    def _deps(self, eng, reads, writes):
        need = {}

        def add(ev):
            s, v = ev
            k = id(s)
            if k not in need or need[k][1] < v:
                need[k] = (s, v)

        for r in reads:
            for ev in self.writers.get(r, {}).values():
                add(ev)
        for w in writes:
            for ev in self.writers.get(w, {}).values():
                add(ev)
            for ev in self.readers.get(w, {}).values():
                add(ev)
        waits = []
        for k, (s, v) in need.items():
            if s is self.esem[eng] and (eng == 'pe' or not self.same_engine_sync):
                continue
            nm = self.dsem_name.get(k)
            if nm is not None and self.dsems[nm] is s:
                v = max(v, self.dcnt[nm])
            if self.waited[eng].get(k, 0) >= v:
                continue
            self.waited[eng][k] = v
            waits.append((s, v))
        return waits

    def _commit(self, ev, reads, writes):
        k = id(ev[0])
        for r in reads:
            d = self.readers.setdefault(r, {})
            if k not in d or d[k][1] < ev[1]:
                d[k] = ev
        for w in writes:
            d = self.writers.setdefault(w, {})
            if k not in d or d[k][1] < ev[1]:
                d[k] = ev
            self.readers[w] = {}

    def op(self, eng, fn, reads=(), writes=()):
        reads = _keys(reads)
        writes = _keys(writes)
        waits = self._deps(eng, reads, writes)
        if self.ecnt[eng] >= 16000:
            self.esem[eng] = self.es.enter_context(self.nc.semaphore('s_%s_%d' % (eng, len(self.ops[eng]))))
            self.ecnt[eng] = 0
        self.ecnt[eng] += 1
        ev = (self.esem[eng], self.ecnt[eng])
        self.ops[eng].append((waits, fn, ev, 1))
        self._commit(ev, reads, writes)

    def dma(self, q, fn, reads, writes, sem_name, dram_w=()):
        reads = _keys(reads)
        writes = _keys(writes)
        waits = self._deps(q, reads, writes)
        writes = writes + _keys(dram_w)
        if sem_name not in self.dsems:
            self.dsems[sem_name] = self.es.enter_context(self.nc.semaphore('d_' + sem_name))
            self.dcnt[sem_name] = 0
            self.dsem_name[id(self.dsems[sem_name])] = sem_name
        if self.dcnt[sem_name] >= 16000:
            self.dsems[sem_name] = self.es.enter_context(
                self.nc.semaphore('d_%s_%d' % (sem_name, len(self.ops[q]))))
            self.dcnt[sem_name] = 0
            self.dsem_name[id(self.dsems[sem_name])] = sem_name
        self.dcnt[sem_name] += 16
        ev = (self.dsems[sem_name], self.dcnt[sem_name])
        self.ops[q].append((waits, fn, ev, 16))
        self._commit(ev, reads, writes)

    def final_wait(self, eng, resources):
        resources = _keys(resources)
        waits = self._deps(eng, resources, resources)
        self.ops[eng].append((waits, None, None, 0))

    def emit(self):
        nc = self.nc
        with nc.Block() as block:
            for e, deco in [('pe', block.tensor), ('dve', block.vector), ('act', block.scalar),
                            ('pool', block.gpsimd), ('sp', block.sync)]:
                ops = self.ops[e]

                @deco
                def _(engine, ops=ops):
                    for waits, fn, ev, inc in ops:
                        for (s, v) in waits:
                            engine.wait_ge(s, v)
                        if fn is None:
                            continue
                        ins = fn(engine)
                        ins.then_inc(ev[0], inc)

    def mm(self, out, lhsT, rhs, start, stop, R, W, **kw):
        self.op('pe', lambda e: e.matmul(out, lhsT=lhsT, rhs=rhs, start=start, stop=stop, **kw), R, W)

    def tr(self, out, in_, ident, R, W):
        self.op('pe', lambda e: e.transpose(out, in_, ident), R, W)

    def act(self, out, in_, func, R, W, scale=None, bias=None):
        kw = {}
        if scale is not None:
            kw['scale'] = scale
        if bias is not None:
            kw['bias'] = bias
        self.op('act', lambda e: e.activation(out=out, in_=in_, func=func, **kw), R, W)

    def tt(self, eng, out, in0, in1, op, R, W):
        self.op(eng, lambda e: e.tensor_tensor(out=out, in0=in0, in1=in1, op=op), R, W)

    def ts(self, eng, out, in0, s1, op0, R, W, s2=None, op1=None):
        if op1 is None:
            if eng == 'pool':
                s2, op1 = (0.0, ALU.add) if op0 == ALU.mult else (1.0, ALU.mult)
                self.op(eng, lambda e: e.tensor_scalar(out=out, in0=in0, scalar1=s1, scalar2=s2, op0=op0, op1=op1), R, W)
            else:
                self.op(eng, lambda e: e.tensor_scalar(out=out, in0=in0, scalar1=s1, scalar2=None, op0=op0), R, W)
        else:
            self.op(eng, lambda e: e.tensor_scalar(out=out, in0=in0, scalar1=s1, scalar2=s2, op0=op0, op1=op1), R, W)

    def stt(self, out, in0, scalar, in1, op0, op1, R, W):
        self.op('dve', lambda e: e.scalar_tensor_tensor(out=out, in0=in0, scalar=scalar, in1=in1, op0=op0, op1=op1), R, W)

    def cp(self, eng, out, in_, R, W):
        if eng == 'act':
            self.op('act', lambda e: e.activation(out=out, in_=in_, func=AF.Copy), R, W)
        else:
            self.op(eng, lambda e: e.tensor_copy(out=out, in_=in_), R, W)

    def memset(self, eng, ap, val, W):
        self.op(eng, lambda e: e.memset(ap, val), [], W)

    def load(self, out, in_, W, sem=None, R=(), q='sp', slow=False):
        sem = 'ld_' + _keys(W)[0]
        if slow:
            self.dma(q, lambda e: e.dma_start(out=out, in_=in_, allow_slow_non_contiguous=True), R, W, sem)
        else:
            self.dma(q, lambda e: e.dma_start(out=out, in_=in_), R, W, sem)

    def store(self, out, in_, src, dram_key, q='act'):
        sem = 'st_' + _keys([src])[0]
        self.dma(q, lambda e: e.dma_start(out=out, in_=in_), [src], [], sem, dram_w=[dram_key])


def bc(ap, shape):
    return ap.broadcast_to(list(shape))


def build_program(S, depth, use_rg=True, use_gdn=True, use_s5=True, gdn_stop=99, same_engine_sync=True):
    nc = bass.Bass("TRN2", target_bir_lowering=False)
    NCH = S // T
    assert S % T == 0

    def din(name, shape):
        return nc.dram_tensor(name, list(shape), F32, kind="ExternalInput").ap()

    x_d = din("x", [S, D])
    p_d = din("p", [depth, S, DPLE])
    prm = {name: din(name, shape) for name, shape in PARAM_SHAPES(depth)}
    out_d = nc.dram_tensor("out", [S, D], F32, kind="ExternalOutput").ap()
    win_s = [nc.dram_tensor("win_s%d" % l, [24, 128, 8, 256], BF16, kind="Internal").ap() for l in range(depth)]
    wba_s = [nc.dram_tensor("wba_s%d" % l, [128, 8, 16], BF16, kind="Internal").ap() for l in range(depth)]
    wout_s = [nc.dram_tensor("wout_s%d" % l, [8, 128, 16, 128], BF16, kind="Internal").ap() for l in range(depth)]
    wgate_s = [nc.dram_tensor("wgate_s%d" % l, [8, 128, 8, 128], BF16, kind="Internal").ap() for l in range(depth)]
    hscr = nc.dram_tensor("hscr", [8, 128, S], F32, kind="Internal").ap()

    with ExitStack() as es:
        P = Prog(nc, es, same_engine_sync=same_engine_sync)
        sb = P.sbuf

        psd = [es.enter_context(nc.psum_tensor("psd%d" % i, [128, 1024], F32)) for i in range(4)]

        def ps(bank, c0=0, w=512, p0=0, p1=128):
            base = (bank % 2) * 512 + c0
            ap = psd[bank // 2][p0:p1, base:base + w]
            keys = ['psb%d' % b for b in range(bank + c0 // 512, bank + (c0 + w - 1) // 512 + 1)]
            return ap, keys

        def psbf(bank, c0, w, p0=0, p1=128):
            t = psd[bank // 2][p0:p1, (bank % 2) * 512:(bank % 2) * 512 + 512].bitcast(BF16)
            return t[:, c0:c0 + w], ['psb%d' % bank]

        ident = sb('ident', [128, 128], F32)
        identb = sb('identb', [128, 128], BF16)
        onesb = sb('onesb', [128, 128], BF16)
        U64 = sb('U64', [64, 64], F32)
        Mst = sb('Mst', [64, 64], F32)
        ones64 = sb('ones64', [64, 64], F32)
        sel63 = sb('sel63', [64, 128], F32)
        P.memset('pool', ident.t[:], 1.0, [ident])
        P.op('pool', lambda e: e.affine_select(out=ident.t[:], in_=ident.t[:], pattern=[[-1, 128]], compare_op=ALU.is_equal,
                                               fill=0.0, base=0, channel_multiplier=1), [ident], [ident])
        P.cp('pool', identb.t[:], ident.t[:], [ident], [identb])
        P.memset('pool', onesb.t[:], 1.0, [onesb])
        P.memset('pool', U64.t[:], 1.0, [U64])
        P.op('pool', lambda e: e.affine_select(out=U64.t[:], in_=U64.t[:], pattern=[[1, 64]], compare_op=ALU.is_ge,
                                               fill=0.0, base=0, channel_multiplier=-1), [U64], [U64])
        P.memset('pool', Mst.t[:], 1.0, [Mst])
        P.op('pool', lambda e: e.affine_select(out=Mst.t[:], in_=Mst.t[:], pattern=[[-1, 64]], compare_op=ALU.is_gt,
                                               fill=0.0, base=0, channel_multiplier=1), [Mst], [Mst])
        Ust = sb('Ust', [64, 64], F32)
        P.memset('pool', Ust.t[:], 1.0, [Ust])
        P.op('pool', lambda e: e.affine_select(out=Ust.t[:], in_=Ust.t[:], pattern=[[1, 64]], compare_op=ALU.is_gt,
                                               fill=0.0, base=0, channel_multiplier=-1), [Ust], [Ust])
        P.memset('pool', ones64.t[:], 1.0, [ones64])
        P.memset('pool', sel63.t[:], 1.0, [sel63])
        P.op('pool', lambda e: e.affine_select(out=sel63.t[:], in_=sel63.t[:], pattern=[[0, 128]], compare_op=ALU.is_equal,
                                               fill=0.0, base=-63, channel_multiplier=1), [sel63], [sel63])

        stg = [sb('stg%d' % i, [128, 1536], F32) for i in range(2)]
        G = [sb('G%d' % i, [128, 1024], F32) for i in range(4)]
        stgb = [Buf(G[i].t[:].bitcast(BF16), G[i].k) for i in range(2)]

        colsB = sb('colsB', [128, 72], F32)
        gcw = sb('gcw', [128, 96], F32)
        rgc = sb('rgc', [128, 16], F32)
        rgWa = sb('rgWa', [128, 4, 128], BF16)
        rgWx = sb('rgWx', [128, 4, 128], BF16)
        wglu = sb('wglu', [128, 4, 512], BF16)
        wproj = sb('wproj', [128, 2, 1024], BF16)
        negA = sb('negA', [64, 8], F32)
        dtb = sb('dtb', [64, 8], F32)
        RB = dict(norm_g=0, ple_norm_g=8, rg_conv_b=16, rg_b_a=20, rg_b_x=24, rg_lambda=28, s5_d=32, s5_b_glu=36,
                  gdn_norm_g=40, rg_conv_w=41, final=57)

        KT = sb('KT', [128, 4, 8, 128], BF16)
        EB = sb('EB', [128, 4, 8, 2, 128], BF16)
        Ctab = sb('Ctab', [128, 8, 2, 16, 32], BF16)
        Dcos = sb('Dcos', [128, 16, CB], F32)
        Dsin = sb('Dsin', [128, 16, CB], F32)
        Rho = sb('Rho', [128, 16, CB], F32)
        rhoc = sb('rhoc', [128, 16], F32)

        hT = sb('hT', [128, 8, T], F32)
        hnT = sb('hnT', [128, 8, T], BF16)
        sqb = Buf(G[3].t[:].bitcast(BF16).rearrange("p (k t) -> p k t", k=8), G[3].k)
        rstd = sb('rstd', [128, T], F32)
        mixT = sb('mixT', [128, 16, T], BF16)
        raw = sb('raw', [128, 24, 3 + T], BF16)
        histrg = sb('histrg', [128, 4, 3], BF16)
        histg = sb('histg', [128, 24, 3], BF16)
        gate = sb('gate', [128, 8, T], BF16)
        u5 = sb('u5', [128, 4, T], BF16)
        NRING = 5
        wring = [sb('wring%d' % i, [128, 2048], BF16) for i in range(NRING)]
        wba = sb('wba', [128, 8, 16], BF16)
        dg = [sb('dg%d' % i, [128, 128], BF16) for i in range(4)]
        tf = [sb('tf%d' % i, [128, T], F32) for i in range(8)]
        tb = [sb('tb%d' % i, [128, T], BF16) for i in range(2)]
        rgcar = sb('rgcar', [128, 4], F32)
        pT = sb('pT', [128, 2, T], BF16)
        XPre = sb('XPre', [128, 16, CB + 1], F32)
        XPim = sb('XPim', [128, 16, CB + 1], F32)
        XPb = sb('XPb', [128, 2, 16, CB], BF16)
        z5b = sb('z5b', [128, 4, T], BF16)
        s5big = [Buf(G[i // 2].t[:, (i % 2) * 512:(i % 2) * 512 + 512], G[i // 2].k) for i in range(6)]
        qn = sb('qn', [128, 8, T], BF16)
        kn = sb('kn', [128, 8, T], BF16)
        vs = sb('vs', [128, 8, T], BF16)
        Sst = sb('Sst', [128, 8, 128], F32)
        Sb = sb('Sb', [128, 8, 128], BF16)
        NI = T // GC
        gsm = {n: sb('g_' + n, [64, NI * 8], F32) for n in ['bt', 'nbt', 'xa', 'ex', 'sp', 'gp', 'g', 'gcum', 'egc', 'dgl', 'kds', 'bge']}
        gsm.update({n: sb('g_' + n, [64, 8], F32) for n in ['ss', 'ln', 'rs']})
        egl = sb('egl', [128, NI * 8], F32)
        gw = [sb('gw%d' % i, [64, 8, 64], F32) for i in range(7)]
        aqkTs = [sb('aqkT%d' % i, [64, 8, 64], BF16) for i in range(2)]
        gx = [Buf(G[i].t[0:64, :].rearrange("p (h e) -> p h e", h=8), G[i].k) for i in range(4)]
        kdec = sb('kdec', [64, 8, 128], BF16)
        vnew = sb('vnew', [64, 8, 128], BF16)
        wTb = sb('wTb', [128, 8, 64], BF16)

        s5t = {n: sb('s5_' + n, [128, 16], F32) for n in
               ['are', 'aim', 'ldt', 'dt', 'ard', 'th', 'cr', 'den', 'inv', 'cfr', 'cfi', 't0', 't1', 'tqb']}
        hTf = hT.t[:].rearrange("p k t -> p (k t)")
        Lre = Buf(hTf[:, 0:144].rearrange("p (j i) -> p j i", j=9), hT.k)
        Lim = Buf(hTf[:, 256:400].rearrange("p (j i) -> p j i", j=9), hT.k)
        jv = Buf(hTf[:, 512:640].rearrange("p (j i) -> p j i", j=8), hT.k)
        jvi = Buf(G[2].t[:, 0:128].bitcast(I32).rearrange("p (j i) -> p j i", j=8), G[2].k)
        mask2 = sb('mask2', [128, 2, 16], F32)
        s5int = Buf(G[3].t[:, 0:512].bitcast(I32), G[3].k)
        cvi = Buf(G[3].t[:, 512:1024].bitcast(I32).rearrange("p (i c) -> p i c", i=16), G[3].k)
        knf = kn.t[:].rearrange("p k t -> p (k t)").bitcast(F32)
        qnf = qn.t[:].rearrange("p k t -> p (k t)").bitcast(F32)
        vsf = vs.t[:].rearrange("p k t -> p (k t)").bitcast(F32)
        hnf = hnT.t[:].rearrange("p k t -> p (k t)").bitcast(F32)
        v3_ = lambda ap, a: ap.rearrange("p (a b) -> p a b", a=a)
        bre = Buf(v3_(vsf[:, 0:256], 16), vs.k)
        bim = Buf(v3_(vsf[:, 256:512], 16), vs.k)
        cre = Buf(v3_(vsf[:, 512:768], 16), vs.k)
        cim = Buf(v3_(vsf[:, 768:1024], 16), vs.k)
        Bre = Buf(v3_(qnf[:, 0:256], 16), qn.k)
        Bim = Buf(v3_(qnf[:, 256:512], 16), qn.k)
        Cbr = Buf(v3_(qnf[:, 512:1024], 16), qn.k)
        Cbi = Buf(v3_(knf[:, 0:512], 16), kn.k)
        cv = Buf(v3_(knf[:, 512:1024], 16), kn.k)
        maskW = Buf(hnf[:, 0:512].rearrange("p (a g c) -> p a g c", a=4, g=8), hnT.k)
        P.memset('pool', mask2.t[:], 0.0, [mask2])
        P.memset('pool', mask2.t[0:64, 0, :], 1.0, [mask2])
        P.memset('pool', mask2.t[64:128, 1, :], 1.0, [mask2])

        def load_layer_consts(l):
            sA = stg[0]
            P.load(sA.t[0:96, 0:128], prm['gdn_conv_w'][l].rearrange("k (j p) -> (k j) p", p=128), [sA], 'stg0')
            o, kk = ps(7, 0, 96)
            P.tr(o, sA.t[0:96, 0:128], ident.t[0:96, 0:96], [sA, ident], kk)
            P.cp('dve', gcw.t[:], o, kk, [gcw])
            sBt = stg[1]
            rows = [('norm_g', prm['norm_g'][l], 8), ('ple_norm_g', prm['ple_norm_g'][l], 8), ('rg_conv_b', prm['rg_conv_b'][l], 4),
                    ('rg_b_a', prm['rg_b_a'][l], 4), ('rg_b_x', prm['rg_b_x'][l], 4), ('rg_lambda', prm['rg_lambda'][l], 4),
                    ('s5_d', prm['s5_d'][l], 4), ('s5_b_glu', prm['s5_b_glu'][l], 4), ('gdn_norm_g', prm['gdn_norm_g'][l], 1)]
            for name, ap, n in rows:
                r0 = RB[name]
                P.load(sBt.t[r0:r0 + n, 0:128], ap.rearrange("(k p) -> k p", p=128), [sBt], 'stg1')
            P.load(sBt.t[41:57, 0:128], prm['rg_conv_w'][l].rearrange("k (j p) -> (k j) p", p=128), [sBt], 'stg1')
            P.load(sBt.t[57:65, 0:128], prm['final_norm_g'].rearrange("(k p) -> k p", p=128), [sBt], 'stg1')
            o, kk = ps(7, 128, 65)
            P.tr(o, sBt.t[0:65, 0:128], ident.t[0:65, 0:65], [sBt, ident], kk)
            P.cp('dve', colsB.t[:, 0:65], o, kk, [colsB])
            z = s5t['t0'].t[:, 0:4]
            acc = s5t['t1'].t[:, 0:4]
            P.act(z, colsB.t[:, 28:32], AF.Exp, [colsB], [s5t['t0']], scale=-1.0)
            P.ts('dve', acc, z, -1.0 / 9.0, ALU.mult, [s5t['t0']], [s5t['t1']], s2=1.0 / 8.0, op1=ALU.add)
            for k in range(7, 0, -1):
                P.tt('dve', acc, acc, z, ALU.mult, [s5t['t0'], s5t['t1']], [s5t['t1']])
                P.ts('dve', acc, acc, -1.0, ALU.mult, [s5t['t1']], [s5t['t1']], s2=1.0 / k, op1=ALU.add)
            P.tt('dve', acc, acc, z, ALU.mult, [s5t['t0'], s5t['t1']], [s5t['t1']])
            P.ts('dve', rgc.t[:, 0:4], acc, -8.0, ALU.mult, [s5t['t1']], [rgc])
            P.ts('dve', rgc.t[:, 4:8], acc, -16.0, ALU.mult, [s5t['t1']], [rgc])
            P.ts('dve', rgc.t[:, 8:16], colsB.t[:, 20:28], -1.0, ALU.mult, [colsB], [rgc])
            for (src, dst) in [(prm['rg_w_a'][l], rgWa), (prm['rg_w_x'][l], rgWx)]:
                s = stg[0]
                P.memset('pool', s.t[:, 0:512], 0.0, [s])
                sv = s.t[:, 0:512].rearrange("p (t j) -> p t j", t=4)
                for h2 in range(2):
                    P.load(sv[h2 * 64:(h2 + 1) * 64, :, h2 * 64:(h2 + 1) * 64],
                           src.rearrange("(t h2) i j -> h2 i t j", h2=2)[h2], [s], 'stg0')
                P.cp('pool', dst.t[:], sv, [s], [dst])
            for hh in range(2):
                s = stg[hh]
                P.load(s.t[:, 0:1024].rearrange("p (k n) -> p k n", k=2),
                       prm['s5_w_glu'][l].rearrange("(k p) n -> p k n", p=128)[:, 2 * hh:2 * hh + 2, :], [s], 'stg%d' % hh)
                P.cp('dve', wglu.t[:, 2 * hh:2 * hh + 2, :], s.t[:, 0:1024].rearrange("p (k n) -> p k n", k=2), [s], [wglu])
            for hh in range(2):
                s = stg[hh]
                P.load(s.t[:, 0:1024], prm['ple_w_proj'][l][hh * 128:(hh + 1) * 128, :], [s], 'stg%d' % hh)
                P.cp('pool', wproj.t[:, hh, :], s.t[:, 0:1024], [s], [wproj])
            P.load(gsm['xa'].t[:, 0:8], prm['gdn_a_log'][l].partition_broadcast(64), [gsm['xa']], 'gsm')
            P.act(negA.t[:], gsm['xa'].t[:, 0:8], AF.Exp, [gsm['xa']], [negA])
            P.load(dtb.t[:], prm['gdn_dt_bias'][l].partition_broadcast(64), [dtb], 'gsm')
            if use_s5:
                s5_setup(l)

        def sincos(tq, n, out_sin, out_cos, Rk, Wk_sin, Wk_cos):
            ti = s5int.t[:, 0:n]
            tfl = s5big[4].t[:, 0:n]
            fr = s5big[5].t[:, 0:n]
            for (shift, out, Wk) in [(0.0, out_sin, Wk_sin), (0.25, out_cos, Wk_cos)]:
                src = tq
                if shift != 0.0:
                    P.ts('dve', fr, tq, shift, ALU.add, Rk, [s5big[5]])
                    src = fr
                    R2 = [s5big[5]]
                else:
                    R2 = Rk
                P.cp('dve', ti, src, R2, [s5int])
                P.cp('dve', tfl, ti, [s5int], [s5big[4]])
                P.tt('dve', fr, src, tfl, ALU.subtract, R2 + [s5big[4]], [s5big[5]])
                P.act(out, fr, AF.Sin, [s5big[5]], Wk, scale=TWO_PI)

        def s5_setup(l):
            t = s5t
            P.op('pool', lambda e: e.iota(cvi.t[:], pattern=[[0, 16], [1, CB]], base=1, channel_multiplier=0), [], [cvi])
            P.cp('pool', cv.t[:], cvi.t[:], [cvi], [cv])
            P.op('pool', lambda e: e.iota(jvi.t, pattern=[[1, 8], [0, 16]], base=1, channel_multiplier=0), [], [jvi])
            P.cp('pool', jv.t, jvi.t, [jvi], [jv])
            P.memset('pool', maskW.t[:], 0.0, [maskW])
            for jj in range(4):
                P.memset('pool', maskW.t[0:64, jj, 2 * jj, :], 1.0, [maskW])
                P.memset('pool', maskW.t[64:128, jj, 2 * jj + 1, :], 1.0, [maskW])
            for name, src in [('are', prm['s5_a_re'][l]), ('aim', prm['s5_a_im'][l])]:
                for g2 in range(2):
                    P.load(t[name].t[g2 * 64:(g2 + 1) * 64, :], src.rearrange("(i g2) n -> g2 n i", g2=2)[g2], [t[name]], 's5ld', slow=True)
            for g2 in range(2):
                P.load(t['ldt'].t[g2 * 64:(g2 + 1) * 64, :], prm['s5_log_dt'][l].rearrange("(i g2) -> g2 i", g2=2)[g2].partition_broadcast(64),
                       [t['ldt']], 's5ld', slow=True)
            for (dst, src) in [(bre, prm['s5_b_re'][l]), (bim, prm['s5_b_im'][l])]:
                for g2 in range(2):
                    P.load(dst.t[g2 * 64:(g2 + 1) * 64, :, :], src.rearrange("(i g2) n c -> g2 n i c", g2=2)[g2], [dst], 's5ld')
            for (dst, src) in [(cre, prm['s5_c_re'][l]), (cim, prm['s5_c_im'][l])]:
                for g2 in range(2):
                    for i_ in range(16):
                        P.load(dst.t[g2 * 64:(g2 + 1) * 64, i_, :],
                               src.rearrange("(i g2) c n -> g2 i n c", g2=2)[g2, i_], [dst], 's5ld', slow=True)
            P.act(t['dt'].t[:], t['ldt'].t[:], AF.Exp, [t['ldt']], [t['dt']])
            P.tt('dve', t['ard'].t[:], t['are'].t[:], t['dt'].t[:], ALU.mult, [t['are'], t['dt']], [t['ard']])
            P.tt('dve', t['th'].t[:], t['aim'].t[:], t['dt'].t[:], ALU.mult, [t['aim'], t['dt']], [t['th']])
            A0 = s5big[0].t[:, 0:128].rearrange("p (j i) -> p j i", j=8)
            A1 = s5big[1].t[:, 0:128].rearrange("p (j i) -> p j i", j=8)
            A2 = s5big[2].t[:, 0:128].rearrange("p (j i) -> p j i", j=8)
            A3 = s5big[3].t[:, 0:128].rearrange("p (j i) -> p j i", j=8)
            P.tt('dve', A0, jv.t, bc(t['ard'].t[:].unsqueeze(1), [128, 8, 16]), ALU.mult, [jv, t['ard']], [s5big[0]])
            P.act(A0, A0, AF.Exp, [s5big[0]], [s5big[0]])
            P.tt('dve', A1, jv.t, bc(t['th'].t[:].unsqueeze(1), [128, 8, 16]), ALU.mult, [jv, t['th']], [s5big[1]])
            P.ts('dve', A1, A1, 1.0 / TWO_PI, ALU.mult, [s5big[1]], [s5big[1]])
            sincos(s5big[1].t[:, 0:128], 128, s5big[2].t[:, 0:128], s5big[3].t[:, 0:128], [s5big[1]], [s5big[2]], [s5big[3]])
            P.memset('dve', Lre.t[:, 0, :], 1.0, [Lre])
            P.memset('dve', Lim.t[:, 0, :], 0.0, [Lim])
            P.tt('dve', Lre.t[:, 1:9, :], A0, A3, ALU.mult, [s5big[0], s5big[3]], [Lre])
            P.tt('dve', Lim.t[:, 1:9, :], A0, A2, ALU.mult, [s5big[0], s5big[2]], [Lim])
            P.ts('dve', t['tqb'].t[:], t['th'].t[:], float(L5) / TWO_PI, ALU.mult, [t['th']], [t['tqb']])
            B0 = s5big[0].t[:, 0:16 * CB].rearrange("p (i c) -> p i c", i=16)
            P.tt('dve', B0, cv.t[:], bc(t['tqb'].t[:].unsqueeze(2), [128, 16, CB]), ALU.mult, [cv, t['tqb']], [s5big[0]])
            sincos(s5big[0].t[:, 0:16 * CB], 16 * CB, Dsin.t[:].rearrange("p i c -> p (i c)"), Dcos.t[:].rearrange("p i c -> p (i c)"),
                   [s5big[0]], [Dsin], [Dcos])
            P.act(rhoc.t[:], t['ard'].t[:], AF.Exp, [t['ard']], [rhoc], scale=float(L5))
            P.cp('dve', Rho.t[:], bc(rhoc.t[:].unsqueeze(2), [128, 16, CB]), [rhoc], [Rho])
            P.memset('dve', Rho.t[:, :, 0:1], 0.0, [Rho])
            P.ts('dve', t['cr'].t[:], Lre.t[:, 1, :], -1.0, ALU.add, [Lre], [t['cr']])
            P.tt('dve', t['den'].t[:], t['are'].t[:], t['are'].t[:], ALU.mult, [t['are']], [t['den']])
            P.tt('dve', t['t0'].t[:], t['aim'].t[:], t['aim'].t[:], ALU.mult, [t['aim']], [t['t0']])
            P.tt('dve', t['den'].t[:], t['den'].t[:], t['t0'].t[:], ALU.add, [t['den'], t['t0']], [t['den']])
            P.op('dve', lambda e: e.reciprocal(out=t['inv'].t[:], in_=t['den'].t[:]), [t['den']], [t['inv']])
            P.tt('dve', t['t0'].t[:], t['cr'].t[:], t['are'].t[:], ALU.mult, [t['cr'], t['are']], [t['t0']])
            P.tt('dve', t['t1'].t[:], Lim.t[:, 1, :], t['aim'].t[:], ALU.mult, [Lim, t['aim']], [t['t1']])
            P.tt('dve', t['t0'].t[:], t['t0'].t[:], t['t1'].t[:], ALU.add, [t['t0'], t['t1']], [t['t0']])
            P.tt('dve', t['cfr'].t[:], t['t0'].t[:], t['inv'].t[:], ALU.mult, [t['t0'], t['inv']], [t['cfr']])
            P.tt('dve', t['t0'].t[:], Lim.t[:, 1, :], t['are'].t[:], ALU.mult, [Lim, t['are']], [t['t0']])
            P.tt('dve', t['t1'].t[:], t['cr'].t[:], t['aim'].t[:], ALU.mult, [t['cr'], t['aim']], [t['t1']])
            P.tt('dve', t['t0'].t[:], t['t0'].t[:], t['t1'].t[:], ALU.subtract, [t['t0'], t['t1']], [t['t0']])
            P.tt('dve', t['cfi'].t[:], t['t0'].t[:], t['inv'].t[:], ALU.mult, [t['t0'], t['inv']], [t['cfi']])

            def cmul(out_re, out_im, a_re, a_im, b_re, b_im, shape, Ra, Rb, Wre, Wim, neg_im=False):
                n = 1
                for d_ in shape[1:]:
                    n *= d_
                v0 = s5big[4].t[:, 0:n]
                v1 = s5big[5].t[:, 0:n]
                if len(shape) == 3:
                    v0 = v0.rearrange("p (a b) -> p a b", a=shape[1])
                    v1 = v1.rearrange("p (a b) -> p a b", a=shape[1])
                P.tt('dve', v0, a_re, b_re, ALU.mult, Ra + Rb, [s5big[4]])
                P.tt('dve', v1, a_im, b_im, ALU.mult, Ra + Rb, [s5big[5]])
                P.tt('dve', out_re, v0, v1, ALU.subtract, [s5big[4], s5big[5]], Wre)
                P.tt('dve', v0, a_re, b_im, ALU.mult, Ra + Rb, [s5big[4]])
                P.tt('dve', v1, a_im, b_re, ALU.mult, Ra + Rb, [s5big[5]])
                P.tt('dve', out_im, v0, v1, ALU.add, [s5big[4], s5big[5]], Wim)
                if neg_im:
                    P.ts('dve', out_im, out_im, -1.0, ALU.mult, Wim, Wim)

            sh3 = [128, 16, 16]
            cmul(Bre.t[:], Bim.t[:], bc(t['cfr'].t[:].unsqueeze(2), sh3), bc(t['cfi'].t[:].unsqueeze(2), sh3), bre.t[:], bim.t[:],
                 sh3, [t['cfr'], t['cfi']], [bre, bim], [Bre], [Bim])
            for (dst, src) in [(Cbr, cre), (Cbi, cim)]:
                for i4 in range(4):
                    P.tt('dve', dst.t[:, 4 * i4:4 * i4 + 4, :].rearrange("p i (g c) -> p i g c", g=2),
                         bc(src.t[:, 4 * i4:4 * i4 + 4, :].unsqueeze(2), [128, 4, 2, 16]),
                         bc(mask2.t[:].unsqueeze(1), [128, 4, 2, 16]), ALU.mult, [src, mask2], [dst])
            shb = [128, 16, 32]
            for s in range(8):
                lr = bc(Lre.t[:, s + 1, :].unsqueeze(2), shb)
                li = bc(Lim.t[:, s + 1, :].unsqueeze(2), shb)
                cr_o = s5big[0].t[:, 0:512].rearrange("p (a b) -> p a b", a=16)
                ci_o = s5big[1].t[:, 0:512].rearrange("p (a b) -> p a b", a=16)
                cmul(cr_o, ci_o, lr, li, Cbr.t[:], Cbi.t[:], shb, [Lre, Lim], [Cbr, Cbi], [s5big[0]], [s5big[1]], neg_im=True)
                P.cp('pool', Ctab.t[:, s, 0, :, :], cr_o, [s5big[0]], [Ctab])
                P.cp('pool', Ctab.t[:, s, 1, :, :], ci_o, [s5big[1]], [Ctab])
            Pre = s5big[0].t[:, 0:256].rearrange("p (a b) -> p a b", a=16)
            Pim = s5big[1].t[:, 0:256].rearrange("p (a b) -> p a b", a=16)
            Pbr = s5big[2].t[:, 0:512].rearrange("p (a b) -> p a b", a=16)
            Pbi = s5big[3].t[:, 0:512].rearrange("p (a b) -> p a b", a=16)
            Pwr = stg[0].t[:, 0:512]
            Pwi = stg[0].t[:, 512:1024]
            for j in range(8):
                lr = bc(Lre.t[:, j, :].unsqueeze(2), sh3)
                li = bc(Lim.t[:, j, :].unsqueeze(2), sh3)
                cmul(Pre, Pim, lr, li, Bre.t[:], Bim.t[:], sh3, [Lre, Lim], [Bre, Bim], [s5big[0]], [s5big[1]])
                for (dst, src, kd, ks) in [(Pbr, Pre, s5big[2], s5big[0]), (Pbi, Pim, s5big[3], s5big[1])]:
                    for i4 in range(4):
                        P.tt('dve', dst[:, 4 * i4:4 * i4 + 4, :].rearrange("p i (g c) -> p i g c", g=2),
                             bc(src[:, 4 * i4:4 * i4 + 4, :].unsqueeze(2), [128, 4, 2, 16]),
                             bc(mask2.t[:].unsqueeze(1), [128, 4, 2, 16]), ALU.mult, [ks, mask2], [kd])
                sp_ = 7 - j
                for ct in range(4):
                    for part, (src, kd) in enumerate([(Pbr, s5big[2]), (Pbi, s5big[3])]):
                        o, kk = ps(6, part * 128, 128)
                        P.tr(o, src[:, 4 * ct:4 * ct + 4, :].rearrange("p a b -> p (a b)"), ident.t[:], [kd, ident], kk)
                        P.cp('act', EB.t[:, ct, sp_, part, :], o, kk, [EB])
                    for (dstw, src, ks, sgn) in [(Pwr, Pre, s5big[0], 1.0), (Pwi, Pim, s5big[1], -1.0)]:
                        P.tt('dve', dstw.rearrange("p (a g c) -> p a g c", a=4, g=8),
                             bc(src[:, 4 * ct:4 * ct + 4, :].unsqueeze(2), [128, 4, 8, 16]), maskW.t[:], ALU.mult, [ks, maskW], [stg[0]])
                    P.ts('dve', Pwi, Pwi, -1.0, ALU.mult, [stg[0]], [stg[0]])
                    o, kk = ps(7, 256, 128)
                    for jj in range(4):
                        i = 4 * ct + jj
                        P.mm(o[:, 32 * jj:32 * jj + 32], Pwr[:, 128 * jj:128 * jj + 128], Cbr.t[:, i, :], True, False, [stg[0], Cbr], kk)
                        P.mm(o[:, 32 * jj:32 * jj + 32], Pwi[:, 128 * jj:128 * jj + 128], Cbi.t[:, i, :], False, True, [stg[0], Cbi], kk)
                    P.cp('act', KT.t[:, ct, j, :], o, kk, [KT])

        def rmsnorm(gcol0, out_bf=None, out_f32=None):
            P.act(sqb.t[:], hT.t[:], AF.Square, [hT], [sqb])
            o, kk = ps(7, 0, T)
            for k in range(8):
                P.mm(o, onesb.t[:], sqb.t[:, k, :], k == 0, k == 7, [onesb, sqb], kk)
            P.act(rstd.t[:], o, AF.Ln, kk, [rstd], scale=1.0 / D, bias=EPS)
            P.act(rstd.t[:], rstd.t[:], AF.Exp, [rstd], [rstd], scale=-0.5)
            dst = out_bf if out_bf is not None else out_f32
            for k in range(8):
                P.stt(dst.t[:, k, :], hT.t[:, k, :], colsB.t[:, gcol0 + k:gcol0 + k + 1], rstd.t[:], ALU.mult, ALU.mult,
                      [hT, colsB, rstd], [dst])

        wcount = [0]

        def ring_next():
            b = wring[wcount[0] % NRING]
            wcount[0] += 1
            return b

        def stream_w(l, g):
            b = ring_next()
            v = Buf(b.t[:].rearrange("p (k n) -> p k n", k=8), b.k)
            P.load(v.t, win_s[l][g], [b], R=['win_s%d' % l])
            return v

        pcount = [0]

        def inproj_tile(wb, col0):
            slot = pcount[0] % 4
            pcount[0] += 1
            o, kk = ps(slot, 0, T)
            for k in range(8):
                P.mm(o, wb.t[:, k, col0:col0 + 128], hnT.t[:, k, :], k == 0, k == 7, [wb, hnT], kk)
            return o, kk

        ecount = [0]

        def evac_engine():
            ecount[0] += 1
            return 'act' if ecount[0] % 2 else 'dve'

        dgc = [0]

        def conv_tile(src_buf, jt, wcols, col_of_tap, R):
            slot = pcount[0] % 4
            pcount[0] += 1
            o, kk = ps(slot, 0, T)
            for k in range(4):
                d = dg[dgc[0] % 4]
                dgc[0] += 1
                c = col_of_tap(k)
                P.ts('pool', d.t[:], identb.t[:], wcols.t[:, c:c + 1], ALU.mult, [identb, wcols], [d])
                P.mm(o, d.t[:], src_buf.t[:, jt, k:k + T], k == 0, k == 3, [d] + R, kk)
            return o, kk

        def rg_chunk(l, c):
            if c == 0:
                P.memset('pool', raw.t[:, 0:4, 0:3], 0.0, ['raw_rg'])
            else:
                P.cp('pool', raw.t[:, 0:4, 0:3], histrg.t[:], [histrg], ['raw_rg'])
            for g in range(4):
                wb = stream_w(l, g)
                for n in range(2):
                    o, kk = inproj_tile(wb, n * 128)
                    j = (g % 2) * 2 + n
                    if g < 2:
                        P.cp(evac_engine(), raw.t[:, j, 3:3 + T], o, kk, ['raw_rg'])
                    else:
                        P.act(gate.t[:, j, :], o, AF.Silu, kk, ['gate_rg'])
            P.cp('pool', histrg.t[:], raw.t[:, 0:4, T:T + 3], ['raw_rg'], [histrg])
            yield 'inproj'
            def rg_tile(j):
                p_ = j % 2
                r_, gi_, xj, m_ = tf[4 * p_], tf[4 * p_ + 1], tf[4 * p_ + 2], tf[4 * p_ + 3]
                xb_ = tb[p_]
                o, kk = conv_tile(raw, j, colsB, lambda k: RB['rg_conv_w'] + k * 4 + j, ['raw_rg'])
                yield
                P.act(xj.t[:], o, AF.Identity, kk, [xj], bias=colsB.t[:, 16 + j:17 + j])
                yield
                P.cp('dve', xb_.t[:], xj.t[:], [xj], [xb_])
                yield
                oa, ka = ps(4 + 2 * p_, 0, T)
                ox, kx = ps(5 + 2 * p_, 0, T)
                P.mm(oa, rgWa.t[:, j, :], xb_.t[:], True, True, [rgWa, xb_], ka)
                P.mm(ox, rgWx.t[:, j, :], xb_.t[:], True, True, [rgWx, xb_], kx)
                yield
                P.act(r_.t[:], oa, AF.Exp, ka + [rgc], [r_], scale=-1.0, bias=rgc.t[:, 8 + j:9 + j])
                P.act(gi_.t[:], ox, AF.Exp, kx + [rgc], [gi_], scale=-1.0, bias=rgc.t[:, 12 + j:13 + j])
                yield
                P.act(r_.t[:], r_.t[:], AF.Ln, [r_], [r_], bias=1.0)
                P.act(gi_.t[:], gi_.t[:], AF.Ln, [gi_], [gi_], bias=1.0)
                yield
                P.act(r_.t[:], r_.t[:], AF.Exp, [r_], [r_], scale=-1.0)
                P.act(gi_.t[:], gi_.t[:], AF.Exp, [gi_], [gi_], scale=-1.0)
                yield
                P.tt('pool', gi_.t[:], gi_.t[:], xj.t[:], ALU.mult, [gi_, xj], [gi_])
                a_ = xj
                P.act(m_.t[:], r_.t[:], AF.Exp, [r_, rgc], [m_], scale=rgc.t[:, 4 + j:5 + j])
                yield
                P.act(a_.t[:], r_.t[:], AF.Exp, [r_, rgc, gi_], [a_], scale=rgc.t[:, j:j + 1])
                yield
                P.act(m_.t[:], m_.t[:], AF.Sqrt, [m_], [m_], scale=-1.0, bias=1.0)
                yield
                P.tt('dve', m_.t[:], m_.t[:], gi_.t[:], ALU.mult, [m_, gi_], [m_])
                yield
                hr = r_
                if c == 0:
                    P.op('dve', lambda e: e.tensor_tensor_scan(out=hr.t[:], data0=a_.t[:], data1=m_.t[:], initial=0.0,
                                                               op0=ALU.mult, op1=ALU.add), [a_, m_], [hr])
                else:
                    P.op('dve', lambda e: e.tensor_tensor_scan(out=hr.t[:], data0=a_.t[:], data1=m_.t[:],
                                                               initial=rgcar.t[:, j:j + 1], op0=ALU.mult, op1=ALU.add),
                         [a_, m_, rgcar], [hr])
                yield
                P.cp('dve', rgcar.t[:, j:j + 1], hr.t[:, T - 1:T], [hr], [rgcar])
                P.tt('pool', mixT.t[:, j, :], hr.t[:], gate.t[:, j, :], ALU.mult, [hr, 'gate_rg'], ['mix_rg'])

            for j0 in (0, 2):
                pair = [rg_tile(j0), rg_tile(j0 + 1)]
                live = [True, True]
                next(pair[0], None)
                next(pair[0], None)
                while any(live):
                    for q_ in (1, 0):
                        if live[q_]:
                            try:
                                next(pair[q_])
                            except StopIteration:
                                live[q_] = False
                yield 'pair'

        def s5_chunk(l, c):
            for g in range(4):
                wb = stream_w(l, 20 + g)
                for n in range(2):
                    o, kk = inproj_tile(wb, n * 128)
                    j = (g % 2) * 2 + n
                    if g < 2:
                        P.cp(evac_engine(), u5.t[:, j, :], o, kk, [u5])
                    else:
                        P.act(gate.t[:, 4 + j, :], o, AF.Silu, kk, ['gate_s5'])
            if c == 0:
                P.memset('pool', XPre.t[:, :, 0:1], 0.0, [XPre])
                P.memset('pool', XPim.t[:, :, 0:1], 0.0, [XPim])
            pe_ = [ps(4, 0, 512), ps(5, 0, 512)]
            for i in range(16):
                ct, jj = i // 4, i % 4
                uv = u5.t[32 * jj:32 * jj + 32, ct, :].rearrange("p (c s) -> p s c", s=L5)
                for part in range(2):
                    o, kk = pe_[part]
                    for s_ in range(L5):
                        P.mm(o[:, i * CB:(i + 1) * CB], EB.t[32 * jj:32 * jj + 32, ct, s_, part, :], uv[:, s_, :],
                             s_ == 0, s_ == L5 - 1, [EB, u5], kk, tile_position=(32 * jj, 0))
            ere, kre = pe_[0]
            eim, kim = pe_[1]
            t1, t2, mre, mim, qre, qim = s5big[0], s5big[1], s5big[2], s5big[3], s5big[4], s5big[5]
            dcs = Dcos.t[:].rearrange("p i c -> p (i c)")
            dsn = Dsin.t[:].rearrange("p i c -> p (i c)")
            P.tt('dve', t1.t[:], ere, dcs, ALU.mult, kre + [Dcos], [t1])
            P.tt('dve', t2.t[:], eim, dsn, ALU.mult, kim + [Dsin], [t2])
            P.tt('pool', mre.t[:], t1.t[:], t2.t[:], ALU.add, [t1, t2], [mre])
            P.tt('dve', t1.t[:], eim, dcs, ALU.mult, kim + [Dcos], [t1])
            P.tt('dve', t2.t[:], ere, dsn, ALU.mult, kre + [Dsin], [t2])
            P.tt('pool', mim.t[:], t1.t[:], t2.t[:], ALU.subtract, [t1, t2], [mim])
            for (m_, XP) in [(mre, XPre), (mim, XPim)]:
                mv = m_.t[:].rearrange("p (i c) -> p i c", i=16)
                P.tt('pool', s5t['t0'].t[:].unsqueeze(2), rhoc.t[:].unsqueeze(2), XP.t[:, :, 0:1], ALU.mult, [rhoc, XP], [s5t['t0']])
                P.tt('pool', mv[:, :, 0:1], mv[:, :, 0:1], s5t['t0'].t[:].unsqueeze(2), ALU.add, [m_, s5t['t0']], [m_])
            rhf = Rho.t[:].rearrange("p i c -> p (i c)")
            for (m_, q_) in [(mre, qre), (mim, qim)]:
                P.op('dve', lambda e, m_=m_, q_=q_: e.tensor_tensor_scan(out=q_.t[:], data0=rhf, data1=m_.t[:], initial=0.0,
                                                                        op0=ALU.mult, op1=ALU.add), [Rho, m_], [q_])
            qrv = qre.t[:].rearrange("p (i c) -> p i c", i=16)
            qiv = qim.t[:].rearrange("p (i c) -> p i c", i=16)
            t1v = t1.t[:].rearrange("p (i c) -> p i c", i=16)
            t2v = t2.t[:].rearrange("p (i c) -> p i c", i=16)
            P.tt('dve', t1v, qrv, Dcos.t[:], ALU.mult, [qre, Dcos], [t1])
            P.tt('pool', t2v, qiv, Dsin.t[:], ALU.mult, [qim, Dsin], [t2])
            P.tt('dve', XPre.t[:, :, 1:CB + 1], t1v, t2v, ALU.subtract, [t1, t2], [XPre])
            P.tt('dve', t1v, qrv, Dsin.t[:], ALU.mult, [qre, Dsin], [t1])
            P.tt('pool', t2v, qiv, Dcos.t[:], ALU.mult, [qim, Dcos], [t2])
            P.tt('dve', XPim.t[:, :, 1:CB + 1], t1v, t2v, ALU.add, [t1, t2], [XPim])
            P.cp('pool', XPb.t[:, 0, :, :], XPre.t[:, :, 0:CB], [XPre], [XPb])
            P.cp('pool', XPb.t[:, 1, :, :], XPim.t[:, :, 0:CB], [XPim], [XPb])
            P.cp('pool', XPre.t[:, :, 0:1], XPre.t[:, :, CB:CB + 1], [XPre, XPb], [XPre])
            P.cp('pool', XPim.t[:, :, 0:1], XPim.t[:, :, CB:CB + 1], [XPim, XPb], [XPim])
            y5t = [Buf(G[3].t[:, ct_ * T:(ct_ + 1) * T], G[3].k) for ct_ in range(4)]
            yield 'part1'
            for ct in range(4):
                if ct == 2:
                    yield 'y01'
                o, kk = ps(6 + (ct % 2), 0, T)
                uv = u5.t[:, ct, :].rearrange("p (c s) -> p s c", s=L5)
                for s_ in range(L5):
                    oc = o[:, s_ * CB:(s_ + 1) * CB]
                    for sp_ in range(s_ + 1):
                        P.mm(oc, KT.t[:, ct, s_ - sp_, :], uv[:, sp_, :], sp_ == 0, False, [KT, u5], kk)
                    for jj in range(4):
                        i = 4 * ct + jj
                        for part in range(2):
                            P.mm(o[32 * jj:32 * jj + 32, s_ * CB:(s_ + 1) * CB], Ctab.t[:, s_, part, i, :], XPb.t[:, part, i, :],
                                 False, (part == 1), [Ctab, XPb], kk, tile_position=(0, 32 * jj))
                P.stt(y5t[ct].t.rearrange("p (c s) -> p s c", s=L5), uv, colsB.t[:, 32 + ct:33 + ct],
                      o.rearrange("p (s c) -> p s c", s=L5), ALU.mult, ALU.add, [u5, colsB] + kk, [y5t[ct]])
            yield 'y23'
            def gelu_tile(ct):
                y_, z_ = y5t[ct], tf[4 + ct]
                P.tt('pool', z_.t[:], y_.t, y_.t, ALU.mult, [y_], [z_])
                yield
                P.ts('pool', z_.t[:], z_.t[:], 0.044715, ALU.mult, [z_], [z_], s2=1.0, op1=ALU.add)
                yield
                P.tt('pool', z_.t[:], z_.t[:], y_.t, ALU.mult, [z_, y_], [z_])
                yield
                P.act(z_.t[:], z_.t[:], AF.Sigmoid, [z_], [z_], scale=1.5957691216057308)
                yield
                P.tt('dve', z_.t[:], z_.t[:], y_.t, ALU.mult, [z_, y_], [z_])
                yield
                P.cp('pool', z5b.t[:, ct, :], z_.t[:], [z_], [z5b])

            def rr(gens):
                live = [True] * len(gens)
                while any(live):
                    for q_ in range(len(gens)):
                        if live[q_]:
                            try:
                                next(gens[q_])
                            except StopIteration:
                                live[q_] = False

            rr([gelu_tile(ct) for ct in range(4)])

            def glu_tile(m):
                slot = pcount[0] % 4
                pcount[0] += 1
                o, kk = ps(slot, 0, T)
                for k in range(4):
                    P.mm(o, wglu.t[:, k, m * 128:(m + 1) * 128], z5b.t[:, k, :], k == 0, k == 3, [wglu, z5b], kk)
                yield
                gl = tf[m]
                P.act(gl.t[:], o, AF.Sigmoid, kk, [gl], bias=colsB.t[:, 36 + m:37 + m])
                yield
                P.tt('pool', gl.t[:], gl.t[:], tf[4 + m].t[:], ALU.mult, [gl, tf[4 + m]], [gl])
                yield
                P.tt('dve', mixT.t[:, 12 + m, :], gl.t[:], gate.t[:, 4 + m, :], ALU.mult, [gl, 'gate_s5'], ['mix_s5'])

            rr([glu_tile(m) for m in range(4)])

        def gdn_chunk(l, c):
            if c == 0:
                P.memset('pool', raw.t[:, :, 0:3], 0.0, ['raw_rg', 'raw_g0', 'raw_g1', 'raw_g2'])
                P.memset('pool', Sst.t[:], 0.0, [Sst])
                P.memset('pool', Sb.t[:], 0.0, [Sb])
            else:
                P.cp('pool', raw.t[:, :, 0:3], histg.t[:], [histg], ['raw_rg', 'raw_g0', 'raw_g1', 'raw_g2'])
            P.load(wba.t[:], wba_s[l], [wba], R=['wba_s%d' % l])
            def rawk(j):
                return (['raw_rg'] if j < 4 else []) + ['raw_g%d' % (j // 8)]

            def inproj_q(qtr):
                for g in range(4 * qtr, 4 * qtr + 4):
                    wb = stream_w(l, 4 + g)
                    for n in range(2):
                        o, kk = inproj_tile(wb, n * 128)
                        j = g * 2 + n
                        if j < 24:
                            P.cp(evac_engine(), raw.t[:, j, 3:3 + T], o, kk, rawk(j))
                        else:
                            P.act(gate.t[:, j - 24, :], o, AF.Silu, kk, ['gate_rg', 'gate_s5', 'gate_g'])

            def conv_q(qtr):
                P.cp('pool', histg.t[:, 8 * qtr:8 * qtr + 8, :], raw.t[:, 8 * qtr:8 * qtr + 8, T:T + 3], ['raw_rg', 'raw_g%d' % qtr], [histg])
                def conv_norm_tile(j):
                    o, kk = conv_tile(raw, j, gcw, lambda k: k * 24 + j, rawk(j))
                    yield
                    sl, lv, rs_ = tf[(j % 2) * 3], tf[(j % 2) * 3 + 1], tf[(j % 2) * 3 + 2]
                    P.act(lv.t[:], o, AF.Exp, kk, [lv], scale=-1.0)
                    yield
                    P.act(lv.t[:], lv.t[:], AF.Ln, [lv], [lv], bias=1.0)
                    yield
                    P.act(lv.t[:], lv.t[:], AF.Exp, [lv], [lv], scale=-1.0)
                    yield
                    if j >= 16:
                        P.tt('dve', vs.t[:, j - 16, :], o, lv.t[:], ALU.mult, kk + [lv], [vs])
                        return
                    sq_ = tb[j % 2]
                    P.tt('dve', sl.t[:], o, lv.t[:], ALU.mult, kk + [lv], [sl])
                    yield
                    P.tt('pool', sq_.t[:], sl.t[:], sl.t[:], ALU.mult, [sl], [sq_])
                    yield
                    o2, k2 = ps(4 + (j % 2), 0, T)
                    P.mm(o2, onesb.t[:], sq_.t[:], True, True, [onesb, sq_], k2)
                    yield
                    P.act(lv.t[:], o2, AF.Ln, k2, [lv], bias=EPS)
                    yield
                    if j < 8:
                        P.act(rs_.t[:], lv.t[:], AF.Exp, [lv], [rs_], scale=-0.5, bias=-0.5 * math.log(128.0))
                        yield
                        P.tt('dve', qn.t[:, j, :], sl.t[:], rs_.t[:], ALU.mult, [sl, rs_], [qn])
                    else:
                        P.act(rs_.t[:], lv.t[:], AF.Exp, [lv], [rs_], scale=-0.5)
                        yield
                        P.tt('dve', kn.t[:, j - 8, :], sl.t[:], rs_.t[:], ALU.mult, [sl, rs_], [kn])

                for j0 in range(8 * qtr, 8 * qtr + 8, 2):
                    pair = [conv_norm_tile(j0), conv_norm_tile(j0 + 1)]
                    live = [True, True]
                    next(pair[0], None)
                    next(pair[0], None)
                    while any(live):
                        for q_ in (1, 0):
                            if live[q_]:
                                try:
                                    next(pair[q_])
                                except StopIteration:
                                    live[q_] = False
            inproj_q(0)
            inproj_q(1)
            conv_q(0)
            inproj_q(2)
            conv_q(1)
            inproj_q(3)
            conv_q(2)
            if gdn_stop < 7 and c == 0 and l == 0:
                P.memset('pool', mixT.t[:, 4:12, :], 0.0, ['mix_g'])
            if gdn_stop >= 1:
                gdn_scalars(l, c)
            gens = [gdn_inner(l, c, gci) for gci in range(T // GC)]

            def run_until(gen, tags):
                while True:
                    try:
                        t_ = next(gen)
                    except StopIteration:
                        return
                    if t_ in tags:
                        return

            def rr_until(g1, tags1, g2, tags2):
                d1 = d2 = False
                while not (d1 and d2):
                    if not d2:
                        try:
                            d2 = next(g2) in tags2
                        except StopIteration:
                            d2 = True
                    if not d1:
                        try:
                            d1 = next(g1) in tags1
                        except StopIteration:
                            d1 = True

            n_i = T // GC
            run_until(gens[0], ('AB',))
            for gci in range(n_i):
                if gci + 1 < n_i:
                    rr_until(gens[gci], ('C',), gens[gci + 1], ('A_done',))
                    rr_until(gens[gci + 1], ('AB',), gens[gci], ())
                else:
                    run_until(gens[gci], ('C',))
                    run_until(gens[gci], ())

        def gdn_scalars(l, c):
            g = gsm
            v3 = lambda ap: ap.rearrange("p (i h) -> p i h", i=NI)
            o, kk = ps(0, 0, NI * 16, 0, 64)
            for gci in range(NI):
                for k in range(8):
                    P.mm(o[:, gci * 16:(gci + 1) * 16], hnT.t[:, k, gci * GC:(gci + 1) * GC], wba.t[:, k, :], k == 0, k == 7, [hnT, wba], kk)
            ov = o.rearrange("p (i c) -> p i c", i=NI)
            P.act(v3(g['bt'].t[:]), ov[:, :, 0:8], AF.Sigmoid, kk, [g['bt']])
            P.tt('dve', v3(g['xa'].t[:]), ov[:, :, 8:16], bc(dtb.t[:].unsqueeze(1), [64, NI, 8]), ALU.add, kk + [dtb], [g['xa']])
            P.act(g['ex'].t[:], g['xa'].t[:], AF.Exp, [g['xa']], [g['ex']])
            P.act(g['sp'].t[:], g['ex'].t[:], AF.Ln, [g['ex']], [g['sp']], bias=1.0)
            P.tt('dve', v3(g['gp'].t[:]), v3(g['sp'].t[:]), bc(negA.t[:].unsqueeze(1), [64, NI, 8]), ALU.mult, [g['sp'], negA], [g['gp']])
            P.ts('dve', g['g'].t[:], g['gp'].t[:], -1.0, ALU.mult, [g['gp']], [g['g']])
            P.ts('pool', g['nbt'].t[:], g['bt'].t[:], -1.0, ALU.mult, [g['bt']], [g['nbt']])
            oc, kc = ps(0, 64, NI * 8, 0, 64)
            P.mm(oc, U64.t[:], g['g'].t[:], True, True, [U64, g['g']], kc)
            P.cp('dve', g['gcum'].t[:], oc, kc, [g['gcum']])
            ol, kl = ps(0, 128, NI * 8)
            P.mm(ol, sel63.t[:], g['gcum'].t[:], True, True, [sel63, g['gcum']], kl)
            P.act(egl.t[:], ol, AF.Exp, kl, [egl])
            P.act(g['egc'].t[:], g['gcum'].t[:], AF.Exp, [g['gcum']], [g['egc']])
            P.tt('dve', g['dgl'].t[:], ol[0:64, :], g['gcum'].t[:], ALU.subtract, kl + [g['gcum']], [g['dgl']])
            P.act(g['kds'].t[:], g['dgl'].t[:], AF.Exp, [g['dgl']], [g['kds']])
            P.tt('dve', g['bge'].t[:], g['bt'].t[:], g['egc'].t[:], ALU.mult, [g['bt'], g['egc']], [g['bge']])

        def hb(ap, n):
            return bc(ap.unsqueeze(2), [64, 8, n])

        def m8(ap64):
            return bc(ap64.unsqueeze(1), [64, 8, 64])

        def gdn_inner(l, c, gci):
            t0 = gci * GC
            cs = slice(t0, t0 + GC)
            fl = lambda b_: b_.t[:].rearrange("p h j -> p (h j)")
            aqkT = aqkTs[gci % 2]
            if gdn_stop < 1:
                return
            g = {n: (Buf(gsm[n].t[:, gci * 8:(gci + 1) * 8], gsm[n].k) if n not in ('ss', 'ln', 'rs') else Buf(gsm[n].t[:], gsm[n].k)) for n in gsm}
            eglv = egl.t[:, gci * 8:(gci + 1) * 8]
            if gdn_stop < 2:
                return
            NGU, GBC, E1, E2, DK = gw[0], gw[1], gw[2], gw[3], gw[4]
            P.tt('dve', NGU.t[:], m8(U64.t[:]), hb(g['gp'].t, 64), ALU.mult, [U64, g['gp']], [NGU])
            P.cp('pool', GBC.t[:], hb(g['g'].t, 64), [g['g']], [GBC])
            oD, kD = ps(1, 0, 512, 0, 64)
            P.mm(oD, U64.t[:], fl(GBC), True, False, [U64, GBC], kD)
            P.mm(oD, ones64.t[:], fl(NGU), False, True, [ones64, NGU], kD)
            yield 'a'
            P.ts('dve', fl(E1), oD, 0.0, ALU.min, kD, [E1])
            P.ts('dve', fl(E2), oD, -1.0, ALU.mult, kD, [E2], s2=0.0, op1=ALU.min)
            P.act(fl(E1), fl(E1), AF.Exp, [E1], [E1])
            P.act(fl(E2), fl(E2), AF.Exp, [E2], [E2])
            yield 'a'
            P.tt('pool', DK.t[:], m8(Mst.t[:]), hb(g['nbt'].t, 64), ALU.mult, [Mst, g['nbt']], [DK])
            P.tt('pool', DK.t[:], DK.t[:], E1.t[:], ALU.mult, [DK, E1], [DK])
            E2u = E2
            DQ = gw[6]
            P.tt('pool', DQ.t[:], E2u.t[:], m8(U64.t[:]), ALU.mult, [E2u, U64], [DQ])
            yield 'a'
            if gdn_stop < 3:
                return
            Bd = gw[1]
            P.tt('dve', Bd.t[:], m8(ident.t[0:64, 0:64]), hb(g['nbt'].t, 64), ALU.mult, [ident, g['nbt']], [Bd])
            oB, kB = ps(1, 0, 512, 0, 64)
            P.mm(oB, ones64.t[:], fl(Bd), True, True, [ones64, Bd], kB)
            MK = gw[1]
            P.tt('dve', MK.t[:], E2u.t[:], m8(Ust.t[:]), ALU.mult, [E2u, Ust], [MK])
            P.tt('dve', fl(MK), fl(MK), oB, ALU.mult, [MK] + kB, [MK])
            okk, kkk = ps(2, 0, 512, 0, 64)
            oqk, kqk = ps(3, 0, 512, 0, 64)
            for h in range(8):
                P.mm(okk[:, h * 64:(h + 1) * 64], kn.t[:, h, cs], kn.t[:, h, cs], True, True, [kn], kkk)
            for h in range(8):
                P.mm(oqk[:, h * 64:(h + 1) * 64], kn.t[:, h, cs], qn.t[:, h, cs], True, True, [kn, qn], kqk)
            def bfv(b_):
                return Buf(b_.t[:].rearrange("p h j -> p (h j)").bitcast(BF16)[:, 0:512].rearrange("p (h j) -> p h j", h=8), b_.k)
            CH_BF16 = True
            if CH_BF16:
                Nb = [bfv(gw[5]), bfv(gw[6])]
                Mb = [bfv(gw[0]), bfv(gw[1])]
                PTb = [bfv(gw[2]), bfv(gw[3])]
            else:
                Nb = [gw[5], gw[6]]
                Mb = [gw[0], gw[1]]
                PTb = [gw[2], gw[4]]
            P.tt('dve', fl(Nb[0]), okk, fl(DK), ALU.mult, kkk + [DK], [Nb[0]])
            P.tt('dve', fl(aqkT), oqk, fl(DQ), ALU.mult, kqk + [DQ], [aqkT])
            yield 'a'
            if gdn_stop < 4:
                return
            P.tt('dve', fl(Mb[0]), okk, fl(MK), ALU.mult, kkk + [MK], [Mb[0]])
            P.tt('dve', PTb[0].t[:], Mb[0].t[:], m8(ident.t[0:64, 0:64]), ALU.add, [Mb[0], ident], [PTb[0]])
            cur = 0
            for lev in range(1, 6):
                nxt = 1 - cur
                oN, kN = ps(1, 0, 512, 0, 64)
                for h in range(8):
                    P.mm(oN[:, h * 64:(h + 1) * 64], Mb[cur].t[:, h, :], Nb[cur].t[:, h, :], True, True, [Mb[cur], Nb[cur]], kN)
                if lev < 5:
                    oM, kM = ps(2, 0, 512, 0, 64)
                    for h in range(8):
                        P.mm(oM[:, h * 64:(h + 1) * 64], Nb[cur].t[:, h, :], Mb[cur].t[:, h, :], True, True, [Mb[cur], Nb[cur]], kM)
                P.cp('act', fl(Nb[nxt]), oN, kN, [Nb[nxt]])
                if lev < 5:
                    P.cp('dve', fl(Mb[nxt]), oM, kM, [Mb[nxt]])
                oP, kP = ps(3, 0, 512, 0, 64)
                for h in range(8):
                    P.mm(oP[:, h * 64:(h + 1) * 64], Nb[nxt].t[:, h, :], PTb[cur].t[:, h, :], True, True, [Nb[nxt], PTb[cur]], kP)
                P.tt('dve', fl(PTb[nxt]), oP, fl(PTb[cur]), ALU.add, kP + [PTb[cur]], [PTb[nxt]])
                cur = nxt
                yield 'a'
            PT = PTb[cur]
            yield 'A_done'
            if gdn_stop < 5:
                return
            okt, kkt = psbf(4, 0, 1024, 0, 64)
            ovt, kvt = psbf(5, 0, 1024, 0, 64)
            for h in range(8):
                P.tr(okt[:, h * 128:(h + 1) * 128], kn.t[:, h, cs], identb.t[:], [kn, identb], kkt)
            for h in range(8):
                P.tr(ovt[:, h * 128:(h + 1) * 128], vs.t[:, h, cs], identb.t[:], [vs, identb], kvt)
            yield 'b'
            kbg, vb, usb, ob = gx[0], gx[1], gx[2], gx[3]
            if CH_BF16:
                bx = lambda b_: Buf(b_.t[:].rearrange("p h e -> p (h e)").bitcast(BF16)[:, 0:1024].rearrange("p (h e) -> p h e", h=8), b_.k)
                kbg, vb = bx(gx[0]), bx(gx[1])
            fx = lambda b_: b_.t[:].rearrange("p h e -> p (h e)")
            v3 = lambda ap: ap.rearrange("p (h e) -> p h e", h=8)
            P.tt('dve', kbg.t[:], v3(okt), hb(g['bge'].t, 128), ALU.mult, kkt + [g['bge']], [kbg])
            P.tt('dve', kdec.t[:], v3(okt), hb(g['kds'].t, 128), ALU.mult, kkt + [g['kds']], [kdec])
            P.tt('dve', vb.t[:], v3(ovt), hb(g['bt'].t, 128), ALU.mult, kvt + [g['bt']], [vb])
            yield 'b'
            ou, ku = ps(6, 0, 1024, 0, 64)
            for h in range(8):
                P.mm(ou[:, h * 128:(h + 1) * 128], PT.t[:, h, :], vb.t[:, h, :], True, True, [PT, vb], ku)
            ow, kw = ps(0, 0, 512)
            for h in range(8):
                P.mm(ow[:, h * 64:(h + 1) * 64], kbg.t[:, h, :], PT.t[:, h, :], True, True, [PT, kbg], kw)
            yield 'b'
            P.cp('act', fx(usb), ou, ku, [usb])
            P.cp('act', wTb.t[:].rearrange("p h c -> p (h c)"), ow, kw, [wTb])
            if gdn_stop < 6:
                return
            yield 'AB'
            o1, k1 = ps(4, 0, 1024, 0, 64)
            for h in range(8):
                P.mm(o1[:, h * 128:(h + 1) * 128], wTb.t[:, h, :], Sb.t[:, h, :], True, True, [wTb, Sb], k1)
            o2, k2 = ps(6, 0, 1024, 0, 64)
            for h in range(8):
                P.mm(o2[:, h * 128:(h + 1) * 128], qn.t[:, h, cs], Sb.t[:, h, :], True, True, [qn, Sb], k2)
            yield 'c'
            P.tt('dve', fx(vnew), fx(usb), o1, ALU.subtract, [usb] + k1, [vnew])
            P.tt('dve', ob.t[:], v3(o2), hb(g['egc'].t, 128), ALU.mult, k2 + [g['egc']], [ob])
            yield 'c'
            o3, k3 = ps(4, 0, 1024, 0, 64)
            for h in range(8):
                P.mm(o3[:, h * 128:(h + 1) * 128], aqkT.t[:, h, :], vnew.t[:, h, :], True, True, [aqkT, vnew], k3)
            o4, k4 = ps(6, 0, 1024)
            for h in range(8):
                P.mm(o4[:, h * 128:(h + 1) * 128], kdec.t[:, h, :], vnew.t[:, h, :], True, True, [kdec, vnew], k4)
            yield 'c'
            P.tt('dve', fx(ob), fx(ob), o3, ALU.add, [ob] + k3, [ob])
            for h in range(8):
                P.stt(Sst.t[:, h, :], Sst.t[:, h, :], eglv[:, h:h + 1], o4[:, h * 128:(h + 1) * 128], ALU.mult, ALU.add,
                      [Sst, egl] + k4, [Sst])
            P.cp('pool', Sb.t[:], Sst.t[:], [Sst], [Sb])
            if gdn_stop < 7:
                return
            yield 'C'
            bview = lambda b_: Buf(b_.t[:].rearrange("p h j -> p (h j)").bitcast(BF16).rearrange("p (h e) -> p h e", h=8), b_.k)
            osq = bview(gw[1])
            P.tt('pool', osq.t, ob.t[:], ob.t[:], ALU.mult, [ob], [osq])
            yield 'd'
            P.op('dve', lambda e: e.tensor_reduce(out=g['ss'].t, in_=osq.t, axis=AX.X, op=ALU.add), [osq], [g['ss']])
            yield 'd'
            P.act(g['ln'].t, g['ss'].t, AF.Ln, [g['ss']], [g['ln']], scale=1.0 / 128.0, bias=EPS)
            yield 'd'
            P.act(g['rs'].t, g['ln'].t, AF.Exp, [g['ln']], [g['rs']], scale=-0.5)
            yield 'd'
            on = bview(gw[0])
            P.tt('dve', on.t, ob.t[:], hb(g['rs'].t, 128), ALU.mult, [ob, g['rs']], [on])
            yield 'd'
            oo, ko = psbf(1, 0, 512)
            for h in range(8):
                P.tr(oo[:, h * 64:(h + 1) * 64], on.t[:, h, :], identb.t[0:64, 0:64], [on, identb], ko)
            P.stt(mixT.t[:, 4:12, cs], oo.rearrange("p (h c) -> p h c", h=8), colsB.t[:, 40:41], gate.t[:, :, cs],
                  ALU.mult, ALU.mult, ko + [colsB, 'gate_g'], ['mix_g'])

        ocount = [0]

        def chunk(l, c):
            tok0 = c * T
            last = (l == depth - 1)
            if l == 0:
                for a_ in range(2):
                    P.load(stg[a_].t[:, 0:1024], x_d[tok0 + a_ * 128:tok0 + (a_ + 1) * 128, :], [stg[a_]], 'stg%d' % a_)
                for half in range(2):
                    o, kk = ps(6, 0, 1024)
                    for k4 in range(4):
                        for a_ in range(2):
                            kt_ = 4 * half + k4
                            P.tr(o[:, k4 * 256 + a_ * 128:k4 * 256 + a_ * 128 + 128], stg[a_].t[:, kt_ * 128:(kt_ + 1) * 128], ident.t[:],
                                 [stg[a_], ident], kk)
                    P.cp('act' if half else 'dve', hT.t[:, 4 * half:4 * half + 4, :].rearrange("p k t -> p (k t)"), o, kk, [hT])
            else:
                P.load(hT.t[:], hscr.rearrange("k p t -> p k t")[:, :, tok0:tok0 + T], [hT], 'hT', R=['hscr'])
            rmsnorm(RB['norm_g'], out_bf=hnT)
            if not use_rg and c == 0 and l == 0:
                P.memset('pool', mixT.t[:, 0:4, :], 0.0, ['mix_rg'])
            if not use_s5 and c == 0 and l == 0:
                P.memset('pool', mixT.t[:, 12:16, :], 0.0, ['mix_s5'])
            gr = rg_chunk(l, c) if use_rg else iter(())
            g5 = s5_chunk(l, c) if use_s5 else iter(())
            next(gr, None)
            next(g5, None)
            next(gr, None)
            next(g5, None)
            next(gr, None)
            for _ in gr:
                pass
            for _ in g5:
                pass
            if use_gdn:
                gdn_chunk(l, c)
            elif c == 0 and l == 0:
                P.memset('pool', mixT.t[:, 4:12, :], 0.0, ['mix_g'])
            for m in range(8):
                b_ = ring_next()
                wo = Buf(b_.t[:].rearrange("p (k n) -> p k n", k=16), b_.k)
                P.load(wo.t, wout_s[l][m], [b_], R=['wout_s%d' % l])
                slot = pcount[0] % 4
                pcount[0] += 1
                o, kk = ps(slot, 0, T)
                for k in range(16):
                    mk = 'mix_rg' if k < 4 else ('mix_g' if k < 12 else 'mix_s5')
                    P.mm(o, wo.t[:, k, :], mixT.t[:, k, :], k == 0, k == 15, [wo, mk], kk)
                P.tt('dve', hT.t[:, m, :], hT.t[:, m, :], o, ALU.add, [hT] + kk, [hT])
            rmsnorm(RB['ple_norm_g'], out_bf=hnT)
            ptok = stg[1]
            pv = ptok.t[:, 1024:1536].rearrange("p (a d) -> p a d", a=2)
            P.load(pv, p_d[l, tok0:tok0 + T, :].rearrange("(a p) d -> p a d", p=128), [ptok], 'stg1')
            o, kk = ps(6, 0, 512)
            for k in range(2):
                for a in range(2):
                    P.tr(o[:, k * 256 + a * 128:k * 256 + a * 128 + 128], pv[:, a, k * 128:(k + 1) * 128], ident.t[:], [ptok, ident], kk)
            P.cp('act', pT.t[:].rearrange("p k t -> p (k t)"), o, kk, [pT])
            for m in range(8):
                b_ = ring_next()
                wg = Buf(b_.t[:, 0:1024].rearrange("p (k n) -> p k n", k=8), b_.k)
                P.load(wg.t, wgate_s[l][m], [b_], R=['wgate_s%d' % l])
                slot = pcount[0] % 4
                pcount[0] += 1
                o, kk = ps(slot, 0, T)
                for k in range(8):
                    P.mm(o, wg.t[:, k, :], hnT.t[:, k, :], k == 0, k == 7, [wg, hnT], kk)
                gt_ = tf[m % 2]
                P.act(gt_.t[:], o, AF.Sigmoid, kk, [gt_])
                o2, k2 = ps(4 + m % 2, 0, T)
                for k in range(2):
                    P.mm(o2, wproj.t[:, k, m * 128:(m + 1) * 128], pT.t[:, k, :], k == 0, k == 1, [wproj, pT], k2)
                P.tt('dve', gt_.t[:], gt_.t[:], o2, ALU.mult, [gt_] + k2, [gt_])
                P.tt('pool', hT.t[:, m, :], hT.t[:, m, :], gt_.t[:], ALU.add, [hT, gt_], [hT])
            if not last:
                P.store(hscr.rearrange("k p t -> p k t")[:, :, tok0:tok0 + T], hT.t[:], hT, 'hscr')
            else:
                P.act(sqb.t[:], hT.t[:], AF.Square, [hT], [sqb])
                o, kk = ps(7, 0, T)
                for k in range(8):
                    P.mm(o, onesb.t[:], sqb.t[:, k, :], k == 0, k == 7, [onesb, sqb], kk)
                P.act(rstd.t[:], o, AF.Ln, kk, [rstd], scale=1.0 / D, bias=EPS)
                P.act(rstd.t[:], rstd.t[:], AF.Exp, [rstd], [rstd], scale=-0.5)
                hf = stg[0]
                hfv = hf.t[:, 0:1024].rearrange("p (k t) -> p k t", k=4)
                otok = stg[1]
                for half in range(2):
                    for k4 in range(4):
                        kt_ = 4 * half + k4
                        P.stt(hfv[:, k4, :], hT.t[:, kt_, :], colsB.t[:, 57 + kt_:58 + kt_], rstd.t[:], ALU.mult, ALU.mult,
                              [hT, colsB, rstd], [hf])
                    for a_ in range(2):
                        o, kk = ps(6 + a_, 0, 512)
                        for k4 in range(4):
                            P.tr(o[:, k4 * 128:(k4 + 1) * 128], hfv[:, k4, a_ * 128:(a_ + 1) * 128], ident.t[:], [hf, ident], kk)
                        P.cp('act' if a_ else 'dve', otok.t[:, a_ * 512:(a_ + 1) * 512], o, kk, [otok])
                    P.store(out_d[tok0:tok0 + T, half * 512:(half + 1) * 512].rearrange("(a p) d -> p a d", p=128),
                            otok.t[:, 0:1024].rearrange("p (a d) -> p a d", a=2), otok, 'out')

        cnt = [0]
        NPSTG = 2
        pstg = [stg[0], stg[1],
                Buf(hT.t[:].rearrange("p k t -> p (k t)")[:, 0:1536], hT.k),
                Buf(mixT.t[:].rearrange("p a t -> p (a t)").bitcast(F32)[:, 0:1536], ('mix_rg', 'mix_g', 'mix_s5'))]
        pstgb = [stgb[0], stgb[1],
                 Buf(qn.t[:].rearrange("p k t -> p (k t)"), qn.k),
                 Buf(kn.t[:].rearrange("p k t -> p (k t)"), kn.k)]

        def prep_piece(src_ap, ncols, stores):
            i = cnt[0] % NPSTG
            cnt[0] += 1
            sf, sbf = pstg[i], pstgb[i]
            P.load(sf.t[:, 0:ncols], src_ap, [sf])
            P.cp('dve' if cnt[0] % 2 else 'pool', sbf.t[:, 0:ncols], sf.t[:, 0:ncols], [sf], [sbf])
            for (dst, c0, w, key) in stores:
                srcv = sbf.t[:, c0:c0 + w]
                if len(dst.shape) == 3:
                    srcv = srcv.rearrange("p (g j) -> p g j", g=dst.shape[1])
                P.store(dst, srcv, sbf, key)

        for l in range(depth):
            w_in = prm['w_in'][l]
            for r in range(8):
                rows = slice(r * 128, (r + 1) * 128)
                prep_piece(w_in[rows, 0:1024], 1024, [(win_s[l][0:4, :, r, :].rearrange("g p j -> p g j"), 0, 1024, 'win_s%d' % l)])
                prep_piece(w_in[rows, 1024:2560], 1536, [(win_s[l][4:10, :, r, :].rearrange("g p j -> p g j"), 0, 1536, 'win_s%d' % l)])
                prep_piece(w_in[rows, 2560:4096], 1536, [(win_s[l][10:16, :, r, :].rearrange("g p j -> p g j"), 0, 1536, 'win_s%d' % l)])
                prep_piece(w_in[rows, 4096:5120], 1024, [(win_s[l][16:20, :, r, :].rearrange("g p j -> p g j"), 0, 1024, 'win_s%d' % l)])
                prep_piece(w_in[rows, 5120:6160], 1040, [(wba_s[l][:, r, :], 0, 16, 'wba_s%d' % l),
                                                         (win_s[l][20:24, :, r, :].rearrange("g p j -> p g j"), 16, 1024, 'win_s%d' % l)])
            for r in range(16):
                prep_piece(prm['w_out'][l][r * 128:(r + 1) * 128, :], 1024,
                           [(wout_s[l][:, :, r, :].rearrange("m p j -> p m j"), 0, 1024, 'wout_s%d' % l)])
            for r in range(8):
                prep_piece(prm['ple_w_gate'][l][r * 128:(r + 1) * 128, :], 1024,
                           [(wgate_s[l][:, :, r, :].rearrange("m p j -> p m j"), 0, 1024, 'wgate_s%d' % l)])


        for l in range(depth):
            load_layer_consts(l)
            for c in range(NCH):
                chunk(l, c)
        P.final_wait('act', ['out'])
        P.emit()
    return nc


_CACHE = {}


def kernel(**inputs):
    B = inputs['x'].shape[0]
    S = inputs['x'].shape[1]
    depth = inputs['p'].shape[0]
    key = (S, depth)
    if key not in _CACHE:
        _CACHE[key] = build_program(S, depth)
    nc = _CACHE[key]
    shared = {name: np.ascontiguousarray(inputs[name], dtype=np.float32) for name, _ in PARAM_SHAPES(depth)}
    in_maps = []
    for b in range(B):
        m = dict(shared)
        m['x'] = np.ascontiguousarray(inputs['x'][b], dtype=np.float32)
        m['p'] = np.ascontiguousarray(inputs['p'][:, b], dtype=np.float32)
        in_maps.append(m)
    res = run_bass_kernel_spmd(nc, in_maps, core_ids=list(range(B)))
    return np.stack([r['out'] for r in res.results], axis=0).astype(np.float32)
```

```python
import math
import numpy as np
import concourse.bass as bass
import concourse.mybir as mybir
from concourse.bass_utils import run_bass_kernel_spmd
from contextlib import ExitStack

F32 = mybir.dt.float32
BF16 = mybir.dt.bfloat16
I32 = mybir.dt.int32
AF = mybir.ActivationFunctionType
ALU = mybir.AluOpType
AX = mybir.AxisListType

D = 1024
NIN = 6160
DMIX = 2048
DPLE = 256
T = 256
GC = 64
L5 = 8
CB = T // L5
EPS = 1e-6
TWO_PI = 2.0 * math.pi

ENG = ['pe', 'dve', 'act', 'pool', 'sp']

PARAM_SHAPES = lambda L: [
    ('norm_g', [L, 1024]), ('w_in', [L, 1024, 6160]), ('rg_conv_w', [L, 4, 512]), ('rg_conv_b', [L, 512]),
    ('rg_w_a', [L, 8, 64, 64]), ('rg_b_a', [L, 512]), ('rg_w_x', [L, 8, 64, 64]), ('rg_b_x', [L, 512]),
    ('rg_lambda', [L, 512]), ('gdn_conv_w', [L, 4, 3072]), ('gdn_a_log', [L, 8]), ('gdn_dt_bias', [L, 8]),
    ('gdn_norm_g', [L, 128]), ('s5_a_re', [L, 32, 64]), ('s5_a_im', [L, 32, 64]), ('s5_b_re', [L, 32, 64, 16]),
    ('s5_b_im', [L, 32, 64, 16]), ('s5_c_re', [L, 32, 16, 64]), ('s5_c_im', [L, 32, 16, 64]), ('s5_d', [L, 512]),
    ('s5_log_dt', [L, 32]), ('s5_w_glu', [L, 512, 512]), ('s5_b_glu', [L, 512]), ('w_out', [L, 2048, 1024]),
    ('ple_norm_g', [L, 1024]), ('ple_w_gate', [L, 1024, 1024]), ('ple_w_proj', [L, 256, 1024]),
    ('final_norm_g', [1024]),
]


class Buf:
    def __init__(self, t, k):
        self.t = t
        self.k = k


def _keys(lst):
    out = []
    for r in lst:
        if isinstance(r, Buf):
            if isinstance(r.k, (list, tuple)):
                out.extend(r.k)
            else:
                out.append(r.k)
        elif isinstance(r, (list, tuple, set)):
            out.extend(_keys(r))
        else:
            out.append(r)
    return out


class Prog:
    def __init__(self, nc, es, same_engine_sync=True):
        self.nc = nc
        self.es = es
        self.ops = {e: [] for e in ENG}
        self.esem = {e: es.enter_context(nc.semaphore('s_' + e)) for e in ENG}
        self.ecnt = {e: 0 for e in ENG}
        self.waited = {e: {} for e in ENG}
        self.writers = {}
        self.readers = {}
        self.dsems = {}
        self.dcnt = {}
        self.dsem_name = {}
        self.same_engine_sync = same_engine_sync

    def sbuf(self, name, shape, dt):
        return Buf(self.es.enter_context(self.nc.sbuf_tensor(name, list(shape), dt)), name)

    def _deps(self, eng, reads, writes):
        need = {}

        def add(ev):
            s, v = ev
            k = id(s)
            if k not in need or need[k][1] < v:
                need[k] = (s, v)

        for r in reads:
            for ev in self.writers.get(r, {}).values():
                add(ev)
        for w in writes:
            for ev in self.writers.get(w, {}).values():
                add(ev)
            for ev in self.readers.get(w, {}).values():
                add(ev)
        waits = []
        for k, (s, v) in need.items():
            if s is self.esem[eng] and (eng == 'pe' or not self.same_engine_sync):
                continue
            nm = self.dsem_name.get(k)
            if nm is not None and self.dsems[nm] is s:
                v = max(v, self.dcnt[nm])
            if self.waited[eng].get(k, 0) >= v:
                continue
            self.waited[eng][k] = v
            waits.append((s, v))
        return waits

    def _commit(self, ev, reads, writes):
        k = id(ev[0])
        for r in reads:
            d = self.readers.setdefault(r, {})
            if k not in d or d[k][1] < ev[1]:
                d[k] = ev
        for w in writes:
            d = self.writers.setdefault(w, {})
            if k not in d or d[k][1] < ev[1]:
                d[k] = ev
            self.readers[w] = {}

    def op(self, eng, fn, reads=(), writes=()):
        reads = _keys(reads)
        writes = _keys(writes)
        waits = self._deps(eng, reads, writes)
        if self.ecnt[eng] >= 16000:
            self.esem[eng] = self.es.enter_context(self.nc.semaphore('s_%s_%d' % (eng, len(self.ops[eng]))))
            self.ecnt[eng] = 0
        self.ecnt[eng] += 1
        ev = (self.esem[eng], self.ecnt[eng])
        self.ops[eng].append((waits, fn, ev, 1))
        self._commit(ev, reads, writes)

    def dma(self, q, fn, reads, writes, sem_name, dram_w=()):
        reads = _keys(reads)
        writes = _keys(writes)
        waits = self._deps(q, reads, writes)
        writes = writes + _keys(dram_w)
        if sem_name not in self.dsems:
            self.dsems[sem_name] = self.es.enter_context(self.nc.semaphore('d_' + sem_name))
            self.dcnt[sem_name] = 0
            self.dsem_name[id(self.dsems[sem_name])] = sem_name
        if self.dcnt[sem_name] >= 16000:
            self.dsems[sem_name] = self.es.enter_context(
                self.nc.semaphore('d_%s_%d' % (sem_name, len(self.ops[q]))))
            self.dcnt[sem_name] = 0
            self.dsem_name[id(self.dsems[sem_name])] = sem_name
        self.dcnt[sem_name] += 16
        ev = (self.dsems[sem_name], self.dcnt[sem_name])
        self.ops[q].append((waits, fn, ev, 16))
        self._commit(ev, reads, writes)

    def final_wait(self, eng, resources):
        resources = _keys(resources)
        waits = self._deps(eng, resources, resources)
        self.ops[eng].append((waits, None, None, 0))

    def emit(self):
        nc = self.nc
        with nc.Block() as block:
            for e, deco in [('pe', block.tensor), ('dve', block.vector), ('act', block.scalar),
                            ('pool', block.gpsimd), ('sp', block.sync)]:
                ops = self.ops[e]

                @deco
                def _(engine, ops=ops):
                    for waits, fn, ev, inc in ops:
                        for (s, v) in waits:
                            engine.wait_ge(s, v)
                        if fn is None:
                            continue
                        ins = fn(engine)
                        ins.then_inc(ev[0], inc)

    def mm(self, out, lhsT, rhs, start, stop, R, W, **kw):
        self.op('pe', lambda e: e.matmul(out, lhsT=lhsT, rhs=rhs, start=start, stop=stop, **kw), R, W)

    def tr(self, out, in_, ident, R, W):
        self.op('pe', lambda e: e.transpose(out, in_, ident), R, W)

    def act(self, out, in_, func, R, W, scale=None, bias=None):
        kw = {}
        if scale is not None:
            kw['scale'] = scale
        if bias is not None:
            kw['bias'] = bias
        self.op('act', lambda e: e.activation(out=out, in_=in_, func=func, **kw), R, W)

    def tt(self, eng, out, in0, in1, op, R, W):
        self.op(eng, lambda e: e.tensor_tensor(out=out, in0=in0, in1=in1, op=op), R, W)

    def ts(self, eng, out, in0, s1, op0, R, W, s2=None, op1=None):
        if op1 is None:
            if eng == 'pool':
                s2, op1 = (0.0, ALU.add) if op0 == ALU.mult else (1.0, ALU.mult)
                self.op(eng, lambda e: e.tensor_scalar(out=out, in0=in0, scalar1=s1, scalar2=s2, op0=op0, op1=op1), R, W)
            else:
                self.op(eng, lambda e: e.tensor_scalar(out=out, in0=in0, scalar1=s1, scalar2=None, op0=op0), R, W)
        else:
            self.op(eng, lambda e: e.tensor_scalar(out=out, in0=in0, scalar1=s1, scalar2=s2, op0=op0, op1=op1), R, W)

    def stt(self, out, in0, scalar, in1, op0, op1, R, W):
        self.op('dve', lambda e: e.scalar_tensor_tensor(out=out, in0=in0, scalar=scalar, in1=in1, op0=op0, op1=op1), R, W)

    def cp(self, eng, out, in_, R, W):
        if eng == 'act':
            self.op('act', lambda e: e.activation(out=out, in_=in_, func=AF.Copy), R, W)
        else:
            self.op(eng, lambda e: e.tensor_copy(out=out, in_=in_), R, W)

    def memset(self, eng, ap, val, W):
        self.op(eng, lambda e: e.memset(ap, val), [], W)

    def load(self, out, in_, W, sem=None, R=(), q='sp', slow=False):
        sem = 'ld_' + _keys(W)[0]
        if slow:
            self.dma(q, lambda e: e.dma_start(out=out, in_=in_, allow_slow_non_contiguous=True), R, W, sem)
        else:
            self.dma(q, lambda e: e.dma_start(out=out, in_=in_), R, W, sem)

    def store(self, out, in_, src, dram_key, q='act'):
        sem = 'st_' + _keys([src])[0]
        self.dma(q, lambda e: e.dma_start(out=out, in_=in_), [src], [], sem, dram_w=[dram_key])


def bc(ap, shape):
    return ap.broadcast_to(list(shape))


def build_program(S, depth, use_rg=True, use_gdn=True, use_s5=True, gdn_stop=99, same_engine_sync=True):
    nc = bass.Bass("TRN2", target_bir_lowering=False)
    NCH = S // T
    assert S % T == 0

    def din(name, shape):
        return nc.dram_tensor(name, list(shape), F32, kind="ExternalInput").ap()

    x_d = din("x", [S, D])
    p_d = din("p", [depth, S, DPLE])
    prm = {name: din(name, shape) for name, shape in PARAM_SHAPES(depth)}
    out_d = nc.dram_tensor("out", [S, D], F32, kind="ExternalOutput").ap()
    win_s = [nc.dram_tensor("win_s%d" % l, [24, 128, 8, 256], BF16, kind="Internal").ap() for l in range(depth)]
    wba_s = [nc.dram_tensor("wba_s%d" % l, [128, 8, 16], BF16, kind="Internal").ap() for l in range(depth)]
    wout_s = [nc.dram_tensor("wout_s%d" % l, [8, 128, 16, 128], BF16, kind="Internal").ap() for l in range(depth)]
    wgate_s = [nc.dram_tensor("wgate_s%d" % l, [8, 128, 8, 128], BF16, kind="Internal").ap() for l in range(depth)]
    hscr = nc.dram_tensor("hscr", [8, 128, S], F32, kind="Internal").ap()

    with ExitStack() as es:
        P = Prog(nc, es, same_engine_sync=same_engine_sync)
        sb = P.sbuf

        psd = [es.enter_context(nc.psum_tensor("psd%d" % i, [128, 1024], F32)) for i in range(4)]

        def ps(bank, c0=0, w=512, p0=0, p1=128):
            base = (bank % 2) * 512 + c0
            ap = psd[bank // 2][p0:p1, base:base + w]
            keys = ['psb%d' % b for b in range(bank + c0 // 512, bank + (c0 + w - 1) // 512 + 1)]
            return ap, keys

        def psbf(bank, c0, w, p0=0, p1=128):
            t = psd[bank // 2][p0:p1, (bank % 2) * 512:(bank % 2) * 512 + 512].bitcast(BF16)
            return t[:, c0:c0 + w], ['psb%d' % bank]

        ident = sb('ident', [128, 128], F32)
        identb = sb('identb', [128, 128], BF16)
        onesb = sb('onesb', [128, 128], BF16)
        U64 = sb('U64', [64, 64], F32)
        Mst = sb('Mst', [64, 64], F32)
        ones64 = sb('ones64', [64, 64], F32)
        sel63 = sb('sel63', [64, 128], F32)
        P.memset('pool', ident.t[:], 1.0, [ident])
        P.op('pool', lambda e: e.affine_select(out=ident.t[:], in_=ident.t[:], pattern=[[-1, 128]], compare_op=ALU.is_equal,
                                               fill=0.0, base=0, channel_multiplier=1), [ident], [ident])
        P.cp('pool', identb.t[:], ident.t[:], [ident], [identb])
        P.memset('pool', onesb.t[:], 1.0, [onesb])
        P.memset('pool', U64.t[:], 1.0, [U64])
        P.op('pool', lambda e: e.affine_select(out=U64.t[:], in_=U64.t[:], pattern=[[1, 64]], compare_op=ALU.is_ge,
                                               fill=0.0, base=0, channel_multiplier=-1), [U64], [U64])
        P.memset('pool', Mst.t[:], 1.0, [Mst])
        P.op('pool', lambda e: e.affine_select(out=Mst.t[:], in_=Mst.t[:], pattern=[[-1, 64]], compare_op=ALU.is_gt,
                                               fill=0.0, base=0, channel_multiplier=1), [Mst], [Mst])
        Ust = sb('Ust', [64, 64], F32)
        P.memset('pool', Ust.t[:], 1.0, [Ust])
        P.op('pool', lambda e: e.affine_select(out=Ust.t[:], in_=Ust.t[:], pattern=[[1, 64]], compare_op=ALU.is_gt,
                                               fill=0.0, base=0, channel_multiplier=-1), [Ust], [Ust])
        P.memset('pool', ones64.t[:], 1.0, [ones64])
        P.memset('pool', sel63.t[:], 1.0, [sel63])
        P.op('pool', lambda e: e.affine_select(out=sel63.t[:], in_=sel63.t[:], pattern=[[0, 128]], compare_op=ALU.is_equal,
                                               fill=0.0, base=-63, channel_multiplier=1), [sel63], [sel63])

        stg = [sb('stg%d' % i, [128, 1536], F32) for i in range(2)]
        G = [sb('G%d' % i, [128, 1024], F32) for i in range(4)]
        stgb = [Buf(G[i].t[:].bitcast(BF16), G[i].k) for i in range(2)]

        colsB = sb('colsB', [128, 72], F32)
        gcw = sb('gcw', [128, 96], F32)
        rgc = sb('rgc', [128, 16], F32)
        rgWa = sb('rgWa', [128, 4, 128], BF16)
        rgWx = sb('rgWx', [128, 4, 128], BF16)
        wglu = sb('wglu', [128, 4, 512], BF16)
        wproj = sb('wproj', [128, 2, 1024], BF16)
        negA = sb('negA', [64, 8], F32)
        dtb = sb('dtb', [64, 8], F32)
        RB = dict(norm_g=0, ple_norm_g=8, rg_conv_b=16, rg_b_a=20, rg_b_x=24, rg_lambda=28, s5_d=32, s5_b_glu=36,
                  gdn_norm_g=40, rg_conv_w=41, final=57)

        KT = sb('KT', [128, 4, 8, 128], BF16)
        EB = sb('EB', [128, 4, 8, 2, 128], BF16)
        Ctab = sb('Ctab', [128, 8, 2, 16, 32], BF16)
        Dcos = sb('Dcos', [128, 16, CB], F32)
        Dsin = sb('Dsin', [128, 16, CB], F32)
        Rho = sb('Rho', [128, 16, CB], F32)
        rhoc = sb('rhoc', [128, 16], F32)

        hT = sb('hT', [128, 8, T], F32)
        hnT = sb('hnT', [128, 8, T], BF16)
        sqb = Buf(G[3].t[:].bitcast(BF16).rearrange("p (k t) -> p k t", k=8), G[3].k)
        rstd = sb('rstd', [128, T], F32)
        mixT = sb('mixT', [128, 16, T], BF16)
        raw = sb('raw', [128, 24, 3 + T], BF16)
        histrg = sb('histrg', [128, 4, 3], BF16)
        histg = sb('histg', [128, 24, 3], BF16)
        gate = sb('gate', [128, 8, T], BF16)
        u5 = sb('u5', [128, 4, T], BF16)
        NRING = 5
        wring = [sb('wring%d' % i, [128, 2048], BF16) for i in range(NRING)]
        wba = sb('wba', [128, 8, 16], BF16)
        dg = [sb('dg%d' % i, [128, 128], BF16) for i in range(4)]
        tf = [sb('tf%d' % i, [128, T], F32) for i in range(8)]
        tb = [sb('tb%d' % i, [128, T], BF16) for i in range(2)]
        rgcar = sb('rgcar', [128, 4], F32)
        pT = sb('pT', [128, 2, T], BF16)
        XPre = sb('XPre', [128, 16, CB + 1], F32)
        XPim = sb('XPim', [128, 16, CB + 1], F32)
        XPb = sb('XPb', [128, 2, 16, CB], BF16)
        z5b = sb('z5b', [128, 4, T], BF16)
        s5big = [Buf(G[i // 2].t[:, (i % 2) * 512:(i % 2) * 512 + 512], G[i // 2].k) for i in range(6)]
        qn = sb('qn', [128, 8, T], BF16)
        kn = sb('kn', [128, 8, T], BF16)
        vs = sb('vs', [128, 8, T], BF16)
        Sst = sb('Sst', [128, 8, 128], F32)
        Sb = sb('Sb', [128, 8, 128], BF16)
        NI = T // GC
        gsm = {n: sb('g_' + n, [64, NI * 8], F32) for n in ['bt', 'nbt', 'xa', 'ex', 'sp', 'gp', 'g', 'gcum', 'egc', 'dgl', 'kds', 'bge']}
        gsm.update({n: sb('g_' + n, [64, 8], F32) for n in ['ss', 'ln', 'rs']})
        egl = sb('egl', [128, NI * 8], F32)
        gw = [sb('gw%d' % i, [64, 8, 64], F32) for i in range(7)]
        aqkTs = [sb('aqkT%d' % i, [64, 8, 64], BF16) for i in range(2)]
        gx = [Buf(G[i].t[0:64, :].rearrange("p (h e) -> p h e", h=8), G[i].k) for i in range(4)]
        kdec = sb('kdec', [64, 8, 128], BF16)
        vnew = sb('vnew', [64, 8, 128], BF16)
        wTb = sb('wTb', [128, 8, 64], BF16)

        s5t = {n: sb('s5_' + n, [128, 16], F32) for n in
               ['are', 'aim', 'ldt', 'dt', 'ard', 'th', 'cr', 'den', 'inv', 'cfr', 'cfi', 't0', 't1', 'tqb']}
        hTf = hT.t[:].rearrange("p k t -> p (k t)")
        Lre = Buf(hTf[:, 0:144].rearrange("p (j i) -> p j i", j=9), hT.k)
        Lim = Buf(hTf[:, 256:400].rearrange("p (j i) -> p j i", j=9), hT.k)
        jv = Buf(hTf[:, 512:640].rearrange("p (j i) -> p j i", j=8), hT.k)
        jvi = Buf(G[2].t[:, 0:128].bitcast(I32).rearrange("p (j i) -> p j i", j=8), G[2].k)
        mask2 = sb('mask2', [128, 2, 16], F32)
        s5int = Buf(G[3].t[:, 0:512].bitcast(I32), G[3].k)
        cvi = Buf(G[3].t[:, 512:1024].bitcast(I32).rearrange("p (i c) -> p i c", i=16), G[3].k)
        knf = kn.t[:].rearrange("p k t -> p (k t)").bitcast(F32)
        qnf = qn.t[:].rearrange("p k t -> p (k t)").bitcast(F32)
        vsf = vs.t[:].rearrange("p k t -> p (k t)").bitcast(F32)
        hnf = hnT.t[:].rearrange("p k t -> p (k t)").bitcast(F32)
        v3_ = lambda ap, a: ap.rearrange("p (a b) -> p a b", a=a)
        bre = Buf(v3_(vsf[:, 0:256], 16), vs.k)
        bim = Buf(v3_(vsf[:, 256:512], 16), vs.k)
        cre = Buf(v3_(vsf[:, 512:768], 16), vs.k)
        cim = Buf(v3_(vsf[:, 768:1024], 16), vs.k)
        Bre = Buf(v3_(qnf[:, 0:256], 16), qn.k)
        Bim = Buf(v3_(qnf[:, 256:512], 16), qn.k)
        Cbr = Buf(v3_(qnf[:, 512:1024], 16), qn.k)
        Cbi = Buf(v3_(knf[:, 0:512], 16), kn.k)
        cv = Buf(v3_(knf[:, 512:1024], 16), kn.k)
        maskW = Buf(hnf[:, 0:512].rearrange("p (a g c) -> p a g c", a=4, g=8), hnT.k)
        P.memset('pool', mask2.t[:], 0.0, [mask2])
        P.memset('pool', mask2.t[0:64, 0, :], 1.0, [mask2])
        P.memset('pool', mask2.t[64:128, 1, :], 1.0, [mask2])

        def load_layer_consts(l):
            sA = stg[0]
            P.load(sA.t[0:96, 0:128], prm['gdn_conv_w'][l].rearrange("k (j p) -> (k j) p", p=128), [sA], 'stg0')
            o, kk = ps(7, 0, 96)
            P.tr(o, sA.t[0:96, 0:128], ident.t[0:96, 0:96], [sA, ident], kk)
            P.cp('dve', gcw.t[:], o, kk, [gcw])
            sBt = stg[1]
            rows = [('norm_g', prm['norm_g'][l], 8), ('ple_norm_g', prm['ple_norm_g'][l], 8), ('rg_conv_b', prm['rg_conv_b'][l], 4),
                    ('rg_b_a', prm['rg_b_a'][l], 4), ('rg_b_x', prm['rg_b_x'][l], 4), ('rg_lambda', prm['rg_lambda'][l], 4),
                    ('s5_d', prm['s5_d'][l], 4), ('s5_b_glu', prm['s5_b_glu'][l], 4), ('gdn_norm_g', prm['gdn_norm_g'][l], 1)]
            for name, ap, n in rows:
                r0 = RB[name]
                P.load(sBt.t[r0:r0 + n, 0:128], ap.rearrange("(k p) -> k p", p=128), [sBt], 'stg1')
            P.load(sBt.t[41:57, 0:128], prm['rg_conv_w'][l].rearrange("k (j p) -> (k j) p", p=128), [sBt], 'stg1')
            P.load(sBt.t[57:65, 0:128], prm['final_norm_g'].rearrange("(k p) -> k p", p=128), [sBt], 'stg1')
            o, kk = ps(7, 128, 65)
            P.tr(o, sBt.t[0:65, 0:128], ident.t[0:65, 0:65], [sBt, ident], kk)
            P.cp('dve', colsB.t[:, 0:65], o, kk, [colsB])
            z = s5t['t0'].t[:, 0:4]
            acc = s5t['t1'].t[:, 0:4]
            P.act(z, colsB.t[:, 28:32], AF.Exp, [colsB], [s5t['t0']], scale=-1.0)
            P.ts('dve', acc, z, -1.0 / 9.0, ALU.mult, [s5t['t0']], [s5t['t1']], s2=1.0 / 8.0, op1=ALU.add)
            for k in range(7, 0, -1):
                P.tt('dve', acc, acc, z, ALU.mult, [s5t['t0'], s5t['t1']], [s5t['t1']])
                P.ts('dve', acc, acc, -1.0, ALU.mult, [s5t['t1']], [s5t['t1']], s2=1.0 / k, op1=ALU.add)
            P.tt('dve', acc, acc, z, ALU.mult, [s5t['t0'], s5t['t1']], [s5t['t1']])
            P.ts('dve', rgc.t[:, 0:4], acc, -8.0, ALU.mult, [s5t['t1']], [rgc])
            P.ts('dve', rgc.t[:, 4:8], acc, -16.0, ALU.mult, [s5t['t1']], [rgc])
            P.ts('dve', rgc.t[:, 8:16], colsB.t[:, 20:28], -1.0, ALU.mult, [colsB], [rgc])
            for (src, dst) in [(prm['rg_w_a'][l], rgWa), (prm['rg_w_x'][l], rgWx)]:
                s = stg[0]
                P.memset('pool', s.t[:, 0:512], 0.0, [s])
                sv = s.t[:, 0:512].rearrange("p (t j) -> p t j", t=4)
                for h2 in range(2):
                    P.load(sv[h2 * 64:(h2 + 1) * 64, :, h2 * 64:(h2 + 1) * 64],
                           src.rearrange("(t h2) i j -> h2 i t j", h2=2)[h2], [s], 'stg0')
                P.cp('pool', dst.t[:], sv, [s], [dst])
            for hh in range(2):
                s = stg[hh]
                P.load(s.t[:, 0:1024].rearrange("p (k n) -> p k n", k=2),
                       prm['s5_w_glu'][l].rearrange("(k p) n -> p k n", p=128)[:, 2 * hh:2 * hh + 2, :], [s], 'stg%d' % hh)
                P.cp('dve', wglu.t[:, 2 * hh:2 * hh + 2, :], s.t[:, 0:1024].rearrange("p (k n) -> p k n", k=2), [s], [wglu])
            for hh in range(2):
                s = stg[hh]
                P.load(s.t[:, 0:1024], prm['ple_w_proj'][l][hh * 128:(hh + 1) * 128, :], [s], 'stg%d' % hh)
                P.cp('pool', wproj.t[:, hh, :], s.t[:, 0:1024], [s], [wproj])
            P.load(gsm['xa'].t[:, 0:8], prm['gdn_a_log'][l].partition_broadcast(64), [gsm['xa']], 'gsm')
            P.act(negA.t[:], gsm['xa'].t[:, 0:8], AF.Exp, [gsm['xa']], [negA])
            P.load(dtb.t[:], prm['gdn_dt_bias'][l].partition_broadcast(64), [dtb], 'gsm')
            if use_s5:
                s5_setup(l)

        def sincos(tq, n, out_sin, out_cos, Rk, Wk_sin, Wk_cos):
            ti = s5int.t[:, 0:n]
            tfl = s5big[4].t[:, 0:n]
            fr = s5big[5].t[:, 0:n]
            for (shift, out, Wk) in [(0.0, out_sin, Wk_sin), (0.25, out_cos, Wk_cos)]:
                src = tq
                if shift != 0.0:
                    P.ts('dve', fr, tq, shift, ALU.add, Rk, [s5big[5]])
                    src = fr
                    R2 = [s5big[5]]
                else:
                    R2 = Rk
                P.cp('dve', ti, src, R2, [s5int])
                P.cp('dve', tfl, ti, [s5int], [s5big[4]])
                P.tt('dve', fr, src, tfl, ALU.subtract, R2 + [s5big[4]], [s5big[5]])
                P.act(out, fr, AF.Sin, [s5big[5]], Wk, scale=TWO_PI)

        def s5_setup(l):
            t = s5t
            P.op('pool', lambda e: e.iota(cvi.t[:], pattern=[[0, 16], [1, CB]], base=1, channel_multiplier=0), [], [cvi])
            P.cp('pool', cv.t[:], cvi.t[:], [cvi], [cv])
            P.op('pool', lambda e: e.iota(jvi.t, pattern=[[1, 8], [0, 16]], base=1, channel_multiplier=0), [], [jvi])
            P.cp('pool', jv.t, jvi.t, [jvi], [jv])
            P.memset('pool', maskW.t[:], 0.0, [maskW])
            for jj in range(4):
                P.memset('pool', maskW.t[0:64, jj, 2 * jj, :], 1.0, [maskW])
                P.memset('pool', maskW.t[64:128, jj, 2 * jj + 1, :], 1.0, [maskW])
            for name, src in [('are', prm['s5_a_re'][l]), ('aim', prm['s5_a_im'][l])]:
                for g2 in range(2):
                    P.load(t[name].t[g2 * 64:(g2 + 1) * 64, :], src.rearrange("(i g2) n -> g2 n i", g2=2)[g2], [t[name]], 's5ld', slow=True)
            for g2 in range(2):
                P.load(t['ldt'].t[g2 * 64:(g2 + 1) * 64, :], prm['s5_log_dt'][l].rearrange("(i g2) -> g2 i", g2=2)[g2].partition_broadcast(64),
                       [t['ldt']], 's5ld', slow=True)
            for (dst, src) in [(bre, prm['s5_b_re'][l]), (bim, prm['s5_b_im'][l])]:
                for g2 in range(2):
                    P.load(dst.t[g2 * 64:(g2 + 1) * 64, :, :], src.rearrange("(i g2) n c -> g2 n i c", g2=2)[g2], [dst], 's5ld')
            for (dst, src) in [(cre, prm['s5_c_re'][l]), (cim, prm['s5_c_im'][l])]:
                for g2 in range(2):
                    for i_ in range(16):
                        P.load(dst.t[g2 * 64:(g2 + 1) * 64, i_, :],
                               src.rearrange("(i g2) c n -> g2 i n c", g2=2)[g2, i_], [dst], 's5ld', slow=True)
            P.act(t['dt'].t[:], t['ldt'].t[:], AF.Exp, [t['ldt']], [t['dt']])
            P.tt('dve', t['ard'].t[:], t['are'].t[:], t['dt'].t[:], ALU.mult, [t['are'], t['dt']], [t['ard']])
            P.tt('dve', t['th'].t[:], t['aim'].t[:], t['dt'].t[:], ALU.mult, [t['aim'], t['dt']], [t['th']])
            A0 = s5big[0].t[:, 0:128].rearrange("p (j i) -> p j i", j=8)
            A1 = s5big[1].t[:, 0:128].rearrange("p (j i) -> p j i", j=8)
            A2 = s5big[2].t[:, 0:128].rearrange("p (j i) -> p j i", j=8)
            A3 = s5big[3].t[:, 0:128].rearrange("p (j i) -> p j i", j=8)
            P.tt('dve', A0, jv.t, bc(t['ard'].t[:].unsqueeze(1), [128, 8, 16]), ALU.mult, [jv, t['ard']], [s5big[0]])
            P.act(A0, A0, AF.Exp, [s5big[0]], [s5big[0]])
            P.tt('dve', A1, jv.t, bc(t['th'].t[:].unsqueeze(1), [128, 8, 16]), ALU.mult, [jv, t['th']], [s5big[1]])
            P.ts('dve', A1, A1, 1.0 / TWO_PI, ALU.mult, [s5big[1]], [s5big[1]])
            sincos(s5big[1].t[:, 0:128], 128, s5big[2].t[:, 0:128], s5big[3].t[:, 0:128], [s5big[1]], [s5big[2]], [s5big[3]])
            P.memset('dve', Lre.t[:, 0, :], 1.0, [Lre])
            P.memset('dve', Lim.t[:, 0, :], 0.0, [Lim])
            P.tt('dve', Lre.t[:, 1:9, :], A0, A3, ALU.mult, [s5big[0], s5big[3]], [Lre])
            P.tt('dve', Lim.t[:, 1:9, :], A0, A2, ALU.mult, [s5big[0], s5big[2]], [Lim])
            P.ts('dve', t['tqb'].t[:], t['th'].t[:], float(L5) / TWO_PI, ALU.mult, [t['th']], [t['tqb']])
            B0 = s5big[0].t[:, 0:16 * CB].rearrange("p (i c) -> p i c", i=16)
            P.tt('dve', B0, cv.t[:], bc(t['tqb'].t[:].unsqueeze(2), [128, 16, CB]), ALU.mult, [cv, t['tqb']], [s5big[0]])
            sincos(s5big[0].t[:, 0:16 * CB], 16 * CB, Dsin.t[:].rearrange("p i c -> p (i c)"), Dcos.t[:].rearrange("p i c -> p (i c)"),
                   [s5big[0]], [Dsin], [Dcos])
            P.act(rhoc.t[:], t['ard'].t[:], AF.Exp, [t['ard']], [rhoc], scale=float(L5))
            P.cp('dve', Rho.t[:], bc(rhoc.t[:].unsqueeze(2), [128, 16, CB]), [rhoc], [Rho])
            P.memset('dve', Rho.t[:, :, 0:1], 0.0, [Rho])
            P.ts('dve', t['cr'].t[:], Lre.t[:, 1, :], -1.0, ALU.add, [Lre], [t['cr']])
            P.tt('dve', t['den'].t[:], t['are'].t[:], t['are'].t[:], ALU.mult, [t['are']], [t['den']])
            P.tt('dve', t['t0'].t[:], t['aim'].t[:], t['aim'].t[:], ALU.mult, [t['aim']], [t['t0']])
            P.tt('dve', t['den'].t[:], t['den'].t[:], t['t0'].t[:], ALU.add, [t['den'], t['t0']], [t['den']])
            P.op('dve', lambda e: e.reciprocal(out=t['inv'].t[:], in_=t['den'].t[:]), [t['den']], [t['inv']])
            P.tt('dve', t['t0'].t[:], t['cr'].t[:], t['are'].t[:], ALU.mult, [t['cr'], t['are']], [t['t0']])
            P.tt('dve', t['t1'].t[:], Lim.t[:, 1, :], t['aim'].t[:], ALU.mult, [Lim, t['aim']], [t['t1']])
            P.tt('dve', t['t0'].t[:], t['t0'].t[:], t['t1'].t[:], ALU.add, [t['t0'], t['t1']], [t['t0']])
            P.tt('dve', t['cfr'].t[:], t['t0'].t[:], t['inv'].t[:], ALU.mult, [t['t0'], t['inv']], [t['cfr']])
            P.tt('dve', t['t0'].t[:], Lim.t[:, 1, :], t['are'].t[:], ALU.mult, [Lim, t['are']], [t['t0']])
            P.tt('dve', t['t1'].t[:], t['cr'].t[:], t['aim'].t[:], ALU.mult, [t['cr'], t['aim']], [t['t1']])
            P.tt('dve', t['t0'].t[:], t['t0'].t[:], t['t1'].t[:], ALU.subtract, [t['t0'], t['t1']], [t['t0']])
            P.tt('dve', t['cfi'].t[:], t['t0'].t[:], t['inv'].t[:], ALU.mult, [t['t0'], t['inv']], [t['cfi']])

            def cmul(out_re, out_im, a_re, a_im, b_re, b_im, shape, Ra, Rb, Wre, Wim, neg_im=False):
                n = 1
                for d_ in shape[1:]:
                    n *= d_
                v0 = s5big[4].t[:, 0:n]
                v1 = s5big[5].t[:, 0:n]
                if len(shape) == 3:
                    v0 = v0.rearrange("p (a b) -> p a b", a=shape[1])
                    v1 = v1.rearrange("p (a b) -> p a b", a=shape[1])
                P.tt('dve', v0, a_re, b_re, ALU.mult, Ra + Rb, [s5big[4]])
                P.tt('dve', v1, a_im, b_im, ALU.mult, Ra + Rb, [s5big[5]])
                P.tt('dve', out_re, v0, v1, ALU.subtract, [s5big[4], s5big[5]], Wre)
                P.tt('dve', v0, a_re, b_im, ALU.mult, Ra + Rb, [s5big[4]])
                P.tt('dve', v1, a_im, b_re, ALU.mult, Ra + Rb, [s5big[5]])
                P.tt('dve', out_im, v0, v1, ALU.add, [s5big[4], s5big[5]], Wim)
                if neg_im:
                    P.ts('dve', out_im, out_im, -1.0, ALU.mult, Wim, Wim)

            sh3 = [128, 16, 16]
            cmul(Bre.t[:], Bim.t[:], bc(t['cfr'].t[:].unsqueeze(2), sh3), bc(t['cfi'].t[:].unsqueeze(2), sh3), bre.t[:], bim.t[:],
                 sh3, [t['cfr'], t['cfi']], [bre, bim], [Bre], [Bim])
            for (dst, src) in [(Cbr, cre), (Cbi, cim)]:
                for i4 in range(4):
                    P.tt('dve', dst.t[:, 4 * i4:4 * i4 + 4, :].rearrange("p i (g c) -> p i g c", g=2),
                         bc(src.t[:, 4 * i4:4 * i4 + 4, :].unsqueeze(2), [128, 4, 2, 16]),
                         bc(mask2.t[:].unsqueeze(1), [128, 4, 2, 16]), ALU.mult, [src, mask2], [dst])
            shb = [128, 16, 32]
            for s in range(8):
                lr = bc(Lre.t[:, s + 1, :].unsqueeze(2), shb)
                li = bc(Lim.t[:, s + 1, :].unsqueeze(2), shb)
                cr_o = s5big[0].t[:, 0:512].rearrange("p (a b) -> p a b", a=16)
                ci_o = s5big[1].t[:, 0:512].rearrange("p (a b) -> p a b", a=16)
                cmul(cr_o, ci_o, lr, li, Cbr.t[:], Cbi.t[:], shb, [Lre, Lim], [Cbr, Cbi], [s5big[0]], [s5big[1]], neg_im=True)
                P.cp('pool', Ctab.t[:, s, 0, :, :], cr_o, [s5big[0]], [Ctab])
                P.cp('pool', Ctab.t[:, s, 1, :, :], ci_o, [s5big[1]], [Ctab])
            Pre = s5big[0].t[:, 0:256].rearrange("p (a b) -> p a b", a=16)
            Pim = s5big[1].t[:, 0:256].rearrange("p (a b) -> p a b", a=16)
            Pbr = s5big[2].t[:, 0:512].rearrange("p (a b) -> p a b", a=16)
            Pbi = s5big[3].t[:, 0:512].rearrange("p (a b) -> p a b", a=16)
            Pwr = stg[0].t[:, 0:512]
            Pwi = stg[0].t[:, 512:1024]
            for j in range(8):
                lr = bc(Lre.t[:, j, :].unsqueeze(2), sh3)
                li = bc(Lim.t[:, j, :].unsqueeze(2), sh3)
                cmul(Pre, Pim, lr, li, Bre.t[:], Bim.t[:], sh3, [Lre, Lim], [Bre, Bim], [s5big[0]], [s5big[1]])
                for (dst, src, kd, ks) in [(Pbr, Pre, s5big[2], s5big[0]), (Pbi, Pim, s5big[3], s5big[1])]:
                    for i4 in range(4):
                        P.tt('dve', dst[:, 4 * i4:4 * i4 + 4, :].rearrange("p i (g c) -> p i g c", g=2),
                             bc(src[:, 4 * i4:4 * i4 + 4, :].unsqueeze(2), [128, 4, 2, 16]),
                             bc(mask2.t[:].unsqueeze(1), [128, 4, 2, 16]), ALU.mult, [ks, mask2], [kd])
                sp_ = 7 - j
                for ct in range(4):
                    for part, (src, kd) in enumerate([(Pbr, s5big[2]), (Pbi, s5big[3])]):
                        o, kk = ps(6, part * 128, 128)
                        P.tr(o, src[:, 4 * ct:4 * ct + 4, :].rearrange("p a b -> p (a b)"), ident.t[:], [kd, ident], kk)
                        P.cp('act', EB.t[:, ct, sp_, part, :], o, kk, [EB])
                    for (dstw, src, ks, sgn) in [(Pwr, Pre, s5big[0], 1.0), (Pwi, Pim, s5big[1], -1.0)]:
                        P.tt('dve', dstw.rearrange("p (a g c) -> p a g c", a=4, g=8),
                             bc(src[:, 4 * ct:4 * ct + 4, :].unsqueeze(2), [128, 4, 8, 16]), maskW.t[:], ALU.mult, [ks, maskW], [stg[0]])
                    P.ts('dve', Pwi, Pwi, -1.0, ALU.mult, [stg[0]], [stg[0]])
                    o, kk = ps(7, 256, 128)
                    for jj in range(4):
                        i = 4 * ct + jj
                        P.mm(o[:, 32 * jj:32 * jj + 32], Pwr[:, 128 * jj:128 * jj + 128], Cbr.t[:, i, :], True, False, [stg[0], Cbr], kk)
                        P.mm(o[:, 32 * jj:32 * jj + 32], Pwi[:, 128 * jj:128 * jj + 128], Cbi.t[:, i, :], False, True, [stg[0], Cbi], kk)
                    P.cp('act', KT.t[:, ct, j, :], o, kk, [KT])

        def rmsnorm(gcol0, out_bf=None, out_f32=None):
            P.act(sqb.t[:], hT.t[:], AF.Square, [hT], [sqb])
            o, kk = ps(7, 0, T)
            for k in range(8):
                P.mm(o, onesb.t[:], sqb.t[:, k, :], k == 0, k == 7, [onesb, sqb], kk)
            P.act(rstd.t[:], o, AF.Ln, kk, [rstd], scale=1.0 / D, bias=EPS)
            P.act(rstd.t[:], rstd.t[:], AF.Exp, [rstd], [rstd], scale=-0.5)
            dst = out_bf if out_bf is not None else out_f32
            for k in range(8):
                P.stt(dst.t[:, k, :], hT.t[:, k, :], colsB.t[:, gcol0 + k:gcol0 + k + 1], rstd.t[:], ALU.mult, ALU.mult,
                      [hT, colsB, rstd], [dst])

        wcount = [0]

        def ring_next():
            b = wring[wcount[0] % NRING]
            wcount[0] += 1
            return b

        def stream_w(l, g):
            b = ring_next()
            v = Buf(b.t[:].rearrange("p (k n) -> p k n", k=8), b.k)
            P.load(v.t, win_s[l][g], [b], R=['win_s%d' % l])
            return v

        pcount = [0]
        pbase = [0]

        def inproj_tile(wb, col0):
            slot = pcount[0] % 4
            pcount[0] += 1
            o, kk = ps(pbase[0] + slot, 0, T)
            for k in range(8):
                P.mm(o, wb.t[:, k, col0:col0 + 128], hnT.t[:, k, :], k == 0, k == 7, [wb, hnT], kk)
            return o, kk

        ecount = [0]

        def evac_engine():
            ecount[0] += 1
            return 'act' if ecount[0] % 2 else 'dve'

        dgc = [0]

        def conv_tile(src_buf, jt, wcols, col_of_tap, R):
            slot = pcount[0] % 4
            pcount[0] += 1
            o, kk = ps(pbase[0] + slot, 0, T)
            for k in range(4):
                d = dg[dgc[0] % 4]
                dgc[0] += 1
                c = col_of_tap(k)
                P.ts('pool', d.t[:], identb.t[:], wcols.t[:, c:c + 1], ALU.mult, [identb, wcols], [d])
                P.mm(o, d.t[:], src_buf.t[:, jt, k:k + T], k == 0, k == 3, [d] + R, kk)
            return o, kk

        def rg_chunk(l, c):
            if c == 0:
                P.memset('pool', raw.t[:, 0:4, 0:3], 0.0, ['raw_rg'])
            else:
                P.cp('pool', raw.t[:, 0:4, 0:3], histrg.t[:], [histrg], ['raw_rg'])
            for g in range(4):
                wb = stream_w(l, g)
                for n in range(2):
                    o, kk = inproj_tile(wb, n * 128)
                    j = (g % 2) * 2 + n
                    if g < 2:
                        P.cp(evac_engine(), raw.t[:, j, 3:3 + T], o, kk, ['raw_rg'])
                    else:
                        P.act(gate.t[:, j, :], o, AF.Silu, kk, ['gate_rg'])
            P.cp('pool', histrg.t[:], raw.t[:, 0:4, T:T + 3], ['raw_rg'], [histrg])
            yield 'inproj'
            def rg_tile(j):
                p_ = j % 2
                r_, gi_, xj, m_ = tf[4 * p_], tf[4 * p_ + 1], tf[4 * p_ + 2], tf[4 * p_ + 3]
                xb_ = tb[p_]
                o, kk = conv_tile(raw, j, colsB, lambda k: RB['rg_conv_w'] + k * 4 + j, ['raw_rg'])
                yield
                P.act(xj.t[:], o, AF.Identity, kk, [xj], bias=colsB.t[:, 16 + j:17 + j])
                yield
                P.cp('dve', xb_.t[:], xj.t[:], [xj], [xb_])
                yield
                oa, ka = ps(4 + 2 * p_, 0, T)
                ox, kx = ps(5 + 2 * p_, 0, T)
                P.mm(oa, rgWa.t[:, j, :], xb_.t[:], True, True, [rgWa, xb_], ka)
                P.mm(ox, rgWx.t[:, j, :], xb_.t[:], True, True, [rgWx, xb_], kx)
                yield
                P.act(r_.t[:], oa, AF.Exp, ka + [rgc], [r_], scale=-1.0, bias=rgc.t[:, 8 + j:9 + j])
                P.act(gi_.t[:], ox, AF.Exp, kx + [rgc], [gi_], scale=-1.0, bias=rgc.t[:, 12 + j:13 + j])
                yield
                P.act(r_.t[:], r_.t[:], AF.Ln, [r_], [r_], bias=1.0)
                P.act(gi_.t[:], gi_.t[:], AF.Ln, [gi_], [gi_], bias=1.0)
                yield
                P.act(r_.t[:], r_.t[:], AF.Exp, [r_], [r_], scale=-1.0)
                P.act(gi_.t[:], gi_.t[:], AF.Exp, [gi_], [gi_], scale=-1.0)
                yield
                P.tt('pool', gi_.t[:], gi_.t[:], xj.t[:], ALU.mult, [gi_, xj], [gi_])
                a_ = xj
                P.act(m_.t[:], r_.t[:], AF.Exp, [r_, rgc], [m_], scale=rgc.t[:, 4 + j:5 + j])
                yield
                P.act(a_.t[:], r_.t[:], AF.Exp, [r_, rgc, gi_], [a_], scale=rgc.t[:, j:j + 1])
                yield
                P.act(m_.t[:], m_.t[:], AF.Sqrt, [m_], [m_], scale=-1.0, bias=1.0)
                yield
                P.tt('dve', m_.t[:], m_.t[:], gi_.t[:], ALU.mult, [m_, gi_], [m_])
                yield
                hr = r_
                if c == 0:
                    P.op('dve', lambda e: e.tensor_tensor_scan(out=hr.t[:], data0=a_.t[:], data1=m_.t[:], initial=0.0,
                                                               op0=ALU.mult, op1=ALU.add), [a_, m_], [hr])
                else:
                    P.op('dve', lambda e: e.tensor_tensor_scan(out=hr.t[:], data0=a_.t[:], data1=m_.t[:],
                                                               initial=rgcar.t[:, j:j + 1], op0=ALU.mult, op1=ALU.add),
                         [a_, m_, rgcar], [hr])
                yield
                P.cp('dve', rgcar.t[:, j:j + 1], hr.t[:, T - 1:T], [hr], [rgcar])
                P.tt('pool', mixT.t[:, j, :], hr.t[:], gate.t[:, j, :], ALU.mult, [hr, 'gate_rg'], ['mix_rg'])

            for j0 in (0, 2):
                pair = [rg_tile(j0), rg_tile(j0 + 1)]
                live = [True, True]
                next(pair[0], None)
                next(pair[0], None)
                while any(live):
                    for q_ in (1, 0):
                        if live[q_]:
                            try:
                                next(pair[q_])
                            except StopIteration:
                                live[q_] = False
                yield 'pair'

        def s5_chunk(l, c):
            for g in range(4):
                wb = stream_w(l, 20 + g)
                for n in range(2):
                    o, kk = inproj_tile(wb, n * 128)
                    j = (g % 2) * 2 + n
                    if g < 2:
                        P.cp(evac_engine(), u5.t[:, j, :], o, kk, [u5])
                    else:
                        P.act(gate.t[:, 4 + j, :], o, AF.Silu, kk, ['gate_s5'])
            if c == 0:
                P.memset('pool', XPre.t[:, :, 0:1], 0.0, [XPre])
                P.memset('pool', XPim.t[:, :, 0:1], 0.0, [XPim])
            pe_ = [ps(4, 0, 512), ps(5, 0, 512)]
            for i in range(16):
                ct, jj = i // 4, i % 4
                uv = u5.t[32 * jj:32 * jj + 32, ct, :].rearrange("p (c s) -> p s c", s=L5)
                for part in range(2):
                    o, kk = pe_[part]
                    for s_ in range(L5):
                        P.mm(o[:, i * CB:(i + 1) * CB], EB.t[32 * jj:32 * jj + 32, ct, s_, part, :], uv[:, s_, :],
                             s_ == 0, s_ == L5 - 1, [EB, u5], kk, tile_position=(32 * jj, 0))
            ere, kre = pe_[0]
            eim, kim = pe_[1]
            t1, t2, mre, mim, qre, qim = s5big[0], s5big[1], s5big[2], s5big[3], s5big[4], s5big[5]
            dcs = Dcos.t[:].rearrange("p i c -> p (i c)")
            dsn = Dsin.t[:].rearrange("p i c -> p (i c)")
            P.tt('dve', t1.t[:], ere, dcs, ALU.mult, kre + [Dcos], [t1])
            P.tt('dve', t2.t[:], eim, dsn, ALU.mult, kim + [Dsin], [t2])
            P.tt('pool', mre.t[:], t1.t[:], t2.t[:], ALU.add, [t1, t2], [mre])
            P.tt('dve', t1.t[:], eim, dcs, ALU.mult, kim + [Dcos], [t1])
            P.tt('dve', t2.t[:], ere, dsn, ALU.mult, kre + [Dsin], [t2])
            P.tt('pool', mim.t[:], t1.t[:], t2.t[:], ALU.subtract, [t1, t2], [mim])
            for (m_, XP) in [(mre, XPre), (mim, XPim)]:
                mv = m_.t[:].rearrange("p (i c) -> p i c", i=16)
                P.tt('pool', s5t['t0'].t[:].unsqueeze(2), rhoc.t[:].unsqueeze(2), XP.t[:, :, 0:1], ALU.mult, [rhoc, XP], [s5t['t0']])
                P.tt('pool', mv[:, :, 0:1], mv[:, :, 0:1], s5t['t0'].t[:].unsqueeze(2), ALU.add, [m_, s5t['t0']], [m_])
            rhf = Rho.t[:].rearrange("p i c -> p (i c)")
            for (m_, q_) in [(mre, qre), (mim, qim)]:
                P.op('dve', lambda e, m_=m_, q_=q_: e.tensor_tensor_scan(out=q_.t[:], data0=rhf, data1=m_.t[:], initial=0.0,
                                                                        op0=ALU.mult, op1=ALU.add), [Rho, m_], [q_])
            qrv = qre.t[:].rearrange("p (i c) -> p i c", i=16)
            qiv = qim.t[:].rearrange("p (i c) -> p i c", i=16)
            t1v = t1.t[:].rearrange("p (i c) -> p i c", i=16)
            t2v = t2.t[:].rearrange("p (i c) -> p i c", i=16)
            P.tt('dve', t1v, qrv, Dcos.t[:], ALU.mult, [qre, Dcos], [t1])
            P.tt('pool', t2v, qiv, Dsin.t[:], ALU.mult, [qim, Dsin], [t2])
            P.tt('dve', XPre.t[:, :, 1:CB + 1], t1v, t2v, ALU.subtract, [t1, t2], [XPre])
            P.tt('dve', t1v, qrv, Dsin.t[:], ALU.mult, [qre, Dsin], [t1])
            P.tt('pool', t2v, qiv, Dcos.t[:], ALU.mult, [qim, Dcos], [t2])
            P.tt('dve', XPim.t[:, :, 1:CB + 1], t1v, t2v, ALU.add, [t1, t2], [XPim])
            P.cp('pool', XPb.t[:, 0, :, :], XPre.t[:, :, 0:CB], [XPre], [XPb])
            P.cp('pool', XPb.t[:, 1, :, :], XPim.t[:, :, 0:CB], [XPim], [XPb])
            P.cp('pool', XPre.t[:, :, 0:1], XPre.t[:, :, CB:CB + 1], [XPre, XPb], [XPre])
            P.cp('pool', XPim.t[:, :, 0:1], XPim.t[:, :, CB:CB + 1], [XPim, XPb], [XPim])
            y5t = [Buf(G[3].t[:, ct_ * T:(ct_ + 1) * T], G[3].k) for ct_ in range(4)]
            yield 'part1'
            for ct in range(4):
                if ct == 2:
                    yield 'y01'
                o, kk = ps(6 + (ct % 2), 0, T)
                uv = u5.t[:, ct, :].rearrange("p (c s) -> p s c", s=L5)
                for s_ in range(L5):
                    oc = o[:, s_ * CB:(s_ + 1) * CB]
                    for sp_ in range(s_ + 1):
                        P.mm(oc, KT.t[:, ct, s_ - sp_, :], uv[:, sp_, :], sp_ == 0, False, [KT, u5], kk)
                    for jj in range(4):
                        i = 4 * ct + jj
                        for part in range(2):
                            P.mm(o[32 * jj:32 * jj + 32, s_ * CB:(s_ + 1) * CB], Ctab.t[:, s_, part, i, :], XPb.t[:, part, i, :],
                                 False, (part == 1), [Ctab, XPb], kk, tile_position=(0, 32 * jj))
                P.stt(y5t[ct].t.rearrange("p (c s) -> p s c", s=L5), uv, colsB.t[:, 32 + ct:33 + ct],
                      o.rearrange("p (s c) -> p s c", s=L5), ALU.mult, ALU.add, [u5, colsB] + kk, [y5t[ct]])
            yield 'y23'
            def gelu_tile(ct):
                y_, z_ = y5t[ct], tf[4 + ct]
                P.tt('pool', z_.t[:], y_.t, y_.t, ALU.mult, [y_], [z_])
                yield
                P.ts('pool', z_.t[:], z_.t[:], 0.044715, ALU.mult, [z_], [z_], s2=1.0, op1=ALU.add)
                yield
                P.tt('pool', z_.t[:], z_.t[:], y_.t, ALU.mult, [z_, y_], [z_])
                yield
                P.act(z_.t[:], z_.t[:], AF.Sigmoid, [z_], [z_], scale=1.5957691216057308)
                yield
                P.tt('dve', z_.t[:], z_.t[:], y_.t, ALU.mult, [z_, y_], [z_])
                yield
                P.cp('pool', z5b.t[:, ct, :], z_.t[:], [z_], [z5b])

            def rr(gens):
                live = [True] * len(gens)
                while any(live):
                    for q_ in range(len(gens)):
                        if live[q_]:
                            try:
                                next(gens[q_])
                            except StopIteration:
                                live[q_] = False

            rr([gelu_tile(ct) for ct in range(4)])

            def glu_tile(m):
                slot = pcount[0] % 4
                pcount[0] += 1
                o, kk = ps(slot, 0, T)
                for k in range(4):
                    P.mm(o, wglu.t[:, k, m * 128:(m + 1) * 128], z5b.t[:, k, :], k == 0, k == 3, [wglu, z5b], kk)
                yield
                gl = tf[m]
                P.act(gl.t[:], o, AF.Sigmoid, kk, [gl], bias=colsB.t[:, 36 + m:37 + m])
                yield
                P.tt('pool', gl.t[:], gl.t[:], tf[4 + m].t[:], ALU.mult, [gl, tf[4 + m]], [gl])
                yield
                P.tt('dve', mixT.t[:, 12 + m, :], gl.t[:], gate.t[:, 4 + m, :], ALU.mult, [gl, 'gate_s5'], ['mix_s5'])

            rr([glu_tile(m) for m in range(4)])

        def gdn_chunk(l, c):
            if c == 0:
                P.memset('pool', raw.t[:, :, 0:3], 0.0, ['raw_rg', 'raw_g0', 'raw_g1', 'raw_g2'])
                P.memset('pool', Sst.t[:], 0.0, [Sst])
                P.memset('pool', Sb.t[:], 0.0, [Sb])
            else:
                P.cp('pool', raw.t[:, :, 0:3], histg.t[:], [histg], ['raw_rg', 'raw_g0', 'raw_g1', 'raw_g2'])
            P.load(wba.t[:], wba_s[l], [wba], R=['wba_s%d' % l])
            def rawk(j):
                return (['raw_rg'] if j < 4 else []) + ['raw_g%d' % (j // 8)]

            def inproj_q(qtr):
                for g in range(4 * qtr, 4 * qtr + 4):
                    wb = stream_w(l, 4 + g)
                    for n in range(2):
                        o, kk = inproj_tile(wb, n * 128)
                        j = g * 2 + n
                        if j < 24:
                            P.cp(evac_engine(), raw.t[:, j, 3:3 + T], o, kk, rawk(j))
                        else:
                            P.act(gate.t[:, j - 24, :], o, AF.Silu, kk, ['gate_rg', 'gate_s5', 'gate_g'])
                    yield 'x'

            def conv_q(qtr):
                P.cp('pool', histg.t[:, 8 * qtr:8 * qtr + 8, :], raw.t[:, 8 * qtr:8 * qtr + 8, T:T + 3], ['raw_rg', 'raw_g%d' % qtr], [histg])
                def conv_norm_tile(j):
                    o, kk = conv_tile(raw, j, gcw, lambda k: k * 24 + j, rawk(j))
                    yield
                    sl, lv, rs_ = tf[(j % 2) * 3], tf[(j % 2) * 3 + 1], tf[(j % 2) * 3 + 2]
                    P.act(lv.t[:], o, AF.Exp, kk, [lv], scale=-1.0)
                    yield
                    P.act(lv.t[:], lv.t[:], AF.Ln, [lv], [lv], bias=1.0)
                    yield
                    P.act(lv.t[:], lv.t[:], AF.Exp, [lv], [lv], scale=-1.0)
                    yield
                    if j >= 16:
                        P.tt('dve', vs.t[:, j - 16, :], o, lv.t[:], ALU.mult, kk + [lv], [vs])
                        return
                    sq_ = tb[j % 2]
                    P.tt('dve', sl.t[:], o, lv.t[:], ALU.mult, kk + [lv], [sl])
                    yield
                    P.tt('pool', sq_.t[:], sl.t[:], sl.t[:], ALU.mult, [sl], [sq_])
                    yield
                    o2, k2 = ps(4 + (j % 2), 0, T)
                    P.mm(o2, onesb.t[:], sq_.t[:], True, True, [onesb, sq_], k2)
                    yield
                    P.act(lv.t[:], o2, AF.Ln, k2, [lv], bias=EPS)
                    yield
                    if j < 8:
                        P.act(rs_.t[:], lv.t[:], AF.Exp, [lv], [rs_], scale=-0.5, bias=-0.5 * math.log(128.0))
                        yield
                        P.tt('dve', qn.t[:, j, :], sl.t[:], rs_.t[:], ALU.mult, [sl, rs_], [qn])
                    else:
                        P.act(rs_.t[:], lv.t[:], AF.Exp, [lv], [rs_], scale=-0.5)
                        yield
                        P.tt('dve', kn.t[:, j - 8, :], sl.t[:], rs_.t[:], ALU.mult, [sl, rs_], [kn])

                for j0 in range(8 * qtr, 8 * qtr + 8, 2):
                    pair = [conv_norm_tile(j0), conv_norm_tile(j0 + 1)]
                    live = [True, True]
                    next(pair[0], None)
                    next(pair[0], None)
                    while any(live):
                        for q_ in (1, 0):
                            if live[q_]:
                                try:
                                    next(pair[q_])
                                except StopIteration:
                                    live[q_] = False
                        yield 'x'

            def drain(gen):
                for _ in gen:
                    pass

            drain(inproj_q(0))
            drain(inproj_q(1))
            drain(conv_q(0))
            drain(inproj_q(2))
            drain(conv_q(1))
            if gdn_stop < 7 and c == 0 and l == 0:
                P.memset('pool', mixT.t[:, 4:12, :], 0.0, ['mix_g'])
            if gdn_stop >= 1:
                gdn_scalars(l, c)
            gens = [gdn_inner(l, c, gci) for gci in range(T // GC)]

            def tail_x():
                for t_ in inproj_q(3):
                    yield t_
                for t_ in conv_q(2):
                    yield t_

            def tail_x_banked():
                g_ = tail_x()
                while True:
                    pbase[0] = 4
                    try:
                        t_ = next(g_)
                    except StopIteration:
                        pbase[0] = 0
                        return
                    pbase[0] = 0
                    yield t_

            def run_until(gen, tags):
                while True:
                    try:
                        t_ = next(gen)
                    except StopIteration:
                        return
                    if t_ in tags:
                        return

            def rr_until(g1, tags1, g2, tags2):
                d1 = d2 = False
                while not (d1 and d2):
                    if not d2:
                        try:
                            d2 = next(g2) in tags2
                        except StopIteration:
                            d2 = True
                    if not d1:
                        try:
                            d1 = next(g1) in tags1
                        except StopIteration:
                            d1 = True

            n_i = T // GC
            rr_until(gens[0], ('A_done',), tail_x_banked(), ())
            run_until(gens[0], ('AB',))
            for gci in range(n_i):
                if gci + 1 < n_i:
                    rr_until(gens[gci], ('C',), gens[gci + 1], ('A_done',))
                    rr_until(gens[gci + 1], ('AB',), gens[gci], ())
                else:
                    run_until(gens[gci], ('C',))
                    run_until(gens[gci], ())

        def gdn_scalars(l, c):
            g = gsm
            v3 = lambda ap: ap.rearrange("p (i h) -> p i h", i=NI)
            o, kk = ps(0, 0, NI * 16, 0, 64)
            for gci in range(NI):
                for k in range(8):
                    P.mm(o[:, gci * 16:(gci + 1) * 16], hnT.t[:, k, gci * GC:(gci + 1) * GC], wba.t[:, k, :], k == 0, k == 7, [hnT, wba], kk)
            ov = o.rearrange("p (i c) -> p i c", i=NI)
            P.act(v3(g['bt'].t[:]), ov[:, :, 0:8], AF.Sigmoid, kk, [g['bt']])
            P.tt('dve', v3(g['xa'].t[:]), ov[:, :, 8:16], bc(dtb.t[:].unsqueeze(1), [64, NI, 8]), ALU.add, kk + [dtb], [g['xa']])
            P.act(g['ex'].t[:], g['xa'].t[:], AF.Exp, [g['xa']], [g['ex']])
            P.act(g['sp'].t[:], g['ex'].t[:], AF.Ln, [g['ex']], [g['sp']], bias=1.0)
            P.tt('dve', v3(g['gp'].t[:]), v3(g['sp'].t[:]), bc(negA.t[:].unsqueeze(1), [64, NI, 8]), ALU.mult, [g['sp'], negA], [g['gp']])
            P.ts('dve', g['g'].t[:], g['gp'].t[:], -1.0, ALU.mult, [g['gp']], [g['g']])
            P.ts('pool', g['nbt'].t[:], g['bt'].t[:], -1.0, ALU.mult, [g['bt']], [g['nbt']])
            oc, kc = ps(0, 64, NI * 8, 0, 64)
            P.mm(oc, U64.t[:], g['g'].t[:], True, True, [U64, g['g']], kc)
            P.cp('dve', g['gcum'].t[:], oc, kc, [g['gcum']])
            ol, kl = ps(0, 128, NI * 8)
            P.mm(ol, sel63.t[:], g['gcum'].t[:], True, True, [sel63, g['gcum']], kl)
            P.act(egl.t[:], ol, AF.Exp, kl, [egl])
            P.act(g['egc'].t[:], g['gcum'].t[:], AF.Exp, [g['gcum']], [g['egc']])
            P.tt('dve', g['dgl'].t[:], ol[0:64, :], g['gcum'].t[:], ALU.subtract, kl + [g['gcum']], [g['dgl']])
            P.act(g['kds'].t[:], g['dgl'].t[:], AF.Exp, [g['dgl']], [g['kds']])
            P.tt('dve', g['bge'].t[:], g['bt'].t[:], g['egc'].t[:], ALU.mult, [g['bt'], g['egc']], [g['bge']])

        def hb(ap, n):
            return bc(ap.unsqueeze(2), [64, 8, n])

        def m8(ap64):
            return bc(ap64.unsqueeze(1), [64, 8, 64])

        def gdn_inner(l, c, gci):
            t0 = gci * GC
            cs = slice(t0, t0 + GC)
            fl = lambda b_: b_.t[:].rearrange("p h j -> p (h j)")
            aqkT = aqkTs[gci % 2]
            if gdn_stop < 1:
                return
            g = {n: (Buf(gsm[n].t[:, gci * 8:(gci + 1) * 8], gsm[n].k) if n not in ('ss', 'ln', 'rs') else Buf(gsm[n].t[:], gsm[n].k)) for n in gsm}
            eglv = egl.t[:, gci * 8:(gci + 1) * 8]
            if gdn_stop < 2:
                return
            NGU, GBC, E1, E2, DK = gw[0], gw[1], gw[2], gw[3], gw[4]
            P.tt('dve', NGU.t[:], m8(U64.t[:]), hb(g['gp'].t, 64), ALU.mult, [U64, g['gp']], [NGU])
            P.cp('pool', GBC.t[:], hb(g['g'].t, 64), [g['g']], [GBC])
            oD, kD = ps(1, 0, 512, 0, 64)
            P.mm(oD, U64.t[:], fl(GBC), True, False, [U64, GBC], kD)
            P.mm(oD, ones64.t[:], fl(NGU), False, True, [ones64, NGU], kD)
            yield 'a'
            P.ts('dve', fl(E1), oD, 0.0, ALU.min, kD, [E1])
            P.ts('dve', fl(E2), oD, -1.0, ALU.mult, kD, [E2], s2=0.0, op1=ALU.min)
            P.act(fl(E1), fl(E1), AF.Exp, [E1], [E1])
            P.act(fl(E2), fl(E2), AF.Exp, [E2], [E2])
            yield 'a'
            P.tt('pool', DK.t[:], m8(Mst.t[:]), hb(g['nbt'].t, 64), ALU.mult, [Mst, g['nbt']], [DK])
            P.tt('pool', DK.t[:], DK.t[:], E1.t[:], ALU.mult, [DK, E1], [DK])
            E2u = E2
            DQ = gw[6]
            P.tt('pool', DQ.t[:], E2u.t[:], m8(U64.t[:]), ALU.mult, [E2u, U64], [DQ])
            yield 'a'
            if gdn_stop < 3:
                return
            Bd = gw[1]
            P.tt('dve', Bd.t[:], m8(ident.t[0:64, 0:64]), hb(g['nbt'].t, 64), ALU.mult, [ident, g['nbt']], [Bd])
            oB, kB = ps(1, 0, 512, 0, 64)
            P.mm(oB, ones64.t[:], fl(Bd), True, True, [ones64, Bd], kB)
            MK = gw[1]
            P.tt('dve', MK.t[:], E2u.t[:], m8(Ust.t[:]), ALU.mult, [E2u, Ust], [MK])
            P.tt('dve', fl(MK), fl(MK), oB, ALU.mult, [MK] + kB, [MK])
            okk, kkk = ps(2, 0, 512, 0, 64)
            oqk, kqk = ps(3, 0, 512, 0, 64)
            for h in range(8):
                P.mm(okk[:, h * 64:(h + 1) * 64], kn.t[:, h, cs], kn.t[:, h, cs], True, True, [kn], kkk)
            for h in range(8):
                P.mm(oqk[:, h * 64:(h + 1) * 64], kn.t[:, h, cs], qn.t[:, h, cs], True, True, [kn, qn], kqk)
            def bfv(b_):
                return Buf(b_.t[:].rearrange("p h j -> p (h j)").bitcast(BF16)[:, 0:512].rearrange("p (h j) -> p h j", h=8), b_.k)
            CH_BF16 = True
            if CH_BF16:
                Nb = [bfv(gw[5]), bfv(gw[6])]
                Mb = [bfv(gw[0]), bfv(gw[1])]
                PTb = [bfv(gw[2]), bfv(gw[3])]
            else:
                Nb = [gw[5], gw[6]]
                Mb = [gw[0], gw[1]]
                PTb = [gw[2], gw[4]]
            P.tt('dve', fl(Nb[0]), okk, fl(DK), ALU.mult, kkk + [DK], [Nb[0]])
            P.tt('dve', fl(aqkT), oqk, fl(DQ), ALU.mult, kqk + [DQ], [aqkT])
            yield 'a'
            if gdn_stop < 4:
                return
            P.tt('dve', fl(Mb[0]), okk, fl(MK), ALU.mult, kkk + [MK], [Mb[0]])
            P.tt('dve', PTb[0].t[:], Mb[0].t[:], m8(ident.t[0:64, 0:64]), ALU.add, [Mb[0], ident], [PTb[0]])
            cur = 0
            for lev in range(1, 6):
                nxt = 1 - cur
                oN, kN = ps(1, 0, 512, 0, 64)
                for h in range(8):
                    P.mm(oN[:, h * 64:(h + 1) * 64], Mb[cur].t[:, h, :], Nb[cur].t[:, h, :], True, True, [Mb[cur], Nb[cur]], kN)
                if lev < 5:
                    oM, kM = ps(2, 0, 512, 0, 64)
                    for h in range(8):
                        P.mm(oM[:, h * 64:(h + 1) * 64], Nb[cur].t[:, h, :], Mb[cur].t[:, h, :], True, True, [Mb[cur], Nb[cur]], kM)
                P.cp('act', fl(Nb[nxt]), oN, kN, [Nb[nxt]])
                if lev < 5:
                    P.cp('dve', fl(Mb[nxt]), oM, kM, [Mb[nxt]])
                oP, kP = ps(3, 0, 512, 0, 64)
                for h in range(8):
                    P.mm(oP[:, h * 64:(h + 1) * 64], Nb[nxt].t[:, h, :], PTb[cur].t[:, h, :], True, True, [Nb[nxt], PTb[cur]], kP)
                P.tt('dve', fl(PTb[nxt]), oP, fl(PTb[cur]), ALU.add, kP + [PTb[cur]], [PTb[nxt]])
                cur = nxt
                yield 'a'
            PT = PTb[cur]
            yield 'A_done'
            if gdn_stop < 5:
                return
            okt, kkt = psbf(4, 0, 1024, 0, 64)
            ovt, kvt = psbf(5, 0, 1024, 0, 64)
            for h in range(8):
                P.tr(okt[:, h * 128:(h + 1) * 128], kn.t[:, h, cs], identb.t[:], [kn, identb], kkt)
            for h in range(8):
                P.tr(ovt[:, h * 128:(h + 1) * 128], vs.t[:, h, cs], identb.t[:], [vs, identb], kvt)
            yield 'b'
            kbg, vb, usb, ob = gx[0], gx[1], gx[2], gx[3]
            if CH_BF16:
                bx = lambda b_: Buf(b_.t[:].rearrange("p h e -> p (h e)").bitcast(BF16)[:, 0:1024].rearrange("p (h e) -> p h e", h=8), b_.k)
                kbg, vb = bx(gx[0]), bx(gx[1])
            fx = lambda b_: b_.t[:].rearrange("p h e -> p (h e)")
            v3 = lambda ap: ap.rearrange("p (h e) -> p h e", h=8)
            P.tt('dve', kbg.t[:], v3(okt), hb(g['bge'].t, 128), ALU.mult, kkt + [g['bge']], [kbg])
            P.tt('dve', kdec.t[:], v3(okt), hb(g['kds'].t, 128), ALU.mult, kkt + [g['kds']], [kdec])
            P.tt('dve', vb.t[:], v3(ovt), hb(g['bt'].t, 128), ALU.mult, kvt + [g['bt']], [vb])
            yield 'b'
            ou, ku = ps(6, 0, 1024, 0, 64)
            for h in range(8):
                P.mm(ou[:, h * 128:(h + 1) * 128], PT.t[:, h, :], vb.t[:, h, :], True, True, [PT, vb], ku)
            ow, kw = ps(0, 0, 512)
            for h in range(8):
                P.mm(ow[:, h * 64:(h + 1) * 64], kbg.t[:, h, :], PT.t[:, h, :], True, True, [PT, kbg], kw)
            yield 'b'
            P.cp('act', fx(usb), ou, ku, [usb])
            P.cp('act', wTb.t[:].rearrange("p h c -> p (h c)"), ow, kw, [wTb])
            if gdn_stop < 6:
                return
            yield 'AB'
            o1, k1 = ps(4, 0, 1024, 0, 64)
            for h in range(8):
                P.mm(o1[:, h * 128:(h + 1) * 128], wTb.t[:, h, :], Sb.t[:, h, :], True, True, [wTb, Sb], k1)
            o2, k2 = ps(6, 0, 1024, 0, 64)
            for h in range(8):
                P.mm(o2[:, h * 128:(h + 1) * 128], qn.t[:, h, cs], Sb.t[:, h, :], True, True, [qn, Sb], k2)
            yield 'c'
            P.tt('dve', fx(vnew), fx(usb), o1, ALU.subtract, [usb] + k1, [vnew])
            P.tt('dve', ob.t[:], v3(o2), hb(g['egc'].t, 128), ALU.mult, k2 + [g['egc']], [ob])
            yield 'c'
            o3, k3 = ps(4, 0, 1024, 0, 64)
            for h in range(8):
                P.mm(o3[:, h * 128:(h + 1) * 128], aqkT.t[:, h, :], vnew.t[:, h, :], True, True, [aqkT, vnew], k3)
            o4, k4 = ps(6, 0, 1024)
            for h in range(8):
                P.mm(o4[:, h * 128:(h + 1) * 128], kdec.t[:, h, :], vnew.t[:, h, :], True, True, [kdec, vnew], k4)
            yield 'c'
            P.tt('dve', fx(ob), fx(ob), o3, ALU.add, [ob] + k3, [ob])
            for h in range(8):
                P.stt(Sst.t[:, h, :], Sst.t[:, h, :], eglv[:, h:h + 1], o4[:, h * 128:(h + 1) * 128], ALU.mult, ALU.add,
                      [Sst, egl] + k4, [Sst])
            P.cp('pool', Sb.t[:], Sst.t[:], [Sst], [Sb])
            if gdn_stop < 7:
                return
            yield 'C'
            bview = lambda b_: Buf(b_.t[:].rearrange("p h j -> p (h j)").bitcast(BF16).rearrange("p (h e) -> p h e", h=8), b_.k)
            osq = bview(gw[1])
            P.tt('pool', osq.t, ob.t[:], ob.t[:], ALU.mult, [ob], [osq])
            yield 'd'
            P.op('dve', lambda e: e.tensor_reduce(out=g['ss'].t, in_=osq.t, axis=AX.X, op=ALU.add), [osq], [g['ss']])
            yield 'd'
            P.act(g['ln'].t, g['ss'].t, AF.Ln, [g['ss']], [g['ln']], scale=1.0 / 128.0, bias=EPS)
            yield 'd'
            P.act(g['rs'].t, g['ln'].t, AF.Exp, [g['ln']], [g['rs']], scale=-0.5)
            yield 'd'
            on = bview(gw[0])
            P.tt('dve', on.t, ob.t[:], hb(g['rs'].t, 128), ALU.mult, [ob, g['rs']], [on])
            yield 'd'
            oo, ko = psbf(1, 0, 512)
            for h in range(8):
                P.tr(oo[:, h * 64:(h + 1) * 64], on.t[:, h, :], identb.t[0:64, 0:64], [on, identb], ko)
            P.stt(mixT.t[:, 4:12, cs], oo.rearrange("p (h c) -> p h c", h=8), colsB.t[:, 40:41], gate.t[:, :, cs],
                  ALU.mult, ALU.mult, ko + [colsB, 'gate_g'], ['mix_g'])

        ocount = [0]

        def chunk(l, c):
            tok0 = c * T
            last = (l == depth - 1)
            if l == 0:
                for a_ in range(2):
                    P.load(stg[a_].t[:, 0:1024], x_d[tok0 + a_ * 128:tok0 + (a_ + 1) * 128, :], [stg[a_]], 'stg%d' % a_)
                for half in range(2):
                    o, kk = ps(6, 0, 1024)
                    for k4 in range(4):
                        for a_ in range(2):
                            kt_ = 4 * half + k4
                            P.tr(o[:, k4 * 256 + a_ * 128:k4 * 256 + a_ * 128 + 128], stg[a_].t[:, kt_ * 128:(kt_ + 1) * 128], ident.t[:],
                                 [stg[a_], ident], kk)
                    P.cp('act' if half else 'dve', hT.t[:, 4 * half:4 * half + 4, :].rearrange("p k t -> p (k t)"), o, kk, [hT])
            else:
                P.load(hT.t[:], hscr.rearrange("k p t -> p k t")[:, :, tok0:tok0 + T], [hT], 'hT', R=['hscr'])
            rmsnorm(RB['norm_g'], out_bf=hnT)
            if not use_rg and c == 0 and l == 0:
                P.memset('pool', mixT.t[:, 0:4, :], 0.0, ['mix_rg'])
            if not use_s5 and c == 0 and l == 0:
                P.memset('pool', mixT.t[:, 12:16, :], 0.0, ['mix_s5'])
            gr = rg_chunk(l, c) if use_rg else iter(())
            g5 = s5_chunk(l, c) if use_s5 else iter(())
            next(gr, None)
            next(g5, None)
            next(gr, None)
            next(g5, None)
            next(gr, None)
            for _ in gr:
                pass
            for _ in g5:
                pass
            if use_gdn:
                gdn_chunk(l, c)
            elif c == 0 and l == 0:
                P.memset('pool', mixT.t[:, 4:12, :], 0.0, ['mix_g'])
            for m in range(8):
                b_ = ring_next()
                wo = Buf(b_.t[:].rearrange("p (k n) -> p k n", k=16), b_.k)
                P.load(wo.t, wout_s[l][m], [b_], R=['wout_s%d' % l])
                slot = pcount[0] % 4
                pcount[0] += 1
                o, kk = ps(slot, 0, T)
                for k in range(16):
                    mk = 'mix_rg' if k < 4 else ('mix_g' if k < 12 else 'mix_s5')
                    P.mm(o, wo.t[:, k, :], mixT.t[:, k, :], k == 0, k == 15, [wo, mk], kk)
                P.tt('dve', hT.t[:, m, :], hT.t[:, m, :], o, ALU.add, [hT] + kk, [hT])
            rmsnorm(RB['ple_norm_g'], out_bf=hnT)
            ptok = stg[1]
            pv = ptok.t[:, 1024:1536].rearrange("p (a d) -> p a d", a=2)
            P.load(pv, p_d[l, tok0:tok0 + T, :].rearrange("(a p) d -> p a d", p=128), [ptok], 'stg1')
            o, kk = ps(6, 0, 512)
            for k in range(2):
                for a in range(2):
                    P.tr(o[:, k * 256 + a * 128:k * 256 + a * 128 + 128], pv[:, a, k * 128:(k + 1) * 128], ident.t[:], [ptok, ident], kk)
            P.cp('act', pT.t[:].rearrange("p k t -> p (k t)"), o, kk, [pT])
            for m in range(8):
                b_ = ring_next()
                wg = Buf(b_.t[:, 0:1024].rearrange("p (k n) -> p k n", k=8), b_.k)
                P.load(wg.t, wgate_s[l][m], [b_], R=['wgate_s%d' % l])
                slot = pcount[0] % 4
                pcount[0] += 1
                o, kk = ps(slot, 0, T)
                for k in range(8):
                    P.mm(o, wg.t[:, k, :], hnT.t[:, k, :], k == 0, k == 7, [wg, hnT], kk)
                gt_ = tf[m % 2]
                P.act(gt_.t[:], o, AF.Sigmoid, kk, [gt_])
                o2, k2 = ps(4 + m % 2, 0, T)
                for k in range(2):
                    P.mm(o2, wproj.t[:, k, m * 128:(m + 1) * 128], pT.t[:, k, :], k == 0, k == 1, [wproj, pT], k2)
                P.tt('dve', gt_.t[:], gt_.t[:], o2, ALU.mult, [gt_] + k2, [gt_])
                P.tt('pool', hT.t[:, m, :], hT.t[:, m, :], gt_.t[:], ALU.add, [hT, gt_], [hT])
            if not last:
                P.store(hscr.rearrange("k p t -> p k t")[:, :, tok0:tok0 + T], hT.t[:], hT, 'hscr')
            else:
                P.act(sqb.t[:], hT.t[:], AF.Square, [hT], [sqb])
                o, kk = ps(7, 0, T)
                for k in range(8):
                    P.mm(o, onesb.t[:], sqb.t[:, k, :], k == 0, k == 7, [onesb, sqb], kk)
                P.act(rstd.t[:], o, AF.Ln, kk, [rstd], scale=1.0 / D, bias=EPS)
                P.act(rstd.t[:], rstd.t[:], AF.Exp, [rstd], [rstd], scale=-0.5)
                hf = stg[0]
                hfv = hf.t[:, 0:1024].rearrange("p (k t) -> p k t", k=4)
                otok = stg[1]
                for half in range(2):
                    for k4 in range(4):
                        kt_ = 4 * half + k4
                        P.stt(hfv[:, k4, :], hT.t[:, kt_, :], colsB.t[:, 57 + kt_:58 + kt_], rstd.t[:], ALU.mult, ALU.mult,
                              [hT, colsB, rstd], [hf])
                    for a_ in range(2):
                        o, kk = ps(6 + a_, 0, 512)
                        for k4 in range(4):
                            P.tr(o[:, k4 * 128:(k4 + 1) * 128], hfv[:, k4, a_ * 128:(a_ + 1) * 128], ident.t[:], [hf, ident], kk)
                        P.cp('act' if a_ else 'dve', otok.t[:, a_ * 512:(a_ + 1) * 512], o, kk, [otok])
                    P.store(out_d[tok0:tok0 + T, half * 512:(half + 1) * 512].rearrange("(a p) d -> p a d", p=128),
                            otok.t[:, 0:1024].rearrange("p (a d) -> p a d", a=2), otok, 'out')

        cnt = [0]
        NPSTG = 2
        pstg = [stg[0], stg[1],
                Buf(hT.t[:].rearrange("p k t -> p (k t)")[:, 0:1536], hT.k),
                Buf(mixT.t[:].rearrange("p a t -> p (a t)").bitcast(F32)[:, 0:1536], ('mix_rg', 'mix_g', 'mix_s5'))]
        pstgb = [stgb[0], stgb[1],
                 Buf(qn.t[:].rearrange("p k t -> p (k t)"), qn.k),
                 Buf(kn.t[:].rearrange("p k t -> p (k t)"), kn.k)]

        def prep_piece(src_ap, ncols, stores):
            i = cnt[0] % NPSTG
            cnt[0] += 1
            sf, sbf = pstg[i], pstgb[i]
            P.load(sf.t[:, 0:ncols], src_ap, [sf])
            P.cp('dve' if cnt[0] % 2 else 'pool', sbf.t[:, 0:ncols], sf.t[:, 0:ncols], [sf], [sbf])
            for (dst, c0, w, key) in stores:
                srcv = sbf.t[:, c0:c0 + w]
                if len(dst.shape) == 3:
                    srcv = srcv.rearrange("p (g j) -> p g j", g=dst.shape[1])
                P.store(dst, srcv, sbf, key)

        for l in range(depth):
            w_in = prm['w_in'][l]
            for r in range(8):
                rows = slice(r * 128, (r + 1) * 128)
                prep_piece(w_in[rows, 0:1024], 1024, [(win_s[l][0:4, :, r, :].rearrange("g p j -> p g j"), 0, 1024, 'win_s%d' % l)])
                prep_piece(w_in[rows, 1024:2560], 1536, [(win_s[l][4:10, :, r, :].rearrange("g p j -> p g j"), 0, 1536, 'win_s%d' % l)])
                prep_piece(w_in[rows, 2560:4096], 1536, [(win_s[l][10:16, :, r, :].rearrange("g p j -> p g j"), 0, 1536, 'win_s%d' % l)])
                prep_piece(w_in[rows, 4096:5120], 1024, [(win_s[l][16:20, :, r, :].rearrange("g p j -> p g j"), 0, 1024, 'win_s%d' % l)])
                prep_piece(w_in[rows, 5120:6160], 1040, [(wba_s[l][:, r, :], 0, 16, 'wba_s%d' % l),
                                                         (win_s[l][20:24, :, r, :].rearrange("g p j -> p g j"), 16, 1024, 'win_s%d' % l)])
            for r in range(16):
                prep_piece(prm['w_out'][l][r * 128:(r + 1) * 128, :], 1024,
                           [(wout_s[l][:, :, r, :].rearrange("m p j -> p m j"), 0, 1024, 'wout_s%d' % l)])
            for r in range(8):
                prep_piece(prm['ple_w_gate'][l][r * 128:(r + 1) * 128, :], 1024,
                           [(wgate_s[l][:, :, r, :].rearrange("m p j -> p m j"), 0, 1024, 'wgate_s%d' % l)])


        for l in range(depth):
            load_layer_consts(l)
            for c in range(NCH):
                chunk(l, c)
        P.final_wait('act', ['out'])
        P.emit()
    return nc


_CACHE = {}


def kernel(**inputs):
    B = inputs['x'].shape[0]
    S = inputs['x'].shape[1]
    depth = inputs['p'].shape[0]
    key = (S, depth)
    if key not in _CACHE:
        _CACHE[key] = build_program(S, depth)
    nc = _CACHE[key]
    shared = {name: np.ascontiguousarray(inputs[name], dtype=np.float32) for name, _ in PARAM_SHAPES(depth)}
    in_maps = []
    for b in range(B):
        m = dict(shared)
        m['x'] = np.ascontiguousarray(inputs['x'][b], dtype=np.float32)
        m['p'] = np.ascontiguousarray(inputs['p'][:, b], dtype=np.float32)
        in_maps.append(m)
    res = run_bass_kernel_spmd(nc, in_maps, core_ids=list(range(B)))
    return np.stack([r['out'] for r in res.results], axis=0).astype(np.float32)
```
